# Optimizing a Trainium2 kernel written in Bass

```python
import math
import jax
import jax.numpy as jnp
from jax import lax
import numpy as np

D_MODEL = 1024
BATCH = 8
SEQ = 2048
DEPTH = 2

GRID_W = 64
CTX_LEN = 256
N_BRANCH = 4
BRANCH_W = 256
NORM_EPS = 1e-6
ROPE_BASE = 10000.0
Q_BLOCK = 128
ADA_CHUNKS = 6
ADA_INIT = 0.3

MLA_HEADS = 4
MLA_NOPE = 64
MLA_ROPE = 32
MLA_QK = MLA_NOPE + MLA_ROPE
MLA_V = 64
Q_LORA = 256
KV_LORA = 128

FNET_GROUPS = 4
FNET_GW = BRANCH_W // FNET_GROUPS

S5_GROUP_CH = 16
S5_GROUPS = BRANCH_W // S5_GROUP_CH
S5_STATE = 64
S5_DT_MIN = 0.001
S5_DT_MAX = 0.1

RET_HEADS = 4
RET_HD = BRANCH_W // RET_HEADS
RET_CHUNK = 128

D_FF = 4 * D_MODEL

STATE_SIZES = (KV_LORA, MLA_ROPE, BRANCH_W, BRANCH_W, BRANCH_W)
MAIN_SIZES = (Q_LORA, BRANCH_W, BRANCH_W, BRANCH_W, N_BRANCH * D_MODEL)
STATE_COLS = sum(STATE_SIZES)
IN_COLS = STATE_COLS + sum(MAIN_SIZES)

kernel_name = 'hybrid_mla_fnet_s5_retention_dit_block'


def split_cols(z, sizes):
    return jnp.split(z, np.cumsum(sizes)[:-1].tolist(), axis=-1)


def rms_norm(x, w):
    xf = x.astype(jnp.float32)
    y = xf * lax.rsqrt(jnp.mean(xf * xf, axis=-1, keepdims=True) + NORM_EPS)
    return (y * w.astype(jnp.float32)).astype(x.dtype)


def ada_rms(x, w, shift, scale):
    return rms_norm(x, w) * (1.0 + scale) + shift


def rotary(x, pos):
    half = x.shape[-1] // 2
    freqs = ROPE_BASE ** (-jnp.arange(half, dtype=jnp.float32) / half)
    ang = pos.astype(jnp.float32)[:, None] * freqs
    cos = jnp.cos(ang)[:, None, :]
    sin = jnp.sin(ang)[:, None, :]
    xf = x.astype(jnp.float32)
    x1, x2 = xf[..., :half], xf[..., half:]
    return jnp.concatenate([x1 * cos - x2 * sin, x1 * sin + x2 * cos], axis=-1).astype(x.dtype)


def axial_rotary(x, rows, cols):
    h = x.shape[-1] // 2
    return jnp.concatenate([rotary(x[..., :h], rows), rotary(x[..., h:], cols)], axis=-1)


def mla_keys(kv_c, k_r, kv_norm, w_ukv, qk_k, rows, cols):
    B, L, _ = kv_c.shape
    kv = (rms_norm(kv_c, kv_norm) @ w_ukv).reshape(B, L, MLA_HEADS, MLA_NOPE + MLA_V)
    k_nope, v = kv[..., :MLA_NOPE], kv[..., MLA_NOPE:]
    k_r = jnp.broadcast_to(k_r[:, :, None, :], (B, L, MLA_HEADS, MLA_ROPE))
    k = rms_norm(jnp.concatenate([k_nope, k_r], axis=-1), qk_k)
    if rows is not None:
        k = jnp.concatenate([k[..., :MLA_NOPE], axial_rotary(k[..., MLA_NOPE:], rows, cols)], axis=-1)
    return k, v


def mla_queries(q_c, q_norm, w_uq, qk_q, rows, cols):
    B, L, _ = q_c.shape
    q = (rms_norm(q_c, q_norm) @ w_uq).reshape(B, L, MLA_HEADS, MLA_QK)
    q = rms_norm(q, qk_q)
    if rows is None:
        return q
    return jnp.concatenate([q[..., :MLA_NOPE], axial_rotary(q[..., MLA_NOPE:], rows, cols)], axis=-1)


def block_softmax_attention(q, k, v):
    B, Lq, H, dk = q.shape
    dv = v.shape[-1]
    nb = Lq // Q_BLOCK
    scale = dk ** -0.5
    qb = q.reshape(B, nb, Q_BLOCK, H, dk).swapaxes(0, 1)

    def attend(qi):
        s = jnp.einsum('bqhd,bkhd->bhqk', qi, k).astype(jnp.float32) * scale
        p = jax.nn.softmax(s, axis=-1).astype(v.dtype)
        return jnp.einsum('bhqk,bkhd->bqhd', p, v)

    o = lax.map(attend, qb)
    return o.swapaxes(0, 1).reshape(B, Lq, H * dv)


def fourier_mix(u):
    B, L, _ = u.shape
    ug = u.astype(jnp.float32).reshape(B, L, FNET_GROUPS, FNET_GW)
    y = jnp.fft.fft2(ug, axes=(1, 3), norm='ortho').real
    return y.reshape(B, L, BRANCH_W).astype(u.dtype)


def s5_discretise(lam_re, lam_im, log_step, b_re, b_im):
    lam = lax.complex(lam_re.astype(jnp.float32), lam_im.astype(jnp.float32))
    step = jnp.exp(log_step.astype(jnp.float32))[:, None]
    lam_bar = jnp.exp(lam * step)
    b = lax.complex(b_re.astype(jnp.float32), b_im.astype(jnp.float32))
    b_bar = ((lam_bar - 1.0) / lam)[..., None] * b
    return lam_bar, b_bar


def linear_recurrence_op(left, right):
    a_l, b_l = left
    a_r, b_r = right
    return a_l * a_r, a_r * b_l + b_r


def s5_scan(u, lam_bar, b_bar, x0, reverse):
    B, L, _ = u.shape
    ug = u.astype(jnp.float32).reshape(B, L, S5_GROUPS, S5_GROUP_CH).astype(jnp.complex64)
    bu = jnp.einsum('blgh,gph->blgp', ug, b_bar)
    if reverse:
        bu = jnp.flip(bu, axis=1)
    if x0 is not None:
        bu = bu.at[:, 0].add(lam_bar * x0)
    a = jnp.broadcast_to(lam_bar, bu.shape)
    _, xs = lax.associative_scan(linear_recurrence_op, (a, bu), axis=1)
    return jnp.flip(xs, axis=1) if reverse else xs


def s5_readout(u, xs_f, xs_b, c_f, c_b, d, w_glu):
    B, L, _ = u.shape
    y = (jnp.einsum('blgp,ghp->blgh', xs_f, c_f).real
         + jnp.einsum('blgp,ghp->blgh', xs_b, c_b).real)
    y = y.reshape(B, L, BRANCH_W).astype(u.dtype) + d * u
    y = jax.nn.gelu(y)
    val, gate = jnp.split(y @ w_glu, 2, axis=-1)
    return val * jax.nn.sigmoid(gate)


def retention_heads(t, pos):
    B, L, _ = t.shape
    t = t.reshape(B, L, RET_HEADS, RET_HD)
    if pos is not None:
        t = rotary(t, pos)
    return t.transpose(0, 2, 1, 3)


def retention_chunkwise(q, k, v, log_g, s0):
    B, H, L, dk = q.shape
    n = L // RET_CHUNK
    idx = jnp.arange(RET_CHUNK, dtype=jnp.float32)
    diff = idx[:, None] - idx[None, :]
    intra = jnp.where(diff >= 0, jnp.exp(log_g[:, None, None] * jnp.maximum(diff, 0.0)), 0.0)
    q_dec = jnp.exp(log_g[:, None] * (idx + 1.0))
    k_dec = jnp.exp(log_g[:, None] * (RET_CHUNK - 1.0 - idx))
    chunk_dec = jnp.exp(log_g * RET_CHUNK)

    def blocks(t):
        return t.astype(jnp.float32).reshape(B, H, n, RET_CHUNK, t.shape[-1]).transpose(2, 0, 1, 3, 4)

    def step(s, qkv):
        qi, ki, vi = qkv
        att = jnp.einsum('bhqd,bhkd->bhqk', qi, ki) * intra
        o = (jnp.einsum('bhqk,bhkv->bhqv', att, vi)
             + jnp.einsum('bhqd,bhdv->bhqv', qi * q_dec[..., None], s))
        s = s * chunk_dec[:, None, None] + jnp.einsum('bhkd,bhkv->bhdv', ki * k_dec[..., None], vi)
        return s, o

    s, o = lax.scan(step, s0, (blocks(q), blocks(k), blocks(v)))
    return o.transpose(1, 2, 0, 3, 4).reshape(B, H, L, -1), s


def retention_final_state(k, v, log_g):
    L = k.shape[2]
    w = jnp.exp(log_g[:, None] * (L - 1.0 - jnp.arange(L, dtype=jnp.float32)))
    return jnp.einsum('bhld,bhlv,hl->bhdv', k.astype(jnp.float32), v.astype(jnp.float32), w)


def retention_bidir(q, k, v, log_g_f, log_g_b, s0_f, s0_b):
    o_f, s_f = retention_chunkwise(q, k, v, log_g_f, s0_f)
    o_b, s_b = retention_chunkwise(jnp.flip(q, 2), jnp.flip(k, 2), jnp.flip(v, 2), log_g_b, s0_b)
    return o_f + jnp.flip(o_b, 2), s_f, s_b


def retention_output(o, g, gn_w):
    B, H, L, dv = o.shape
    of = o.astype(jnp.float32).transpose(0, 2, 1, 3)
    mu = jnp.mean(of, axis=-1, keepdims=True)
    var = jnp.mean(jnp.square(of - mu), axis=-1, keepdims=True)
    y = ((of - mu) * lax.rsqrt(var + NORM_EPS)).reshape(B, L, H * dv) * gn_w.astype(jnp.float32)
    return (jax.nn.silu(g.astype(jnp.float32)) * y).astype(g.dtype)


def merge_branches(o_a, o_b, o_c, o_d, gates, w_branch, w_out):
    B, L, _ = gates.shape
    o = jnp.stack([o_a.astype(gates.dtype), o_b.astype(gates.dtype),
                   o_c.astype(gates.dtype), o_d.astype(gates.dtype)], axis=2)
    y = jnp.einsum('blnw,nwd->blnd', o, w_branch)
    g = jax.nn.sigmoid(gates.reshape(B, L, N_BRANCH, D_MODEL).astype(jnp.float32)).astype(y.dtype)
    return jnp.sum(g * y, axis=2) @ w_out


def sq_relu_mlp(h, w1, w2):
    return jnp.square(jax.nn.relu(h @ w1)) @ w2


def setup_inputs(seed: int = 0) -> dict:
    key = jax.random.key(seed)
    ks = jax.random.split(key, 30)
    f32 = jnp.float32

    def nrm(i, shape, scale):
        return jax.random.normal(ks[i], shape, f32) * scale

    def gain(i, shape):
        return 1.0 + 0.01 * jax.random.normal(ks[i], shape, f32)

    Ld = DEPTH
    n_idx = jnp.arange(S5_STATE, dtype=f32)
    gamma0 = 1.0 - 2.0 ** (-5.0 - np.arange(RET_HEADS))
    logit0 = jnp.asarray(np.log(gamma0 / (1.0 - gamma0)), dtype=f32)
    s5_shape = (Ld, 2, S5_GROUPS, S5_STATE)
    return {
        'x': nrm(0, (BATCH, SEQ, D_MODEL), 1.0),
        'c': nrm(1, (BATCH, D_MODEL), 1.0),
        'ctx': nrm(2, (BATCH, CTX_LEN, D_MODEL), 1.0),
        'c_ctx': nrm(3, (D_MODEL,), 1.0),
        'ada_w': nrm(4, (Ld, D_MODEL, ADA_CHUNKS * D_MODEL), ADA_INIT * D_MODEL ** -0.5),
        'ada_b': nrm(5, (Ld, ADA_CHUNKS * D_MODEL), 0.01),
        'norm_mix_w': gain(6, (Ld, D_MODEL)),
        'norm_ffn_w': gain(7, (Ld, D_MODEL)),
        'w_in': nrm(8, (Ld, D_MODEL, IN_COLS), D_MODEL ** -0.5),
        'mla_q_norm': gain(9, (Ld, Q_LORA)),
        'mla_w_uq': nrm(10, (Ld, Q_LORA, MLA_HEADS * MLA_QK), Q_LORA ** -0.5),
        'mla_kv_norm': gain(11, (Ld, KV_LORA)),
        'mla_w_ukv': nrm(12, (Ld, KV_LORA, MLA_HEADS * (MLA_NOPE + MLA_V)), KV_LORA ** -0.5),
        'mla_qk_norm_q': gain(13, (Ld, MLA_QK)),
        'mla_qk_norm_k': gain(14, (Ld, MLA_QK)),
        's5_lam_re': -0.5 + nrm(15, s5_shape, 0.01),
        's5_lam_im': math.pi * n_idx + nrm(16, s5_shape, 0.01),
        's5_log_step': jax.random.uniform(ks[17], (Ld, 2, S5_GROUPS), f32,
                                          math.log(S5_DT_MIN), math.log(S5_DT_MAX)),
        's5_b_re': nrm(18, (Ld, 2, S5_GROUPS, S5_STATE, S5_GROUP_CH), (2 * S5_GROUP_CH) ** -0.5),
        's5_b_im': nrm(19, (Ld, 2, S5_GROUPS, S5_STATE, S5_GROUP_CH), (2 * S5_GROUP_CH) ** -0.5),
        's5_c_re': nrm(20, (Ld, 2, S5_GROUPS, S5_GROUP_CH, S5_STATE), (2 * S5_STATE) ** -0.5),
        's5_c_im': nrm(21, (Ld, 2, S5_GROUPS, S5_GROUP_CH, S5_STATE), (2 * S5_STATE) ** -0.5),
        's5_d': nrm(22, (Ld, BRANCH_W), 1.0),
        's5_w_glu': nrm(23, (Ld, BRANCH_W, 2 * BRANCH_W), BRANCH_W ** -0.5),
        'ret_decay_logit': logit0 + nrm(24, (Ld, 2, RET_HEADS), 0.01),
        'ret_gn_w': gain(25, (Ld, BRANCH_W)),
        'w_branch': nrm(26, (Ld, N_BRANCH, BRANCH_W, D_MODEL), BRANCH_W ** -0.5),
        'w_out': nrm(27, (Ld, D_MODEL, D_MODEL), D_MODEL ** -0.5),
        'ffn_w1': nrm(28, (Ld, D_MODEL, D_FF), D_MODEL ** -0.5),
        'ffn_w2': nrm(29, (Ld, D_FF, D_MODEL), D_FF ** -0.5),
    }


def reference(x, c, ctx, c_ctx, ada_w, ada_b, norm_mix_w, norm_ffn_w, w_in,
              mla_q_norm, mla_w_uq, mla_kv_norm, mla_w_ukv, mla_qk_norm_q, mla_qk_norm_k,
              s5_lam_re, s5_lam_im, s5_log_step, s5_b_re, s5_b_im, s5_c_re, s5_c_im,
              s5_d, s5_w_glu, ret_decay_logit, ret_gn_w, w_branch, w_out, ffn_w1, ffn_w2):
    f32 = jnp.float32
    B, L, _ = x.shape
    ROWS = L // GRID_W
    rows = jnp.repeat(jnp.arange(ROWS, dtype=f32), GRID_W)
    cols = jnp.tile(jnp.arange(GRID_W, dtype=f32), ROWS)
    pos = jnp.arange(L, dtype=f32)
    ret_scale = RET_HD ** -0.5
    zero_ret = jnp.zeros((B, RET_HEADS, RET_HD, RET_HD), f32)
    full_sizes = STATE_SIZES + MAIN_SIZES

    xl, xc = x, ctx
    for l in range(DEPTH):
        last = l == DEPTH - 1
        mod_l = (jax.nn.silu(c) @ ada_w[l] + ada_b[l])[:, None, :]
        mod_c = (jax.nn.silu(c_ctx) @ ada_w[l] + ada_b[l])[None, None, :]
        sh1, sc1, g1, sh2, sc2, g2 = jnp.split(mod_l, ADA_CHUNKS, axis=-1)
        csh1, csc1, cg1, csh2, csc2, cg2 = jnp.split(mod_c, ADA_CHUNKS, axis=-1)

        lam_f, bbar_f = s5_discretise(s5_lam_re[l, 0], s5_lam_im[l, 0], s5_log_step[l, 0],
                                      s5_b_re[l, 0], s5_b_im[l, 0])
        lam_b, bbar_b = s5_discretise(s5_lam_re[l, 1], s5_lam_im[l, 1], s5_log_step[l, 1],
                                      s5_b_re[l, 1], s5_b_im[l, 1])
        cmat_f = lax.complex(s5_c_re[l, 0].astype(f32), s5_c_im[l, 0].astype(f32))
        cmat_b = lax.complex(s5_c_re[l, 1].astype(f32), s5_c_im[l, 1].astype(f32))
        log_g_f = jax.nn.log_sigmoid(ret_decay_logit[l, 0].astype(f32))
        log_g_b = jax.nn.log_sigmoid(ret_decay_logit[l, 1].astype(f32))

        hc = ada_rms(xc, norm_mix_w[l], csh1, csc1)
        if last:
            zc = split_cols(hc @ w_in[l][:, :STATE_COLS], STATE_SIZES)
        else:
            zc = split_cols(hc @ w_in[l], full_sizes)
        k_ctx, v_ctx = mla_keys(zc[0], zc[1], mla_kv_norm[l], mla_w_ukv[l], mla_qk_norm_k[l], None, None)
        xs_cf = s5_scan(zc[2], lam_f, bbar_f, None, False)
        xs_cb = s5_scan(zc[2], lam_b, bbar_b, None, True)
        rk_c = retention_heads(zc[3], None) * ret_scale
        rv_c = retention_heads(zc[4], None)
        if last:
            s_f = retention_final_state(rk_c, rv_c, log_g_f)
            s_b = retention_final_state(jnp.flip(rk_c, 2), jnp.flip(rv_c, 2), log_g_b)
        else:
            rq_c = retention_heads(zc[7], None)
            o_ret_c, s_f, s_b = retention_bidir(rq_c, rk_c, rv_c, log_g_f, log_g_b, zero_ret, zero_ret)

        hl = ada_rms(xl, norm_mix_w[l], sh1, sc1)
        zl = split_cols(hl @ w_in[l], full_sizes)
        k_l, v_l = mla_keys(zl[0], zl[1], mla_kv_norm[l], mla_w_ukv[l], mla_qk_norm_k[l], rows, cols)
        q_l = mla_queries(zl[5], mla_q_norm[l], mla_w_uq[l], mla_qk_norm_q[l], rows, cols)
        o_a = block_softmax_attention(q_l, jnp.concatenate([k_ctx, k_l], axis=1),
                                      jnp.concatenate([v_ctx, v_l], axis=1))
        o_b = fourier_mix(zl[6])
        xs_f = s5_scan(zl[2], lam_f, bbar_f, xs_cf[:, -1], False)
        xs_b = s5_scan(zl[2], lam_b, bbar_b, xs_cb[:, 0], True)
        o_c = s5_readout(zl[2], xs_f, xs_b, cmat_f, cmat_b, s5_d[l], s5_w_glu[l])
        rq = retention_heads(zl[7], pos)
        rk = retention_heads(zl[3], pos) * ret_scale
        rv = retention_heads(zl[4], None)
        o_ret, _, _ = retention_bidir(rq, rk, rv, log_g_f, log_g_b, s_f, s_b)
        o_d = retention_output(o_ret, zl[8], ret_gn_w[l])
        xl_new = xl + g1 * merge_branches(o_a, o_b, o_c, o_d, zl[9], w_branch[l], w_out[l])
        xl_new = xl_new + g2 * sq_relu_mlp(ada_rms(xl_new, norm_ffn_w[l], sh2, sc2), ffn_w1[l], ffn_w2[l])

        if not last:
            q_cx = mla_queries(zc[5], mla_q_norm[l], mla_w_uq[l], mla_qk_norm_q[l], None, None)
            oc_a = block_softmax_attention(q_cx, k_ctx, v_ctx)
            oc_b = fourier_mix(zc[6])
            oc_c = s5_readout(zc[2], xs_cf, xs_cb, cmat_f, cmat_b, s5_d[l], s5_w_glu[l])
            oc_d = retention_output(o_ret_c, zc[8], ret_gn_w[l])
            xc = xc + cg1 * merge_branches(oc_a, oc_b, oc_c, oc_d, zc[9], w_branch[l], w_out[l])
            xc = xc + cg2 * sq_relu_mlp(ada_rms(xc, norm_ffn_w[l], csh2, csc2), ffn_w1[l], ffn_w2[l])
        xl = xl_new
    return xl
```

```python
import contextlib
import math
import os
import numpy as np
import ml_dtypes
import concourse.bass as bass
import concourse.mybir as mybir
from concourse.bass_utils import run_bass_kernel_spmd

F32 = mybir.dt.float32
BF16 = mybir.dt.bfloat16
ALU = mybir.AluOpType
AF = mybir.ActivationFunctionType
AX = mybir.AxisListType
NPBF = ml_dtypes.bfloat16

ENGS = ("pe", "act", "dve", "pool", "sp")
NDSLOT = 8

D = 1024
NT = 2304
NTT = 18
LAT0 = 256
EPS = 1e-6
DEPTH = 2


class T:
    __slots__ = ("name", "w", "rs", "excl")

    def __init__(self, name="", excl=False):
        self.name = name
        self.w = None
        self.rs = []
        self.excl = excl


class Prog:
    def __init__(self, nc, stack, same_sync=True):
        self.nc = nc
        self.same_sync = same_sync
        self.q = {e: [] for e in ENGS}
        self.cnt = {e: 0 for e in ENGS}
        self.sems = {}
        for e in ENGS:
            self.sems[("c", e)] = stack.enter_context(nc.semaphore("c_" + e))
        self.dq = ("sp", "pool", "act")
        self.dcnt = {}
        self.dn = {e: 0 for e in self.dq}
        for e in self.dq:
            for s in range(NDSLOT):
                self.sems[("d", e, s)] = stack.enter_context(nc.semaphore("d_%s%d" % (e, s)))
                self.dcnt[(e, s)] = 0
        self.known = {e: {} for e in ENGS}
        self.kstop = None
        self.kcount = 0

    def _deps(self, eng, r, w):
        deps = {}

        def add(h):
            if h is None:
                return
            k, v = h
            if k == ("c", eng) and (eng == "pe" or not self.same_sync):
                return
            if deps.get(k, 0) < v:
                deps[k] = v
        for t in r:
            add(t.w)
        for t in w:
            add(t.w)
            for h in t.rs:
                add(h)
        out = []
        kn = self.known[eng]
        for k, v in deps.items():
            if kn.get(k, 0) >= v:
                continue
            kn[k] = v
            out.append((k, v))
        return out

    def _mark(self, h, r, w):
        for t in r:
            t.rs.append(h)
            if len(t.rs) > 64:
                best = {}
                for k, v in t.rs:
                    if best.get(k, 0) < v:
                        best[k] = v
                t.rs = list(best.items())
        for t in w:
            t.w = h
            t.rs = []

    def op(self, eng, fn, r=(), w=()):
        if self.kstop is not None:
            self.kcount += 1
            if self.kcount > self.kstop:
                return None
        if eng != "pe":
            ex = [t for t in r if t.excl]
            if ex:
                r = [t for t in r if not t.excl]
                w = list(w) + ex
        waits = self._deps(eng, r, w)
        self.cnt[eng] += 1
        h = (("c", eng), self.cnt[eng])
        self.q[eng].append((fn, waits, (h[0], 1)))
        self._mark(h, r, w)
        return h

    def dma(self, eng, fn, r=(), w=()):
        s = self.dn[eng] % NDSLOT
        self.dn[eng] += 1
        waits = self._deps(eng, r, w)
        k = ("d", eng, s)
        prev = self.dcnt[(eng, s)]
        if prev > 0 and self.known[eng].get(k, 0) < prev:
            self.known[eng][k] = prev
            waits.append((k, prev))
        self.dcnt[(eng, s)] = prev + 16
        h = (k, prev + 16)
        self.q[eng].append((fn, waits, (k, 16)))
        self._mark(h, r, w)
        return h

    def barrier(self):
        hs = [(("c", e), self.cnt[e]) for e in ENGS if self.cnt[e] > 0]
        hs += [(("d", e, s), v) for (e, s), v in self.dcnt.items() if v > 0]
        for e in ENGS:
            self.wait_all(e, [h for h in hs if h[0] != ("c", e)])

    def wait_all(self, eng, hs):
        waits = []
        for k, v in hs:
            if self.known[eng].get(k, 0) < v:
                self.known[eng][k] = v
                waits.append((k, v))
        self.q[eng].append((None, waits, None))

    def emit(self):
        nc = self.nc
        sems = self.sems
        q = self.q

        def run(e, engobj):
            for fn, waits, inc in q[e]:
                for k, v in waits:
                    engobj.wait_ge(sems[k], v)
                if fn is None:
                    continue
                ins = fn(engobj)
                ins.then_inc(sems[inc[0]], inc[1])

        with nc.Block() as block:
            @block.tensor
            def _(eng):
                run("pe", eng)

            @block.scalar
            def _(eng):
                run("act", eng)

            @block.vector
            def _(eng):
                run("dve", eng)

            @block.gpsimd
            def _(eng):
                run("pool", eng)

            @block.sync
            def _(eng):
                run("sp", eng)


class Arena:
    def __init__(self, nc, stack, nbytes):
        self.t = stack.enter_context(nc.sbuf_tensor("arena", [128, nbytes // 4], F32))
        self.nbytes = nbytes
        self.off = 0

    def alloc(self, shape, dtype, at=None):
        esz = 2 if dtype == BF16 else 4
        n = int(np.prod(shape)) * esz
        n4 = (n + 3) // 4
        if at is None:
            at = self.off
            self.off += n4 * 4
        assert at % 4 == 0 and at + n4 * 4 <= self.nbytes, (at, n, self.nbytes)
        ap = self.t[:, at // 4: at // 4 + n4]
        if dtype != F32:
            ap = ap.bitcast(dtype)
        if len(shape) == 2:
            ap = ap.rearrange("p (a b) -> p a b", b=shape[1])
        elif len(shape) == 3:
            ap = ap.rearrange("p (a b c) -> p a b c", b=shape[1], c=shape[2])
        elif len(shape) == 4:
            ap = ap.rearrange("p (a b c d) -> p a b c d", b=shape[1], c=shape[2], d=shape[3])
        return ap


def _rope_tables():
    half = 8
    freqs = (10000.0 ** (-np.arange(half, dtype=np.float32) / half)).astype(np.float32)
    t = np.arange(2048)
    rows = (t // 64).astype(np.float32)
    cols = (t % 64).astype(np.float32)
    ang = np.concatenate([rows[:, None] * freqs[None], cols[:, None] * freqs[None]], axis=1)
    cos = np.ones((NT, 16), np.float32)
    sin = np.zeros((NT, 16), np.float32)
    cos[LAT0:] = np.cos(ang)
    sin[LAT0:] = np.sin(ang)
    cos = cos.reshape(NTT, 128, 16).transpose(1, 0, 2)
    sin = sin.reshape(NTT, 128, 16).transpose(1, 0, 2)
    return np.ascontiguousarray(cos), np.ascontiguousarray(sin)


def _fnet_consts():
    ci = np.arange(64)
    c64 = np.cos(2 * np.pi * np.outer(ci, ci) / 64.0)
    s64 = np.sin(2 * np.pi * np.outer(ci, ci) / 64.0)
    cs = np.zeros((2, 128, 512), np.float64)
    for j in range(2):
        for gl in range(2):
            g = 2 * j + gl
            cs[j, gl * 64:(gl + 1) * 64, g * 64:(g + 1) * 64] = c64
            cs[j, gl * 64:(gl + 1) * 64, 256 + g * 64:256 + (g + 1) * 64] = s64
    out = {"cs64": np.ascontiguousarray(cs.transpose(1, 0, 2)).astype(np.float32).astype(NPBF)}
    for L in (2048, 256):
        li = np.arange(L)
        m = np.outer(li, li) % L
        ang = 2 * np.pi * m / L
        sc = 1.0 / math.sqrt(L * 64.0)
        tab = np.stack([np.cos(ang) * sc, -np.sin(ang) * sc], axis=0)
        out["dft%d" % L] = tab.astype(np.float32).astype(NPBF)
    return out


def _s5_layouts(inp):
    f = np.float32
    out = {}
    m = np.arange(128)[:, None]
    t = np.arange(128)[None, :]
    tri = np.stack([(m <= t), (m >= t)], axis=1).astype(f)
    out["s5tri"] = tri.astype(NPBF)
    erow = np.stack([np.broadcast_to(t.astype(f), (128, 128)), np.broadcast_to(127.0 - t.astype(f), (128, 128))], axis=1)
    out["s5erow"] = np.ascontiguousarray(erow, f)
    p = np.arange(128, dtype=f)[:, None]
    out["s5ecol"] = np.ascontiguousarray(np.concatenate([p, 127.0 - p, -p, -(127.0 - p)], axis=1), f)
    out["s5d"] = np.ascontiguousarray(np.asarray(inp["s5_d"], f).reshape(DEPTH, 2, 128).transpose(0, 2, 1))
    re = np.asarray(inp["s5_lam_re"], f).reshape(DEPTH, 2, 1024)
    im = np.asarray(inp["s5_lam_im"], f).reshape(DEPTH, 2, 1024)
    ls = np.repeat(np.asarray(inp["s5_log_step"], f), 64, axis=-1)
    trip = np.stack([re, im, ls], axis=2)
    out["s5pp"] = np.ascontiguousarray(trip.reshape(DEPTH, 2, 3, 8, 128).transpose(0, 1, 4, 2, 3))
    out["s5row"] = np.ascontiguousarray(np.broadcast_to(trip[:, :, None], (DEPTH, 2, 128, 3, 1024)), f)
    bre = np.asarray(inp["s5_b_re"], f)
    bim = np.asarray(inp["s5_b_im"], f)
    sb = np.zeros((DEPTH, 2, 128, 2, 2, 512), f)
    for ri, bb in enumerate((bre, bim)):
        for kt in range(2):
            for gl in range(8):
                g = 8 * kt + gl
                sb[:, :, gl * 16:(gl + 1) * 16, kt, ri, gl * 64:(gl + 1) * 64] = bb[:, :, g].transpose(0, 1, 3, 2)
    out["s5b"] = sb
    cre = np.asarray(inp["s5_c_re"], f)
    cim = np.asarray(inp["s5_c_im"], f)
    sc = np.zeros((DEPTH, 2, 2, 8, 128, 128), f)
    for ri, cm in enumerate((cre, cim)):
        for st in range(8):
            for gl in range(2):
                g = 2 * st + gl
                col = (g % 8) * 16
                sc[:, :, ri, st, gl * 64:(gl + 1) * 64, col:col + 16] = cm[:, :, g].transpose(0, 1, 3, 2)
    out["s5c"] = sc
    out["s5_w_glu"] = np.ascontiguousarray(inp["s5_w_glu"], f)
    return out


def _ret_consts():
    half = 32
    freqs = (10000.0 ** (-np.arange(half, dtype=np.float32) / half)).astype(np.float32)
    pos = np.arange(2048, dtype=np.float32)
    ang = pos[:, None] * freqs[None]
    cos = np.ones((NT, 32), np.float32)
    sin = np.zeros((NT, 32), np.float32)
    cos[LAT0:] = np.cos(ang)
    sin[LAT0:] = np.sin(ang)
    tm = lambda a: np.ascontiguousarray(a.reshape(NTT, 128, 32).transpose(1, 0, 2))
    out = {"rrc": tm(cos), "rrs": tm(sin)}
    k = np.arange(128, dtype=np.float32)[:, None]
    q = np.arange(128, dtype=np.float32)[None, :]
    retc = np.stack([np.broadcast_to(q + 1.0, (128, 128)), np.broadcast_to(128.0 - q, (128, 128)),
                     np.maximum(q - k, 0.0), np.maximum(k - q, 0.0),
                     (q >= k).astype(np.float32), (k >= q).astype(np.float32)], axis=1)
    out["retc"] = np.ascontiguousarray(retc, np.float32)
    out["retp"] = np.ascontiguousarray(np.concatenate([k, 127.0 - k], axis=1), np.float32)
    return out


def prep_common(inp):
    f = np.float32
    c = {}
    c["ada_w"] = np.ascontiguousarray(inp["ada_w"], f)
    c["ada_bT"] = np.ascontiguousarray(inp["ada_b"].reshape(DEPTH, 48, 128).transpose(0, 2, 1), f)
    nw = np.concatenate([inp["norm_mix_w"].reshape(DEPTH, 8, 128), inp["norm_ffn_w"].reshape(DEPTH, 8, 128)], axis=1)
    c["nw"] = np.ascontiguousarray(nw.transpose(0, 2, 1), f)
    c["w_in"] = np.ascontiguousarray(inp["w_in"], f)
    bc = lambda v: np.ascontiguousarray(np.broadcast_to(v[:, None, :], (DEPTH, 128, v.shape[-1])), f)
    c["kvw"] = bc(inp["mla_kv_norm"])
    c["qnw"] = bc(inp["mla_q_norm"])
    c["qkq"] = bc(np.tile(inp["mla_qk_norm_q"], (1, 4)))
    c["qkk"] = bc(np.tile(inp["mla_qk_norm_k"], (1, 4)))
    c["w_ukv"] = np.ascontiguousarray(inp["mla_w_ukv"], f)
    c["w_uq"] = np.ascontiguousarray(inp["mla_w_uq"], f)
    cos, sin = _rope_tables()
    c["ropec"] = cos
    c["ropes"] = sin
    c["w_branch"] = np.ascontiguousarray(inp["w_branch"], f)
    c["w_out"] = np.ascontiguousarray(inp["w_out"], f)
    c["ffn_w1"] = np.ascontiguousarray(inp["ffn_w1"], f)
    c["ffn_w2"] = np.ascontiguousarray(inp["ffn_w2"], f)
    c.update(_fnet_consts())
    c.update(_ret_consts())
    c.update(_s5_layouts(inp))
    lg = np.asarray(inp["ret_decay_logit"], f)
    c["rlog_row"] = np.ascontiguousarray(np.broadcast_to(lg.reshape(DEPTH, 1, 8), (DEPTH, 128, 8)), f)
    pp = np.zeros((DEPTH, 128, 2, 2), f)
    for pair in range(2):
        for d_ in range(2):
            pp[:, 0:64, pair, d_] = lg[:, d_, 2 * pair][:, None]
            pp[:, 64:128, pair, d_] = lg[:, d_, 2 * pair + 1][:, None]
    c["rlog_pp"] = pp
    c["gnw"] = bc(inp["ret_gn_w"])
    c["identb"] = np.eye(128, dtype=f).astype(NPBF)
    c["identf"] = np.eye(128, dtype=f)
    return c


def prep_core(inp, b):
    f = np.float32
    d = {}
    xt = np.concatenate([inp["ctx"][b], inp["x"][b]], axis=0).T
    d["xT"] = np.ascontiguousarray(xt, f)
    d["cT"] = np.ascontiguousarray(np.stack([inp["c"][b], inp["c_ctx"]], axis=1), f)
    return d


def build(specs, stage=99, taps=(), mixers=("mla", "s5", "ret", "fnet")):
    nc = bass.Bass("TRN2", target_bir_lowering=False)
    dr = {}
    for name, (shape, dt) in specs.items():
        dr[name] = nc.dram_tensor(name, list(shape), dt, kind="ExternalInput").ap()
    outT = nc.dram_tensor("outT", [D, 2048], F32, kind="ExternalOutput").ap()
    tapd = {}
    for name, shape, dt in taps:
        tapd[name] = nc.dram_tensor("tap_" + name, list(shape), dt, kind="ExternalOutput").ap()

    with contextlib.ExitStack() as st:
        P = Prog(nc, st)
        A = Arena(nc, st, 206 * 1024)
        ps = [st.enter_context(nc.psum_tensor("ps%d" % i, [128, 512], F32)) for i in range(8)]
        tps = [T("ps%d" % i, excl=True) for i in range(8)]
        out_handles = []

        def mm(out, lhsT, rhs, start=True, stop=True, r=(), w=()):
            return P.op("pe", lambda e: e.matmul(out, lhsT=lhsT, rhs=rhs, start=start, stop=stop,
                                                 skip_group_check=True), r, w)

        def tr(out, in_, ident, r=(), w=()):
            return P.op("pe", lambda e: e.transpose(out=out, in_=in_, identity=ident), r, w)

        def act(out, in_, func, r=(), w=(), scale=1.0, bias=0.0, accum=None):
            if accum is None:
                return P.op("act", lambda e: e.activation(out=out, in_=in_, func=func, scale=scale, bias=bias), r, w)
            return P.op("act", lambda e: e.activation(out=out, in_=in_, func=func, scale=scale, bias=bias,
                                                      accum_out=accum), r, w)

        def tt(eng, out, a, b, op, r=(), w=()):
            return P.op(eng, lambda e: e.tensor_tensor(out=out, in0=a, in1=b, op=op), r, w)

        def ts(eng, out, a, s1, s2, op0, op1=None, r=(), w=()):
            if op1 is None:
                return P.op(eng, lambda e: e.tensor_scalar(out=out, in0=a, scalar1=s1, scalar2=None, op0=op0), r, w)
            return P.op(eng, lambda e: e.tensor_scalar(out=out, in0=a, scalar1=s1, scalar2=s2, op0=op0, op1=op1), r, w)

        def stt(out, a, s, b, op0, op1, r=(), w=()):
            return P.op("dve", lambda e: e.scalar_tensor_tensor(out=out, in0=a, scalar=s, in1=b, op0=op0, op1=op1), r, w)

        def cp(eng, out, in_, r=(), w=()):
            if eng == "act":
                return P.op(eng, lambda e: e.activation(out=out, in_=in_, func=AF.Copy), r, w)
            return P.op(eng, lambda e: e.tensor_copy(out=out, in_=in_), r, w)

        def red(out, in_, r=(), w=()):
            return P.op("dve", lambda e: e.tensor_reduce(out=out, in_=in_, axis=AX.X, op=ALU.add), r, w)

        def rcp(out, in_, r=(), w=()):
            return P.op("dve", lambda e: e.reciprocal(out=out, in_=in_), r, w)

        def mset(eng, out, val, w=()):
            return P.op(eng, lambda e: e.memset(out, val), (), w)

        def dma(out, in_, r=(), w=(), q="sp"):
            return P.dma(q, lambda e: e.dma_start(out=out, in_=in_), r, w)

        def tap(name, src, r):
            if name in tapd:
                out_handles.append(dma(tapd[name], src, r=r))

        xT = A.alloc([8, NT], F32)
        hT = A.alloc([8, NT], BF16)
        t_x = [[T("x%d_%d" % (k, g)) for g in range(5)] for k in range(8)]
        t_h = [T("h%d" % g) for g in range(5)]
        identb = A.alloc([128], BF16)
        identf = A.alloc([128], F32)
        onesb = A.alloc([128], BF16)
        mod = A.alloc([DEPTH, 48, 2], F32)
        a1 = A.alloc([DEPTH, 16, 2], F32)
        nwt = A.alloc([DEPTH, 16], F32)
        scT = A.alloc([8, 2], F32)
        eps_t = A.alloc([1], F32)
        t_const = T("const")
        t_mod = T("mod")
        NSTG = 2
        NRING = 5
        stg = [A.alloc([1024], F32) for _ in range(NSTG)]
        t_stg = [T("stg%d" % i) for i in range(NSTG)]
        ring = [A.alloc([1024], BF16) for _ in range(NRING)]
        t_ring = [T("ring%d" % i) for i in range(NRING)]
        sidx = [0]
        ridx = [0]
        DYN0 = A.off
        OT0 = A.nbytes - 4 * 2 * NT * 2
        oT = [A.alloc([2, NT], BF16, at=OT0 + i * 2 * NT * 2) for i in range(4)]
        t_oT = [[T("o%d_%d" % (i, g)) for g in range(5)] for i in range(4)]

        def GRP(g):
            return (0, 256) if g == 0 else (LAT0 + 512 * (g - 1), 512)

        def grp_of_tile(tt_):
            return 0 if tt_ < 2 else 1 + (tt_ - 2) // 4

        def next_stg():
            s = sidx[0] % NSTG
            sidx[0] += 1
            return s

        def load_cast(dst, t_dst, src, shape_str=None, **kw):
            s = next_stg()
            n = int(np.prod(src.shape[1:]))
            assert n <= 1024, n
            sv = stg[s][:, 0:n]
            if shape_str is not None:
                sv = sv.rearrange(shape_str, **kw)
            dma(sv, src, w=[t_stg[s]])
            cp("pool", dst, sv, r=[t_stg[s]], w=[t_dst])

        def load_ring(src, shape_str=None, **kw):
            i = ridx[0] % NRING
            ridx[0] += 1
            n = int(np.prod(src.shape[1:]))
            dv = ring[i][:, 0:n]
            if shape_str is not None:
                dv = dv.rearrange(shape_str, **kw)
            load_cast(dv, t_ring[i], src, shape_str, **kw)
            return dv, t_ring[i]

        xsrc = dr["xT"].rearrange("(k p) t -> p k t", p=128)
        for k in range(8):
            dma(xT[:, k, :], xsrc[:, k, :], w=t_x[k])
        dma(identb, dr["identb"], w=[t_const])
        dma(identf, dr["identf"], w=[t_const])
        dma(scT, dr["cT"].rearrange("(k p) j -> p k j", p=128), w=[t_const])
        dma(nwt, dr["nw"].rearrange("l p k -> p l k"), w=[t_const])
        mset("dve", onesb, 1.0, w=[t_const])
        mset("dve", eps_t, EPS, w=[t_const])
        act(scT, scT, AF.Silu, r=[t_const], w=[t_const])

        for l in range(DEPTH):
            P.barrier()
            bT = A.alloc([48], F32, at=DYN0)
            t_bT = T()
            dma(bT, dr["ada_bT"][l], w=[t_bT])
            psm = ps[l][:, 0:96].rearrange("p (c s) -> p c s", s=2)
            wsrc = dr["ada_w"][l].rearrange("(k p) c -> p k c", p=128)
            for ct in range(48):
                s = next_stg()
                sv = stg[s][:, 0:1024].rearrange("p (k c) -> p k c", c=128)
                dma(sv, wsrc[:, :, ct * 128:(ct + 1) * 128], w=[t_stg[s]])
                for k in range(8):
                    mm(psm[:, ct, :], sv[:, k, :], scT[:, k, :], start=(k == 0), stop=(k == 7),
                       r=[t_stg[s], t_const], w=[tps[l]])
            tt("dve", mod[:, l, :, :], psm, bT[:, :].unsqueeze(2).to_broadcast([128, 48, 2]), ALU.add,
               r=[tps[l], t_bT], w=[t_mod])
            for j, c0 in ((0, 8), (1, 32)):
                stt(a1[:, l, j * 8:(j + 1) * 8, :], mod[:, l, c0:c0 + 8, :], 1.0,
                    nwt[:, l, j * 8:(j + 1) * 8].unsqueeze(2).to_broadcast([128, 8, 2]),
                    ALU.add, ALU.mult, r=[t_mod, t_const], w=[t_mod])
        tap("mod", mod, [t_mod])

        def norm_phase(l, which, groups):
            sh0 = 0 if which == 0 else 24
            P.barrier()
            sq = A.alloc([2, 8, 512], BF16, at=DYN0)
            rstd = A.alloc([2, 512], F32, at=DYN0 + 2 * 8 * 512 * 2)
            tmp = A.alloc([2, 512], F32, at=DYN0 + 2 * 8 * 512 * 2 + 2 * 512 * 4)
            t_sq = [T(), T()]
            t_rs = [T(), T()]
            t_tmp = [T(), T()]
            for gi, g in enumerate(groups):
                t0, n = GRP(g)
                s = 1 if g == 0 else 0
                b = gi % 2
                pb = 2 + b
                for k in range(8):
                    act(sq[:, b, k, 0:n], xT[:, k, t0:t0 + n], AF.Square, r=[t_x[k][g]], w=[t_sq[b]])
                for k in range(8):
                    mm(ps[pb][:, 0:n], onesb, sq[:, b, k, 0:n], start=(k == 0), stop=(k == 7),
                       r=[t_sq[b], t_const], w=[tps[pb]])
                act(rstd[:, b, 0:n], ps[pb][:, 0:n], AF.Sqrt, r=[tps[pb], t_const], w=[t_rs[b]],
                    scale=1.0 / D, bias=eps_t[:, 0:1])
                rcp(rstd[:, b, 0:n], rstd[:, b, 0:n], r=[t_rs[b]], w=[t_rs[b]])
                for k in range(8):
                    tb = k % 2
                    tt("dve", tmp[:, tb, 0:n], xT[:, k, t0:t0 + n], rstd[:, b, 0:n], ALU.mult,
                       r=[t_x[k][g], t_rs[b]], w=[t_tmp[tb]])
                    act(hT[:, k, t0:t0 + n], tmp[:, tb, 0:n], AF.Identity, r=[t_tmp[tb], t_mod], w=[t_h[g]],
                        scale=a1[:, l, which * 8 + k, s:s + 1], bias=mod[:, l, sh0 + k, s:s + 1])

        def mk_alloc(limit):
            o = [DYN0]

            def al(shape, dt):
                ap = A.alloc(shape, dt, at=o[0])
                n = int(np.prod(shape)) * (2 if dt == BF16 else 4)
                o[0] += (n + 3) // 4 * 4
                assert o[0] <= limit, (o[0], limit)
                return ap
            al.o = o
            return al

        def mla_phase(l, last, slot):
            P.barrier()
            al = mk_alloc(OT0 + slot * 2 * NT * 2)
            Wm = al([8, 416], BF16)
            Wukv = al([512], BF16)
            Wuq = al([2, 384], BF16)
            kvw = al([128], F32)
            qnw = al([256], F32)
            qkq = al([4, 96], F32)
            qkk = al([4, 96], F32)
            rc = al([NTT, 2, 8], F32)
            rs_ = al([NTT, 2, 8], F32)
            KT = al([4, NT], BF16)
            QT = al([4, 512], BF16)
            V1 = al([NTT, 4, 65], BF16)
            PT = al([2, 512], BF16)
            kvn = al([2, 128], BF16)
            kvnT = al([2, 128], BF16)
            qn = al([2, 256], BF16)
            qnT = al([2, 2, 128], BF16)
            kf = al([2, 4, 96], F32)
            sqs = al([4, 96], F32)
            kb = al([2, 4, 96], BF16)
            sm = al([2, 16], F32)
            rt = al([4, 4, 2, 8], F32)
            oat = al([4, 256], BF16)
            rinv = al([4], F32)
            t_W = T("Wm")
            t_small = T("mlasmall")
            t_KT = [T() for _ in range(NTT)]
            t_QT = [T() for _ in range(4)]
            t_V1 = [T() for _ in range(NTT)]
            t_PT = [T(), T()]
            t_kvn = [T(), T()]
            t_kvnT = [T(), T()]
            t_qn = [T(), T()]
            t_qnT = [T(), T()]
            t_kf = [T(), T()]
            t_sqs = T()
            t_kb = [T(), T()]
            t_sm = [T(), T()]
            t_rt = T()
            t_oat = [T() for _ in range(4)]
            t_rinv = T()

            wsrc = dr["w_in"][l].rearrange("(k p) c -> p k c", p=128)
            for k in range(0, 8, 2):
                load_cast(Wm[:, k:k + 2, 0:160], t_W, wsrc[:, k:k + 2, 0:160], "p (k c) -> p k c", c=160)
                load_cast(Wm[:, k:k + 2, 160:416], t_W, wsrc[:, k:k + 2, 928:1184], "p (k c) -> p k c", c=256)
            load_cast(Wukv, t_W, dr["w_ukv"][l])
            load_cast(Wuq, t_W, dr["w_uq"][l].rearrange("(j p) c -> p j c", p=128), "p (j c) -> p j c", c=384)
            dma(kvw, dr["kvw"][l], w=[t_small])
            dma(qnw, dr["qnw"][l], w=[t_small])
            dma(qkq, dr["qkq"][l].rearrange("p (h d) -> p h d", d=96), w=[t_small])
            dma(qkk, dr["qkk"][l].rearrange("p (h d) -> p h d", d=96), w=[t_small])
            dma(rc, dr["ropec"].rearrange("p t (a b) -> p t a b", b=8), w=[t_small])
            dma(rs_, dr["ropes"].rearrange("p t (a b) -> p t a b", b=8), w=[t_small])
            mset("dve", V1[:, :, :, 64:65], 1.0, w=t_V1)
            if stage < 0.5:
                return

            def headnorm_rope(pb, wq, tt_, dst, t_dst, tbank):
                x = kf[:, pb]
                t_x_ = t_kf[pb]
                tt("dve", sqs, x, x, ALU.mult, r=[t_x_], w=[t_sqs])
                st_ = sm[:, pb, 0:4]
                red(st_, sqs, r=[t_sqs], w=[t_sm[pb]])
                act(st_, st_, AF.Sqrt, r=[t_sm[pb], t_const], w=[t_sm[pb]], scale=1.0 / 96, bias=eps_t[:, 0:1])
                rcp(st_, st_, r=[t_sm[pb]], w=[t_sm[pb]])
                tt("dve", x, x, st_.unsqueeze(2).to_broadcast([128, 4, 96]), ALU.mult, r=[t_x_, t_sm[pb]], w=[t_x_])
                tt("dve", x, x, wq, ALU.mult, r=[t_x_, t_small], w=[t_x_])
                y = kb[:, pb]
                t_y = t_kb[pb]
                cp("dve", y[:, :, 0:64], x[:, :, 0:64], r=[t_x_], w=[t_y])
                xr = x[:, :, 64:96].rearrange("p h (a two b) -> p h a two b", two=2, b=8)
                yr = y[:, :, 64:96].rearrange("p h (a two b) -> p h a two b", two=2, b=8)
                cosb = rc[:, tt_].unsqueeze(1).to_broadcast([128, 4, 2, 8])
                sinb = rs_[:, tt_].unsqueeze(1).to_broadcast([128, 4, 2, 8])
                x1 = xr[:, :, :, 0, :]
                x2 = xr[:, :, :, 1, :]
                tt("dve", rt[:, 0], x1, cosb, ALU.mult, r=[t_x_, t_small], w=[t_rt])
                tt("dve", rt[:, 1], x2, sinb, ALU.mult, r=[t_x_, t_small], w=[t_rt])
                tt("dve", yr[:, :, :, 0, :], rt[:, 0], rt[:, 1], ALU.subtract, r=[t_rt], w=[t_y])
                tt("dve", rt[:, 2], x1, sinb, ALU.mult, r=[t_x_, t_small], w=[t_rt])
                tt("dve", rt[:, 3], x2, cosb, ALU.mult, r=[t_x_, t_small], w=[t_rt])
                tt("dve", yr[:, :, :, 1, :], rt[:, 2], rt[:, 3], ALU.add, r=[t_rt], w=[t_y])
                pst = ps[tbank][:, :].bitcast(BF16)
                for h in range(4):
                    tr(pst[0:96, h * 128:(h + 1) * 128], y[:, h, :], identb, r=[t_y, t_const], w=[tps[tbank]])
                cp("act", dst, pst[0:96, 0:512].rearrange("p (h t) -> p h t", t=128), r=[tps[tbank]], w=[t_dst])

            if os.environ.get("KSTOP"):
                P.kstop = int(os.environ["KSTOP"])
            for tt_ in range(NTT if stage >= 0.65 else int(os.environ.get('NTILES', '1'))):
                g = grp_of_tile(tt_)
                pb = tt_ % 2
                t0 = tt_ * 128
                bz = pb
                for k in range(8):
                    mm(ps[bz][:, 0:160], hT[:, k, t0:t0 + 128], Wm[:, k, 0:160], start=(k == 0), stop=(k == 7),
                       r=[t_h[g], t_W], w=[tps[bz]])
                z = ps[bz]
                ssk = sm[:, pb, 8:9]
                act(kf[:, pb].rearrange("p h d -> p (h d)")[:, 0:128], z[:, 0:128], AF.Square,
                    r=[tps[bz]], w=[t_kf[pb], t_sm[pb]], accum=ssk)
                act(ssk, ssk, AF.Sqrt, r=[t_sm[pb], t_const], w=[t_sm[pb]], scale=1.0 / 128, bias=eps_t[:, 0:1])
                rcp(ssk, ssk, r=[t_sm[pb]], w=[t_sm[pb]])
                stt(kvn[:, pb], z[:, 0:128], ssk, kvw, ALU.mult, ALU.mult, r=[tps[bz], t_sm[pb], t_small], w=[t_kvn[pb]])
                pst = ps[2 + pb][:, :].bitcast(BF16)
                tr(pst[:, 0:128], kvn[:, pb], identb, r=[t_kvn[pb], t_const], w=[tps[2 + pb]])
                cp("act", kvnT[:, pb], pst[:, 0:128], r=[tps[2 + pb]], w=[t_kvnT[pb]])
                bkv = 4 + pb
                mm(ps[bkv][:, :], kvnT[:, pb], Wukv, r=[t_kvnT[pb], t_W], w=[tps[bkv]])
                kvv = ps[bkv][:, :].rearrange("p (h c) -> p h c", c=128)
                cp("act", V1[:, tt_, :, 0:64], kvv[:, :, 64:128], r=[tps[bkv]], w=[t_V1[tt_]])
                cp("dve", kf[:, pb, :, 0:64], kvv[:, :, 0:64], r=[tps[bkv]], w=[t_kf[pb]])
                cp("dve", kf[:, pb, :, 64:96], z[:, 128:160].unsqueeze(1).to_broadcast([128, 4, 32]),
                   r=[tps[bz]], w=[t_kf[pb]])
                if os.environ.get("NOHN") != "1":
                    headnorm_rope(pb, qkk, tt_, KT[0:96, :, t0:t0 + 128], t_KT[tt_], 2 + pb)
            P.kstop = None
            tap("KT", KT[0:96], t_KT)
            tap("V1", V1, t_V1)
            if stage < 0.7:
                return

            scale = 96 ** -0.5
            qgroups = [(g, GRP(g)[0], GRP(g)[1], list(range(NTT))) for g in range(1, 5)]
            if not last:
                qgroups = [(0, 0, 256, [0, 1])] + qgroups
            it = 0
            for (g, q0, nq, ktiles) in qgroups:
                nqs = nq // 128
                for qs in range(nqs):
                    tt_ = q0 // 128 + qs
                    pb = qs % 2
                    t0 = tt_ * 128
                    bz = pb
                    for k in range(8):
                        mm(ps[bz][:, 0:256], hT[:, k, t0:t0 + 128], Wm[:, k, 160:416], start=(k == 0), stop=(k == 7),
                           r=[t_h[g], t_W], w=[tps[bz]])
                    z = ps[bz]
                    ssq = sm[:, pb, 9:10]
                    act(qn[:, pb], z[:, 0:256], AF.Square, r=[tps[bz]], w=[t_qn[pb], t_sm[pb]], accum=ssq)
                    act(ssq, ssq, AF.Sqrt, r=[t_sm[pb], t_const], w=[t_sm[pb]], scale=1.0 / 256, bias=eps_t[:, 0:1])
                    rcp(ssq, ssq, r=[t_sm[pb]], w=[t_sm[pb]])
                    stt(qn[:, pb], z[:, 0:256], ssq, qnw, ALU.mult, ALU.mult, r=[tps[bz], t_sm[pb], t_small], w=[t_qn[pb]])
                    pst = ps[2 + pb][:, :].bitcast(BF16)
                    for j in range(2):
                        tr(pst[:, j * 128:(j + 1) * 128], qn[:, pb, j * 128:(j + 1) * 128], identb,
                           r=[t_qn[pb], t_const], w=[tps[2 + pb]])
                    cp("act", qnT[:, pb], pst[:, 0:256].rearrange("p (j t) -> p j t", t=128), r=[tps[2 + pb]], w=[t_qnT[pb]])
                    bq = 2 + pb
                    for j in range(2):
                        mm(ps[bq][:, 0:384], qnT[:, pb, j], Wuq[:, j, :], start=(j == 0), stop=(j == 1),
                           r=[t_qnT[pb], t_W], w=[tps[bq]])
                    cp("dve", kf[:, pb], ps[bq][:, 0:384].rearrange("p (h d) -> p h d", d=96), r=[tps[bq]], w=[t_kf[pb]])
                    headnorm_rope(pb, qkq, tt_, QT[0:96, :, qs * 128:(qs + 1) * 128], t_QT[qs], 2 + pb)
                if g == 1:
                    tap("QT", QT[0:96], t_QT)
                if stage < 0.8:
                    continue
                for h in range(4):
                    for ki, kt in enumerate(ktiles):
                        sb = it % 2
                        it += 1
                        mm(ps[sb][:, 0:nq], KT[0:96, h, kt * 128:(kt + 1) * 128], QT[0:96, h, 0:nq],
                           r=[t_KT[kt]] + t_QT[0:nqs], w=[tps[sb]])
                        act(PT[:, sb, 0:nq], ps[sb][:, 0:nq], AF.Exp, r=[tps[sb]], w=[t_PT[sb]], scale=scale)
                        for qs in range(nqs):
                            mm(ps[4 + qs][:, 0:65], PT[:, sb, qs * 128:(qs + 1) * 128], V1[:, kt, h, :],
                               start=(ki == 0), stop=(ki == len(ktiles) - 1),
                               r=[t_PT[sb], t_V1[kt]], w=[tps[4 + qs]])
                    for qs in range(nqs):
                        rcp(rinv[:, qs:qs + 1], ps[4 + qs][:, 64:65], r=[tps[4 + qs]], w=[t_rinv])
                        ts("dve", oat[:, qs, h * 64:(h + 1) * 64], ps[4 + qs][:, 0:64], rinv[:, qs:qs + 1], None,
                           ALU.mult, r=[tps[4 + qs], t_rinv], w=[t_oat[qs]])
                for qs in range(nqs):
                    tb = 2 + qs % 2
                    pst = ps[tb][:, :].bitcast(BF16)
                    for j in range(2):
                        tr(pst[:, j * 128:(j + 1) * 128], oat[:, qs, j * 128:(j + 1) * 128], identb,
                           r=[t_oat[qs], t_const], w=[tps[tb]])
                    tq = q0 + qs * 128
                    cp("act", oT[slot][:, :, tq:tq + 128], pst[:, 0:256].rearrange("p (j t) -> p j t", t=128),
                       r=[tps[tb]], w=[t_oT[slot][g]])
            tap("oaT", oT[slot], t_oT[slot])

        def fnet_phase(l, last, slot):
            P.barrier()
            al = mk_alloc(OT0 + slot * 2 * NT * 2)
            UT = al([2, NT], BF16)
            AB = al([NTT, 512], BF16)
            CS = al([2, 512], BF16)
            tb = al([2, 2, 1024], BF16)
            c256 = al([2, 2, 256], BF16)
            t_UT = [T() for _ in range(5)]
            t_AB = [T() for _ in range(NTT)]
            t_CS = T()
            t_tb = [T(), T()]
            t_c256 = T()
            dma(CS, dr["cs64"], w=[t_CS])
            for lt in range(2):
                dma(c256[:, lt], dr["dft256"][:, lt * 128:(lt + 1) * 128, :].rearrange("c p n -> p c n"), w=[t_c256])
            wsrc = dr["w_in"][l].rearrange("(k p) c -> p k c", p=128)
            groups = [g for g in range(5) if not (last and g == 0)]
            bi = 0
            for j in range(2):
                wv, t_wv = load_ring(wsrc[:, :, 1184 + j * 128:1184 + (j + 1) * 128], "p (k c) -> p k c", c=128)
                for g in groups:
                    t0, n = GRP(g)
                    pb = bi % 4
                    bi += 1
                    for k in range(8):
                        mm(ps[pb][:, 0:n], wv[:, k, :], hT[:, k, t0:t0 + n], start=(k == 0), stop=(k == 7),
                           r=[t_wv, t_h[g]], w=[tps[pb]])
                    cp("act", UT[:, j, t0:t0 + n], ps[pb][:, 0:n], r=[tps[pb]], w=[t_UT[g]])
            for tt_ in (range(2, NTT) if last else range(NTT)):
                g = grp_of_tile(tt_)
                pb = 4 + tt_ % 4
                for j in range(2):
                    mm(ps[pb][:, :], UT[:, j, tt_ * 128:(tt_ + 1) * 128], CS[:, j, :], start=(j == 0), stop=(j == 1),
                       r=[t_UT[g], t_CS], w=[tps[pb]])
                cp("dve", AB[:, tt_, :], ps[pb][:, :], r=[tps[pb]], w=[t_AB[tt_]])
            if not last:
                for j in range(2):
                    pb = j
                    n_mm = 0
                    for lt in range(2):
                        for cs_ in range(2):
                            mm(ps[pb][:, 0:256], AB[:, lt, cs_ * 256 + j * 128:cs_ * 256 + (j + 1) * 128],
                               c256[:, lt, cs_, :], start=(n_mm == 0), stop=(n_mm == 3),
                               r=[t_AB[lt], t_c256], w=[tps[pb]])
                            n_mm += 1
                    cp("act", oT[slot][:, j, 0:256], ps[pb][:, 0:256], r=[tps[pb]], w=[t_oT[slot][0]])
            it = 0
            for half in range(2):
                banks = [4 * half + i for i in range(4)]
                for lt in range(16):
                    b_ = it % 2
                    it += 1
                    dma(tb[:, b_], dr["dft2048"][:, lt * 128:(lt + 1) * 128, half * 1024:(half + 1) * 1024]
                        .rearrange("c p n -> p c n"), w=[t_tb[b_]])
                    for j in range(2):
                        for cs_ in range(2):
                            for lg in range(2):
                                bk = banks[j * 2 + lg]
                                mm(ps[bk][:, :], AB[:, 2 + lt, cs_ * 256 + j * 128:cs_ * 256 + (j + 1) * 128],
                                   tb[:, b_, cs_, lg * 512:(lg + 1) * 512],
                                   start=(lt == 0 and cs_ == 0), stop=(lt == 15 and cs_ == 1),
                                   r=[t_AB[2 + lt], t_tb[b_]], w=[tps[bk]])
                for j in range(2):
                    for lg in range(2):
                        bk = banks[j * 2 + lg]
                        g = 1 + half * 2 + lg
                        t0 = LAT0 + half * 1024 + lg * 512
                        cp("act", oT[slot][:, j, t0:t0 + 512], ps[bk][:, :], r=[tps[bk]], w=[t_oT[slot][g]])
            tap("obT", oT[slot], t_oT[slot])

        def ret_phase(l, last, slot):
            P.barrier()
            al = mk_alloc(OT0 + slot * 2 * NT * 2)
            rrc = al([NTT, 32], F32)
            rrs = al([NTT, 32], F32)
            retc = al([6, 128], F32)
            retp = al([2], F32)
            lgrow = al([8], F32)
            lgpp = al([2, 2], F32)
            gnw = al([256], F32)
            Wr = al([8, 512], BF16)
            QZ = al([2, NT], BF16)
            KT_ = al([NT], BF16)
            Vb = al([NTT, 128], BF16)
            sg = al([NTT, 128], BF16)
            Sfp = al([NTT, 128], BF16)
            Sbn = Wr.rearrange("p k c -> p (k c)")[:, 0:NTT * 128].rearrange("p (i c) -> p i c", c=128)
            Ub = al([NTT, 128], BF16)
            kdec = al([2, 128], F32)
            qdec = al([2, 128], F32)
            Mk = al([2, 128], F32)
            aab = al([2], F32)
            Sf = al([128], F32)
            Sb = al([128], F32)
            rt = al([2, 4, 2, 32], F32)
            Qb = al([2, 128], BF16)
            Kb = al([2, 128], BF16)
            Kd = al([1, 2, 128], BF16)
            Qd = al([1, 2, 2, 128], BF16)
            attm = al([2, 2, 128], BF16)
            xc = al([2, 64], F32)
            xcq = xc.rearrange("p a d -> p (a d)")
            sq = al([2, 64], F32)
            st_ = al([2, 4], F32)
            od = al([2, 128], BF16)
            t_c = T()
            t_lg = T()
            t_Wr = T()
            t_QK = [T() for _ in range(NTT)]
            t_Vb = [T() for _ in range(NTT)]
            t_sg = [T() for _ in range(NTT)]
            t_Sfp = [T() for _ in range(NTT)]
            t_Sbn = [T() for _ in range(NTT)]
            t_Ub = [T() for _ in range(NTT)]
            t_tab = T()
            t_S = T()
            t_rt = [T(), T()]
            t_Qb = [T(), T()]
            t_Kb = [T(), T()]
            t_Kd = [T(), T()]
            t_Qd = [T(), T()]
            t_attm = [T(), T()]
            t_gn = T()
            t_od = [T(), T()]
            dma(rrc, dr["rrc"], w=[t_c])
            dma(rrs, dr["rrs"], w=[t_c])
            dma(retc, dr["retc"], w=[t_c])
            dma(retp, dr["retp"], w=[t_c])
            dma(lgrow, dr["rlog_row"][l], w=[t_lg])
            dma(lgpp, dr["rlog_pp"][l], w=[t_lg])
            dma(gnw, dr["gnw"][l], w=[t_c])
            for v in (lgrow, lgpp.rearrange("p a b -> p (a b)")):
                act(v, v, AF.Exp, r=[t_lg], w=[t_lg], scale=-1.0)
                act(v, v, AF.Ln, r=[t_lg], w=[t_lg], bias=1.0)
                ts("dve", v, v, -1.0, None, ALU.mult, r=[t_lg], w=[t_lg])
            wsrc = dr["w_in"][l].rearrange("(k p) c -> p k c", p=128)
            for pair in range(2):
                P.barrier()
                offs = [416 + pair * 128, 672 + pair * 128, 1440 + pair * 128, 1696 + pair * 128]
                for ci, c0 in enumerate(offs):
                    load_cast(Wr[:, :, ci * 128:(ci + 1) * 128], t_Wr, wsrc[:, :, c0:c0 + 128], "p (k c) -> p k c", c=128)
                for hh in range(2):
                    head = 2 * pair + hh
                    cs_ = slice(hh * 64, (hh + 1) * 64)
                    act(kdec[:, 0, cs_], lgrow[:, head:head + 1].to_broadcast([128, 64]), AF.Exp, r=[t_lg, t_c], w=[t_tab],
                        scale=retp[:, 1:2])
                    act(kdec[:, 1, cs_], lgrow[:, 4 + head:5 + head].to_broadcast([128, 64]), AF.Exp, r=[t_lg, t_c], w=[t_tab],
                        scale=retp[:, 0:1])
                    act(Mk[:, hh, :], retc[:, 2, :], AF.Exp, r=[t_lg, t_c], w=[t_tab], scale=lgrow[:, head:head + 1])
                    tt("dve", Mk[:, hh, :], Mk[:, hh, :], retc[:, 4, :], ALU.mult, r=[t_tab, t_c], w=[t_tab])
                    act(xcq, retc[:, 3, :], AF.Exp, r=[t_lg, t_c], w=[t_tab], scale=lgrow[:, 4 + head:5 + head])
                    tt("dve", xcq, xcq, retc[:, 5, :], ALU.mult, r=[t_tab, t_c], w=[t_tab])
                    stt(Mk[:, hh, :], Mk[:, hh, :], 1.0, xcq, ALU.mult, ALU.add, r=[t_tab], w=[t_tab])
                ts("dve", Mk, Mk, 0.125, None, ALU.mult, r=[t_tab], w=[t_tab])
                ts("dve", kdec, kdec, 0.125, None, ALU.mult, r=[t_tab], w=[t_tab])
                act(qdec[:, 0, :], retc[:, 0, :], AF.Exp, r=[t_lg, t_c], w=[t_tab], scale=lgpp[:, pair, 0:1])
                act(qdec[:, 1, :], retc[:, 1, :], AF.Exp, r=[t_lg, t_c], w=[t_tab], scale=lgpp[:, pair, 1:2])
                act(aab, lgpp[:, pair, :], AF.Exp, r=[t_lg], w=[t_tab], scale=128.0)
                mset("dve", Sf, 0.0, w=[t_S])
                mset("dve", Sb, 0.0, w=[t_S])
                mset("pool", QZ, 0.0, w=t_QK)

                def rope(src, cosT, sinT, dst, pb, t_dst, t_src):
                    x = src.rearrange("p (h two b) -> p h two b", two=2, b=32)
                    y = dst.rearrange("p (h two b) -> p h two b", two=2, b=32)
                    cb = cosT.unsqueeze(1).to_broadcast([128, 2, 32])
                    sb_ = sinT.unsqueeze(1).to_broadcast([128, 2, 32])
                    r_ = rt[:, pb]
                    tt("dve", r_[:, 0], x[:, :, 0, :], cb, ALU.mult, r=[t_src, t_c], w=[t_rt[pb]])
                    tt("dve", r_[:, 1], x[:, :, 1, :], sb_, ALU.mult, r=[t_src, t_c], w=[t_rt[pb]])
                    tt("dve", y[:, :, 0, :], r_[:, 0], r_[:, 1], ALU.subtract, r=[t_rt[pb]], w=[t_dst])
                    tt("dve", r_[:, 2], x[:, :, 0, :], sb_, ALU.mult, r=[t_src, t_c], w=[t_rt[pb]])
                    tt("dve", r_[:, 3], x[:, :, 1, :], cb, ALU.mult, r=[t_src, t_c], w=[t_rt[pb]])
                    tt("dve", y[:, :, 1, :], r_[:, 2], r_[:, 3], ALU.add, r=[t_rt[pb]], w=[t_dst])

                for i in range(NTT):
                    g = grp_of_tile(i)
                    pb = i % 2
                    t0 = i * 128
                    bz = pb
                    for k in range(8):
                        mm(ps[bz][:, :], hT[:, k, t0:t0 + 128], Wr[:, k, :], start=(k == 0), stop=(k == 7),
                           r=[t_h[g], t_Wr], w=[tps[bz]])
                    z = ps[bz]
                    rope(z[:, 256:384], rrc[:, i, :], rrs[:, i, :], Qb[:, pb], pb, t_Qb[pb], tps[bz])
                    rope(z[:, 0:128], rrc[:, i, :], rrs[:, i, :], Kb[:, pb], pb, t_Kb[pb], tps[bz])
                    cp("act", Vb[:, i, :], z[:, 128:256], r=[tps[bz]], w=[t_Vb[i]])
                    act(sg[:, i, :], z[:, 384:512], AF.Silu, r=[tps[bz]], w=[t_sg[i]])
                    tb_ = 2 + pb
                    pst = ps[tb_][:, :].bitcast(BF16)
                    tr(pst[:, 0:128], Qb[:, pb], identb, r=[t_Qb[pb], t_const], w=[tps[tb_]])
                    tr(pst[:, 128:256], Kb[:, pb], identb, r=[t_Kb[pb], t_const], w=[tps[tb_]])
                    cp("act", QZ[0:64, 0, t0:t0 + 128], pst[0:64, 0:128], r=[tps[tb_]], w=[t_QK[i]])
                    cp("act", QZ[64:128, 1, t0:t0 + 128], pst[64:128, 0:128], r=[tps[tb_]], w=[t_QK[i]])
                    cp("act", KT_[:, t0:t0 + 128], pst[:, 128:256], r=[tps[tb_]], w=[t_QK[i]])
                    tt("pool", Kd[:, 0, 0], Kb[:, pb], kdec[:, 0], ALU.mult, r=[t_Kb[pb], t_tab], w=[t_Kd[0]])
                    tt("pool", Kd[:, 0, 1], Kb[:, pb], kdec[:, 1], ALU.mult, r=[t_Kb[pb], t_tab], w=[t_Kd[0]])
                    bu = 4 + pb
                    mm(ps[bu][:, 0:128], Kd[:, 0, 0], Vb[:, i, :], r=[t_Kd[0], t_Vb[i]], w=[tps[bu]])
                    mm(ps[bu][:, 128:256], Kd[:, 0, 1], Vb[:, i, :], r=[t_Kd[0], t_Vb[i]], w=[tps[bu]])
                    cp("dve", Sfp[:, i, :], Sf, r=[t_S], w=[t_Sfp[i]])
                    stt(Sf, Sf, aab[:, 0:1], ps[bu][:, 0:128], ALU.mult, ALU.add, r=[t_S, t_tab, tps[bu]], w=[t_S])
                    cp("act", Ub[:, i, :], ps[bu][:, 128:256], r=[tps[bu]], w=[t_Ub[i]])
                P.barrier()
                for i in [1, 0] + list(range(NTT - 1, 1, -1)):
                    cp("dve", Sbn[:, i, :], Sb, r=[t_S], w=[t_Sbn[i]])
                    stt(Sb, Sb, aab[:, 1:2], Ub[:, i, :], ALU.mult, ALU.add, r=[t_S, t_tab, t_Ub[i]], w=[t_S])
                for i in range(2 if last else 0, NTT):
                    g = grp_of_tile(i)
                    pb = i % 2
                    t0 = i * 128
                    ba = pb
                    for hh in range(2):
                        mm(ps[ba][:, hh * 128:(hh + 1) * 128], KT_[:, t0:t0 + 128], QZ[:, hh, t0:t0 + 128],
                           r=[t_QK[i]], w=[tps[ba]])
                    tt("dve", attm[:, pb], ps[ba][:, 0:256].rearrange("p (a t) -> p a t", t=128), Mk, ALU.mult,
                       r=[tps[ba], t_tab], w=[t_attm[pb]])
                    for hh in range(2):
                        tt("pool", Qd[:, 0, hh, 0], QZ[:, hh, t0:t0 + 128], qdec[:, 0], ALU.mult, r=[t_QK[i], t_tab], w=[t_Qd[0]])
                        tt("pool", Qd[:, 0, hh, 1], QZ[:, hh, t0:t0 + 128], qdec[:, 1], ALU.mult, r=[t_QK[i], t_tab], w=[t_Qd[0]])
                    bo = 4 + pb
                    for hh in range(2):
                        rs_ = slice(hh * 64, (hh + 1) * 64)
                        mm(ps[bo][:, rs_], attm[:, pb, hh, :], Vb[:, i, rs_], start=True, stop=False,
                           r=[t_attm[pb], t_Vb[i]], w=[tps[bo]])
                        mm(ps[bo][:, rs_], Qd[:, 0, hh, 0, :], Sfp[:, i, rs_], start=False, stop=False,
                           r=[t_Qd[0], t_Sfp[i]], w=[tps[bo]])
                        mm(ps[bo][:, rs_], Qd[:, 0, hh, 1, :], Sbn[:, i, rs_], start=False, stop=True,
                           r=[t_Qd[0], t_Sbn[i]], w=[tps[bo]])
                    o = ps[bo][:, 0:128].rearrange("p (a d) -> p a d", d=64)
                    red(st_[:, 0, 0:2], o, r=[tps[bo]], w=[t_gn])
                    ts("dve", st_[:, 0, 0:2], st_[:, 0, 0:2], -1.0 / 64, None, ALU.mult, r=[t_gn], w=[t_gn])
                    tt("dve", xc, o, st_[:, 0, 0:2].unsqueeze(2).to_broadcast([128, 2, 64]), ALU.add, r=[tps[bo], t_gn], w=[t_gn])
                    tt("dve", sq, xc, xc, ALU.mult, r=[t_gn], w=[t_gn])
                    red(st_[:, 1, 0:2], sq, r=[t_gn], w=[t_gn])
                    act(st_[:, 1, 0:2], st_[:, 1, 0:2], AF.Sqrt, r=[t_gn, t_const], w=[t_gn], scale=1.0 / 64, bias=eps_t[:, 0:1])
                    rcp(st_[:, 1, 0:2], st_[:, 1, 0:2], r=[t_gn], w=[t_gn])
                    tt("dve", xc, xc, st_[:, 1, 0:2].unsqueeze(2).to_broadcast([128, 2, 64]), ALU.mult, r=[t_gn], w=[t_gn])
                    tt("dve", xc, xc, gnw[:, pair * 128:(pair + 1) * 128].rearrange("p (a d) -> p a d", d=64), ALU.mult,
                       r=[t_gn, t_c], w=[t_gn])
                    tt("dve", od[:, pb].rearrange("p (a d) -> p a d", d=64), xc,
                       sg[:, i, :].rearrange("p (a d) -> p a d", d=64), ALU.mult, r=[t_gn, t_sg[i]], w=[t_od[pb]])
                    tb_ = 2 + pb
                    pst = ps[tb_][:, :].bitcast(BF16)
                    tr(pst[:, 0:128], od[:, pb], identb, r=[t_od[pb], t_const], w=[tps[tb_]])
                    cp("act", oT[slot][:, pair, t0:t0 + 128], pst[:, 0:128], r=[tps[tb_]], w=[t_oT[slot][g]])
            tap("odT", oT[slot], t_oT[slot])

        I32 = mybir.dt.int32
        TWO_PI = 2.0 * math.pi

        def s5_phase(l, last, slot):
            P.barrier()
            al = mk_alloc(OT0 + slot * 2 * NT * 2)
            uT = al([2, NT], BF16)
            yf = al([2, NT], BF16)
            E = al([2, 1024], BF16)
            Fm = al([8, 2, 128], BF16)
            Bb = al([2, 2, 512], BF16)
            Cc = al([2, 8, 128], BF16)
            Tri = al([2, 128], BF16)
            pp = al([3, 8], F32)
            sm = al([12, 8], F32)
            cst = al([4], F32)
            erow = al([2, 128], F32)
            ecol = al([4], F32)
            dvec = al([2], F32)
            xl = al([8, 2], F32)
            cc = al([8, 2], F32)
            woff = al.o[0]
            W = al([2, 1024], BF16)
            xx = al([2, 8, 128], BF16)
            tW = al([2, 256], F32)
            tq = al([4, 128], F32)
            ysc = al([4, 128], F32)
            t_u = [T() for _ in range(5)]
            t_yf = [T() for _ in range(NTT)]
            t_tab = T()
            t_pp = T()
            t_c = T()
            t_W = [T(), T()]
            t_xx = [T(), T()]
            t_tW = T()
            t_tq = T()
            t_xl = T()
            t_cc = T()
            t_ysc = T()
            t_blk = T()

            dma(Tri, dr["s5tri"], w=[t_c])
            dma(erow, dr["s5erow"], w=[t_c])
            dma(ecol, dr["s5ecol"], w=[t_c])
            dma(dvec, dr["s5d"][l], w=[t_c])
            mset("dve", cst[:, 0:1], -math.pi, w=[t_c])

            wsrc = dr["w_in"][l].rearrange("(k p) c -> p k c", p=128)
            bi = 0
            for j in range(2):
                wv, t_wv = load_ring(wsrc[:, :, 160 + j * 128:160 + (j + 1) * 128], "p (k c) -> p k c", c=128)
                for g in range(5):
                    t0, n = GRP(g)
                    pb = bi % 4
                    bi += 1
                    for k in range(8):
                        mm(ps[pb][:, 0:n], wv[:, k, :], hT[:, k, t0:t0 + n], start=(k == 0), stop=(k == 7),
                           r=[t_wv, t_h[g]], w=[tps[pb]])
                    cp("act", uT[:, j, t0:t0 + n], ps[pb][:, 0:n], r=[tps[pb]], w=[t_u[g]])
            tap("uT", uT, t_u)

            def cplx_pow(out_re, out_im, phase, mag, n, conj, tmp):
                r_, n_i, f_, m_ = tmp
                ts("dve", r_, phase, 1.0 / TWO_PI, None, ALU.mult, r=[t_blk], w=[t_blk])
                cp("dve", n_i.bitcast(I32), r_, r=[t_blk], w=[t_blk])
                cp("dve", f_, n_i.bitcast(I32), r=[t_blk], w=[t_blk])
                tt("dve", f_, r_, f_, ALU.subtract, r=[t_blk], w=[t_blk])
                ts("dve", m_, f_, 0.0, None, ALU.is_lt, r=[t_blk], w=[t_blk])
                tt("dve", f_, f_, m_, ALU.add, r=[t_blk], w=[t_blk])
                act(r_, f_, AF.Sin, r=[t_blk, t_c], w=[t_blk], scale=TWO_PI, bias=cst[:, 0:1])
                ts("dve", f_, f_, 0.25, None, ALU.add, r=[t_blk], w=[t_blk])
                ts("dve", m_, f_, 1.0, None, ALU.is_ge, r=[t_blk], w=[t_blk])
                tt("dve", f_, f_, m_, ALU.subtract, r=[t_blk], w=[t_blk])
                act(m_, f_, AF.Sin, r=[t_blk, t_c], w=[t_blk], scale=TWO_PI, bias=cst[:, 0:1])
                stt(out_re, mag, -1.0, m_, ALU.mult, ALU.mult, r=[t_blk], w=[t_blk, t_tab])
                if conj:
                    tt("dve", out_im, mag, r_, ALU.mult, r=[t_blk], w=[t_blk, t_tab])
                else:
                    stt(out_im, mag, -1.0, r_, ALU.mult, ALU.mult, r=[t_blk], w=[t_blk, t_tab])

            glw = None
            for d_ in range(2):
                P.barrier()
                B_ = [A.alloc([256], F32, at=woff + i * 1024) for i in range(16)]
                dma(pp, dr["s5pp"][l, d_], w=[t_pp])
                act(pp[:, 2, :], pp[:, 2, :], AF.Exp, r=[t_pp], w=[t_pp])
                App = sm[:, 0, :]
                Bpp = sm[:, 1, :]
                tt("dve", App, pp[:, 0, :], pp[:, 2, :], ALU.mult, r=[t_pp], w=[t_blk])
                tt("dve", Bpp, pp[:, 1, :], pp[:, 2, :], ALU.mult, r=[t_pp], w=[t_blk])
                l1re = sm[:, 2, :]
                l1im = sm[:, 3, :]
                mg = sm[:, 4, :]
                act(mg, App, AF.Exp, r=[t_blk], w=[t_blk])
                tmp8 = [B_[0][:, 0:8], B_[0][:, 8:16], B_[0][:, 16:24], B_[0][:, 24:32]]
                cplx_pow(l1re, l1im, Bpp, mg, 8, False, tmp8)
                br = sm[:, 5, :]
                den = sm[:, 6, :]
                kre = sm[:, 7, :]
                kim = sm[:, 8, :]
                nkre = sm[:, 9, :]
                nkim = sm[:, 10, :]
                t8 = sm[:, 11, :]
                ts("dve", br, l1re, -1.0, None, ALU.add, r=[t_blk], w=[t_blk])
                tt("dve", den, pp[:, 0, :], pp[:, 0, :], ALU.mult, r=[t_pp], w=[t_blk])
                tt("dve", t8, pp[:, 1, :], pp[:, 1, :], ALU.mult, r=[t_pp], w=[t_blk])
                tt("dve", den, den, t8, ALU.add, r=[t_blk], w=[t_blk])
                rcp(den, den, r=[t_blk], w=[t_blk])
                tt("dve", kre, br, pp[:, 0, :], ALU.mult, r=[t_blk, t_pp], w=[t_blk])
                tt("dve", t8, l1im, pp[:, 1, :], ALU.mult, r=[t_blk, t_pp], w=[t_blk])
                tt("dve", kre, kre, t8, ALU.add, r=[t_blk], w=[t_blk])
                tt("dve", kre, kre, den, ALU.mult, r=[t_blk], w=[t_blk])
                tt("dve", kim, l1im, pp[:, 0, :], ALU.mult, r=[t_blk, t_pp], w=[t_blk])
                tt("dve", t8, br, pp[:, 1, :], ALU.mult, r=[t_blk, t_pp], w=[t_blk])
                tt("dve", kim, kim, t8, ALU.subtract, r=[t_blk], w=[t_blk])
                tt("dve", kim, kim, den, ALU.mult, r=[t_blk], w=[t_blk])
                ts("dve", nkre, kre, -1.0, None, ALU.mult, r=[t_blk], w=[t_blk])
                ts("dve", nkim, kim, -1.0, None, ALU.mult, r=[t_blk], w=[t_blk])
                Cre = A.alloc([8, 128], F32, at=woff + 1 * 1024)
                Cim = A.alloc([8, 128], F32, at=woff + 5 * 1024)
                for ri, Cdst in enumerate((Cre, Cim)):
                    dma(Cdst, dr["s5c"][l, d_, ri].rearrange("a s c -> s a c"), w=[t_blk])
                tC = B_[9][:, 0:128]
                for st in range(8):
                    ts("dve", tC, Cre[:, st, :], kre[:, st:st + 1], None, ALU.mult, r=[t_blk], w=[t_blk])
                    stt(Cc[:, 0, st, :], Cim[:, st, :], nkim[:, st:st + 1], tC, ALU.mult, ALU.add, r=[t_blk], w=[t_tab])
                    ts("dve", tC, Cre[:, st, :], nkim[:, st:st + 1], None, ALU.mult, r=[t_blk], w=[t_blk])
                    stt(Cc[:, 1, st, :], Cim[:, st, :], nkre[:, st:st + 1], tC, ALU.mult, ALU.add, r=[t_blk], w=[t_tab])
                for kt in range(2):
                    load_cast(Bb[:, kt], t_tab, dr["s5b"][l, d_, :, kt], "p (a c) -> p a c", c=512)
                er = erow[:, d_, :]
                for st in range(8):
                    ph = B_[9][:, 0:128]
                    mgb = B_[9][:, 128:256]
                    ts("dve", ph, er, Bpp[:, st:st + 1], None, ALU.mult, r=[t_c, t_blk], w=[t_blk])
                    act(mgb, er, AF.Exp, r=[t_c, t_blk], w=[t_blk], scale=App[:, st:st + 1])
                    tmpb = [B_[10][:, 0:128], B_[10][:, 128:256], B_[11][:, 0:128], B_[11][:, 128:256]]
                    cplx_pow(Fm[:, st, 0, :], Fm[:, st, 1, :], ph, mgb, 128, False, tmpb)
                row = A.alloc([3, 256], F32, at=woff + 1 * 1024)
                for cb in range(4):
                    dma(row, dr["s5row"][l, d_, :, :, cb * 256:(cb + 1) * 256], w=[t_blk])
                    act(row[:, 2, :], row[:, 2, :], AF.Exp, r=[t_blk], w=[t_blk])
                    Ab = B_[4]
                    Bk = B_[5]
                    tt("dve", Ab, row[:, 0, :], row[:, 2, :], ALU.mult, r=[t_blk], w=[t_blk])
                    tt("dve", Bk, row[:, 1, :], row[:, 2, :], ALU.mult, r=[t_blk], w=[t_blk])
                    ph = B_[6]
                    mgb = B_[7]
                    ts("dve", ph, Bk, ecol[:, d_:d_ + 1], None, ALU.mult, r=[t_blk, t_c], w=[t_blk])
                    act(mgb, Ab, AF.Exp, r=[t_blk, t_c], w=[t_blk], scale=ecol[:, 2 + d_:3 + d_])
                    tmpb = [B_[8], B_[9], B_[10], B_[11]]
                    cplx_pow(E[:, 0, cb * 256:(cb + 1) * 256], E[:, 1, cb * 256:(cb + 1) * 256], ph, mgb, 256, True, tmpb)
                if d_ == 0:
                    tap("s5E", E, [t_tab])
                    tap("s5F", Fm, [t_tab])
                    tap("s5C", Cc, [t_tab])
                P.barrier()
                order = list(range(NTT)) if d_ == 0 else [1, 0] + list(range(NTT - 1, 1, -1))
                lastcol = 127 if d_ == 0 else 0
                mset("dve", cc, 0.0, w=[t_cc])
                for idx, i in enumerate(order):
                    g = grp_of_tile(i)
                    t0 = i * 128
                    pb = idx % 2
                    for nb in range(4):
                        kt = nb % 2
                        ri = nb // 2
                        mm(ps[nb][:, :], uT[:, kt, t0:t0 + 128], Bb[:, kt, ri, :], r=[t_u[g], t_tab], w=[tps[nb]])
                    for qb in range(4):
                        hb = qb // 2
                        sl = slice(qb * 256, (qb + 1) * 256)
                        pl = slice((qb % 2) * 256, (qb % 2 + 1) * 256)
                        tt("dve", tW[:, 0, :], ps[hb][:, pl], E[:, 0, sl], ALU.mult, r=[tps[hb], t_tab], w=[t_tW])
                        tt("dve", tW[:, 1, :], ps[2 + hb][:, pl], E[:, 1, sl], ALU.mult, r=[tps[2 + hb], t_tab], w=[t_tW])
                        tt("pool", W[:, 0, sl], tW[:, 0, :], tW[:, 1, :], ALU.subtract, r=[t_tW], w=[t_W[0]])
                        tt("dve", tW[:, 0, :], ps[hb][:, pl], E[:, 1, sl], ALU.mult, r=[tps[hb], t_tab], w=[t_tW])
                        tt("dve", tW[:, 1, :], ps[2 + hb][:, pl], E[:, 0, sl], ALU.mult, r=[tps[2 + hb], t_tab], w=[t_tW])
                        tt("pool", W[:, 1, sl], tW[:, 0, :], tW[:, 1, :], ALU.add, r=[t_tW], w=[t_W[1]])
                    for ri in range(2):
                        for st in range(8):
                            bk = 4 + ri * 2 + st // 4
                            mm(ps[bk][:, (st % 4) * 128:(st % 4 + 1) * 128], W[:, ri, st * 128:(st + 1) * 128], Tri[:, d_, :],
                               r=[t_W[ri], t_c], w=[tps[bk]])
                    for st in range(8):
                        cs_ = slice((st % 4) * 128, (st % 4 + 1) * 128)
                        Sre = ps[4 + st // 4][:, cs_]
                        Sim = ps[6 + st // 4][:, cs_]
                        stt(tq[:, 0, :], Sre, cc[:, st, 0:1], Fm[:, st, 0, :], ALU.add, ALU.mult,
                            r=[tps[4 + st // 4], t_cc, t_tab], w=[t_tq])
                        stt(tq[:, 1, :], Sim, cc[:, st, 1:2], Fm[:, st, 1, :], ALU.add, ALU.mult,
                            r=[tps[6 + st // 4], t_cc, t_tab], w=[t_tq])
                        stt(tq[:, 2, :], Sre, cc[:, st, 0:1], Fm[:, st, 1, :], ALU.add, ALU.mult,
                            r=[tps[4 + st // 4], t_cc, t_tab], w=[t_tq])
                        stt(tq[:, 3, :], Sim, cc[:, st, 1:2], Fm[:, st, 0, :], ALU.add, ALU.mult,
                            r=[tps[6 + st // 4], t_cc, t_tab], w=[t_tq])
                        tt("dve", xx[:, 0, st, :], tq[:, 0, :], tq[:, 1, :], ALU.subtract, r=[t_tq], w=[t_xx[0]])
                        tt("dve", xx[:, 1, st, :], tq[:, 2, :], tq[:, 3, :], ALU.add, r=[t_tq], w=[t_xx[1]])
                        tt("dve", xl[:, st, 0:1], tq[:, 0, lastcol:lastcol + 1], tq[:, 1, lastcol:lastcol + 1], ALU.subtract,
                           r=[t_tq], w=[t_xl])
                        tt("dve", xl[:, st, 1:2], tq[:, 2, lastcol:lastcol + 1], tq[:, 3, lastcol:lastcol + 1], ALU.add,
                           r=[t_tq], w=[t_xl])
                    ta_ = sm[:, 5, :]
                    tb_ = sm[:, 6, :]
                    tt("dve", ta_, l1re, xl[:, :, 0], ALU.mult, r=[t_xl, t_blk], w=[t_blk])
                    tt("dve", tb_, l1im, xl[:, :, 1], ALU.mult, r=[t_xl, t_blk], w=[t_blk])
                    tt("dve", cc[:, :, 0], ta_, tb_, ALU.subtract, r=[t_blk], w=[t_cc])
                    tt("dve", ta_, l1re, xl[:, :, 1], ALU.mult, r=[t_xl, t_blk], w=[t_blk])
                    tt("dve", tb_, l1im, xl[:, :, 0], ALU.mult, r=[t_xl, t_blk], w=[t_blk])
                    tt("dve", cc[:, :, 1], ta_, tb_, ALU.add, r=[t_blk], w=[t_cc])
                    if last and i < 2:
                        continue
                    for j in range(2):
                        n_mm = 0
                        for st in range(4 * j, 4 * j + 4):
                            for ri in range(2):
                                mm(ps[j][:, 0:128], Cc[:, ri, st, :], xx[:, ri, st, :], start=(n_mm == 0), stop=(n_mm == 7),
                                   r=[t_tab, t_xx[ri]], w=[tps[j]])
                                n_mm += 1
                        if d_ == 0:
                            cp("act", yf[:, j, t0:t0 + 128], ps[j][:, 0:128], r=[tps[j]], w=[t_yf[i]])
                        else:
                            y = ysc[:, 0, :]
                            tt("dve", y, ps[j][:, 0:128], yf[:, j, t0:t0 + 128], ALU.add, r=[tps[j], t_yf[i]], w=[t_ysc])
                            stt(y, uT[:, j, t0:t0 + 128], dvec[:, j:j + 1], y, ALU.mult, ALU.add, r=[t_u[g], t_c, t_ysc], w=[t_ysc])
                            tt("dve", ysc[:, 1, :], y, y, ALU.mult, r=[t_ysc], w=[t_ysc])
                            ts("dve", ysc[:, 1, :], ysc[:, 1, :], 0.044715, 1.0, ALU.mult, ALU.add, r=[t_ysc], w=[t_ysc])
                            tt("dve", ysc[:, 1, :], ysc[:, 1, :], y, ALU.mult, r=[t_ysc], w=[t_ysc])
                            act(ysc[:, 2, :], ysc[:, 1, :], AF.Sigmoid, r=[t_ysc], w=[t_ysc], scale=1.5957691216057308)
                            tt("dve", yf[:, j, t0:t0 + 128], y, ysc[:, 2, :], ALU.mult, r=[t_ysc], w=[t_yf[i]])
            tap("s5g", yf, t_yf)
            P.barrier()
            glw = A.alloc([2, 512], BF16, at=woff)
            t_glw = T()
            load_cast(glw[:, 0], t_glw, dr["s5_w_glu"][l][0:128, :])
            load_cast(glw[:, 1], t_glw, dr["s5_w_glu"][l][128:256, :])
            sgt = A.alloc([512], F32, at=woff + 2048)
            t_sgt = T()
            bi = 0
            for g in range(1 if last else 0, 5):
                t0, n = GRP(g)
                tiles = list(range(t0 // 128, (t0 + n) // 128))
                for j in range(2):
                    pv = bi % 2
                    pg = 2 + bi % 2
                    bi += 1
                    for kt in range(2):
                        mm(ps[pv][:, 0:n], glw[:, kt, j * 128:(j + 1) * 128], yf[:, kt, t0:t0 + n], start=(kt == 0), stop=(kt == 1),
                           r=[t_glw] + [t_yf[i] for i in tiles], w=[tps[pv]])
                    for kt in range(2):
                        mm(ps[pg][:, 0:n], glw[:, kt, 256 + j * 128:256 + (j + 1) * 128], yf[:, kt, t0:t0 + n],
                           start=(kt == 0), stop=(kt == 1), r=[t_glw] + [t_yf[i] for i in tiles], w=[tps[pg]])
                    act(sgt[:, 0:n], ps[pg][:, 0:n], AF.Sigmoid, r=[tps[pg]], w=[t_sgt])
                    tt("dve", oT[slot][:, j, t0:t0 + n], ps[pv][:, 0:n], sgt[:, 0:n], ALU.mult, r=[tps[pv], t_sgt],
                       w=[t_oT[slot][g]])
            tap("ocT", oT[slot], t_oT[slot])

        SLOT_OF = {0: 3, 1: 0, 2: 1, 3: 2}

        def merge_phase(l, last):
            P.barrier()
            al = mk_alloc(OT0)
            mT = al([8, NT], BF16)
            sig = al([512], F32)
            acc = al([512], F32)
            t_m = [T() for _ in range(5)]
            t_sig = T()
            t_acc = T()
            wsrc = dr["w_in"][l].rearrange("(k p) c -> p k c", p=128)
            wbsrc = dr["w_branch"][l].rearrange("n (j p) d -> p n j d", p=128)
            groups = [g for g in range(5) if not (last and g == 0)]
            bi = 0
            for d in range(8):
                gw = []
                for n in range(4):
                    c0 = 1952 + n * 1024 + d * 128
                    gw.append(load_ring(wsrc[:, :, c0:c0 + 128], "p (k c) -> p k c", c=128))
                wb, t_wb = load_ring(wbsrc[:, :, :, d * 128:(d + 1) * 128], "p (n j c) -> p n j c", j=2, c=128)
                for g in groups:
                    t0, n_ = GRP(g)
                    for n in range(4):
                        sl = SLOT_OF[n]
                        pa = bi % 2
                        pb = 2 + bi % 2
                        bi += 1
                        wv, t_wv = gw[n]
                        for k in range(8):
                            mm(ps[pa][:, 0:n_], wv[:, k, :], hT[:, k, t0:t0 + n_], start=(k == 0), stop=(k == 7),
                               r=[t_wv, t_h[g]], w=[tps[pa]])
                        for j in range(2):
                            mm(ps[pb][:, 0:n_], wb[:, n, j, :], oT[sl][:, j, t0:t0 + n_], start=(j == 0), stop=(j == 1),
                               r=[t_wb, t_oT[sl][g]], w=[tps[pb]])
                        act(sig[:, 0:n_], ps[pa][:, 0:n_], AF.Sigmoid, r=[tps[pa]], w=[t_sig])
                        if n == 0:
                            tt("dve", acc[:, 0:n_], ps[pb][:, 0:n_], sig[:, 0:n_], ALU.mult, r=[tps[pb], t_sig], w=[t_acc])
                        else:
                            tt("dve", ps[pb][:, 0:n_], ps[pb][:, 0:n_], sig[:, 0:n_], ALU.mult, r=[tps[pb], t_sig], w=[tps[pb]])
                            if n < 3:
                                tt("dve", acc[:, 0:n_], acc[:, 0:n_], ps[pb][:, 0:n_], ALU.add, r=[tps[pb], t_acc], w=[t_acc])
                            else:
                                tt("dve", mT[:, d, t0:t0 + n_], acc[:, 0:n_], ps[pb][:, 0:n_], ALU.add,
                                   r=[tps[pb], t_acc], w=[t_m[g]])
            tap("mT", mT, t_m)
            wosrc = dr["w_out"][l].rearrange("(k p) c -> p k c", p=128)
            for d in range(8):
                wv, t_wv = load_ring(wosrc[:, :, d * 128:(d + 1) * 128], "p (k c) -> p k c", c=128)
                for g in groups:
                    t0, n_ = GRP(g)
                    s_ = 1 if g == 0 else 0
                    pb = 4 + bi % 4
                    bi += 1
                    for k in range(8):
                        mm(ps[pb][:, 0:n_], wv[:, k, :], mT[:, k, t0:t0 + n_], start=(k == 0), stop=(k == 7),
                           r=[t_wv, t_m[g]], w=[tps[pb]])
                    stt(xT[:, d, t0:t0 + n_], ps[pb][:, 0:n_], mod[:, l, 16 + d, s_:s_ + 1], xT[:, d, t0:t0 + n_],
                        ALU.mult, ALU.add, r=[tps[pb], t_mod, t_x[d][g]], w=[t_x[d][g]])

        def ffn_phase(l, last):
            groups = [g for g in range(5) if not (last and g == 0)]
            norm_phase(l, 1, groups)
            P.barrier()
            al = mk_alloc(A.nbytes)
            aT = al([8, NT], BF16)
            rl = al([2, 512], F32)
            t_a = [[T() for _ in range(5)] for _ in range(8)]
            t_rl = [T(), T()]
            w1src = dr["ffn_w1"][l].rearrange("(k p) c -> p k c", p=128)
            w2src = dr["ffn_w2"][l].rearrange("(f p) c -> p f c", p=128)
            bi = 0
            for fb in range(4):
                for f in range(8):
                    F_ = fb * 8 + f
                    wv, t_wv = load_ring(w1src[:, :, F_ * 128:(F_ + 1) * 128], "p (k c) -> p k c", c=128)
                    for g in groups:
                        t0, n_ = GRP(g)
                        pb = bi % 4
                        rb = bi % 2
                        bi += 1
                        for k in range(8):
                            mm(ps[pb][:, 0:n_], wv[:, k, :], hT[:, k, t0:t0 + n_], start=(k == 0), stop=(k == 7),
                               r=[t_wv, t_h[g]], w=[tps[pb]])
                        act(rl[:, rb, 0:n_], ps[pb][:, 0:n_], AF.Relu, r=[tps[pb]], w=[t_rl[rb]])
                        tt("pool", aT[:, f, t0:t0 + n_], rl[:, rb, 0:n_], rl[:, rb, 0:n_], ALU.mult, r=[t_rl[rb]], w=[t_a[f][g]])
                for d in range(8):
                    wv, t_wv = load_ring(w2src[:, fb * 8:(fb + 1) * 8, d * 128:(d + 1) * 128], "p (f c) -> p f c", c=128)
                    for g in groups:
                        t0, n_ = GRP(g)
                        s_ = 1 if g == 0 else 0
                        pb = 4 + bi % 4
                        bi += 1
                        for f in range(8):
                            mm(ps[pb][:, 0:n_], wv[:, f, :], aT[:, f, t0:t0 + n_], start=(f == 0), stop=(f == 7),
                               r=[t_wv, t_a[f][g]], w=[tps[pb]])
                        stt(xT[:, d, t0:t0 + n_], ps[pb][:, 0:n_], mod[:, l, 40 + d, s_:s_ + 1], xT[:, d, t0:t0 + n_],
                            ALU.mult, ALU.add, r=[tps[pb], t_mod, t_x[d][g]], w=[t_x[d][g]])

        for l in range(DEPTH):
            last = (l == DEPTH - 1)
            norm_phase(l, 0, list(range(5)))
            if l == 0:
                tap("hT", hT, t_h)
            if stage <= 0.1:
                break
            if "mla" in mixers:
                mla_phase(l, last, 3)
            if "ret" in mixers:
                ret_phase(l, last, 2)
            if "s5" in mixers:
                s5_phase(l, last, 1)
            if "fnet" in mixers:
                fnet_phase(l, last, 0)
            if stage <= 1:
                break
            merge_phase(l, last)
            ffn_phase(l, last)
            if l == 0:
                tap("x1", xT, [t for k in range(8) for t in t_x[k]])
            if stage <= 2:
                break

        osrc = outT.rearrange("(k p) t -> p k t", p=128)
        for k in range(8):
            out_handles.append(dma(osrc[:, k, :], xT[:, k, LAT0:NT], r=t_x[k]))
        P.wait_all("sp", out_handles)
        P.emit()
    return nc


_CACHE = {}


def _specs_of(d):
    sp = {}
    for k, v in d.items():
        sp[k] = (v.shape, BF16 if v.dtype == NPBF else F32)
    return sp


def run(inputs, stage=99, taps=(), ncores=8, mixers=("mla", "s5", "ret", "fnet")):
    com = prep_common(inputs)
    cores = [prep_core(inputs, b) for b in range(ncores)]
    in_maps = [dict(com, **c) for c in cores]
    nc = build(_specs_of(in_maps[0]), stage=stage, taps=taps, mixers=mixers)
    res = run_bass_kernel_spmd(nc, in_maps, core_ids=list(range(ncores)))
    return res.results


def kernel(**inputs):
    inputs = {k: np.asarray(v) for k, v in inputs.items()}
    res = run(inputs)
    out = np.stack([r["outT"].T for r in res], axis=0)
    return np.ascontiguousarray(out.astype(np.float32))
```

```python
import contextlib
import math
import os
import numpy as np
import ml_dtypes
import concourse.bass as bass
import concourse.mybir as mybir
from concourse.bass_utils import run_bass_kernel_spmd

F32 = mybir.dt.float32
BF16 = mybir.dt.bfloat16
ALU = mybir.AluOpType
AF = mybir.ActivationFunctionType
AX = mybir.AxisListType
NPBF = ml_dtypes.bfloat16

ENGS = ("pe", "act", "dve", "pool", "sp")
NOSELF = tuple(os.environ.get("NOSELF", "pe").split(","))
NDSLOT = 8

D = 1024
NT = 2304
NTT = 18
LAT0 = 256
EPS = 1e-6
DEPTH = 2


class T:
    __slots__ = ("name", "w", "rs", "excl")

    def __init__(self, name="", excl=False):
        self.name = name
        self.w = None
        self.rs = []
        self.excl = excl


class Prog:
    def __init__(self, nc, stack, same_sync=True):
        self.nc = nc
        self.same_sync = same_sync
        self.q = {e: [] for e in ENGS}
        self.cnt = {e: 0 for e in ENGS}
        self.sems = {}
        for e in ENGS:
            self.sems[("c", e)] = stack.enter_context(nc.semaphore("c_" + e))
        self.dq = ("sp", "pool", "act")
        self.dcnt = {}
        self.dn = {e: 0 for e in self.dq}
        for e in self.dq:
            for s in range(NDSLOT):
                self.sems[("d", e, s)] = stack.enter_context(nc.semaphore("d_%s%d" % (e, s)))
                self.dcnt[(e, s)] = 0
        self.known = {e: {} for e in ENGS}
        self.kstop = None
        self.kcount = 0

    def _deps(self, eng, r, w):
        deps = {}

        def add(h):
            if h is None:
                return
            k, v = h
            if k == ("c", eng) and (eng in NOSELF or not self.same_sync):
                return
            if deps.get(k, 0) < v:
                deps[k] = v
        for t in r:
            add(t.w)
        for t in w:
            add(t.w)
            for h in t.rs:
                add(h)
        out = []
        kn = self.known[eng]
        for k, v in deps.items():
            if kn.get(k, 0) >= v:
                continue
            kn[k] = v
            out.append((k, v))
        return out

    def _mark(self, h, r, w):
        for t in r:
            t.rs.append(h)
            if len(t.rs) > 64:
                best = {}
                for k, v in t.rs:
                    if best.get(k, 0) < v:
                        best[k] = v
                t.rs = list(best.items())
        for t in w:
            t.w = h
            t.rs = []

    def op(self, eng, fn, r=(), w=()):
        if self.kstop is not None:
            self.kcount += 1
            if self.kcount > self.kstop:
                return None
        if eng != "pe":
            ex = [t for t in r if t.excl]
            if ex:
                r = [t for t in r if not t.excl]
                w = list(w) + ex
        waits = self._deps(eng, r, w)
        self.cnt[eng] += 1
        h = (("c", eng), self.cnt[eng])
        self.q[eng].append((fn, waits, (h[0], 1)))
        self._mark(h, r, w)
        return h

    def dma(self, eng, fn, r=(), w=()):
        s = self.dn[eng] % NDSLOT
        self.dn[eng] += 1
        waits = self._deps(eng, r, w)
        k = ("d", eng, s)
        prev = self.dcnt[(eng, s)]
        if prev > 0 and self.known[eng].get(k, 0) < prev:
            self.known[eng][k] = prev
            waits.append((k, prev))
        self.dcnt[(eng, s)] = prev + 16
        h = (k, prev + 16)
        self.q[eng].append((fn, waits, (k, 16)))
        self._mark(h, r, w)
        return h

    def barrier(self):
        hs = [(("c", e), self.cnt[e]) for e in ENGS if self.cnt[e] > 0]
        hs += [(("d", e, s), v) for (e, s), v in self.dcnt.items() if v > 0]
        for e in ENGS:
            self.wait_all(e, [h for h in hs if h[0] != ("c", e)])

    def wait_all(self, eng, hs):
        waits = []
        for k, v in hs:
            if self.known[eng].get(k, 0) < v:
                self.known[eng][k] = v
                waits.append((k, v))
        self.q[eng].append((None, waits, None))

    def emit(self):
        nc = self.nc
        sems = self.sems
        q = self.q

        def run(e, engobj):
            for fn, waits, inc in q[e]:
                for k, v in waits:
                    engobj.wait_ge(sems[k], v)
                if fn is None:
                    continue
                ins = fn(engobj)
                ins.then_inc(sems[inc[0]], inc[1])

        with nc.Block() as block:
            @block.tensor
            def _(eng):
                run("pe", eng)

            @block.scalar
            def _(eng):
                run("act", eng)

            @block.vector
            def _(eng):
                run("dve", eng)

            @block.gpsimd
            def _(eng):
                run("pool", eng)

            @block.sync
            def _(eng):
                run("sp", eng)


class Arena:
    def __init__(self, nc, stack, nbytes):
        self.t = stack.enter_context(nc.sbuf_tensor("arena", [128, nbytes // 4], F32))
        self.nbytes = nbytes
        self.off = 0

    def alloc(self, shape, dtype, at=None):
        esz = 2 if dtype == BF16 else 4
        n = int(np.prod(shape)) * esz
        n4 = (n + 3) // 4
        if at is None:
            at = self.off
            self.off += n4 * 4
        assert at % 4 == 0 and at + n4 * 4 <= self.nbytes, (at, n, self.nbytes)
        ap = self.t[:, at // 4: at // 4 + n4]
        if dtype != F32:
            ap = ap.bitcast(dtype)
        if len(shape) == 2:
            ap = ap.rearrange("p (a b) -> p a b", b=shape[1])
        elif len(shape) == 3:
            ap = ap.rearrange("p (a b c) -> p a b c", b=shape[1], c=shape[2])
        elif len(shape) == 4:
            ap = ap.rearrange("p (a b c d) -> p a b c d", b=shape[1], c=shape[2], d=shape[3])
        return ap


def _rope_tables():
    half = 8
    freqs = (10000.0 ** (-np.arange(half, dtype=np.float32) / half)).astype(np.float32)
    t = np.arange(2048)
    rows = (t // 64).astype(np.float32)
    cols = (t % 64).astype(np.float32)
    ang = np.concatenate([rows[:, None] * freqs[None], cols[:, None] * freqs[None]], axis=1)
    cos = np.ones((NT, 16), np.float32)
    sin = np.zeros((NT, 16), np.float32)
    cos[LAT0:] = np.cos(ang)
    sin[LAT0:] = np.sin(ang)
    cos = cos.reshape(NTT, 128, 16).transpose(1, 0, 2)
    sin = sin.reshape(NTT, 128, 16).transpose(1, 0, 2)
    return np.ascontiguousarray(cos), np.ascontiguousarray(sin)


def _fnet_consts():
    ci = np.arange(64)
    c64 = np.cos(2 * np.pi * np.outer(ci, ci) / 64.0)
    s64 = np.sin(2 * np.pi * np.outer(ci, ci) / 64.0)
    cs = np.zeros((2, 128, 512), np.float64)
    for j in range(2):
        for gl in range(2):
            g = 2 * j + gl
            cs[j, gl * 64:(gl + 1) * 64, g * 64:(g + 1) * 64] = c64
            cs[j, gl * 64:(gl + 1) * 64, 256 + g * 64:256 + (g + 1) * 64] = s64
    out = {"cs64": np.ascontiguousarray(cs.transpose(1, 0, 2)).astype(np.float32).astype(NPBF)}
    for L in (2048, 256):
        li = np.arange(L)
        m = np.outer(li, li) % L
        ang = 2 * np.pi * m / L
        sc = 1.0 / math.sqrt(L * 64.0)
        tab = np.stack([np.cos(ang) * sc, -np.sin(ang) * sc], axis=0)
        out["dft%d" % L] = tab.astype(np.float32).astype(NPBF)
    return out


def _s5_layouts(inp):
    f = np.float32
    out = {}
    m = np.arange(128)[:, None]
    t = np.arange(128)[None, :]
    tri = np.stack([(m <= t), (m >= t)], axis=1).astype(f)
    out["s5tri"] = tri.astype(NPBF)
    erow = np.stack([np.broadcast_to(t.astype(f), (128, 128)), np.broadcast_to(127.0 - t.astype(f), (128, 128))], axis=1)
    out["s5erow"] = np.ascontiguousarray(erow, f)
    p = np.arange(128, dtype=f)[:, None]
    out["s5ecol"] = np.ascontiguousarray(np.concatenate([p, 127.0 - p, -p, -(127.0 - p)], axis=1), f)
    out["s5d"] = np.ascontiguousarray(np.asarray(inp["s5_d"], f).reshape(DEPTH, 2, 128).transpose(0, 2, 1))
    re = np.asarray(inp["s5_lam_re"], f).reshape(DEPTH, 2, 1024)
    im = np.asarray(inp["s5_lam_im"], f).reshape(DEPTH, 2, 1024)
    ls = np.repeat(np.asarray(inp["s5_log_step"], f), 64, axis=-1)
    trip = np.stack([re, im, ls], axis=2)
    out["s5pp"] = np.ascontiguousarray(trip.reshape(DEPTH, 2, 3, 8, 128).transpose(0, 1, 4, 2, 3))
    out["s5row"] = np.ascontiguousarray(np.broadcast_to(trip[:, :, None], (DEPTH, 2, 128, 3, 1024)), f)
    bre = np.asarray(inp["s5_b_re"], f)
    bim = np.asarray(inp["s5_b_im"], f)
    sb = np.zeros((DEPTH, 2, 128, 2, 2, 512), f)
    for ri, bb in enumerate((bre, bim)):
        for kt in range(2):
            for gl in range(8):
                g = 8 * kt + gl
                sb[:, :, gl * 16:(gl + 1) * 16, kt, ri, gl * 64:(gl + 1) * 64] = bb[:, :, g].transpose(0, 1, 3, 2)
    out["s5b"] = sb
    cre = np.asarray(inp["s5_c_re"], f)
    cim = np.asarray(inp["s5_c_im"], f)
    sc = np.zeros((DEPTH, 2, 2, 8, 128, 128), f)
    for ri, cm in enumerate((cre, cim)):
        for st in range(8):
            for gl in range(2):
                g = 2 * st + gl
                col = (g % 8) * 16
                sc[:, :, ri, st, gl * 64:(gl + 1) * 64, col:col + 16] = cm[:, :, g].transpose(0, 1, 3, 2)
    out["s5c"] = sc
    out["s5_w_glu"] = np.ascontiguousarray(inp["s5_w_glu"], f)
    return out


def _ret_consts():
    half = 32
    freqs = (10000.0 ** (-np.arange(half, dtype=np.float32) / half)).astype(np.float32)
    pos = np.arange(2048, dtype=np.float32)
    ang = pos[:, None] * freqs[None]
    cos = np.ones((NT, 32), np.float32)
    sin = np.zeros((NT, 32), np.float32)
    cos[LAT0:] = np.cos(ang)
    sin[LAT0:] = np.sin(ang)
    tm = lambda a: np.ascontiguousarray(a.reshape(NTT, 128, 32).transpose(1, 0, 2))
    out = {"rrc": tm(cos), "rrs": tm(sin)}
    k = np.arange(128, dtype=np.float32)[:, None]
    q = np.arange(128, dtype=np.float32)[None, :]
    retc = np.stack([np.broadcast_to(q + 1.0, (128, 128)), np.broadcast_to(128.0 - q, (128, 128)),
                     np.maximum(q - k, 0.0), np.maximum(k - q, 0.0),
                     (q >= k).astype(np.float32), (k >= q).astype(np.float32)], axis=1)
    out["retc"] = np.ascontiguousarray(retc, np.float32)
    out["retp"] = np.ascontiguousarray(np.concatenate([k, 127.0 - k], axis=1), np.float32)
    return out


def prep_common(inp):
    f = np.float32
    c = {}
    c["ada_w"] = np.ascontiguousarray(inp["ada_w"], f)
    c["ada_bT"] = np.ascontiguousarray(inp["ada_b"].reshape(DEPTH, 48, 128).transpose(0, 2, 1), f)
    nw = np.concatenate([inp["norm_mix_w"].reshape(DEPTH, 8, 128), inp["norm_ffn_w"].reshape(DEPTH, 8, 128)], axis=1)
    c["nw"] = np.ascontiguousarray(nw.transpose(0, 2, 1), f)
    c["w_in"] = np.ascontiguousarray(inp["w_in"], f)
    bc = lambda v: np.ascontiguousarray(np.broadcast_to(v[:, None, :], (DEPTH, 128, v.shape[-1])), f)
    c["kvw"] = bc(inp["mla_kv_norm"])
    c["qnw"] = bc(inp["mla_q_norm"])
    c["qkq"] = bc(np.tile(inp["mla_qk_norm_q"], (1, 4)))
    c["qkk"] = bc(np.tile(inp["mla_qk_norm_k"], (1, 4)))
    c["w_ukv"] = np.ascontiguousarray(inp["mla_w_ukv"], f)
    c["w_uq"] = np.ascontiguousarray(inp["mla_w_uq"], f)
    cos, sin = _rope_tables()
    c["ropec"] = cos
    c["ropes"] = sin
    c["w_branch"] = np.ascontiguousarray(inp["w_branch"], f)
    c["w_out"] = np.ascontiguousarray(inp["w_out"], f)
    c["ffn_w1"] = np.ascontiguousarray(inp["ffn_w1"], f)
    c["ffn_w2"] = np.ascontiguousarray(inp["ffn_w2"], f)
    c.update(_fnet_consts())
    c.update(_ret_consts())
    c.update(_s5_layouts(inp))
    lg = np.asarray(inp["ret_decay_logit"], f)
    c["rlog_row"] = np.ascontiguousarray(np.broadcast_to(lg.reshape(DEPTH, 1, 8), (DEPTH, 128, 8)), f)
    pp = np.zeros((DEPTH, 128, 2, 2), f)
    for pair in range(2):
        for d_ in range(2):
            pp[:, 0:64, pair, d_] = lg[:, d_, 2 * pair][:, None]
            pp[:, 64:128, pair, d_] = lg[:, d_, 2 * pair + 1][:, None]
    c["rlog_pp"] = pp
    c["gnw"] = bc(inp["ret_gn_w"])
    c["identb"] = np.eye(128, dtype=f).astype(NPBF)
    c["identf"] = np.eye(128, dtype=f)
    return c


def prep_core(inp, b):
    f = np.float32
    d = {}
    xt = np.concatenate([inp["ctx"][b], inp["x"][b]], axis=0).T
    d["xT"] = np.ascontiguousarray(xt, f)
    d["cT"] = np.ascontiguousarray(np.stack([inp["c"][b], inp["c_ctx"]], axis=1), f)
    return d


def build(specs, stage=99, taps=(), mixers=("mla", "s5", "ret", "fnet")):
    nc = bass.Bass("TRN2", target_bir_lowering=False)
    dr = {}
    for name, (shape, dt) in specs.items():
        dr[name] = nc.dram_tensor(name, list(shape), dt, kind="ExternalInput").ap()
    outT = nc.dram_tensor("outT", [D, 2048], F32, kind="ExternalOutput").ap()
    tapd = {}
    for name, shape, dt in taps:
        tapd[name] = nc.dram_tensor("tap_" + name, list(shape), dt, kind="ExternalOutput").ap()

    with contextlib.ExitStack() as st:
        P = Prog(nc, st)
        A = Arena(nc, st, 206 * 1024)
        ps = [st.enter_context(nc.psum_tensor("ps%d" % i, [128, 512], F32)) for i in range(8)]
        tps = [T("ps%d" % i, excl=True) for i in range(8)]
        out_handles = []

        def mm(out, lhsT, rhs, start=True, stop=True, r=(), w=()):
            return P.op("pe", lambda e: e.matmul(out, lhsT=lhsT, rhs=rhs, start=start, stop=stop,
                                                 skip_group_check=True), r, w)

        def tr(out, in_, ident, r=(), w=()):
            return P.op("pe", lambda e: e.transpose(out=out, in_=in_, identity=ident), r, w)

        def act(out, in_, func, r=(), w=(), scale=1.0, bias=0.0, accum=None):
            if accum is None:
                return P.op("act", lambda e: e.activation(out=out, in_=in_, func=func, scale=scale, bias=bias), r, w)
            return P.op("act", lambda e: e.activation(out=out, in_=in_, func=func, scale=scale, bias=bias,
                                                      accum_out=accum), r, w)

        def tt(eng, out, a, b, op, r=(), w=()):
            return P.op(eng, lambda e: e.tensor_tensor(out=out, in0=a, in1=b, op=op), r, w)

        def ts(eng, out, a, s1, s2, op0, op1=None, r=(), w=()):
            if op1 is None:
                return P.op(eng, lambda e: e.tensor_scalar(out=out, in0=a, scalar1=s1, scalar2=None, op0=op0), r, w)
            return P.op(eng, lambda e: e.tensor_scalar(out=out, in0=a, scalar1=s1, scalar2=s2, op0=op0, op1=op1), r, w)

        def stt(out, a, s, b, op0, op1, r=(), w=()):
            return P.op("dve", lambda e: e.scalar_tensor_tensor(out=out, in0=a, scalar=s, in1=b, op0=op0, op1=op1), r, w)

        def cp(eng, out, in_, r=(), w=()):
            if eng == "act":
                return P.op(eng, lambda e: e.activation(out=out, in_=in_, func=AF.Copy), r, w)
            return P.op(eng, lambda e: e.tensor_copy(out=out, in_=in_), r, w)

        def red(out, in_, r=(), w=()):
            return P.op("dve", lambda e: e.tensor_reduce(out=out, in_=in_, axis=AX.X, op=ALU.add), r, w)

        def rcp(out, in_, r=(), w=()):
            return P.op("dve", lambda e: e.reciprocal(out=out, in_=in_), r, w)

        def mset(eng, out, val, w=()):
            return P.op(eng, lambda e: e.memset(out, val), (), w)

        def dma(out, in_, r=(), w=(), q="sp"):
            return P.dma(q, lambda e: e.dma_start(out=out, in_=in_), r, w)

        def tap(name, src, r):
            if name in tapd:
                out_handles.append(dma(tapd[name], src, r=r))

        xT = A.alloc([8, NT], F32)
        hT = A.alloc([8, NT], BF16)
        t_x = [[T("x%d_%d" % (k, g)) for g in range(5)] for k in range(8)]
        t_h = [T("h%d" % g) for g in range(5)]
        identb = A.alloc([128], BF16)
        identf = A.alloc([128], F32)
        onesb = A.alloc([128], BF16)
        mod = A.alloc([DEPTH, 48, 2], F32)
        a1 = A.alloc([DEPTH, 16, 2], F32)
        nwt = A.alloc([DEPTH, 16], F32)
        scT = A.alloc([8, 2], F32)
        eps_t = A.alloc([1], F32)
        t_const = T("const")
        t_mod = T("mod")
        NSTG = 2
        NRING = 5
        stg = [A.alloc([1024], F32) for _ in range(NSTG)]
        t_stg = [T("stg%d" % i) for i in range(NSTG)]
        ring = [A.alloc([1024], BF16) for _ in range(NRING)]
        t_ring = [T("ring%d" % i) for i in range(NRING)]
        sidx = [0]
        ridx = [0]
        DYN0 = A.off
        OT0 = A.nbytes - 4 * 2 * NT * 2
        oT = [A.alloc([2, NT], BF16, at=OT0 + i * 2 * NT * 2) for i in range(4)]
        t_oT = [[T("o%d_%d" % (i, g)) for g in range(5)] for i in range(4)]

        def GRP(g):
            return (0, 256) if g == 0 else (LAT0 + 512 * (g - 1), 512)

        def grp_of_tile(tt_):
            return 0 if tt_ < 2 else 1 + (tt_ - 2) // 4

        def next_stg():
            s = sidx[0] % NSTG
            sidx[0] += 1
            return s

        def load_cast(dst, t_dst, src, shape_str=None, **kw):
            s = next_stg()
            n = int(np.prod(src.shape[1:]))
            assert n <= 1024, n
            sv = stg[s][:, 0:n]
            if shape_str is not None:
                sv = sv.rearrange(shape_str, **kw)
            dma(sv, src, w=[t_stg[s]])
            cp("pool", dst, sv, r=[t_stg[s]], w=[t_dst])

        def load_ring(src, shape_str=None, **kw):
            i = ridx[0] % NRING
            ridx[0] += 1
            n = int(np.prod(src.shape[1:]))
            dv = ring[i][:, 0:n]
            if shape_str is not None:
                dv = dv.rearrange(shape_str, **kw)
            load_cast(dv, t_ring[i], src, shape_str, **kw)
            return dv, t_ring[i]

        xsrc = dr["xT"].rearrange("(k p) t -> p k t", p=128)
        for k in range(8):
            dma(xT[:, k, :], xsrc[:, k, :], w=t_x[k])
        dma(identb, dr["identb"], w=[t_const])
        dma(identf, dr["identf"], w=[t_const])
        dma(scT, dr["cT"].rearrange("(k p) j -> p k j", p=128), w=[t_const])
        dma(nwt, dr["nw"].rearrange("l p k -> p l k"), w=[t_const])
        mset("dve", onesb, 1.0, w=[t_const])
        mset("dve", eps_t, EPS, w=[t_const])
        act(scT, scT, AF.Silu, r=[t_const], w=[t_const])

        for l in range(DEPTH):
            P.barrier()
            bT = A.alloc([48], F32, at=DYN0)
            modrow = A.alloc([6144], F32, at=DYN0 + 256)
            t_bT = T()
            t_mr = T()
            dma(bT, dr["ada_bT"][l], w=[t_bT])
            wsrc = dr["ada_w"][l].rearrange("(k p) c -> p k c", p=128)
            for nchunk in range(12):
                pb = nchunk % 2
                for j in range(4):
                    s = next_stg()
                    sv = stg[s][:, 0:1024].rearrange("p (k c) -> p k c", c=512)
                    dma(sv, wsrc[:, 2 * j:2 * j + 2, nchunk * 512:(nchunk + 1) * 512], w=[t_stg[s]])
                    for kk in range(2):
                        k = 2 * j + kk
                        mm(ps[pb][0:2, :], scT[:, k, :], sv[:, kk, :], start=(k == 0), stop=(k == 7),
                           r=[t_stg[s], t_const], w=[tps[pb]])
                cp("act", modrow[0:2, nchunk * 512:(nchunk + 1) * 512], ps[pb][0:2, :], r=[tps[pb]], w=[t_mr])
            psm = ps[2 + l][:, 0:96].rearrange("p (c s) -> p c s", s=2)
            for ct in range(48):
                tr(psm[:, ct, :], modrow[0:2, ct * 128:(ct + 1) * 128], identf[0:2, 0:2], r=[t_mr, t_const], w=[tps[2 + l]])
            tt("dve", mod[:, l, :, :], psm, bT[:, :].unsqueeze(2).to_broadcast([128, 48, 2]), ALU.add,
               r=[tps[2 + l], t_bT], w=[t_mod])
            for j, c0 in ((0, 8), (1, 32)):
                stt(a1[:, l, j * 8:(j + 1) * 8, :], mod[:, l, c0:c0 + 8, :], 1.0,
                    nwt[:, l, j * 8:(j + 1) * 8].unsqueeze(2).to_broadcast([128, 8, 2]),
                    ALU.add, ALU.mult, r=[t_mod, t_const], w=[t_mod])
        tap("mod", mod, [t_mod])

        def norm_phase(l, which, groups):
            sh0 = 0 if which == 0 else 24
            P.barrier()
            sq = A.alloc([2, 8, 512], BF16, at=DYN0)
            rstd = A.alloc([2, 512], F32, at=DYN0 + 2 * 8 * 512 * 2)
            tmp = A.alloc([2, 512], F32, at=DYN0 + 2 * 8 * 512 * 2 + 2 * 512 * 4)
            t_sq = [T(), T()]
            t_rs = [T(), T()]
            t_tmp = [T(), T()]
            for gi, g in enumerate(groups):
                t0, n = GRP(g)
                s = 1 if g == 0 else 0
                b = gi % 2
                pb = 2 + b
                for k in range(8):
                    act(sq[:, b, k, 0:n], xT[:, k, t0:t0 + n], AF.Square, r=[t_x[k][g]], w=[t_sq[b]])
                for k in range(8):
                    mm(ps[pb][:, 0:n], onesb, sq[:, b, k, 0:n], start=(k == 0), stop=(k == 7),
                       r=[t_sq[b], t_const], w=[tps[pb]])
                act(rstd[:, b, 0:n], ps[pb][:, 0:n], AF.Sqrt, r=[tps[pb], t_const], w=[t_rs[b]],
                    scale=1.0 / D, bias=eps_t[:, 0:1])
                rcp(rstd[:, b, 0:n], rstd[:, b, 0:n], r=[t_rs[b]], w=[t_rs[b]])
                for k in range(8):
                    tb = k % 2
                    tt("dve", tmp[:, tb, 0:n], xT[:, k, t0:t0 + n], rstd[:, b, 0:n], ALU.mult,
                       r=[t_x[k][g], t_rs[b]], w=[t_tmp[tb]])
                    act(hT[:, k, t0:t0 + n], tmp[:, tb, 0:n], AF.Identity, r=[t_tmp[tb], t_mod], w=[t_h[g]],
                        scale=a1[:, l, which * 8 + k, s:s + 1], bias=mod[:, l, sh0 + k, s:s + 1])

        def mk_alloc(limit):
            o = [DYN0]

            def al(shape, dt):
                ap = A.alloc(shape, dt, at=o[0])
                n = int(np.prod(shape)) * (2 if dt == BF16 else 4)
                o[0] += (n + 3) // 4 * 4
                assert o[0] <= limit, (o[0], limit)
                return ap
            al.o = o
            return al

        def mla_phase(l, last, slot):
            P.barrier()
            al = mk_alloc(OT0 + slot * 2 * NT * 2)
            Wm = al([8, 416], BF16)
            Wukv = al([512], BF16)
            Wuq = al([2, 384], BF16)
            kvw = al([128], F32)
            qnw = al([256], F32)
            qkq = al([4, 96], F32)
            qkk = al([4, 96], F32)
            rc = al([NTT, 2, 8], F32)
            rs_ = al([NTT, 2, 8], F32)
            KT = al([4, NT], BF16)
            QT = al([4, 512], BF16)
            V1 = al([NTT, 4, 65], BF16)
            PT = al([2, 512], BF16)
            kvn = al([2, 128], BF16)
            kvnT = al([2, 128], BF16)
            qn = al([2, 256], BF16)
            qnT = al([2, 2, 128], BF16)
            kf = al([2, 4, 96], F32)
            sqs = al([4, 96], F32)
            kb = al([2, 4, 96], BF16)
            sm = al([2, 16], F32)
            rt = al([4, 4, 2, 8], F32)
            oat = al([4, 256], BF16)
            rinv = al([4], F32)
            t_W = T("Wm")
            t_small = T("mlasmall")
            t_KT = [T() for _ in range(NTT)]
            t_QT = [T() for _ in range(4)]
            t_V1 = [T() for _ in range(NTT)]
            t_PT = [T(), T()]
            t_kvn = [T(), T()]
            t_kvnT = [T(), T()]
            t_qn = [T(), T()]
            t_qnT = [T(), T()]
            t_kf = [T(), T()]
            t_sqs = T()
            t_kb = [T(), T()]
            t_sm = [T(), T()]
            t_rt = T()
            t_oat = [T() for _ in range(4)]
            t_rinv = T()

            wsrc = dr["w_in"][l].rearrange("(k p) c -> p k c", p=128)
            for k in range(0, 8, 2):
                load_cast(Wm[:, k:k + 2, 0:160], t_W, wsrc[:, k:k + 2, 0:160], "p (k c) -> p k c", c=160)
                load_cast(Wm[:, k:k + 2, 160:416], t_W, wsrc[:, k:k + 2, 928:1184], "p (k c) -> p k c", c=256)
            load_cast(Wukv, t_W, dr["w_ukv"][l])
            load_cast(Wuq, t_W, dr["w_uq"][l].rearrange("(j p) c -> p j c", p=128), "p (j c) -> p j c", c=384)
            dma(kvw, dr["kvw"][l], w=[t_small])
            dma(qnw, dr["qnw"][l], w=[t_small])
            dma(qkq, dr["qkq"][l].rearrange("p (h d) -> p h d", d=96), w=[t_small])
            dma(qkk, dr["qkk"][l].rearrange("p (h d) -> p h d", d=96), w=[t_small])
            dma(rc, dr["ropec"].rearrange("p t (a b) -> p t a b", b=8), w=[t_small])
            dma(rs_, dr["ropes"].rearrange("p t (a b) -> p t a b", b=8), w=[t_small])
            mset("dve", V1[:, :, :, 64:65], 1.0, w=t_V1)
            if stage < 0.5:
                return

            def headnorm_rope(pb, wq, tt_, dst, t_dst, tbank):
                x = kf[:, pb]
                t_x_ = t_kf[pb]
                tt("dve", sqs, x, x, ALU.mult, r=[t_x_], w=[t_sqs])
                st_ = sm[:, pb, 0:4]
                red(st_, sqs, r=[t_sqs], w=[t_sm[pb]])
                act(st_, st_, AF.Sqrt, r=[t_sm[pb], t_const], w=[t_sm[pb]], scale=1.0 / 96, bias=eps_t[:, 0:1])
                rcp(st_, st_, r=[t_sm[pb]], w=[t_sm[pb]])
                tt("dve", x, x, st_.unsqueeze(2).to_broadcast([128, 4, 96]), ALU.mult, r=[t_x_, t_sm[pb]], w=[t_x_])
                tt("dve", x, x, wq, ALU.mult, r=[t_x_, t_small], w=[t_x_])
                y = kb[:, pb]
                t_y = t_kb[pb]
                cp("dve", y[:, :, 0:64], x[:, :, 0:64], r=[t_x_], w=[t_y])
                xr = x[:, :, 64:96].rearrange("p h (a two b) -> p h a two b", two=2, b=8)
                yr = y[:, :, 64:96].rearrange("p h (a two b) -> p h a two b", two=2, b=8)
                cosb = rc[:, tt_].unsqueeze(1).to_broadcast([128, 4, 2, 8])
                sinb = rs_[:, tt_].unsqueeze(1).to_broadcast([128, 4, 2, 8])
                x1 = xr[:, :, :, 0, :]
                x2 = xr[:, :, :, 1, :]
                tt("dve", rt[:, 0], x1, cosb, ALU.mult, r=[t_x_, t_small], w=[t_rt])
                tt("dve", rt[:, 1], x2, sinb, ALU.mult, r=[t_x_, t_small], w=[t_rt])
                tt("dve", yr[:, :, :, 0, :], rt[:, 0], rt[:, 1], ALU.subtract, r=[t_rt], w=[t_y])
                tt("dve", rt[:, 2], x1, sinb, ALU.mult, r=[t_x_, t_small], w=[t_rt])
                tt("dve", rt[:, 3], x2, cosb, ALU.mult, r=[t_x_, t_small], w=[t_rt])
                tt("dve", yr[:, :, :, 1, :], rt[:, 2], rt[:, 3], ALU.add, r=[t_rt], w=[t_y])
                pst = ps[tbank][:, :].bitcast(BF16)
                for h in range(4):
                    tr(pst[0:96, h * 128:(h + 1) * 128], y[:, h, :], identb, r=[t_y, t_const], w=[tps[tbank]])
                cp("act", dst, pst[0:96, 0:512].rearrange("p (h t) -> p h t", t=128), r=[tps[tbank]], w=[t_dst])

            if os.environ.get("KSTOP"):
                P.kstop = int(os.environ["KSTOP"])
            for tt_ in range(NTT if stage >= 0.65 else int(os.environ.get('NTILES', '1'))):
                g = grp_of_tile(tt_)
                pb = tt_ % 2
                t0 = tt_ * 128
                bz = pb
                for k in range(8):
                    mm(ps[bz][:, 0:160], hT[:, k, t0:t0 + 128], Wm[:, k, 0:160], start=(k == 0), stop=(k == 7),
                       r=[t_h[g], t_W], w=[tps[bz]])
                z = ps[bz]
                ssk = sm[:, pb, 8:9]
                act(kf[:, pb].rearrange("p h d -> p (h d)")[:, 0:128], z[:, 0:128], AF.Square,
                    r=[tps[bz]], w=[t_kf[pb], t_sm[pb]], accum=ssk)
                act(ssk, ssk, AF.Sqrt, r=[t_sm[pb], t_const], w=[t_sm[pb]], scale=1.0 / 128, bias=eps_t[:, 0:1])
                rcp(ssk, ssk, r=[t_sm[pb]], w=[t_sm[pb]])
                stt(kvn[:, pb], z[:, 0:128], ssk, kvw, ALU.mult, ALU.mult, r=[tps[bz], t_sm[pb], t_small], w=[t_kvn[pb]])
                pst = ps[2 + pb][:, :].bitcast(BF16)
                tr(pst[:, 0:128], kvn[:, pb], identb, r=[t_kvn[pb], t_const], w=[tps[2 + pb]])
                cp("act", kvnT[:, pb], pst[:, 0:128], r=[tps[2 + pb]], w=[t_kvnT[pb]])
                bkv = 4 + pb
                mm(ps[bkv][:, :], kvnT[:, pb], Wukv, r=[t_kvnT[pb], t_W], w=[tps[bkv]])
                kvv = ps[bkv][:, :].rearrange("p (h c) -> p h c", c=128)
                cp("act", V1[:, tt_, :, 0:64], kvv[:, :, 64:128], r=[tps[bkv]], w=[t_V1[tt_]])
                cp("dve", kf[:, pb, :, 0:64], kvv[:, :, 0:64], r=[tps[bkv]], w=[t_kf[pb]])
                cp("dve", kf[:, pb, :, 64:96], z[:, 128:160].unsqueeze(1).to_broadcast([128, 4, 32]),
                   r=[tps[bz]], w=[t_kf[pb]])
                if os.environ.get("NOHN") != "1":
                    headnorm_rope(pb, qkk, tt_, KT[0:96, :, t0:t0 + 128], t_KT[tt_], 2 + pb)
            P.kstop = None
            tap("KT", KT[0:96], t_KT)
            tap("V1", V1, t_V1)
            if stage < 0.7:
                return

            scale = 96 ** -0.5
            qgroups = [(g, GRP(g)[0], GRP(g)[1], list(range(NTT))) for g in range(1, 5)]
            if not last:
                qgroups = [(0, 0, 256, [0, 1])] + qgroups
            it = 0
            for (g, q0, nq, ktiles) in qgroups:
                nqs = nq // 128
                for qs in range(nqs):
                    tt_ = q0 // 128 + qs
                    pb = qs % 2
                    t0 = tt_ * 128
                    bz = pb
                    for k in range(8):
                        mm(ps[bz][:, 0:256], hT[:, k, t0:t0 + 128], Wm[:, k, 160:416], start=(k == 0), stop=(k == 7),
                           r=[t_h[g], t_W], w=[tps[bz]])
                    z = ps[bz]
                    ssq = sm[:, pb, 9:10]
                    act(qn[:, pb], z[:, 0:256], AF.Square, r=[tps[bz]], w=[t_qn[pb], t_sm[pb]], accum=ssq)
                    act(ssq, ssq, AF.Sqrt, r=[t_sm[pb], t_const], w=[t_sm[pb]], scale=1.0 / 256, bias=eps_t[:, 0:1])
                    rcp(ssq, ssq, r=[t_sm[pb]], w=[t_sm[pb]])
                    stt(qn[:, pb], z[:, 0:256], ssq, qnw, ALU.mult, ALU.mult, r=[tps[bz], t_sm[pb], t_small], w=[t_qn[pb]])
                    pst = ps[2 + pb][:, :].bitcast(BF16)
                    for j in range(2):
                        tr(pst[:, j * 128:(j + 1) * 128], qn[:, pb, j * 128:(j + 1) * 128], identb,
                           r=[t_qn[pb], t_const], w=[tps[2 + pb]])
                    cp("act", qnT[:, pb], pst[:, 0:256].rearrange("p (j t) -> p j t", t=128), r=[tps[2 + pb]], w=[t_qnT[pb]])
                    bq = 2 + pb
                    for j in range(2):
                        mm(ps[bq][:, 0:384], qnT[:, pb, j], Wuq[:, j, :], start=(j == 0), stop=(j == 1),
                           r=[t_qnT[pb], t_W], w=[tps[bq]])
                    cp("dve", kf[:, pb], ps[bq][:, 0:384].rearrange("p (h d) -> p h d", d=96), r=[tps[bq]], w=[t_kf[pb]])
                    headnorm_rope(pb, qkq, tt_, QT[0:96, :, qs * 128:(qs + 1) * 128], t_QT[qs], 2 + pb)
                if g == 1:
                    tap("QT", QT[0:96], t_QT)
                if stage < 0.8:
                    continue
                for h in range(4):
                    for ki, kt in enumerate(ktiles):
                        sb = it % 2
                        it += 1
                        mm(ps[sb][:, 0:nq], KT[0:96, h, kt * 128:(kt + 1) * 128], QT[0:96, h, 0:nq],
                           r=[t_KT[kt]] + t_QT[0:nqs], w=[tps[sb]])
                        act(PT[:, sb, 0:nq], ps[sb][:, 0:nq], AF.Exp, r=[tps[sb]], w=[t_PT[sb]], scale=scale)
                        for qs in range(nqs):
                            mm(ps[4 + qs][:, 0:65], PT[:, sb, qs * 128:(qs + 1) * 128], V1[:, kt, h, :],
                               start=(ki == 0), stop=(ki == len(ktiles) - 1),
                               r=[t_PT[sb], t_V1[kt]], w=[tps[4 + qs]])
                    for qs in range(nqs):
                        rcp(rinv[:, qs:qs + 1], ps[4 + qs][:, 64:65], r=[tps[4 + qs]], w=[t_rinv])
                        ts("dve", oat[:, qs, h * 64:(h + 1) * 64], ps[4 + qs][:, 0:64], rinv[:, qs:qs + 1], None,
                           ALU.mult, r=[tps[4 + qs], t_rinv], w=[t_oat[qs]])
                for qs in range(nqs):
                    tb = 2 + qs % 2
                    pst = ps[tb][:, :].bitcast(BF16)
                    for j in range(2):
                        tr(pst[:, j * 128:(j + 1) * 128], oat[:, qs, j * 128:(j + 1) * 128], identb,
                           r=[t_oat[qs], t_const], w=[tps[tb]])
                    tq = q0 + qs * 128
                    cp("act", oT[slot][:, :, tq:tq + 128], pst[:, 0:256].rearrange("p (j t) -> p j t", t=128),
                       r=[tps[tb]], w=[t_oT[slot][g]])
            tap("oaT", oT[slot], t_oT[slot])

        def fnet_phase(l, last, slot):
            P.barrier()
            al = mk_alloc(OT0 + slot * 2 * NT * 2)
            UT = al([2, NT], BF16)
            AB = al([NTT, 512], BF16)
            CS = al([2, 512], BF16)
            tb = al([2, 2, 1024], BF16)
            c256 = al([2, 2, 256], BF16)
            t_UT = [T() for _ in range(5)]
            t_AB = [T() for _ in range(NTT)]
            t_CS = T()
            t_tb = [T(), T()]
            t_c256 = T()
            dma(CS, dr["cs64"], w=[t_CS])
            for lt in range(2):
                dma(c256[:, lt], dr["dft256"][:, lt * 128:(lt + 1) * 128, :].rearrange("c p n -> p c n"), w=[t_c256])
            wsrc = dr["w_in"][l].rearrange("(k p) c -> p k c", p=128)
            groups = [g for g in range(5) if not (last and g == 0)]
            bi = 0
            for j in range(2):
                wv, t_wv = load_ring(wsrc[:, :, 1184 + j * 128:1184 + (j + 1) * 128], "p (k c) -> p k c", c=128)
                for g in groups:
                    t0, n = GRP(g)
                    pb = bi % 4
                    bi += 1
                    for k in range(8):
                        mm(ps[pb][:, 0:n], wv[:, k, :], hT[:, k, t0:t0 + n], start=(k == 0), stop=(k == 7),
                           r=[t_wv, t_h[g]], w=[tps[pb]])
                    cp("act", UT[:, j, t0:t0 + n], ps[pb][:, 0:n], r=[tps[pb]], w=[t_UT[g]])
            for tt_ in (range(2, NTT) if last else range(NTT)):
                g = grp_of_tile(tt_)
                pb = 4 + tt_ % 4
                for j in range(2):
                    mm(ps[pb][:, :], UT[:, j, tt_ * 128:(tt_ + 1) * 128], CS[:, j, :], start=(j == 0), stop=(j == 1),
                       r=[t_UT[g], t_CS], w=[tps[pb]])
                cp("dve", AB[:, tt_, :], ps[pb][:, :], r=[tps[pb]], w=[t_AB[tt_]])
            if not last:
                for j in range(2):
                    pb = j
                    n_mm = 0
                    for lt in range(2):
                        for cs_ in range(2):
                            mm(ps[pb][:, 0:256], AB[:, lt, cs_ * 256 + j * 128:cs_ * 256 + (j + 1) * 128],
                               c256[:, lt, cs_, :], start=(n_mm == 0), stop=(n_mm == 3),
                               r=[t_AB[lt], t_c256], w=[tps[pb]])
                            n_mm += 1
                    cp("act", oT[slot][:, j, 0:256], ps[pb][:, 0:256], r=[tps[pb]], w=[t_oT[slot][0]])
            it = 0
            for half in range(2):
                banks = [4 * half + i for i in range(4)]
                for lt in range(16):
                    b_ = it % 2
                    it += 1
                    dma(tb[:, b_], dr["dft2048"][:, lt * 128:(lt + 1) * 128, half * 1024:(half + 1) * 1024]
                        .rearrange("c p n -> p c n"), w=[t_tb[b_]])
                    for j in range(2):
                        for cs_ in range(2):
                            for lg in range(2):
                                bk = banks[j * 2 + lg]
                                mm(ps[bk][:, :], AB[:, 2 + lt, cs_ * 256 + j * 128:cs_ * 256 + (j + 1) * 128],
                                   tb[:, b_, cs_, lg * 512:(lg + 1) * 512],
                                   start=(lt == 0 and cs_ == 0), stop=(lt == 15 and cs_ == 1),
                                   r=[t_AB[2 + lt], t_tb[b_]], w=[tps[bk]])
                for j in range(2):
                    for lg in range(2):
                        bk = banks[j * 2 + lg]
                        g = 1 + half * 2 + lg
                        t0 = LAT0 + half * 1024 + lg * 512
                        cp("act", oT[slot][:, j, t0:t0 + 512], ps[bk][:, :], r=[tps[bk]], w=[t_oT[slot][g]])
            tap("obT", oT[slot], t_oT[slot])

        def ret_phase(l, last, slot):
            P.barrier()
            al = mk_alloc(OT0 + slot * 2 * NT * 2)
            rrc = al([NTT, 32], F32)
            rrs = al([NTT, 32], F32)
            retc = al([6, 128], F32)
            retp = al([2], F32)
            lgrow = al([8], F32)
            lgpp = al([2, 2], F32)
            gnw = al([256], F32)
            Wr = al([8, 512], BF16)
            QZ = al([2, NT], BF16)
            KT_ = al([NT], BF16)
            Vb = al([NTT, 128], BF16)
            sg = al([NTT, 128], BF16)
            Sfp = al([NTT, 128], BF16)
            Sbn = Wr.rearrange("p k c -> p (k c)")[:, 0:NTT * 128].rearrange("p (i c) -> p i c", c=128)
            Ub = al([NTT, 128], BF16)
            kdec = al([2, 128], F32)
            qdec = al([2, 128], F32)
            Mk = al([2, 128], F32)
            aab = al([2], F32)
            Sf = al([128], F32)
            Sb = al([128], F32)
            rt = al([2, 4, 2, 32], F32)
            Qb = al([2, 128], BF16)
            Kb = al([2, 128], BF16)
            Kd = al([1, 2, 128], BF16)
            Qd = al([1, 2, 2, 128], BF16)
            attm = al([2, 2, 128], BF16)
            xc = al([2, 64], F32)
            xcq = xc.rearrange("p a d -> p (a d)")
            sq = al([2, 64], F32)
            st_ = al([2, 4], F32)
            od = al([2, 128], BF16)
            t_c = T()
            t_lg = T()
            t_Wr = T()
            t_QK = [T() for _ in range(NTT)]
            t_Vb = [T() for _ in range(NTT)]
            t_sg = [T() for _ in range(NTT)]
            t_Sfp = [T() for _ in range(NTT)]
            t_Sbn = [T() for _ in range(NTT)]
            t_Ub = [T() for _ in range(NTT)]
            t_tab = T()
            t_S = T()
            t_rt = [T(), T()]
            t_Qb = [T(), T()]
            t_Kb = [T(), T()]
            t_Kd = [T(), T()]
            t_Qd = [T(), T()]
            t_attm = [T(), T()]
            t_gn = T()
            t_od = [T(), T()]
            dma(rrc, dr["rrc"], w=[t_c])
            dma(rrs, dr["rrs"], w=[t_c])
            dma(retc, dr["retc"], w=[t_c])
            dma(retp, dr["retp"], w=[t_c])
            dma(lgrow, dr["rlog_row"][l], w=[t_lg])
            dma(lgpp, dr["rlog_pp"][l], w=[t_lg])
            dma(gnw, dr["gnw"][l], w=[t_c])
            for v in (lgrow, lgpp.rearrange("p a b -> p (a b)")):
                act(v, v, AF.Exp, r=[t_lg], w=[t_lg], scale=-1.0)
                act(v, v, AF.Ln, r=[t_lg], w=[t_lg], bias=1.0)
                ts("dve", v, v, -1.0, None, ALU.mult, r=[t_lg], w=[t_lg])
            wsrc = dr["w_in"][l].rearrange("(k p) c -> p k c", p=128)
            for pair in range(2):
                P.barrier()
                offs = [416 + pair * 128, 672 + pair * 128, 1440 + pair * 128, 1696 + pair * 128]
                for ci, c0 in enumerate(offs):
                    load_cast(Wr[:, :, ci * 128:(ci + 1) * 128], t_Wr, wsrc[:, :, c0:c0 + 128], "p (k c) -> p k c", c=128)
                for hh in range(2):
                    head = 2 * pair + hh
                    cs_ = slice(hh * 64, (hh + 1) * 64)
                    act(kdec[:, 0, cs_], lgrow[:, head:head + 1].to_broadcast([128, 64]), AF.Exp, r=[t_lg, t_c], w=[t_tab],
                        scale=retp[:, 1:2])
                    act(kdec[:, 1, cs_], lgrow[:, 4 + head:5 + head].to_broadcast([128, 64]), AF.Exp, r=[t_lg, t_c], w=[t_tab],
                        scale=retp[:, 0:1])
                    act(Mk[:, hh, :], retc[:, 2, :], AF.Exp, r=[t_lg, t_c], w=[t_tab], scale=lgrow[:, head:head + 1])
                    tt("dve", Mk[:, hh, :], Mk[:, hh, :], retc[:, 4, :], ALU.mult, r=[t_tab, t_c], w=[t_tab])
                    act(xcq, retc[:, 3, :], AF.Exp, r=[t_lg, t_c], w=[t_tab], scale=lgrow[:, 4 + head:5 + head])
                    tt("dve", xcq, xcq, retc[:, 5, :], ALU.mult, r=[t_tab, t_c], w=[t_tab])
                    stt(Mk[:, hh, :], Mk[:, hh, :], 1.0, xcq, ALU.mult, ALU.add, r=[t_tab], w=[t_tab])
                ts("dve", Mk, Mk, 0.125, None, ALU.mult, r=[t_tab], w=[t_tab])
                ts("dve", kdec, kdec, 0.125, None, ALU.mult, r=[t_tab], w=[t_tab])
                act(qdec[:, 0, :], retc[:, 0, :], AF.Exp, r=[t_lg, t_c], w=[t_tab], scale=lgpp[:, pair, 0:1])
                act(qdec[:, 1, :], retc[:, 1, :], AF.Exp, r=[t_lg, t_c], w=[t_tab], scale=lgpp[:, pair, 1:2])
                act(aab, lgpp[:, pair, :], AF.Exp, r=[t_lg], w=[t_tab], scale=128.0)
                mset("dve", Sf, 0.0, w=[t_S])
                mset("dve", Sb, 0.0, w=[t_S])
                mset("pool", QZ, 0.0, w=t_QK)

                def rope(src, cosT, sinT, dst, pb, t_dst, t_src):
                    x = src.rearrange("p (h two b) -> p h two b", two=2, b=32)
                    y = dst.rearrange("p (h two b) -> p h two b", two=2, b=32)
                    cb = cosT.unsqueeze(1).to_broadcast([128, 2, 32])
                    sb_ = sinT.unsqueeze(1).to_broadcast([128, 2, 32])
                    r_ = rt[:, pb]
                    tt("dve", r_[:, 0], x[:, :, 0, :], cb, ALU.mult, r=[t_src, t_c], w=[t_rt[pb]])
                    tt("dve", r_[:, 1], x[:, :, 1, :], sb_, ALU.mult, r=[t_src, t_c], w=[t_rt[pb]])
                    tt("dve", y[:, :, 0, :], r_[:, 0], r_[:, 1], ALU.subtract, r=[t_rt[pb]], w=[t_dst])
                    tt("dve", r_[:, 2], x[:, :, 0, :], sb_, ALU.mult, r=[t_src, t_c], w=[t_rt[pb]])
                    tt("dve", r_[:, 3], x[:, :, 1, :], cb, ALU.mult, r=[t_src, t_c], w=[t_rt[pb]])
                    tt("dve", y[:, :, 1, :], r_[:, 2], r_[:, 3], ALU.add, r=[t_rt[pb]], w=[t_dst])

                for i in range(NTT):
                    g = grp_of_tile(i)
                    pb = i % 2
                    t0 = i * 128
                    bz = pb
                    for k in range(8):
                        mm(ps[bz][:, :], hT[:, k, t0:t0 + 128], Wr[:, k, :], start=(k == 0), stop=(k == 7),
                           r=[t_h[g], t_Wr], w=[tps[bz]])
                    z = ps[bz]
                    rope(z[:, 256:384], rrc[:, i, :], rrs[:, i, :], Qb[:, pb], pb, t_Qb[pb], tps[bz])
                    rope(z[:, 0:128], rrc[:, i, :], rrs[:, i, :], Kb[:, pb], pb, t_Kb[pb], tps[bz])
                    cp("act", Vb[:, i, :], z[:, 128:256], r=[tps[bz]], w=[t_Vb[i]])
                    act(sg[:, i, :], z[:, 384:512], AF.Silu, r=[tps[bz]], w=[t_sg[i]])
                    tb_ = 2 + pb
                    pst = ps[tb_][:, :].bitcast(BF16)
                    tr(pst[:, 0:128], Qb[:, pb], identb, r=[t_Qb[pb], t_const], w=[tps[tb_]])
                    tr(pst[:, 128:256], Kb[:, pb], identb, r=[t_Kb[pb], t_const], w=[tps[tb_]])
                    cp("act", QZ[0:64, 0, t0:t0 + 128], pst[0:64, 0:128], r=[tps[tb_]], w=[t_QK[i]])
                    cp("act", QZ[64:128, 1, t0:t0 + 128], pst[64:128, 0:128], r=[tps[tb_]], w=[t_QK[i]])
                    cp("act", KT_[:, t0:t0 + 128], pst[:, 128:256], r=[tps[tb_]], w=[t_QK[i]])
                    tt("pool", Kd[:, 0, 0], Kb[:, pb], kdec[:, 0], ALU.mult, r=[t_Kb[pb], t_tab], w=[t_Kd[0]])
                    tt("pool", Kd[:, 0, 1], Kb[:, pb], kdec[:, 1], ALU.mult, r=[t_Kb[pb], t_tab], w=[t_Kd[0]])
                    bu = 4 + pb
                    mm(ps[bu][:, 0:128], Kd[:, 0, 0], Vb[:, i, :], r=[t_Kd[0], t_Vb[i]], w=[tps[bu]])
                    mm(ps[bu][:, 128:256], Kd[:, 0, 1], Vb[:, i, :], r=[t_Kd[0], t_Vb[i]], w=[tps[bu]])
                    cp("dve", Sfp[:, i, :], Sf, r=[t_S], w=[t_Sfp[i]])
                    stt(Sf, Sf, aab[:, 0:1], ps[bu][:, 0:128], ALU.mult, ALU.add, r=[t_S, t_tab, tps[bu]], w=[t_S])
                    cp("act", Ub[:, i, :], ps[bu][:, 128:256], r=[tps[bu]], w=[t_Ub[i]])
                P.barrier()
                for i in [1, 0] + list(range(NTT - 1, 1, -1)):
                    cp("dve", Sbn[:, i, :], Sb, r=[t_S], w=[t_Sbn[i]])
                    stt(Sb, Sb, aab[:, 1:2], Ub[:, i, :], ALU.mult, ALU.add, r=[t_S, t_tab, t_Ub[i]], w=[t_S])
                for i in range(2 if last else 0, NTT):
                    g = grp_of_tile(i)
                    pb = i % 2
                    t0 = i * 128
                    ba = pb
                    for hh in range(2):
                        mm(ps[ba][:, hh * 128:(hh + 1) * 128], KT_[:, t0:t0 + 128], QZ[:, hh, t0:t0 + 128],
                           r=[t_QK[i]], w=[tps[ba]])
                    tt("dve", attm[:, pb], ps[ba][:, 0:256].rearrange("p (a t) -> p a t", t=128), Mk, ALU.mult,
                       r=[tps[ba], t_tab], w=[t_attm[pb]])
                    for hh in range(2):
                        tt("pool", Qd[:, 0, hh, 0], QZ[:, hh, t0:t0 + 128], qdec[:, 0], ALU.mult, r=[t_QK[i], t_tab], w=[t_Qd[0]])
                        tt("pool", Qd[:, 0, hh, 1], QZ[:, hh, t0:t0 + 128], qdec[:, 1], ALU.mult, r=[t_QK[i], t_tab], w=[t_Qd[0]])
                    bo = 4 + pb
                    for hh in range(2):
                        rs_ = slice(hh * 64, (hh + 1) * 64)
                        mm(ps[bo][:, rs_], attm[:, pb, hh, :], Vb[:, i, rs_], start=True, stop=False,
                           r=[t_attm[pb], t_Vb[i]], w=[tps[bo]])
                        mm(ps[bo][:, rs_], Qd[:, 0, hh, 0, :], Sfp[:, i, rs_], start=False, stop=False,
                           r=[t_Qd[0], t_Sfp[i]], w=[tps[bo]])
                        mm(ps[bo][:, rs_], Qd[:, 0, hh, 1, :], Sbn[:, i, rs_], start=False, stop=True,
                           r=[t_Qd[0], t_Sbn[i]], w=[tps[bo]])
                    o = ps[bo][:, 0:128].rearrange("p (a d) -> p a d", d=64)
                    red(st_[:, 0, 0:2], o, r=[tps[bo]], w=[t_gn])
                    ts("dve", st_[:, 0, 0:2], st_[:, 0, 0:2], -1.0 / 64, None, ALU.mult, r=[t_gn], w=[t_gn])
                    tt("dve", xc, o, st_[:, 0, 0:2].unsqueeze(2).to_broadcast([128, 2, 64]), ALU.add, r=[tps[bo], t_gn], w=[t_gn])
                    tt("dve", sq, xc, xc, ALU.mult, r=[t_gn], w=[t_gn])
                    red(st_[:, 1, 0:2], sq, r=[t_gn], w=[t_gn])
                    act(st_[:, 1, 0:2], st_[:, 1, 0:2], AF.Sqrt, r=[t_gn, t_const], w=[t_gn], scale=1.0 / 64, bias=eps_t[:, 0:1])
                    rcp(st_[:, 1, 0:2], st_[:, 1, 0:2], r=[t_gn], w=[t_gn])
                    tt("dve", xc, xc, st_[:, 1, 0:2].unsqueeze(2).to_broadcast([128, 2, 64]), ALU.mult, r=[t_gn], w=[t_gn])
                    tt("dve", xc, xc, gnw[:, pair * 128:(pair + 1) * 128].rearrange("p (a d) -> p a d", d=64), ALU.mult,
                       r=[t_gn, t_c], w=[t_gn])
                    tt("dve", od[:, pb].rearrange("p (a d) -> p a d", d=64), xc,
                       sg[:, i, :].rearrange("p (a d) -> p a d", d=64), ALU.mult, r=[t_gn, t_sg[i]], w=[t_od[pb]])
                    tb_ = 2 + pb
                    pst = ps[tb_][:, :].bitcast(BF16)
                    tr(pst[:, 0:128], od[:, pb], identb, r=[t_od[pb], t_const], w=[tps[tb_]])
                    cp("act", oT[slot][:, pair, t0:t0 + 128], pst[:, 0:128], r=[tps[tb_]], w=[t_oT[slot][g]])
            tap("odT", oT[slot], t_oT[slot])

        I32 = mybir.dt.int32
        TWO_PI = 2.0 * math.pi

        def s5_phase(l, last, slot):
            P.barrier()
            al = mk_alloc(OT0 + slot * 2 * NT * 2)
            uT = al([2, NT], BF16)
            yf = al([2, NT], BF16)
            E = al([2, 1024], BF16)
            Fm = al([8, 2, 128], BF16)
            Bb = al([2, 2, 512], BF16)
            Cc = al([2, 8, 128], BF16)
            Tri = al([2, 128], BF16)
            pp = al([3, 8], F32)
            sm = al([12, 8], F32)
            cst = al([4], F32)
            erow = al([2, 128], F32)
            ecol = al([4], F32)
            dvec = al([2], F32)
            xl = al([8, 2], F32)
            cc = al([8, 2], F32)
            woff = al.o[0]
            W = al([2, 1024], BF16)
            xx = al([2, 8, 128], BF16)
            tW = al([2, 256], F32)
            tq = al([2, 512], F32)
            ysc = A.alloc([4, 128], F32, at=al.o[0] - 2048)
            cT = al([128], F32)
            t_u = [T() for _ in range(5)]
            t_yf = [T() for _ in range(NTT)]
            t_tab = T()
            t_pp = T()
            t_c = T()
            t_W = [T(), T()]
            t_xx = [T(), T()]
            t_tW = T()
            t_tq = T()
            t_xl = T()
            t_cc = T()
            t_ysc = t_tq
            t_cT = T()
            t_blk = T()

            dma(Tri, dr["s5tri"], w=[t_c])
            dma(erow, dr["s5erow"], w=[t_c])
            dma(ecol, dr["s5ecol"], w=[t_c])
            dma(dvec, dr["s5d"][l], w=[t_c])
            mset("dve", cst[:, 0:1], -math.pi, w=[t_c])

            wsrc = dr["w_in"][l].rearrange("(k p) c -> p k c", p=128)
            bi = 0
            for j in range(2):
                wv, t_wv = load_ring(wsrc[:, :, 160 + j * 128:160 + (j + 1) * 128], "p (k c) -> p k c", c=128)
                for g in range(5):
                    t0, n = GRP(g)
                    pb = bi % 4
                    bi += 1
                    for k in range(8):
                        mm(ps[pb][:, 0:n], wv[:, k, :], hT[:, k, t0:t0 + n], start=(k == 0), stop=(k == 7),
                           r=[t_wv, t_h[g]], w=[tps[pb]])
                    cp("act", uT[:, j, t0:t0 + n], ps[pb][:, 0:n], r=[tps[pb]], w=[t_u[g]])
            tap("uT", uT, t_u)

            def cplx_pow(out_re, out_im, phase, mag, n, conj, tmp):
                r_, n_i, f_, m_ = tmp
                ts("dve", r_, phase, 1.0 / TWO_PI, None, ALU.mult, r=[t_blk], w=[t_blk])
                cp("dve", n_i.bitcast(I32), r_, r=[t_blk], w=[t_blk])
                cp("dve", f_, n_i.bitcast(I32), r=[t_blk], w=[t_blk])
                tt("dve", f_, r_, f_, ALU.subtract, r=[t_blk], w=[t_blk])
                ts("dve", m_, f_, 0.0, None, ALU.is_lt, r=[t_blk], w=[t_blk])
                tt("dve", f_, f_, m_, ALU.add, r=[t_blk], w=[t_blk])
                act(r_, f_, AF.Sin, r=[t_blk, t_c], w=[t_blk], scale=TWO_PI, bias=cst[:, 0:1])
                ts("dve", f_, f_, 0.25, None, ALU.add, r=[t_blk], w=[t_blk])
                ts("dve", m_, f_, 1.0, None, ALU.is_ge, r=[t_blk], w=[t_blk])
                tt("dve", f_, f_, m_, ALU.subtract, r=[t_blk], w=[t_blk])
                act(m_, f_, AF.Sin, r=[t_blk, t_c], w=[t_blk], scale=TWO_PI, bias=cst[:, 0:1])
                stt(out_re, mag, -1.0, m_, ALU.mult, ALU.mult, r=[t_blk], w=[t_blk, t_tab])
                if conj:
                    tt("dve", out_im, mag, r_, ALU.mult, r=[t_blk], w=[t_blk, t_tab])
                else:
                    stt(out_im, mag, -1.0, r_, ALU.mult, ALU.mult, r=[t_blk], w=[t_blk, t_tab])

            glw = None
            for d_ in range(2):
                P.barrier()
                B_ = [A.alloc([256], F32, at=woff + i * 1024) for i in range(16)]
                dma(pp, dr["s5pp"][l, d_], w=[t_pp])
                act(pp[:, 2, :], pp[:, 2, :], AF.Exp, r=[t_pp], w=[t_pp])
                App = sm[:, 0, :]
                Bpp = sm[:, 1, :]
                tt("dve", App, pp[:, 0, :], pp[:, 2, :], ALU.mult, r=[t_pp], w=[t_blk])
                tt("dve", Bpp, pp[:, 1, :], pp[:, 2, :], ALU.mult, r=[t_pp], w=[t_blk])
                l1re = sm[:, 2, :]
                l1im = sm[:, 3, :]
                mg = sm[:, 4, :]
                act(mg, App, AF.Exp, r=[t_blk], w=[t_blk])
                tmp8 = [B_[0][:, 0:8], B_[0][:, 8:16], B_[0][:, 16:24], B_[0][:, 24:32]]
                cplx_pow(l1re, l1im, Bpp, mg, 8, False, tmp8)
                br = sm[:, 5, :]
                den = sm[:, 6, :]
                kre = sm[:, 7, :]
                kim = sm[:, 8, :]
                nkre = sm[:, 9, :]
                nkim = sm[:, 10, :]
                t8 = sm[:, 11, :]
                ts("dve", br, l1re, -1.0, None, ALU.add, r=[t_blk], w=[t_blk])
                tt("dve", den, pp[:, 0, :], pp[:, 0, :], ALU.mult, r=[t_pp], w=[t_blk])
                tt("dve", t8, pp[:, 1, :], pp[:, 1, :], ALU.mult, r=[t_pp], w=[t_blk])
                tt("dve", den, den, t8, ALU.add, r=[t_blk], w=[t_blk])
                rcp(den, den, r=[t_blk], w=[t_blk])
                tt("dve", kre, br, pp[:, 0, :], ALU.mult, r=[t_blk, t_pp], w=[t_blk])
                tt("dve", t8, l1im, pp[:, 1, :], ALU.mult, r=[t_blk, t_pp], w=[t_blk])
                tt("dve", kre, kre, t8, ALU.add, r=[t_blk], w=[t_blk])
                tt("dve", kre, kre, den, ALU.mult, r=[t_blk], w=[t_blk])
                tt("dve", kim, l1im, pp[:, 0, :], ALU.mult, r=[t_blk, t_pp], w=[t_blk])
                tt("dve", t8, br, pp[:, 1, :], ALU.mult, r=[t_blk, t_pp], w=[t_blk])
                tt("dve", kim, kim, t8, ALU.subtract, r=[t_blk], w=[t_blk])
                tt("dve", kim, kim, den, ALU.mult, r=[t_blk], w=[t_blk])
                ts("dve", nkre, kre, -1.0, None, ALU.mult, r=[t_blk], w=[t_blk])
                ts("dve", nkim, kim, -1.0, None, ALU.mult, r=[t_blk], w=[t_blk])
                Cre = A.alloc([8, 128], F32, at=woff + 1 * 1024)
                Cim = A.alloc([8, 128], F32, at=woff + 5 * 1024)
                for ri, Cdst in enumerate((Cre, Cim)):
                    dma(Cdst, dr["s5c"][l, d_, ri].rearrange("a s c -> s a c"), w=[t_blk])
                tC = B_[9][:, 0:128]
                for st in range(8):
                    ts("dve", tC, Cre[:, st, :], kre[:, st:st + 1], None, ALU.mult, r=[t_blk], w=[t_blk])
                    stt(Cc[:, 0, st, :], Cim[:, st, :], nkim[:, st:st + 1], tC, ALU.mult, ALU.add, r=[t_blk], w=[t_tab])
                    ts("dve", tC, Cre[:, st, :], nkim[:, st:st + 1], None, ALU.mult, r=[t_blk], w=[t_blk])
                    stt(Cc[:, 1, st, :], Cim[:, st, :], nkre[:, st:st + 1], tC, ALU.mult, ALU.add, r=[t_blk], w=[t_tab])
                for kt in range(2):
                    load_cast(Bb[:, kt], t_tab, dr["s5b"][l, d_, :, kt], "p (a c) -> p a c", c=512)
                er = erow[:, d_, :]
                for st in range(8):
                    ph = B_[9][:, 0:128]
                    mgb = B_[9][:, 128:256]
                    ts("dve", ph, er, Bpp[:, st:st + 1], None, ALU.mult, r=[t_c, t_blk], w=[t_blk])
                    act(mgb, er, AF.Exp, r=[t_c, t_blk], w=[t_blk], scale=App[:, st:st + 1])
                    tmpb = [B_[10][:, 0:128], B_[10][:, 128:256], B_[11][:, 0:128], B_[11][:, 128:256]]
                    cplx_pow(Fm[:, st, 0, :], Fm[:, st, 1, :], ph, mgb, 128, False, tmpb)
                row = A.alloc([3, 256], F32, at=woff + 1 * 1024)
                for cb in range(4):
                    dma(row, dr["s5row"][l, d_, :, :, cb * 256:(cb + 1) * 256], w=[t_blk])
                    act(row[:, 2, :], row[:, 2, :], AF.Exp, r=[t_blk], w=[t_blk])
                    Ab = B_[4]
                    Bk = B_[5]
                    tt("dve", Ab, row[:, 0, :], row[:, 2, :], ALU.mult, r=[t_blk], w=[t_blk])
                    tt("dve", Bk, row[:, 1, :], row[:, 2, :], ALU.mult, r=[t_blk], w=[t_blk])
                    ph = B_[6]
                    mgb = B_[7]
                    ts("dve", ph, Bk, ecol[:, d_:d_ + 1], None, ALU.mult, r=[t_blk, t_c], w=[t_blk])
                    act(mgb, Ab, AF.Exp, r=[t_blk, t_c], w=[t_blk], scale=ecol[:, 2 + d_:3 + d_])
                    tmpb = [B_[8], B_[9], B_[10], B_[11]]
                    cplx_pow(E[:, 0, cb * 256:(cb + 1) * 256], E[:, 1, cb * 256:(cb + 1) * 256], ph, mgb, 256, True, tmpb)
                if d_ == 0:
                    tap("s5E", E, [t_tab])
                    tap("s5F", Fm, [t_tab])
                    tap("s5C", Cc, [t_tab])
                P.barrier()
                order = list(range(NTT)) if d_ == 0 else [1, 0] + list(range(NTT - 1, 1, -1))
                lastcol = 127 if d_ == 0 else 0
                mset("dve", cc, 0.0, w=[t_cc])
                for idx, i in enumerate(order):
                    g = grp_of_tile(i)
                    t0 = i * 128
                    pb = idx % 2
                    for nb in range(4):
                        kt = nb % 2
                        ri = nb // 2
                        mm(ps[nb][:, :], uT[:, kt, t0:t0 + 128], Bb[:, kt, ri, :], r=[t_u[g], t_tab], w=[tps[nb]])
                    for qb in range(4):
                        hb = qb // 2
                        sl = slice(qb * 256, (qb + 1) * 256)
                        pl = slice((qb % 2) * 256, (qb % 2 + 1) * 256)
                        tt("dve", tW[:, 0, :], ps[hb][:, pl], E[:, 0, sl], ALU.mult, r=[tps[hb], t_tab], w=[t_tW])
                        tt("dve", tW[:, 1, :], ps[2 + hb][:, pl], E[:, 1, sl], ALU.mult, r=[tps[2 + hb], t_tab], w=[t_tW])
                        tt("pool", W[:, 0, sl], tW[:, 0, :], tW[:, 1, :], ALU.subtract, r=[t_tW], w=[t_W[0]])
                        tt("dve", tW[:, 0, :], ps[hb][:, pl], E[:, 1, sl], ALU.mult, r=[tps[hb], t_tab], w=[t_tW])
                        tt("dve", tW[:, 1, :], ps[2 + hb][:, pl], E[:, 0, sl], ALU.mult, r=[tps[2 + hb], t_tab], w=[t_tW])
                        tt("pool", W[:, 1, sl], tW[:, 0, :], tW[:, 1, :], ALU.add, r=[t_tW], w=[t_W[1]])
                    tr(ps[2][0:16, 0:128], cc.rearrange("p a b -> p (a b)"), identf, r=[t_cc, t_const], w=[tps[2]])
                    cp("act", cT[0:16, :], ps[2][0:16, 0:128], r=[tps[2]], w=[t_cT])
                    for ri in range(2):
                        for st in range(8):
                            bk = 4 + ri * 2 + st // 4
                            mm(ps[bk][:, (st % 4) * 128:(st % 4 + 1) * 128], W[:, ri, st * 128:(st + 1) * 128], Tri[:, d_, :],
                               start=(st % 4 == 0), stop=False, r=[t_W[ri], t_c], w=[tps[bk]])
                    for ri in range(2):
                        for st in range(8):
                            bk = 4 + ri * 2 + st // 4
                            jj = st * 2 + ri
                            mm(ps[bk][:, (st % 4) * 128:(st % 4 + 1) * 128], cT[0:16, :],
                               identf[0:16, jj:jj + 1].to_broadcast([16, 128]),
                               start=False, stop=True, r=[t_cT, t_const], w=[tps[bk]])
                    for hf in range(2):
                        Sre = ps[4 + hf][:, :].rearrange("p (a t) -> p a t", t=128)
                        Sim = ps[6 + hf][:, :].rearrange("p (a t) -> p a t", t=128)
                        Fre = Fm[:, 4 * hf:4 * hf + 4, 0, :]
                        Fim = Fm[:, 4 * hf:4 * hf + 4, 1, :]
                        q0 = tq[:, 0, :].rearrange("p (a t) -> p a t", t=128)
                        q1 = tq[:, 1, :].rearrange("p (a t) -> p a t", t=128)
                        tt("dve", q0, Sre, Fre, ALU.mult, r=[tps[4 + hf], t_tab], w=[t_tq])
                        tt("dve", q1, Sim, Fim, ALU.mult, r=[tps[6 + hf], t_tab], w=[t_tq])
                        tt("dve", xl[:, 4 * hf:4 * hf + 4, 0], q0[:, :, lastcol], q1[:, :, lastcol], ALU.subtract, r=[t_tq], w=[t_xl])
                        tt("pool", xx[:, 0, 4 * hf:4 * hf + 4, :], q0, q1, ALU.subtract, r=[t_tq], w=[t_xx[0]])
                        tt("dve", q0, Sre, Fim, ALU.mult, r=[tps[4 + hf], t_tab], w=[t_tq])
                        tt("dve", q1, Sim, Fre, ALU.mult, r=[tps[6 + hf], t_tab], w=[t_tq])
                        tt("dve", xl[:, 4 * hf:4 * hf + 4, 1], q0[:, :, lastcol], q1[:, :, lastcol], ALU.add, r=[t_tq], w=[t_xl])
                        tt("pool", xx[:, 1, 4 * hf:4 * hf + 4, :], q0, q1, ALU.add, r=[t_tq], w=[t_xx[1]])
                    ta_ = sm[:, 5, :]
                    tb_ = sm[:, 6, :]
                    tt("dve", ta_, l1re, xl[:, :, 0], ALU.mult, r=[t_xl, t_blk], w=[t_blk])
                    tt("dve", tb_, l1im, xl[:, :, 1], ALU.mult, r=[t_xl, t_blk], w=[t_blk])
                    tt("dve", cc[:, :, 0], ta_, tb_, ALU.subtract, r=[t_blk], w=[t_cc])
                    tt("dve", ta_, l1re, xl[:, :, 1], ALU.mult, r=[t_xl, t_blk], w=[t_blk])
                    tt("dve", tb_, l1im, xl[:, :, 0], ALU.mult, r=[t_xl, t_blk], w=[t_blk])
                    tt("dve", cc[:, :, 1], ta_, tb_, ALU.add, r=[t_blk], w=[t_cc])
                    if last and i < 2:
                        continue
                    for j in range(2):
                        n_mm = 0
                        for st in range(4 * j, 4 * j + 4):
                            for ri in range(2):
                                mm(ps[j][:, 0:128], Cc[:, ri, st, :], xx[:, ri, st, :], start=(n_mm == 0), stop=(n_mm == 7),
                                   r=[t_tab, t_xx[ri]], w=[tps[j]])
                                n_mm += 1
                        if d_ == 0:
                            cp("act", yf[:, j, t0:t0 + 128], ps[j][:, 0:128], r=[tps[j]], w=[t_yf[i]])
                        else:
                            y = ysc[:, 0, :]
                            tt("dve", y, ps[j][:, 0:128], yf[:, j, t0:t0 + 128], ALU.add, r=[tps[j], t_yf[i]], w=[t_ysc])
                            stt(y, uT[:, j, t0:t0 + 128], dvec[:, j:j + 1], y, ALU.mult, ALU.add, r=[t_u[g], t_c, t_ysc], w=[t_ysc])
                            tt("dve", ysc[:, 1, :], y, y, ALU.mult, r=[t_ysc], w=[t_ysc])
                            ts("dve", ysc[:, 1, :], ysc[:, 1, :], 0.044715, 1.0, ALU.mult, ALU.add, r=[t_ysc], w=[t_ysc])
                            tt("dve", ysc[:, 1, :], ysc[:, 1, :], y, ALU.mult, r=[t_ysc], w=[t_ysc])
                            act(ysc[:, 2, :], ysc[:, 1, :], AF.Sigmoid, r=[t_ysc], w=[t_ysc], scale=1.5957691216057308)
                            tt("dve", yf[:, j, t0:t0 + 128], y, ysc[:, 2, :], ALU.mult, r=[t_ysc], w=[t_yf[i]])
            tap("s5g", yf, t_yf)
            P.barrier()
            glw = A.alloc([2, 512], BF16, at=woff)
            t_glw = T()
            load_cast(glw[:, 0], t_glw, dr["s5_w_glu"][l][0:128, :])
            load_cast(glw[:, 1], t_glw, dr["s5_w_glu"][l][128:256, :])
            sgt = A.alloc([512], F32, at=woff + 2048)
            t_sgt = T()
            bi = 0
            for g in range(1 if last else 0, 5):
                t0, n = GRP(g)
                tiles = list(range(t0 // 128, (t0 + n) // 128))
                for j in range(2):
                    pv = bi % 2
                    pg = 2 + bi % 2
                    bi += 1
                    for kt in range(2):
                        mm(ps[pv][:, 0:n], glw[:, kt, j * 128:(j + 1) * 128], yf[:, kt, t0:t0 + n], start=(kt == 0), stop=(kt == 1),
                           r=[t_glw] + [t_yf[i] for i in tiles], w=[tps[pv]])
                    for kt in range(2):
                        mm(ps[pg][:, 0:n], glw[:, kt, 256 + j * 128:256 + (j + 1) * 128], yf[:, kt, t0:t0 + n],
                           start=(kt == 0), stop=(kt == 1), r=[t_glw] + [t_yf[i] for i in tiles], w=[tps[pg]])
                    act(sgt[:, 0:n], ps[pg][:, 0:n], AF.Sigmoid, r=[tps[pg]], w=[t_sgt])
                    tt("dve", oT[slot][:, j, t0:t0 + n], ps[pv][:, 0:n], sgt[:, 0:n], ALU.mult, r=[tps[pv], t_sgt],
                       w=[t_oT[slot][g]])
            tap("ocT", oT[slot], t_oT[slot])

        SLOT_OF = {0: 3, 1: 0, 2: 1, 3: 2}

        def merge_phase(l, last):
            P.barrier()
            al = mk_alloc(OT0)
            mT = al([8, NT], BF16)
            sig = al([512], F32)
            acc = al([512], F32)
            t_m = [T() for _ in range(5)]
            t_sig = T()
            t_acc = T()
            wsrc = dr["w_in"][l].rearrange("(k p) c -> p k c", p=128)
            wbsrc = dr["w_branch"][l].rearrange("n (j p) d -> p n j d", p=128)
            groups = [g for g in range(5) if not (last and g == 0)]
            bi = 0
            for d in range(8):
                gw = []
                for n in range(4):
                    c0 = 1952 + n * 1024 + d * 128
                    gw.append(load_ring(wsrc[:, :, c0:c0 + 128], "p (k c) -> p k c", c=128))
                wb, t_wb = load_ring(wbsrc[:, :, :, d * 128:(d + 1) * 128], "p (n j c) -> p n j c", j=2, c=128)
                for g in groups:
                    t0, n_ = GRP(g)
                    for n in range(4):
                        sl = SLOT_OF[n]
                        pa = bi % 2
                        pb = 2 + bi % 2
                        bi += 1
                        wv, t_wv = gw[n]
                        for k in range(8):
                            mm(ps[pa][:, 0:n_], wv[:, k, :], hT[:, k, t0:t0 + n_], start=(k == 0), stop=(k == 7),
                               r=[t_wv, t_h[g]], w=[tps[pa]])
                        for j in range(2):
                            mm(ps[pb][:, 0:n_], wb[:, n, j, :], oT[sl][:, j, t0:t0 + n_], start=(j == 0), stop=(j == 1),
                               r=[t_wb, t_oT[sl][g]], w=[tps[pb]])
                        act(sig[:, 0:n_], ps[pa][:, 0:n_], AF.Sigmoid, r=[tps[pa]], w=[t_sig])
                        if n == 0:
                            tt("dve", acc[:, 0:n_], ps[pb][:, 0:n_], sig[:, 0:n_], ALU.mult, r=[tps[pb], t_sig], w=[t_acc])
                        else:
                            tt("dve", ps[pb][:, 0:n_], ps[pb][:, 0:n_], sig[:, 0:n_], ALU.mult, r=[tps[pb], t_sig], w=[tps[pb]])
                            if n < 3:
                                tt("dve", acc[:, 0:n_], acc[:, 0:n_], ps[pb][:, 0:n_], ALU.add, r=[tps[pb], t_acc], w=[t_acc])
                            else:
                                tt("dve", mT[:, d, t0:t0 + n_], acc[:, 0:n_], ps[pb][:, 0:n_], ALU.add,
                                   r=[tps[pb], t_acc], w=[t_m[g]])
            tap("mT", mT, t_m)
            wosrc = dr["w_out"][l].rearrange("(k p) c -> p k c", p=128)
            for d in range(8):
                wv, t_wv = load_ring(wosrc[:, :, d * 128:(d + 1) * 128], "p (k c) -> p k c", c=128)
                for g in groups:
                    t0, n_ = GRP(g)
                    s_ = 1 if g == 0 else 0
                    pb = 4 + bi % 4
                    bi += 1
                    for k in range(8):
                        mm(ps[pb][:, 0:n_], wv[:, k, :], mT[:, k, t0:t0 + n_], start=(k == 0), stop=(k == 7),
                           r=[t_wv, t_m[g]], w=[tps[pb]])
                    stt(xT[:, d, t0:t0 + n_], ps[pb][:, 0:n_], mod[:, l, 16 + d, s_:s_ + 1], xT[:, d, t0:t0 + n_],
                        ALU.mult, ALU.add, r=[tps[pb], t_mod, t_x[d][g]], w=[t_x[d][g]])

        def ffn_phase(l, last):
            groups = [g for g in range(5) if not (last and g == 0)]
            norm_phase(l, 1, groups)
            P.barrier()
            al = mk_alloc(A.nbytes)
            aT = al([8, NT], BF16)
            rl = al([2, 512], F32)
            t_a = [[T() for _ in range(5)] for _ in range(8)]
            t_rl = [T(), T()]
            w1src = dr["ffn_w1"][l].rearrange("(k p) c -> p k c", p=128)
            w2src = dr["ffn_w2"][l].rearrange("(f p) c -> p f c", p=128)
            bi = 0
            for fb in range(4):
                for f in range(8):
                    F_ = fb * 8 + f
                    wv, t_wv = load_ring(w1src[:, :, F_ * 128:(F_ + 1) * 128], "p (k c) -> p k c", c=128)
                    for g in groups:
                        t0, n_ = GRP(g)
                        pb = bi % 4
                        rb = bi % 2
                        bi += 1
                        for k in range(8):
                            mm(ps[pb][:, 0:n_], wv[:, k, :], hT[:, k, t0:t0 + n_], start=(k == 0), stop=(k == 7),
                               r=[t_wv, t_h[g]], w=[tps[pb]])
                        act(rl[:, rb, 0:n_], ps[pb][:, 0:n_], AF.Relu, r=[tps[pb]], w=[t_rl[rb]])
                        tt("pool", aT[:, f, t0:t0 + n_], rl[:, rb, 0:n_], rl[:, rb, 0:n_], ALU.mult, r=[t_rl[rb]], w=[t_a[f][g]])
                for d in range(8):
                    wv, t_wv = load_ring(w2src[:, fb * 8:(fb + 1) * 8, d * 128:(d + 1) * 128], "p (f c) -> p f c", c=128)
                    for g in groups:
                        t0, n_ = GRP(g)
                        s_ = 1 if g == 0 else 0
                        pb = 4 + bi % 4
                        bi += 1
                        for f in range(8):
                            mm(ps[pb][:, 0:n_], wv[:, f, :], aT[:, f, t0:t0 + n_], start=(f == 0), stop=(f == 7),
                               r=[t_wv, t_a[f][g]], w=[tps[pb]])
                        stt(xT[:, d, t0:t0 + n_], ps[pb][:, 0:n_], mod[:, l, 40 + d, s_:s_ + 1], xT[:, d, t0:t0 + n_],
                            ALU.mult, ALU.add, r=[tps[pb], t_mod, t_x[d][g]], w=[t_x[d][g]])

        for l in range(DEPTH):
            last = (l == DEPTH - 1)
            norm_phase(l, 0, list(range(5)))
            if l == 0:
                tap("hT", hT, t_h)
            if stage <= 0.1:
                break
            if "mla" in mixers:
                mla_phase(l, last, 3)
            if "ret" in mixers:
                ret_phase(l, last, 2)
            if "s5" in mixers:
                s5_phase(l, last, 1)
            if "fnet" in mixers:
                fnet_phase(l, last, 0)
            if stage <= 1:
                break
            merge_phase(l, last)
            ffn_phase(l, last)
            if l == 0:
                tap("x1", xT, [t for k in range(8) for t in t_x[k]])
            if stage <= 2:
                break

        osrc = outT.rearrange("(k p) t -> p k t", p=128)
        for k in range(8):
            out_handles.append(dma(osrc[:, k, :], xT[:, k, LAT0:NT], r=t_x[k]))
        P.wait_all("sp", out_handles)
        P.emit()
    return nc


_CACHE = {}


def _specs_of(d):
    sp = {}
    for k, v in d.items():
        sp[k] = (v.shape, BF16 if v.dtype == NPBF else F32)
    return sp


def run(inputs, stage=99, taps=(), ncores=8, mixers=("mla", "s5", "ret", "fnet")):
    com = prep_common(inputs)
    cores = [prep_core(inputs, b) for b in range(ncores)]
    in_maps = [dict(com, **c) for c in cores]
    nc = build(_specs_of(in_maps[0]), stage=stage, taps=taps, mixers=mixers)
    res = run_bass_kernel_spmd(nc, in_maps, core_ids=list(range(ncores)))
    return res.results


def kernel(**inputs):
    inputs = {k: np.asarray(v) for k, v in inputs.items()}
    res = run(inputs)
    out = np.stack([r["outT"].T for r in res], axis=0)
    return np.ascontiguousarray(out.astype(np.float32))
```

```python
import contextlib
import math
import os
import numpy as np
import ml_dtypes
import concourse.bass as bass
import concourse.mybir as mybir
from concourse.bass_utils import run_bass_kernel_spmd

F32 = mybir.dt.float32
BF16 = mybir.dt.bfloat16
ALU = mybir.AluOpType
AF = mybir.ActivationFunctionType
AX = mybir.AxisListType
NPBF = ml_dtypes.bfloat16

ENGS = ("pe", "act", "dve", "pool", "sp")
NOSELF = tuple(os.environ.get("NOSELF", "pe").split(","))
NDSLOT = 8

D = 1024
NT = 2304
NTT = 18
LAT0 = 256
EPS = 1e-6
DEPTH = 2


class T:
    __slots__ = ("name", "w", "rs", "excl")

    def __init__(self, name="", excl=False):
        self.name = name
        self.w = None
        self.rs = []
        self.excl = excl


class Prog:
    def __init__(self, nc, stack, same_sync=True):
        self.nc = nc
        self.same_sync = same_sync
        self.q = {e: [] for e in ENGS}
        self.cnt = {e: 0 for e in ENGS}
        self.sems = {}
        for e in ENGS:
            self.sems[("c", e)] = stack.enter_context(nc.semaphore("c_" + e))
        self.dq = ("sp", "pool", "act")
        self.dcnt = {}
        self.dn = {e: 0 for e in self.dq}
        for e in self.dq:
            for s in range(NDSLOT):
                self.sems[("d", e, s)] = stack.enter_context(nc.semaphore("d_%s%d" % (e, s)))
                self.dcnt[(e, s)] = 0
        self.known = {e: {} for e in ENGS}
        self.kstop = None
        self.kcount = 0
        import threading
        self._tls = threading.local()

    def _deps(self, eng, r, w):
        deps = {}

        def add(h):
            if h is None:
                return
            k, v = h
            if k == ("c", eng) and (eng in NOSELF or not self.same_sync):
                return
            if deps.get(k, 0) < v:
                deps[k] = v
        for t in r:
            add(t.w)
        for t in w:
            add(t.w)
            for h in t.rs:
                add(h)
        out = []
        kn = self.known[eng]
        for k, v in deps.items():
            if kn.get(k, 0) >= v:
                continue
            kn[k] = v
            out.append((k, v))
        return out

    def _mark(self, h, r, w):
        for t in r:
            t.rs.append(h)
            if len(t.rs) > 64:
                best = {}
                for k, v in t.rs:
                    if best.get(k, 0) < v:
                        best[k] = v
                t.rs = list(best.items())
        for t in w:
            t.w = h
            t.rs = []

    def interleave(self, thunks):
        import threading
        n = len(thunks)
        if n == 1 or os.environ.get("NOIL") == "1":
            for t in thunks:
                t()
            return
        cv = threading.Condition()
        st = {"turn": 0, "alive": [True] * n, "err": None}

        def nxt(i):
            for d in range(1, n + 1):
                j = (i + d) % n
                if st["alive"][j]:
                    return j
            return None

        def yp(i):
            with cv:
                j = nxt(i)
                if j is None or j == i:
                    return
                st["turn"] = j
                cv.notify_all()
                cv.wait_for(lambda: st["turn"] == i)

        def runner(i):
            try:
                with cv:
                    cv.wait_for(lambda: st["turn"] == i)
                self._tls.yp = (lambda: yp(i))
                thunks[i]()
            except BaseException as e:
                st["err"] = e
            finally:
                self._tls.yp = None
                with cv:
                    st["alive"][i] = False
                    j = nxt(i)
                    st["turn"] = j if j is not None else -1
                    cv.notify_all()

        ths = [threading.Thread(target=runner, args=(i,)) for i in range(n)]
        for t in ths:
            t.start()
        for t in ths:
            t.join()
        if st["err"] is not None:
            raise st["err"]

    def _yield(self):
        yp = getattr(self._tls, "yp", None)
        if yp is not None:
            yp()

    def op(self, eng, fn, r=(), w=()):
        self._yield()
        if self.kstop is not None:
            self.kcount += 1
            if self.kcount > self.kstop:
                return None
        if eng != "pe":
            ex = [t for t in r if t.excl]
            if ex:
                r = [t for t in r if not t.excl]
                w = list(w) + ex
        waits = self._deps(eng, r, w)
        self.cnt[eng] += 1
        h = (("c", eng), self.cnt[eng])
        self.q[eng].append((fn, waits, (h[0], 1)))
        self._mark(h, r, w)
        return h

    def dma(self, eng, fn, r=(), w=()):
        self._yield()
        s = self.dn[eng] % NDSLOT
        self.dn[eng] += 1
        waits = self._deps(eng, r, w)
        k = ("d", eng, s)
        prev = self.dcnt[(eng, s)]
        if prev > 0 and self.known[eng].get(k, 0) < prev:
            self.known[eng][k] = prev
            waits.append((k, prev))
        self.dcnt[(eng, s)] = prev + 16
        h = (k, prev + 16)
        self.q[eng].append((fn, waits, (k, 16)))
        self._mark(h, r, w)
        return h

    def barrier(self):
        hs = [(("c", e), self.cnt[e]) for e in ENGS if self.cnt[e] > 0]
        hs += [(("d", e, s), v) for (e, s), v in self.dcnt.items() if v > 0]
        for e in ENGS:
            self.wait_all(e, [h for h in hs if h[0] != ("c", e)])

    def wait_all(self, eng, hs):
        waits = []
        for k, v in hs:
            if self.known[eng].get(k, 0) < v:
                self.known[eng][k] = v
                waits.append((k, v))
        self.q[eng].append((None, waits, None))

    def emit(self):
        nc = self.nc
        sems = self.sems
        q = self.q

        def run(e, engobj):
            for fn, waits, inc in q[e]:
                for k, v in waits:
                    engobj.wait_ge(sems[k], v)
                if fn is None:
                    continue
                ins = fn(engobj)
                ins.then_inc(sems[inc[0]], inc[1])

        with nc.Block() as block:
            @block.tensor
            def _(eng):
                run("pe", eng)

            @block.scalar
            def _(eng):
                run("act", eng)

            @block.vector
            def _(eng):
                run("dve", eng)

            @block.gpsimd
            def _(eng):
                run("pool", eng)

            @block.sync
            def _(eng):
                run("sp", eng)


class Arena:
    def __init__(self, nc, stack, nbytes):
        self.t = stack.enter_context(nc.sbuf_tensor("arena", [128, nbytes // 4], F32))
        self.nbytes = nbytes
        self.off = 0

    def alloc(self, shape, dtype, at=None):
        esz = 2 if dtype == BF16 else 4
        n = int(np.prod(shape)) * esz
        n4 = (n + 3) // 4
        if at is None:
            at = self.off
            self.off += n4 * 4
        assert at % 4 == 0 and at + n4 * 4 <= self.nbytes, (at, n, self.nbytes)
        ap = self.t[:, at // 4: at // 4 + n4]
        if dtype != F32:
            ap = ap.bitcast(dtype)
        if len(shape) == 2:
            ap = ap.rearrange("p (a b) -> p a b", b=shape[1])
        elif len(shape) == 3:
            ap = ap.rearrange("p (a b c) -> p a b c", b=shape[1], c=shape[2])
        elif len(shape) == 4:
            ap = ap.rearrange("p (a b c d) -> p a b c d", b=shape[1], c=shape[2], d=shape[3])
        return ap


def _rope_tables():
    half = 8
    freqs = (10000.0 ** (-np.arange(half, dtype=np.float32) / half)).astype(np.float32)
    t = np.arange(2048)
    rows = (t // 64).astype(np.float32)
    cols = (t % 64).astype(np.float32)
    ang = np.concatenate([rows[:, None] * freqs[None], cols[:, None] * freqs[None]], axis=1)
    cos = np.ones((NT, 16), np.float32)
    sin = np.zeros((NT, 16), np.float32)
    cos[LAT0:] = np.cos(ang)
    sin[LAT0:] = np.sin(ang)
    cos = cos.reshape(NTT, 128, 16).transpose(1, 0, 2)
    sin = sin.reshape(NTT, 128, 16).transpose(1, 0, 2)
    return np.ascontiguousarray(cos), np.ascontiguousarray(sin)


def _fnet_consts():
    ci = np.arange(64)
    c64 = np.cos(2 * np.pi * np.outer(ci, ci) / 64.0)
    s64 = np.sin(2 * np.pi * np.outer(ci, ci) / 64.0)
    cs = np.zeros((2, 128, 512), np.float64)
    for j in range(2):
        for gl in range(2):
            g = 2 * j + gl
            cs[j, gl * 64:(gl + 1) * 64, g * 64:(g + 1) * 64] = c64
            cs[j, gl * 64:(gl + 1) * 64, 256 + g * 64:256 + (g + 1) * 64] = s64
    out = {"cs64": np.ascontiguousarray(cs.transpose(1, 0, 2)).astype(np.float32).astype(NPBF)}
    for L in (2048, 256):
        li = np.arange(L)
        m = np.outer(li, li) % L
        ang = 2 * np.pi * m / L
        sc = 1.0 / math.sqrt(L * 64.0)
        tab = np.stack([np.cos(ang) * sc, -np.sin(ang) * sc], axis=0)
        out["dft%d" % L] = tab.astype(np.float32).astype(NPBF)
    return out


def _s5_layouts(inp):
    f = np.float32
    out = {}
    m = np.arange(128)[:, None]
    t = np.arange(128)[None, :]
    tri = np.stack([(m <= t), (m >= t)], axis=1).astype(f)
    out["s5tri"] = tri.astype(NPBF)
    erow = np.stack([np.broadcast_to(t.astype(f), (128, 128)), np.broadcast_to(127.0 - t.astype(f), (128, 128))], axis=1)
    out["s5erow"] = np.ascontiguousarray(erow, f)
    p = np.arange(128, dtype=f)[:, None]
    out["s5ecol"] = np.ascontiguousarray(np.concatenate([p, 127.0 - p, -p, -(127.0 - p)], axis=1), f)
    out["s5d"] = np.ascontiguousarray(np.asarray(inp["s5_d"], f).reshape(DEPTH, 2, 128).transpose(0, 2, 1))
    re = np.asarray(inp["s5_lam_re"], f).reshape(DEPTH, 2, 1024)
    im = np.asarray(inp["s5_lam_im"], f).reshape(DEPTH, 2, 1024)
    ls = np.repeat(np.asarray(inp["s5_log_step"], f), 64, axis=-1)
    trip = np.stack([re, im, ls], axis=2)
    out["s5pp"] = np.ascontiguousarray(trip.reshape(DEPTH, 2, 3, 8, 128).transpose(0, 1, 4, 2, 3))
    out["s5row"] = np.ascontiguousarray(np.broadcast_to(trip[:, :, None], (DEPTH, 2, 128, 3, 1024)), f)
    bre = np.asarray(inp["s5_b_re"], f)
    bim = np.asarray(inp["s5_b_im"], f)
    sb = np.zeros((DEPTH, 2, 128, 2, 2, 512), f)
    for ri, bb in enumerate((bre, bim)):
        for kt in range(2):
            for gl in range(8):
                g = 8 * kt + gl
                sb[:, :, gl * 16:(gl + 1) * 16, kt, ri, gl * 64:(gl + 1) * 64] = bb[:, :, g].transpose(0, 1, 3, 2)
    out["s5b"] = sb
    cre = np.asarray(inp["s5_c_re"], f)
    cim = np.asarray(inp["s5_c_im"], f)
    sc = np.zeros((DEPTH, 2, 2, 8, 128, 128), f)
    for ri, cm in enumerate((cre, cim)):
        for st in range(8):
            for gl in range(2):
                g = 2 * st + gl
                col = (g % 8) * 16
                sc[:, :, ri, st, gl * 64:(gl + 1) * 64, col:col + 16] = cm[:, :, g].transpose(0, 1, 3, 2)
    out["s5c"] = sc
    out["s5_w_glu"] = np.ascontiguousarray(inp["s5_w_glu"], f)
    return out


def _ret_consts():
    half = 32
    freqs = (10000.0 ** (-np.arange(half, dtype=np.float32) / half)).astype(np.float32)
    pos = np.arange(2048, dtype=np.float32)
    ang = pos[:, None] * freqs[None]
    cos = np.ones((NT, 32), np.float32)
    sin = np.zeros((NT, 32), np.float32)
    cos[LAT0:] = np.cos(ang)
    sin[LAT0:] = np.sin(ang)
    tm = lambda a: np.ascontiguousarray(a.reshape(NTT, 128, 32).transpose(1, 0, 2))
    out = {"rrc": tm(cos), "rrs": tm(sin)}
    k = np.arange(128, dtype=np.float32)[:, None]
    q = np.arange(128, dtype=np.float32)[None, :]
    retc = np.stack([np.broadcast_to(q + 1.0, (128, 128)), np.broadcast_to(128.0 - q, (128, 128)),
                     np.maximum(q - k, 0.0), np.maximum(k - q, 0.0),
                     (q >= k).astype(np.float32), (k >= q).astype(np.float32)], axis=1)
    out["retc"] = np.ascontiguousarray(retc, np.float32)
    out["retp"] = np.ascontiguousarray(np.concatenate([k, 127.0 - k], axis=1), np.float32)
    return out


def prep_common(inp):
    f = np.float32
    c = {}
    c["ada_w"] = np.ascontiguousarray(inp["ada_w"], f)
    c["ada_bT"] = np.ascontiguousarray(inp["ada_b"].reshape(DEPTH, 48, 128).transpose(0, 2, 1), f)
    nw = np.concatenate([inp["norm_mix_w"].reshape(DEPTH, 8, 128), inp["norm_ffn_w"].reshape(DEPTH, 8, 128)], axis=1)
    c["nw"] = np.ascontiguousarray(nw.transpose(0, 2, 1), f)
    c["w_in"] = np.ascontiguousarray(inp["w_in"], f)
    bc = lambda v: np.ascontiguousarray(np.broadcast_to(v[:, None, :], (DEPTH, 128, v.shape[-1])), f)
    c["kvw"] = bc(inp["mla_kv_norm"])
    c["qnw"] = bc(inp["mla_q_norm"])
    c["qkq"] = bc(np.tile(inp["mla_qk_norm_q"], (1, 4)))
    c["qkk"] = bc(np.tile(inp["mla_qk_norm_k"], (1, 4)))
    c["w_ukv"] = np.ascontiguousarray(inp["mla_w_ukv"], f)
    c["w_uq"] = np.ascontiguousarray(inp["mla_w_uq"], f)
    cos, sin = _rope_tables()
    c["ropec"] = cos
    c["ropes"] = sin
    c["w_branch"] = np.ascontiguousarray(inp["w_branch"], f)
    c["w_out"] = np.ascontiguousarray(inp["w_out"], f)
    c["ffn_w1"] = np.ascontiguousarray(inp["ffn_w1"], f)
    c["ffn_w2"] = np.ascontiguousarray(inp["ffn_w2"], f)
    c.update(_fnet_consts())
    c.update(_ret_consts())
    c.update(_s5_layouts(inp))
    lg = np.asarray(inp["ret_decay_logit"], f)
    c["rlog_row"] = np.ascontiguousarray(np.broadcast_to(lg.reshape(DEPTH, 1, 8), (DEPTH, 128, 8)), f)
    pp = np.zeros((DEPTH, 128, 2, 2), f)
    for pair in range(2):
        for d_ in range(2):
            pp[:, 0:64, pair, d_] = lg[:, d_, 2 * pair][:, None]
            pp[:, 64:128, pair, d_] = lg[:, d_, 2 * pair + 1][:, None]
    c["rlog_pp"] = pp
    c["gnw"] = bc(inp["ret_gn_w"])
    c["identb"] = np.eye(128, dtype=f).astype(NPBF)
    c["identf"] = np.eye(128, dtype=f)
    return c


def prep_core(inp, b):
    f = np.float32
    d = {}
    xt = np.concatenate([inp["ctx"][b], inp["x"][b]], axis=0).T
    d["xT"] = np.ascontiguousarray(xt, f)
    d["cT"] = np.ascontiguousarray(np.stack([inp["c"][b], inp["c_ctx"]], axis=1), f)
    return d


def build(specs, stage=99, taps=(), mixers=("mla", "s5", "ret", "fnet")):
    nc = bass.Bass("TRN2", target_bir_lowering=False)
    dr = {}
    for name, (shape, dt) in specs.items():
        dr[name] = nc.dram_tensor(name, list(shape), dt, kind="ExternalInput").ap()
    outT = nc.dram_tensor("outT", [D, 2048], F32, kind="ExternalOutput").ap()
    tapd = {}
    for name, shape, dt in taps:
        tapd[name] = nc.dram_tensor("tap_" + name, list(shape), dt, kind="ExternalOutput").ap()

    with contextlib.ExitStack() as st:
        P = Prog(nc, st)
        A = Arena(nc, st, 206 * 1024)
        ps = [st.enter_context(nc.psum_tensor("ps%d" % i, [128, 512], F32)) for i in range(8)]
        tps = [T("ps%d" % i, excl=True) for i in range(8)]
        out_handles = []

        def mm(out, lhsT, rhs, start=True, stop=True, r=(), w=()):
            return P.op("pe", lambda e: e.matmul(out, lhsT=lhsT, rhs=rhs, start=start, stop=stop,
                                                 skip_group_check=True), r, w)

        def tr(out, in_, ident, r=(), w=()):
            return P.op("pe", lambda e: e.transpose(out=out, in_=in_, identity=ident), r, w)

        def act(out, in_, func, r=(), w=(), scale=1.0, bias=0.0, accum=None):
            if accum is None:
                return P.op("act", lambda e: e.activation(out=out, in_=in_, func=func, scale=scale, bias=bias), r, w)
            return P.op("act", lambda e: e.activation(out=out, in_=in_, func=func, scale=scale, bias=bias,
                                                      accum_out=accum), r, w)

        def tt(eng, out, a, b, op, r=(), w=()):
            return P.op(eng, lambda e: e.tensor_tensor(out=out, in0=a, in1=b, op=op), r, w)

        def ts(eng, out, a, s1, s2, op0, op1=None, r=(), w=()):
            if op1 is None:
                return P.op(eng, lambda e: e.tensor_scalar(out=out, in0=a, scalar1=s1, scalar2=None, op0=op0), r, w)
            return P.op(eng, lambda e: e.tensor_scalar(out=out, in0=a, scalar1=s1, scalar2=s2, op0=op0, op1=op1), r, w)

        def stt(out, a, s, b, op0, op1, r=(), w=()):
            return P.op("dve", lambda e: e.scalar_tensor_tensor(out=out, in0=a, scalar=s, in1=b, op0=op0, op1=op1), r, w)

        def cp(eng, out, in_, r=(), w=()):
            if eng == "act":
                return P.op(eng, lambda e: e.activation(out=out, in_=in_, func=AF.Copy), r, w)
            return P.op(eng, lambda e: e.tensor_copy(out=out, in_=in_), r, w)

        def red(out, in_, r=(), w=()):
            return P.op("dve", lambda e: e.tensor_reduce(out=out, in_=in_, axis=AX.X, op=ALU.add), r, w)

        def rcp(out, in_, r=(), w=()):
            return P.op("dve", lambda e: e.reciprocal(out=out, in_=in_), r, w)

        def mset(eng, out, val, w=()):
            return P.op(eng, lambda e: e.memset(out, val), (), w)

        def dma(out, in_, r=(), w=(), q="sp"):
            return P.dma(q, lambda e: e.dma_start(out=out, in_=in_), r, w)

        def tap(name, src, r):
            if name in tapd:
                out_handles.append(dma(tapd[name], src, r=r))

        xT = A.alloc([8, NT], F32)
        hT = A.alloc([8, NT], BF16)
        t_x = [[T("x%d_%d" % (k, g)) for g in range(5)] for k in range(8)]
        t_h = [T("h%d" % g) for g in range(5)]
        identb = A.alloc([128], BF16)
        identf = A.alloc([128], F32)
        onesb = A.alloc([128], BF16)
        mod = A.alloc([DEPTH, 48, 2], F32)
        a1 = A.alloc([DEPTH, 16, 2], F32)
        nwt = A.alloc([DEPTH, 16], F32)
        scT = A.alloc([8, 2], F32)
        eps_t = A.alloc([1], F32)
        t_const = T("const")
        t_mod = T("mod")
        NSTG = 2
        NRING = 5
        stg = [A.alloc([1024], F32) for _ in range(NSTG)]
        t_stg = [T("stg%d" % i) for i in range(NSTG)]
        ring = [A.alloc([1024], BF16) for _ in range(NRING)]
        t_ring = [T("ring%d" % i) for i in range(NRING)]
        sidx = [0]
        ridx = [0]
        DYN0 = A.off
        OT0 = A.nbytes - 4 * 2 * NT * 2
        oT = [A.alloc([2, NT], BF16, at=OT0 + i * 2 * NT * 2) for i in range(4)]
        t_oT = [[T("o%d_%d" % (i, g)) for g in range(5)] for i in range(4)]

        def GRP(g):
            return (0, 256) if g == 0 else (LAT0 + 512 * (g - 1), 512)

        def grp_of_tile(tt_):
            return 0 if tt_ < 2 else 1 + (tt_ - 2) // 4

        def next_stg():
            s = sidx[0] % NSTG
            sidx[0] += 1
            return s

        def load_cast(dst, t_dst, src, shape_str=None, **kw):
            s = next_stg()
            n = int(np.prod(src.shape[1:]))
            assert n <= 1024, n
            sv = stg[s][:, 0:n]
            if shape_str is not None:
                sv = sv.rearrange(shape_str, **kw)
            dma(sv, src, w=[t_stg[s]])
            cp("pool", dst, sv, r=[t_stg[s]], w=[t_dst])

        def load_ring(src, shape_str=None, **kw):
            i = ridx[0] % NRING
            ridx[0] += 1
            n = int(np.prod(src.shape[1:]))
            dv = ring[i][:, 0:n]
            if shape_str is not None:
                dv = dv.rearrange(shape_str, **kw)
            load_cast(dv, t_ring[i], src, shape_str, **kw)
            return dv, t_ring[i]

        xsrc = dr["xT"].rearrange("(k p) t -> p k t", p=128)
        for k in range(8):
            dma(xT[:, k, :], xsrc[:, k, :], w=t_x[k])
        dma(identb, dr["identb"], w=[t_const])
        dma(identf, dr["identf"], w=[t_const])
        dma(scT, dr["cT"].rearrange("(k p) j -> p k j", p=128), w=[t_const])
        dma(nwt, dr["nw"].rearrange("l p k -> p l k"), w=[t_const])
        mset("dve", onesb, 1.0, w=[t_const])
        mset("dve", eps_t, EPS, w=[t_const])
        act(scT, scT, AF.Silu, r=[t_const], w=[t_const])

        for l in range(DEPTH):
            P.barrier()
            bT = A.alloc([48], F32, at=DYN0)
            modrow = A.alloc([6144], F32, at=DYN0 + 256)
            t_bT = T()
            t_mr = T()
            dma(bT, dr["ada_bT"][l], w=[t_bT])
            wsrc = dr["ada_w"][l].rearrange("(k p) c -> p k c", p=128)
            for nchunk in range(12):
                pb = nchunk % 2
                for j in range(4):
                    s = next_stg()
                    sv = stg[s][:, 0:1024].rearrange("p (k c) -> p k c", c=512)
                    dma(sv, wsrc[:, 2 * j:2 * j + 2, nchunk * 512:(nchunk + 1) * 512], w=[t_stg[s]])
                    for kk in range(2):
                        k = 2 * j + kk
                        mm(ps[pb][0:2, :], scT[:, k, :], sv[:, kk, :], start=(k == 0), stop=(k == 7),
                           r=[t_stg[s], t_const], w=[tps[pb]])
                cp("act", modrow[0:2, nchunk * 512:(nchunk + 1) * 512], ps[pb][0:2, :], r=[tps[pb]], w=[t_mr])
            psm = ps[2 + l][:, 0:96].rearrange("p (c s) -> p c s", s=2)
            for ct in range(48):
                tr(psm[:, ct, :], modrow[0:2, ct * 128:(ct + 1) * 128], identf[0:2, 0:2], r=[t_mr, t_const], w=[tps[2 + l]])
            tt("dve", mod[:, l, :, :], psm, bT[:, :].unsqueeze(2).to_broadcast([128, 48, 2]), ALU.add,
               r=[tps[2 + l], t_bT], w=[t_mod])
            for j, c0 in ((0, 8), (1, 32)):
                stt(a1[:, l, j * 8:(j + 1) * 8, :], mod[:, l, c0:c0 + 8, :], 1.0,
                    nwt[:, l, j * 8:(j + 1) * 8].unsqueeze(2).to_broadcast([128, 8, 2]),
                    ALU.add, ALU.mult, r=[t_mod, t_const], w=[t_mod])
        tap("mod", mod, [t_mod])

        def norm_phase(l, which, groups):
            sh0 = 0 if which == 0 else 24
            P.barrier()
            sq = A.alloc([2, 8, 512], BF16, at=DYN0)
            rstd = A.alloc([2, 512], F32, at=DYN0 + 2 * 8 * 512 * 2)
            tmp = A.alloc([2, 512], F32, at=DYN0 + 2 * 8 * 512 * 2 + 2 * 512 * 4)
            t_sq = [T(), T()]
            t_rs = [T(), T()]
            t_tmp = [T(), T()]
            for gi, g in enumerate(groups):
                t0, n = GRP(g)
                s = 1 if g == 0 else 0
                b = gi % 2
                pb = 2 + b
                for k in range(8):
                    act(sq[:, b, k, 0:n], xT[:, k, t0:t0 + n], AF.Square, r=[t_x[k][g]], w=[t_sq[b]])
                for k in range(8):
                    mm(ps[pb][:, 0:n], onesb, sq[:, b, k, 0:n], start=(k == 0), stop=(k == 7),
                       r=[t_sq[b], t_const], w=[tps[pb]])
                act(rstd[:, b, 0:n], ps[pb][:, 0:n], AF.Sqrt, r=[tps[pb], t_const], w=[t_rs[b]],
                    scale=1.0 / D, bias=eps_t[:, 0:1])
                rcp(rstd[:, b, 0:n], rstd[:, b, 0:n], r=[t_rs[b]], w=[t_rs[b]])
                for k in range(8):
                    tb = k % 2
                    tt("dve", tmp[:, tb, 0:n], xT[:, k, t0:t0 + n], rstd[:, b, 0:n], ALU.mult,
                       r=[t_x[k][g], t_rs[b]], w=[t_tmp[tb]])
                    act(hT[:, k, t0:t0 + n], tmp[:, tb, 0:n], AF.Identity, r=[t_tmp[tb], t_mod], w=[t_h[g]],
                        scale=a1[:, l, which * 8 + k, s:s + 1], bias=mod[:, l, sh0 + k, s:s + 1])

        def mk_alloc(limit):
            o = [DYN0]

            def al(shape, dt):
                ap = A.alloc(shape, dt, at=o[0])
                n = int(np.prod(shape)) * (2 if dt == BF16 else 4)
                o[0] += (n + 3) // 4 * 4
                assert o[0] <= limit, (o[0], limit)
                return ap
            al.o = o
            return al

        def mla_phase(l, last, slot):
            P.barrier()
            al = mk_alloc(OT0 + slot * 2 * NT * 2)
            Wm = al([8, 416], BF16)
            Wukv = al([512], BF16)
            Wuq = al([2, 384], BF16)
            kvw = al([128], F32)
            qnw = al([256], F32)
            qkq = al([4, 96], F32)
            qkk = al([4, 96], F32)
            rc = al([NTT, 2, 8], F32)
            rs_ = al([NTT, 2, 8], F32)
            KT = al([4, NT], BF16)
            QT = al([4, 512], BF16)
            V1 = al([NTT, 4, 65], BF16)
            PT = al([2, 512], BF16)
            kvn = al([2, 128], BF16)
            kvnT = al([2, 128], BF16)
            qn = al([2, 256], BF16)
            qnT = al([2, 2, 128], BF16)
            kf = al([2, 4, 96], F32)
            sqs2 = al([2, 4, 96], F32)
            kb = al([2, 4, 96], BF16)
            sm = al([2, 16], F32)
            rt2 = al([2, 4, 4, 2, 8], F32) if False else None
            rtA = al([4, 4, 2, 8], F32)
            rtB = al([4, 4, 2, 8], F32)
            oat = al([4, 256], BF16)
            rinv = al([4], F32)
            t_W = T("Wm")
            t_small = T("mlasmall")
            t_KT = [T() for _ in range(NTT)]
            t_QT = [T() for _ in range(4)]
            t_V1 = [T() for _ in range(NTT)]
            t_PT = [T(), T()]
            t_kvn = [T(), T()]
            t_kvnT = [T(), T()]
            t_qn = [T(), T()]
            t_qnT = [T(), T()]
            t_kf = [T(), T()]
            t_sqs2 = [T(), T()]
            t_kb = [T(), T()]
            t_sm = [T(), T()]
            t_rt2 = [T(), T()]
            t_oat = [T() for _ in range(4)]
            t_rinv = T()

            wsrc = dr["w_in"][l].rearrange("(k p) c -> p k c", p=128)
            for k in range(0, 8, 2):
                load_cast(Wm[:, k:k + 2, 0:160], t_W, wsrc[:, k:k + 2, 0:160], "p (k c) -> p k c", c=160)
                load_cast(Wm[:, k:k + 2, 160:416], t_W, wsrc[:, k:k + 2, 928:1184], "p (k c) -> p k c", c=256)
            load_cast(Wukv, t_W, dr["w_ukv"][l])
            load_cast(Wuq, t_W, dr["w_uq"][l].rearrange("(j p) c -> p j c", p=128), "p (j c) -> p j c", c=384)
            dma(kvw, dr["kvw"][l], w=[t_small])
            dma(qnw, dr["qnw"][l], w=[t_small])
            dma(qkq, dr["qkq"][l].rearrange("p (h d) -> p h d", d=96), w=[t_small])
            dma(qkk, dr["qkk"][l].rearrange("p (h d) -> p h d", d=96), w=[t_small])
            dma(rc, dr["ropec"].rearrange("p t (a b) -> p t a b", b=8), w=[t_small])
            dma(rs_, dr["ropes"].rearrange("p t (a b) -> p t a b", b=8), w=[t_small])
            mset("dve", V1[:, :, :, 64:65], 1.0, w=t_V1)
            if stage < 0.5:
                return

            def headnorm_rope(pb, wq, tt_, dst, t_dst, tbank):
                x = kf[:, pb]
                t_x_ = t_kf[pb]
                sqs = sqs2[:, pb]
                t_sqs = t_sqs2[pb]
                rt = rtA if pb == 0 else rtB
                t_rt = t_rt2[pb]
                tt("dve", sqs, x, x, ALU.mult, r=[t_x_], w=[t_sqs])
                st_ = sm[:, pb, 0:4]
                red(st_, sqs, r=[t_sqs], w=[t_sm[pb]])
                act(st_, st_, AF.Sqrt, r=[t_sm[pb], t_const], w=[t_sm[pb]], scale=1.0 / 96, bias=eps_t[:, 0:1])
                rcp(st_, st_, r=[t_sm[pb]], w=[t_sm[pb]])
                tt("dve", x, x, st_.unsqueeze(2).to_broadcast([128, 4, 96]), ALU.mult, r=[t_x_, t_sm[pb]], w=[t_x_])
                tt("dve", x, x, wq, ALU.mult, r=[t_x_, t_small], w=[t_x_])
                y = kb[:, pb]
                t_y = t_kb[pb]
                cp("dve", y[:, :, 0:64], x[:, :, 0:64], r=[t_x_], w=[t_y])
                xr = x[:, :, 64:96].rearrange("p h (a two b) -> p h a two b", two=2, b=8)
                yr = y[:, :, 64:96].rearrange("p h (a two b) -> p h a two b", two=2, b=8)
                cosb = rc[:, tt_].unsqueeze(1).to_broadcast([128, 4, 2, 8])
                sinb = rs_[:, tt_].unsqueeze(1).to_broadcast([128, 4, 2, 8])
                x1 = xr[:, :, :, 0, :]
                x2 = xr[:, :, :, 1, :]
                tt("dve", rt[:, 0], x1, cosb, ALU.mult, r=[t_x_, t_small], w=[t_rt])
                tt("dve", rt[:, 1], x2, sinb, ALU.mult, r=[t_x_, t_small], w=[t_rt])
                tt("dve", yr[:, :, :, 0, :], rt[:, 0], rt[:, 1], ALU.subtract, r=[t_rt], w=[t_y])
                tt("dve", rt[:, 2], x1, sinb, ALU.mult, r=[t_x_, t_small], w=[t_rt])
                tt("dve", rt[:, 3], x2, cosb, ALU.mult, r=[t_x_, t_small], w=[t_rt])
                tt("dve", yr[:, :, :, 1, :], rt[:, 2], rt[:, 3], ALU.add, r=[t_rt], w=[t_y])
                pst = ps[tbank][:, :].bitcast(BF16)
                for h in range(4):
                    tr(pst[0:96, h * 128:(h + 1) * 128], y[:, h, :], identb, r=[t_y, t_const], w=[tps[tbank]])
                cp("act", dst, pst[0:96, 0:512].rearrange("p (h t) -> p h t", t=128), r=[tps[tbank]], w=[t_dst])

            if os.environ.get("KSTOP"):
                P.kstop = int(os.environ["KSTOP"])
            def _kbody(tt_):
                g = grp_of_tile(tt_)
                pb = tt_ % 2
                t0 = tt_ * 128
                bz = pb
                for k in range(8):
                    mm(ps[bz][:, 0:160], hT[:, k, t0:t0 + 128], Wm[:, k, 0:160], start=(k == 0), stop=(k == 7),
                       r=[t_h[g], t_W], w=[tps[bz]])
                z = ps[bz]
                ssk = sm[:, pb, 8:9]
                act(kf[:, pb].rearrange("p h d -> p (h d)")[:, 0:128], z[:, 0:128], AF.Square,
                    r=[tps[bz]], w=[t_kf[pb], t_sm[pb]], accum=ssk)
                act(ssk, ssk, AF.Sqrt, r=[t_sm[pb], t_const], w=[t_sm[pb]], scale=1.0 / 128, bias=eps_t[:, 0:1])
                rcp(ssk, ssk, r=[t_sm[pb]], w=[t_sm[pb]])
                stt(kvn[:, pb], z[:, 0:128], ssk, kvw, ALU.mult, ALU.mult, r=[tps[bz], t_sm[pb], t_small], w=[t_kvn[pb]])
                pst = ps[2 + pb][:, :].bitcast(BF16)
                tr(pst[:, 0:128], kvn[:, pb], identb, r=[t_kvn[pb], t_const], w=[tps[2 + pb]])
                cp("act", kvnT[:, pb], pst[:, 0:128], r=[tps[2 + pb]], w=[t_kvnT[pb]])
                bkv = 4 + pb
                mm(ps[bkv][:, :], kvnT[:, pb], Wukv, r=[t_kvnT[pb], t_W], w=[tps[bkv]])
                kvv = ps[bkv][:, :].rearrange("p (h c) -> p h c", c=128)
                cp("act", V1[:, tt_, :, 0:64], kvv[:, :, 64:128], r=[tps[bkv]], w=[t_V1[tt_]])
                cp("dve", kf[:, pb, :, 0:64], kvv[:, :, 0:64], r=[tps[bkv]], w=[t_kf[pb]])
                cp("dve", kf[:, pb, :, 64:96], z[:, 128:160].unsqueeze(1).to_broadcast([128, 4, 32]),
                   r=[tps[bz]], w=[t_kf[pb]])
                if os.environ.get("NOHN") != "1":
                    headnorm_rope(pb, qkk, tt_, KT[0:96, :, t0:t0 + 128], t_KT[tt_], 2 + pb)
            for tt_ in range(0, NTT, 2):
                P.interleave([(lambda a=tt_: _kbody(a)), (lambda a=tt_ + 1: _kbody(a))])
            P.kstop = None
            tap("KT", KT[0:96], t_KT)
            tap("V1", V1, t_V1)
            if stage < 0.7:
                return

            scale = 96 ** -0.5
            qgroups = [(g, GRP(g)[0], GRP(g)[1], list(range(NTT))) for g in range(1, 5)]
            if not last:
                qgroups = [(0, 0, 256, [0, 1])] + qgroups
            it = 0
            for (g, q0, nq, ktiles) in qgroups:
                nqs = nq // 128

                def _qbody(qs, g=g, q0=q0):
                    tt_ = q0 // 128 + qs
                    pb = qs % 2
                    t0 = tt_ * 128
                    bz = pb
                    for k in range(8):
                        mm(ps[bz][:, 0:256], hT[:, k, t0:t0 + 128], Wm[:, k, 160:416], start=(k == 0), stop=(k == 7),
                           r=[t_h[g], t_W], w=[tps[bz]])
                    z = ps[bz]
                    ssq = sm[:, pb, 9:10]
                    act(qn[:, pb], z[:, 0:256], AF.Square, r=[tps[bz]], w=[t_qn[pb], t_sm[pb]], accum=ssq)
                    act(ssq, ssq, AF.Sqrt, r=[t_sm[pb], t_const], w=[t_sm[pb]], scale=1.0 / 256, bias=eps_t[:, 0:1])
                    rcp(ssq, ssq, r=[t_sm[pb]], w=[t_sm[pb]])
                    stt(qn[:, pb], z[:, 0:256], ssq, qnw, ALU.mult, ALU.mult, r=[tps[bz], t_sm[pb], t_small], w=[t_qn[pb]])
                    pst = ps[2 + pb][:, :].bitcast(BF16)
                    for j in range(2):
                        tr(pst[:, j * 128:(j + 1) * 128], qn[:, pb, j * 128:(j + 1) * 128], identb,
                           r=[t_qn[pb], t_const], w=[tps[2 + pb]])
                    cp("act", qnT[:, pb], pst[:, 0:256].rearrange("p (j t) -> p j t", t=128), r=[tps[2 + pb]], w=[t_qnT[pb]])
                    bq = 2 + pb
                    for j in range(2):
                        mm(ps[bq][:, 0:384], qnT[:, pb, j], Wuq[:, j, :], start=(j == 0), stop=(j == 1),
                           r=[t_qnT[pb], t_W], w=[tps[bq]])
                    cp("dve", kf[:, pb], ps[bq][:, 0:384].rearrange("p (h d) -> p h d", d=96), r=[tps[bq]], w=[t_kf[pb]])
                    headnorm_rope(pb, qkq, tt_, QT[0:96, :, qs * 128:(qs + 1) * 128], t_QT[qs], 2 + pb)
                for qs in range(0, nqs, 2):
                    P.interleave([(lambda a=qs: _qbody(a)), (lambda a=qs + 1: _qbody(a))])
                if g == 1:
                    tap("QT", QT[0:96], t_QT)
                if stage < 0.8:
                    continue
                for h in range(4):
                    for ki, kt in enumerate(ktiles):
                        sb = it % 2
                        it += 1
                        mm(ps[sb][:, 0:nq], KT[0:96, h, kt * 128:(kt + 1) * 128], QT[0:96, h, 0:nq],
                           r=[t_KT[kt]] + t_QT[0:nqs], w=[tps[sb]])
                        act(PT[:, sb, 0:nq], ps[sb][:, 0:nq], AF.Exp, r=[tps[sb]], w=[t_PT[sb]], scale=scale)
                        for qs in range(nqs):
                            mm(ps[4 + qs][:, 0:65], PT[:, sb, qs * 128:(qs + 1) * 128], V1[:, kt, h, :],
                               start=(ki == 0), stop=(ki == len(ktiles) - 1),
                               r=[t_PT[sb], t_V1[kt]], w=[tps[4 + qs]])
                    for qs in range(nqs):
                        rcp(rinv[:, qs:qs + 1], ps[4 + qs][:, 64:65], r=[tps[4 + qs]], w=[t_rinv])
                        ts("dve", oat[:, qs, h * 64:(h + 1) * 64], ps[4 + qs][:, 0:64], rinv[:, qs:qs + 1], None,
                           ALU.mult, r=[tps[4 + qs], t_rinv], w=[t_oat[qs]])
                for qs in range(nqs):
                    tb = 2 + qs % 2
                    pst = ps[tb][:, :].bitcast(BF16)
                    for j in range(2):
                        tr(pst[:, j * 128:(j + 1) * 128], oat[:, qs, j * 128:(j + 1) * 128], identb,
                           r=[t_oat[qs], t_const], w=[tps[tb]])
                    tq = q0 + qs * 128
                    cp("act", oT[slot][:, :, tq:tq + 128], pst[:, 0:256].rearrange("p (j t) -> p j t", t=128),
                       r=[tps[tb]], w=[t_oT[slot][g]])
            tap("oaT", oT[slot], t_oT[slot])

        def fnet_phase(l, last, slot):
            P.barrier()
            al = mk_alloc(OT0 + slot * 2 * NT * 2)
            UT = al([2, NT], BF16)
            AB = al([NTT, 512], BF16)
            CS = al([2, 512], BF16)
            tb = al([2, 2, 1024], BF16)
            c256 = al([2, 2, 256], BF16)
            t_UT = [T() for _ in range(5)]
            t_AB = [T() for _ in range(NTT)]
            t_CS = T()
            t_tb = [T(), T()]
            t_c256 = T()
            dma(CS, dr["cs64"], w=[t_CS])
            for lt in range(2):
                dma(c256[:, lt], dr["dft256"][:, lt * 128:(lt + 1) * 128, :].rearrange("c p n -> p c n"), w=[t_c256])
            wsrc = dr["w_in"][l].rearrange("(k p) c -> p k c", p=128)
            groups = [g for g in range(5) if not (last and g == 0)]
            bi = 0
            for j in range(2):
                wv, t_wv = load_ring(wsrc[:, :, 1184 + j * 128:1184 + (j + 1) * 128], "p (k c) -> p k c", c=128)
                for g in groups:
                    t0, n = GRP(g)
                    pb = bi % 4
                    bi += 1
                    for k in range(8):
                        mm(ps[pb][:, 0:n], wv[:, k, :], hT[:, k, t0:t0 + n], start=(k == 0), stop=(k == 7),
                           r=[t_wv, t_h[g]], w=[tps[pb]])
                    cp("act", UT[:, j, t0:t0 + n], ps[pb][:, 0:n], r=[tps[pb]], w=[t_UT[g]])
            for tt_ in (range(2, NTT) if last else range(NTT)):
                g = grp_of_tile(tt_)
                pb = 4 + tt_ % 4
                for j in range(2):
                    mm(ps[pb][:, :], UT[:, j, tt_ * 128:(tt_ + 1) * 128], CS[:, j, :], start=(j == 0), stop=(j == 1),
                       r=[t_UT[g], t_CS], w=[tps[pb]])
                cp("dve", AB[:, tt_, :], ps[pb][:, :], r=[tps[pb]], w=[t_AB[tt_]])
            if not last:
                for j in range(2):
                    pb = j
                    n_mm = 0
                    for lt in range(2):
                        for cs_ in range(2):
                            mm(ps[pb][:, 0:256], AB[:, lt, cs_ * 256 + j * 128:cs_ * 256 + (j + 1) * 128],
                               c256[:, lt, cs_, :], start=(n_mm == 0), stop=(n_mm == 3),
                               r=[t_AB[lt], t_c256], w=[tps[pb]])
                            n_mm += 1
                    cp("act", oT[slot][:, j, 0:256], ps[pb][:, 0:256], r=[tps[pb]], w=[t_oT[slot][0]])
            it = 0
            for half in range(2):
                banks = [4 * half + i for i in range(4)]
                for lt in range(16):
                    b_ = it % 2
                    it += 1
                    dma(tb[:, b_], dr["dft2048"][:, lt * 128:(lt + 1) * 128, half * 1024:(half + 1) * 1024]
                        .rearrange("c p n -> p c n"), w=[t_tb[b_]])
                    for j in range(2):
                        for cs_ in range(2):
                            for lg in range(2):
                                bk = banks[j * 2 + lg]
                                mm(ps[bk][:, :], AB[:, 2 + lt, cs_ * 256 + j * 128:cs_ * 256 + (j + 1) * 128],
                                   tb[:, b_, cs_, lg * 512:(lg + 1) * 512],
                                   start=(lt == 0 and cs_ == 0), stop=(lt == 15 and cs_ == 1),
                                   r=[t_AB[2 + lt], t_tb[b_]], w=[tps[bk]])
                for j in range(2):
                    for lg in range(2):
                        bk = banks[j * 2 + lg]
                        g = 1 + half * 2 + lg
                        t0 = LAT0 + half * 1024 + lg * 512
                        cp("act", oT[slot][:, j, t0:t0 + 512], ps[bk][:, :], r=[tps[bk]], w=[t_oT[slot][g]])
            tap("obT", oT[slot], t_oT[slot])

        def ret_phase(l, last, slot):
            P.barrier()
            al = mk_alloc(OT0 + slot * 2 * NT * 2)
            rrc = al([NTT, 32], F32)
            rrs = al([NTT, 32], F32)
            retc = al([6, 128], F32)
            retp = al([2], F32)
            lgrow = al([8], F32)
            lgpp = al([2, 2], F32)
            gnw = al([256], F32)
            Wr = al([8, 512], BF16)
            QZ = al([2, NT], BF16)
            KT_ = al([NT], BF16)
            Vb = al([NTT, 128], BF16)
            sg = al([NTT, 128], BF16)
            Sfp = al([NTT, 128], BF16)
            Sbn = Wr.rearrange("p k c -> p (k c)")[:, 0:NTT * 128].rearrange("p (i c) -> p i c", c=128)
            ub_off = al.o[0]
            Ub = al([NTT, 128], BF16)
            kdec = al([2, 128], F32)
            qdec = al([2, 128], F32)
            Mk = al([2, 128], F32)
            aab = al([2], F32)
            Sf = al([128], F32)
            Sb = al([128], F32)
            rt = al([2, 4, 2, 32], F32)
            Qb = al([2, 128], BF16)
            Kb = al([2, 128], BF16)
            Kd = al([2, 2, 128], BF16)
            Qd = A.alloc([2, 2, 2, 128], BF16, at=ub_off)
            attm = A.alloc([2, 2, 128], BF16, at=ub_off + 2048)
            xc2 = al([2, 2, 64], F32)
            xcq = xc2[:, 0].rearrange("p a d -> p (a d)")
            sq2 = al([2, 2, 64], F32)
            st2 = al([2, 2, 4], F32)
            od = A.alloc([2, 128], BF16, at=ub_off + 3072)
            t_c = T()
            t_lg = T()
            t_Wr = T()
            t_QK = [T() for _ in range(NTT)]
            t_Vb = [T() for _ in range(NTT)]
            t_sg = [T() for _ in range(NTT)]
            t_Sfp = [T() for _ in range(NTT)]
            t_Sbn = [T() for _ in range(NTT)]
            t_Ub = [T() for _ in range(NTT)]
            t_tab = T()
            t_S = T()
            t_rt = [T(), T()]
            t_Qb = [T(), T()]
            t_Kb = [T(), T()]
            t_Kd = [T(), T()]
            t_Qd = [T(), T()]
            t_attm = [T(), T()]
            t_gn2 = [T(), T()]
            t_od = [T(), T()]
            dma(rrc, dr["rrc"], w=[t_c])
            dma(rrs, dr["rrs"], w=[t_c])
            dma(retc, dr["retc"], w=[t_c])
            dma(retp, dr["retp"], w=[t_c])
            dma(lgrow, dr["rlog_row"][l], w=[t_lg])
            dma(lgpp, dr["rlog_pp"][l], w=[t_lg])
            dma(gnw, dr["gnw"][l], w=[t_c])
            for v in (lgrow, lgpp.rearrange("p a b -> p (a b)")):
                act(v, v, AF.Exp, r=[t_lg], w=[t_lg], scale=-1.0)
                act(v, v, AF.Ln, r=[t_lg], w=[t_lg], bias=1.0)
                ts("dve", v, v, -1.0, None, ALU.mult, r=[t_lg], w=[t_lg])
            wsrc = dr["w_in"][l].rearrange("(k p) c -> p k c", p=128)
            for pair in range(2):
                P.barrier()
                offs = [416 + pair * 128, 672 + pair * 128, 1440 + pair * 128, 1696 + pair * 128]
                for ci, c0 in enumerate(offs):
                    load_cast(Wr[:, :, ci * 128:(ci + 1) * 128], t_Wr, wsrc[:, :, c0:c0 + 128], "p (k c) -> p k c", c=128)
                for hh in range(2):
                    head = 2 * pair + hh
                    cs_ = slice(hh * 64, (hh + 1) * 64)
                    act(kdec[:, 0, cs_], lgrow[:, head:head + 1].to_broadcast([128, 64]), AF.Exp, r=[t_lg, t_c], w=[t_tab],
                        scale=retp[:, 1:2])
                    act(kdec[:, 1, cs_], lgrow[:, 4 + head:5 + head].to_broadcast([128, 64]), AF.Exp, r=[t_lg, t_c], w=[t_tab],
                        scale=retp[:, 0:1])
                    act(Mk[:, hh, :], retc[:, 2, :], AF.Exp, r=[t_lg, t_c], w=[t_tab], scale=lgrow[:, head:head + 1])
                    tt("dve", Mk[:, hh, :], Mk[:, hh, :], retc[:, 4, :], ALU.mult, r=[t_tab, t_c], w=[t_tab])
                    act(xcq, retc[:, 3, :], AF.Exp, r=[t_lg, t_c], w=[t_tab], scale=lgrow[:, 4 + head:5 + head])
                    tt("dve", xcq, xcq, retc[:, 5, :], ALU.mult, r=[t_tab, t_c], w=[t_tab])
                    stt(Mk[:, hh, :], Mk[:, hh, :], 1.0, xcq, ALU.mult, ALU.add, r=[t_tab], w=[t_tab])
                ts("dve", Mk, Mk, 0.125, None, ALU.mult, r=[t_tab], w=[t_tab])
                ts("dve", kdec, kdec, 0.125, None, ALU.mult, r=[t_tab], w=[t_tab])
                act(qdec[:, 0, :], retc[:, 0, :], AF.Exp, r=[t_lg, t_c], w=[t_tab], scale=lgpp[:, pair, 0:1])
                act(qdec[:, 1, :], retc[:, 1, :], AF.Exp, r=[t_lg, t_c], w=[t_tab], scale=lgpp[:, pair, 1:2])
                act(aab, lgpp[:, pair, :], AF.Exp, r=[t_lg], w=[t_tab], scale=128.0)
                mset("dve", Sf, 0.0, w=[t_S])
                mset("dve", Sb, 0.0, w=[t_S])
                mset("pool", QZ, 0.0, w=t_QK)

                def rope(src, cosT, sinT, dst, pb, t_dst, t_src):
                    x = src.rearrange("p (h two b) -> p h two b", two=2, b=32)
                    y = dst.rearrange("p (h two b) -> p h two b", two=2, b=32)
                    cb = cosT.unsqueeze(1).to_broadcast([128, 2, 32])
                    sb_ = sinT.unsqueeze(1).to_broadcast([128, 2, 32])
                    r_ = rt[:, pb]
                    tt("dve", r_[:, 0], x[:, :, 0, :], cb, ALU.mult, r=[t_src, t_c], w=[t_rt[pb]])
                    tt("dve", r_[:, 1], x[:, :, 1, :], sb_, ALU.mult, r=[t_src, t_c], w=[t_rt[pb]])
                    tt("dve", y[:, :, 0, :], r_[:, 0], r_[:, 1], ALU.subtract, r=[t_rt[pb]], w=[t_dst])
                    tt("dve", r_[:, 2], x[:, :, 0, :], sb_, ALU.mult, r=[t_src, t_c], w=[t_rt[pb]])
                    tt("dve", r_[:, 3], x[:, :, 1, :], cb, ALU.mult, r=[t_src, t_c], w=[t_rt[pb]])
                    tt("dve", y[:, :, 1, :], r_[:, 2], r_[:, 3], ALU.add, r=[t_rt[pb]], w=[t_dst])

                def _p1body(i):
                    g = grp_of_tile(i)
                    pb = i % 2
                    t0 = i * 128
                    bz = pb
                    for k in range(8):
                        mm(ps[bz][:, :], hT[:, k, t0:t0 + 128], Wr[:, k, :], start=(k == 0), stop=(k == 7),
                           r=[t_h[g], t_Wr], w=[tps[bz]])
                    z = ps[bz]
                    rope(z[:, 256:384], rrc[:, i, :], rrs[:, i, :], Qb[:, pb], pb, t_Qb[pb], tps[bz])
                    rope(z[:, 0:128], rrc[:, i, :], rrs[:, i, :], Kb[:, pb], pb, t_Kb[pb], tps[bz])
                    cp("act", Vb[:, i, :], z[:, 128:256], r=[tps[bz]], w=[t_Vb[i]])
                    act(sg[:, i, :], z[:, 384:512], AF.Silu, r=[tps[bz]], w=[t_sg[i]])
                    tb_ = 2 + pb
                    pst = ps[tb_][:, :].bitcast(BF16)
                    tr(pst[:, 0:128], Qb[:, pb], identb, r=[t_Qb[pb], t_const], w=[tps[tb_]])
                    tr(pst[:, 128:256], Kb[:, pb], identb, r=[t_Kb[pb], t_const], w=[tps[tb_]])
                    cp("act", QZ[0:64, 0, t0:t0 + 128], pst[0:64, 0:128], r=[tps[tb_]], w=[t_QK[i]])
                    cp("act", QZ[64:128, 1, t0:t0 + 128], pst[64:128, 0:128], r=[tps[tb_]], w=[t_QK[i]])
                    cp("act", KT_[:, t0:t0 + 128], pst[:, 128:256], r=[tps[tb_]], w=[t_QK[i]])
                    tt("pool", Kd[:, pb, 0], Kb[:, pb], kdec[:, 0], ALU.mult, r=[t_Kb[pb], t_tab], w=[t_Kd[pb]])
                    tt("pool", Kd[:, pb, 1], Kb[:, pb], kdec[:, 1], ALU.mult, r=[t_Kb[pb], t_tab], w=[t_Kd[pb]])
                    bu = 4 + pb
                    mm(ps[bu][:, 0:128], Kd[:, pb, 0], Vb[:, i, :], r=[t_Kd[pb], t_Vb[i]], w=[tps[bu]])
                    mm(ps[bu][:, 128:256], Kd[:, pb, 1], Vb[:, i, :], r=[t_Kd[pb], t_Vb[i]], w=[tps[bu]])

                for i0 in range(0, NTT, 2):
                    P.interleave([(lambda a=i0: _p1body(a)), (lambda a=i0 + 1: _p1body(a))])
                    for i in (i0, i0 + 1):
                        bu = 4 + i % 2
                        cp("dve", Sfp[:, i, :], Sf, r=[t_S], w=[t_Sfp[i]])
                        stt(Sf, Sf, aab[:, 0:1], ps[bu][:, 0:128], ALU.mult, ALU.add, r=[t_S, t_tab, tps[bu]], w=[t_S])
                        cp("act", Ub[:, i, :], ps[bu][:, 128:256], r=[tps[bu]], w=[t_Ub[i]])
                P.barrier()
                for i in [1, 0] + list(range(NTT - 1, 1, -1)):
                    cp("dve", Sbn[:, i, :], Sb, r=[t_S], w=[t_Sbn[i]])
                    stt(Sb, Sb, aab[:, 1:2], Ub[:, i, :], ALU.mult, ALU.add, r=[t_S, t_tab, t_Ub[i]], w=[t_S])
                P.barrier()
                def _p2body(i):
                    g = grp_of_tile(i)
                    pb = i % 2
                    t0 = i * 128
                    xc = xc2[:, pb]
                    sq = sq2[:, pb]
                    st_ = st2[:, pb]
                    t_gn = t_gn2[pb]
                    ba = pb
                    for hh in range(2):
                        mm(ps[ba][:, hh * 128:(hh + 1) * 128], KT_[:, t0:t0 + 128], QZ[:, hh, t0:t0 + 128],
                           r=[t_QK[i]], w=[tps[ba]])
                    tt("dve", attm[:, pb], ps[ba][:, 0:256].rearrange("p (a t) -> p a t", t=128), Mk, ALU.mult,
                       r=[tps[ba], t_tab], w=[t_attm[pb]])
                    for hh in range(2):
                        tt("pool", Qd[:, pb, hh, 0], QZ[:, hh, t0:t0 + 128], qdec[:, 0], ALU.mult, r=[t_QK[i], t_tab], w=[t_Qd[pb]])
                        tt("pool", Qd[:, pb, hh, 1], QZ[:, hh, t0:t0 + 128], qdec[:, 1], ALU.mult, r=[t_QK[i], t_tab], w=[t_Qd[pb]])
                    bo = 4 + pb
                    for hh in range(2):
                        rs_ = slice(hh * 64, (hh + 1) * 64)
                        mm(ps[bo][:, rs_], attm[:, pb, hh, :], Vb[:, i, rs_], start=True, stop=False,
                           r=[t_attm[pb], t_Vb[i]], w=[tps[bo]])
                        mm(ps[bo][:, rs_], Qd[:, pb, hh, 0, :], Sfp[:, i, rs_], start=False, stop=False,
                           r=[t_Qd[pb], t_Sfp[i]], w=[tps[bo]])
                        mm(ps[bo][:, rs_], Qd[:, pb, hh, 1, :], Sbn[:, i, rs_], start=False, stop=True,
                           r=[t_Qd[pb], t_Sbn[i]], w=[tps[bo]])
                    o = ps[bo][:, 0:128].rearrange("p (a d) -> p a d", d=64)
                    red(st_[:, 0, 0:2], o, r=[tps[bo]], w=[t_gn])
                    ts("dve", st_[:, 0, 0:2], st_[:, 0, 0:2], -1.0 / 64, None, ALU.mult, r=[t_gn], w=[t_gn])
                    tt("dve", xc, o, st_[:, 0, 0:2].unsqueeze(2).to_broadcast([128, 2, 64]), ALU.add, r=[tps[bo], t_gn], w=[t_gn])
                    tt("dve", sq, xc, xc, ALU.mult, r=[t_gn], w=[t_gn])
                    red(st_[:, 1, 0:2], sq, r=[t_gn], w=[t_gn])
                    act(st_[:, 1, 0:2], st_[:, 1, 0:2], AF.Sqrt, r=[t_gn, t_const], w=[t_gn], scale=1.0 / 64, bias=eps_t[:, 0:1])
                    rcp(st_[:, 1, 0:2], st_[:, 1, 0:2], r=[t_gn], w=[t_gn])
                    tt("dve", xc, xc, st_[:, 1, 0:2].unsqueeze(2).to_broadcast([128, 2, 64]), ALU.mult, r=[t_gn], w=[t_gn])
                    tt("dve", xc, xc, gnw[:, pair * 128:(pair + 1) * 128].rearrange("p (a d) -> p a d", d=64), ALU.mult,
                       r=[t_gn, t_c], w=[t_gn])
                    tt("dve", od[:, pb].rearrange("p (a d) -> p a d", d=64), xc,
                       sg[:, i, :].rearrange("p (a d) -> p a d", d=64), ALU.mult, r=[t_gn, t_sg[i]], w=[t_od[pb]])
                    tb_ = 2 + pb
                    pst = ps[tb_][:, :].bitcast(BF16)
                    tr(pst[:, 0:128], od[:, pb], identb, r=[t_od[pb], t_const], w=[tps[tb_]])
                    cp("act", oT[slot][:, pair, t0:t0 + 128], pst[:, 0:128], r=[tps[tb_]], w=[t_oT[slot][g]])

                for i0 in range(2 if last else 0, NTT, 2):
                    P.interleave([(lambda a=i0: _p2body(a)), (lambda a=i0 + 1: _p2body(a))])
            tap("odT", oT[slot], t_oT[slot])

        I32 = mybir.dt.int32
        TWO_PI = 2.0 * math.pi

        def s5_phase(l, last, slot):
            P.barrier()
            al = mk_alloc(OT0 + slot * 2 * NT * 2)
            uT = al([2, NT], BF16)
            yf = al([2, NT], BF16)
            E = al([2, 1024], BF16)
            Fm = al([8, 2, 128], BF16)
            Bb = al([2, 2, 512], BF16)
            Cc = al([2, 8, 128], BF16)
            Tri = al([2, 128], BF16)
            pp = al([3, 8], F32)
            sm = al([12, 8], F32)
            cst = al([4], F32)
            erow = al([2, 128], F32)
            ecol = al([4], F32)
            dvec = al([2], F32)
            xl = al([8, 2], F32)
            cc = al([8, 2], F32)
            woff = al.o[0]
            W = al([2, 1024], BF16)
            xx = al([2, 8, 128], BF16)
            tW = al([2, 256], F32)
            tq = al([2, 512], F32)
            ysc = A.alloc([4, 128], F32, at=al.o[0] - 2048)
            cT = al([128], F32)
            t_u = [T() for _ in range(5)]
            t_yf = [T() for _ in range(NTT)]
            t_tab = T()
            t_pp = T()
            t_c = T()
            t_W = [T(), T()]
            t_xx = [T(), T()]
            t_tW = T()
            t_tq = T()
            t_xl = T()
            t_cc = T()
            t_ysc = t_tq
            t_cT = T()
            t_blk = T()

            dma(Tri, dr["s5tri"], w=[t_c])
            dma(erow, dr["s5erow"], w=[t_c])
            dma(ecol, dr["s5ecol"], w=[t_c])
            dma(dvec, dr["s5d"][l], w=[t_c])
            mset("dve", cst[:, 0:1], -math.pi, w=[t_c])

            wsrc = dr["w_in"][l].rearrange("(k p) c -> p k c", p=128)
            bi = 0
            for j in range(2):
                wv, t_wv = load_ring(wsrc[:, :, 160 + j * 128:160 + (j + 1) * 128], "p (k c) -> p k c", c=128)
                for g in range(5):
                    t0, n = GRP(g)
                    pb = bi % 4
                    bi += 1
                    for k in range(8):
                        mm(ps[pb][:, 0:n], wv[:, k, :], hT[:, k, t0:t0 + n], start=(k == 0), stop=(k == 7),
                           r=[t_wv, t_h[g]], w=[tps[pb]])
                    cp("act", uT[:, j, t0:t0 + n], ps[pb][:, 0:n], r=[tps[pb]], w=[t_u[g]])
            tap("uT", uT, t_u)

            def cplx_pow(out_re, out_im, phase, mag, n, conj, tmp):
                r_, n_i, f_, m_ = tmp
                ts("dve", r_, phase, 1.0 / TWO_PI, None, ALU.mult, r=[t_blk], w=[t_blk])
                cp("dve", n_i.bitcast(I32), r_, r=[t_blk], w=[t_blk])
                cp("dve", f_, n_i.bitcast(I32), r=[t_blk], w=[t_blk])
                tt("dve", f_, r_, f_, ALU.subtract, r=[t_blk], w=[t_blk])
                ts("dve", m_, f_, 0.0, None, ALU.is_lt, r=[t_blk], w=[t_blk])
                tt("dve", f_, f_, m_, ALU.add, r=[t_blk], w=[t_blk])
                act(r_, f_, AF.Sin, r=[t_blk, t_c], w=[t_blk], scale=TWO_PI, bias=cst[:, 0:1])
                ts("dve", f_, f_, 0.25, None, ALU.add, r=[t_blk], w=[t_blk])
                ts("dve", m_, f_, 1.0, None, ALU.is_ge, r=[t_blk], w=[t_blk])
                tt("dve", f_, f_, m_, ALU.subtract, r=[t_blk], w=[t_blk])
                act(m_, f_, AF.Sin, r=[t_blk, t_c], w=[t_blk], scale=TWO_PI, bias=cst[:, 0:1])
                stt(out_re, mag, -1.0, m_, ALU.mult, ALU.mult, r=[t_blk], w=[t_blk, t_tab])
                if conj:
                    tt("dve", out_im, mag, r_, ALU.mult, r=[t_blk], w=[t_blk, t_tab])
                else:
                    stt(out_im, mag, -1.0, r_, ALU.mult, ALU.mult, r=[t_blk], w=[t_blk, t_tab])

            glw = None
            for d_ in range(2):
                P.barrier()
                B_ = [A.alloc([256], F32, at=woff + i * 1024) for i in range(16)]
                dma(pp, dr["s5pp"][l, d_], w=[t_pp])
                act(pp[:, 2, :], pp[:, 2, :], AF.Exp, r=[t_pp], w=[t_pp])
                App = sm[:, 0, :]
                Bpp = sm[:, 1, :]
                tt("dve", App, pp[:, 0, :], pp[:, 2, :], ALU.mult, r=[t_pp], w=[t_blk])
                tt("dve", Bpp, pp[:, 1, :], pp[:, 2, :], ALU.mult, r=[t_pp], w=[t_blk])
                l1re = sm[:, 2, :]
                l1im = sm[:, 3, :]
                mg = sm[:, 4, :]
                act(mg, App, AF.Exp, r=[t_blk], w=[t_blk])
                tmp8 = [B_[0][:, 0:8], B_[0][:, 8:16], B_[0][:, 16:24], B_[0][:, 24:32]]
                cplx_pow(l1re, l1im, Bpp, mg, 8, False, tmp8)
                br = sm[:, 5, :]
                den = sm[:, 6, :]
                kre = sm[:, 7, :]
                kim = sm[:, 8, :]
                nkre = sm[:, 9, :]
                nkim = sm[:, 10, :]
                t8 = sm[:, 11, :]
                ts("dve", br, l1re, -1.0, None, ALU.add, r=[t_blk], w=[t_blk])
                tt("dve", den, pp[:, 0, :], pp[:, 0, :], ALU.mult, r=[t_pp], w=[t_blk])
                tt("dve", t8, pp[:, 1, :], pp[:, 1, :], ALU.mult, r=[t_pp], w=[t_blk])
                tt("dve", den, den, t8, ALU.add, r=[t_blk], w=[t_blk])
                rcp(den, den, r=[t_blk], w=[t_blk])
                tt("dve", kre, br, pp[:, 0, :], ALU.mult, r=[t_blk, t_pp], w=[t_blk])
                tt("dve", t8, l1im, pp[:, 1, :], ALU.mult, r=[t_blk, t_pp], w=[t_blk])
                tt("dve", kre, kre, t8, ALU.add, r=[t_blk], w=[t_blk])
                tt("dve", kre, kre, den, ALU.mult, r=[t_blk], w=[t_blk])
                tt("dve", kim, l1im, pp[:, 0, :], ALU.mult, r=[t_blk, t_pp], w=[t_blk])
                tt("dve", t8, br, pp[:, 1, :], ALU.mult, r=[t_blk, t_pp], w=[t_blk])
                tt("dve", kim, kim, t8, ALU.subtract, r=[t_blk], w=[t_blk])
                tt("dve", kim, kim, den, ALU.mult, r=[t_blk], w=[t_blk])
                ts("dve", nkre, kre, -1.0, None, ALU.mult, r=[t_blk], w=[t_blk])
                ts("dve", nkim, kim, -1.0, None, ALU.mult, r=[t_blk], w=[t_blk])
                Cre = A.alloc([8, 128], F32, at=woff + 1 * 1024)
                Cim = A.alloc([8, 128], F32, at=woff + 5 * 1024)
                for ri, Cdst in enumerate((Cre, Cim)):
                    dma(Cdst, dr["s5c"][l, d_, ri].rearrange("a s c -> s a c"), w=[t_blk])
                tC = B_[9][:, 0:128]
                for st in range(8):
                    ts("dve", tC, Cre[:, st, :], kre[:, st:st + 1], None, ALU.mult, r=[t_blk], w=[t_blk])
                    stt(Cc[:, 0, st, :], Cim[:, st, :], nkim[:, st:st + 1], tC, ALU.mult, ALU.add, r=[t_blk], w=[t_tab])
                    ts("dve", tC, Cre[:, st, :], nkim[:, st:st + 1], None, ALU.mult, r=[t_blk], w=[t_blk])
                    stt(Cc[:, 1, st, :], Cim[:, st, :], nkre[:, st:st + 1], tC, ALU.mult, ALU.add, r=[t_blk], w=[t_tab])
                for kt in range(2):
                    load_cast(Bb[:, kt], t_tab, dr["s5b"][l, d_, :, kt], "p (a c) -> p a c", c=512)
                er = erow[:, d_, :]
                for st in range(8):
                    ph = B_[9][:, 0:128]
                    mgb = B_[9][:, 128:256]
                    ts("dve", ph, er, Bpp[:, st:st + 1], None, ALU.mult, r=[t_c, t_blk], w=[t_blk])
                    act(mgb, er, AF.Exp, r=[t_c, t_blk], w=[t_blk], scale=App[:, st:st + 1])
                    tmpb = [B_[10][:, 0:128], B_[10][:, 128:256], B_[11][:, 0:128], B_[11][:, 128:256]]
                    cplx_pow(Fm[:, st, 0, :], Fm[:, st, 1, :], ph, mgb, 128, False, tmpb)
                row = A.alloc([3, 256], F32, at=woff + 1 * 1024)
                for cb in range(4):
                    dma(row, dr["s5row"][l, d_, :, :, cb * 256:(cb + 1) * 256], w=[t_blk])
                    act(row[:, 2, :], row[:, 2, :], AF.Exp, r=[t_blk], w=[t_blk])
                    Ab = B_[4]
                    Bk = B_[5]
                    tt("dve", Ab, row[:, 0, :], row[:, 2, :], ALU.mult, r=[t_blk], w=[t_blk])
                    tt("dve", Bk, row[:, 1, :], row[:, 2, :], ALU.mult, r=[t_blk], w=[t_blk])
                    ph = B_[6]
                    mgb = B_[7]
                    ts("dve", ph, Bk, ecol[:, d_:d_ + 1], None, ALU.mult, r=[t_blk, t_c], w=[t_blk])
                    act(mgb, Ab, AF.Exp, r=[t_blk, t_c], w=[t_blk], scale=ecol[:, 2 + d_:3 + d_])
                    tmpb = [B_[8], B_[9], B_[10], B_[11]]
                    cplx_pow(E[:, 0, cb * 256:(cb + 1) * 256], E[:, 1, cb * 256:(cb + 1) * 256], ph, mgb, 256, True, tmpb)
                if d_ == 0:
                    tap("s5E", E, [t_tab])
                    tap("s5F", Fm, [t_tab])
                    tap("s5C", Cc, [t_tab])
                P.barrier()
                order = list(range(NTT)) if d_ == 0 else [1, 0] + list(range(NTT - 1, 1, -1))
                lastcol = 127 if d_ == 0 else 0
                mset("dve", cc, 0.0, w=[t_cc])
                for idx, i in enumerate(order):
                    g = grp_of_tile(i)
                    t0 = i * 128
                    pb = idx % 2
                    for nb in range(4):
                        kt = nb % 2
                        ri = nb // 2
                        mm(ps[nb][:, :], uT[:, kt, t0:t0 + 128], Bb[:, kt, ri, :], r=[t_u[g], t_tab], w=[tps[nb]])
                    for qb in range(4):
                        hb = qb // 2
                        sl = slice(qb * 256, (qb + 1) * 256)
                        pl = slice((qb % 2) * 256, (qb % 2 + 1) * 256)
                        tt("dve", tW[:, 0, :], ps[hb][:, pl], E[:, 0, sl], ALU.mult, r=[tps[hb], t_tab], w=[t_tW])
                        tt("dve", tW[:, 1, :], ps[2 + hb][:, pl], E[:, 1, sl], ALU.mult, r=[tps[2 + hb], t_tab], w=[t_tW])
                        tt("pool", W[:, 0, sl], tW[:, 0, :], tW[:, 1, :], ALU.subtract, r=[t_tW], w=[t_W[0]])
                        tt("dve", tW[:, 0, :], ps[hb][:, pl], E[:, 1, sl], ALU.mult, r=[tps[hb], t_tab], w=[t_tW])
                        tt("dve", tW[:, 1, :], ps[2 + hb][:, pl], E[:, 0, sl], ALU.mult, r=[tps[2 + hb], t_tab], w=[t_tW])
                        tt("pool", W[:, 1, sl], tW[:, 0, :], tW[:, 1, :], ALU.add, r=[t_tW], w=[t_W[1]])
                    tr(ps[2][0:16, 0:128], cc.rearrange("p a b -> p (a b)"), identf, r=[t_cc, t_const], w=[tps[2]])
                    cp("act", cT[0:16, :], ps[2][0:16, 0:128], r=[tps[2]], w=[t_cT])
                    for ri in range(2):
                        for st in range(8):
                            bk = 4 + ri * 2 + st // 4
                            mm(ps[bk][:, (st % 4) * 128:(st % 4 + 1) * 128], W[:, ri, st * 128:(st + 1) * 128], Tri[:, d_, :],
                               start=(st % 4 == 0), stop=False, r=[t_W[ri], t_c], w=[tps[bk]])
                    for ri in range(2):
                        for st in range(8):
                            bk = 4 + ri * 2 + st // 4
                            jj = st * 2 + ri
                            mm(ps[bk][:, (st % 4) * 128:(st % 4 + 1) * 128], cT[0:16, :],
                               identf[0:16, jj:jj + 1].to_broadcast([16, 128]),
                               start=False, stop=True, r=[t_cT, t_const], w=[tps[bk]])
                    for hf in range(2):
                        Sre = ps[4 + hf][:, :].rearrange("p (a t) -> p a t", t=128)
                        Sim = ps[6 + hf][:, :].rearrange("p (a t) -> p a t", t=128)
                        Fre = Fm[:, 4 * hf:4 * hf + 4, 0, :]
                        Fim = Fm[:, 4 * hf:4 * hf + 4, 1, :]
                        q0 = tq[:, 0, :].rearrange("p (a t) -> p a t", t=128)
                        q1 = tq[:, 1, :].rearrange("p (a t) -> p a t", t=128)
                        tt("dve", q0, Sre, Fre, ALU.mult, r=[tps[4 + hf], t_tab], w=[t_tq])
                        tt("dve", q1, Sim, Fim, ALU.mult, r=[tps[6 + hf], t_tab], w=[t_tq])
                        tt("dve", xl[:, 4 * hf:4 * hf + 4, 0], q0[:, :, lastcol], q1[:, :, lastcol], ALU.subtract, r=[t_tq], w=[t_xl])
                        tt("pool", xx[:, 0, 4 * hf:4 * hf + 4, :], q0, q1, ALU.subtract, r=[t_tq], w=[t_xx[0]])
                        tt("dve", q0, Sre, Fim, ALU.mult, r=[tps[4 + hf], t_tab], w=[t_tq])
                        tt("dve", q1, Sim, Fre, ALU.mult, r=[tps[6 + hf], t_tab], w=[t_tq])
                        tt("dve", xl[:, 4 * hf:4 * hf + 4, 1], q0[:, :, lastcol], q1[:, :, lastcol], ALU.add, r=[t_tq], w=[t_xl])
                        tt("pool", xx[:, 1, 4 * hf:4 * hf + 4, :], q0, q1, ALU.add, r=[t_tq], w=[t_xx[1]])
                    ta_ = sm[:, 5, :]
                    tb_ = sm[:, 6, :]
                    tt("dve", ta_, l1re, xl[:, :, 0], ALU.mult, r=[t_xl, t_blk], w=[t_blk])
                    tt("dve", tb_, l1im, xl[:, :, 1], ALU.mult, r=[t_xl, t_blk], w=[t_blk])
                    tt("dve", cc[:, :, 0], ta_, tb_, ALU.subtract, r=[t_blk], w=[t_cc])
                    tt("dve", ta_, l1re, xl[:, :, 1], ALU.mult, r=[t_xl, t_blk], w=[t_blk])
                    tt("dve", tb_, l1im, xl[:, :, 0], ALU.mult, r=[t_xl, t_blk], w=[t_blk])
                    tt("dve", cc[:, :, 1], ta_, tb_, ALU.add, r=[t_blk], w=[t_cc])
                    if last and i < 2:
                        continue
                    for j in range(2):
                        n_mm = 0
                        for st in range(4 * j, 4 * j + 4):
                            for ri in range(2):
                                mm(ps[j][:, 0:128], Cc[:, ri, st, :], xx[:, ri, st, :], start=(n_mm == 0), stop=(n_mm == 7),
                                   r=[t_tab, t_xx[ri]], w=[tps[j]])
                                n_mm += 1
                        if d_ == 0:
                            cp("act", yf[:, j, t0:t0 + 128], ps[j][:, 0:128], r=[tps[j]], w=[t_yf[i]])
                        else:
                            y = ysc[:, 0, :]
                            tt("dve", y, ps[j][:, 0:128], yf[:, j, t0:t0 + 128], ALU.add, r=[tps[j], t_yf[i]], w=[t_ysc])
                            stt(y, uT[:, j, t0:t0 + 128], dvec[:, j:j + 1], y, ALU.mult, ALU.add, r=[t_u[g], t_c, t_ysc], w=[t_ysc])
                            tt("dve", ysc[:, 1, :], y, y, ALU.mult, r=[t_ysc], w=[t_ysc])
                            ts("dve", ysc[:, 1, :], ysc[:, 1, :], 0.044715, 1.0, ALU.mult, ALU.add, r=[t_ysc], w=[t_ysc])
                            tt("dve", ysc[:, 1, :], ysc[:, 1, :], y, ALU.mult, r=[t_ysc], w=[t_ysc])
                            act(ysc[:, 2, :], ysc[:, 1, :], AF.Sigmoid, r=[t_ysc], w=[t_ysc], scale=1.5957691216057308)
                            tt("dve", yf[:, j, t0:t0 + 128], y, ysc[:, 2, :], ALU.mult, r=[t_ysc], w=[t_yf[i]])
            tap("s5g", yf, t_yf)
            P.barrier()
            glw = A.alloc([2, 512], BF16, at=woff)
            t_glw = T()
            load_cast(glw[:, 0], t_glw, dr["s5_w_glu"][l][0:128, :])
            load_cast(glw[:, 1], t_glw, dr["s5_w_glu"][l][128:256, :])
            sgt = A.alloc([512], F32, at=woff + 2048)
            t_sgt = T()
            bi = 0
            for g in range(1 if last else 0, 5):
                t0, n = GRP(g)
                tiles = list(range(t0 // 128, (t0 + n) // 128))
                for j in range(2):
                    pv = bi % 2
                    pg = 2 + bi % 2
                    bi += 1
                    for kt in range(2):
                        mm(ps[pv][:, 0:n], glw[:, kt, j * 128:(j + 1) * 128], yf[:, kt, t0:t0 + n], start=(kt == 0), stop=(kt == 1),
                           r=[t_glw] + [t_yf[i] for i in tiles], w=[tps[pv]])
                    for kt in range(2):
                        mm(ps[pg][:, 0:n], glw[:, kt, 256 + j * 128:256 + (j + 1) * 128], yf[:, kt, t0:t0 + n],
                           start=(kt == 0), stop=(kt == 1), r=[t_glw] + [t_yf[i] for i in tiles], w=[tps[pg]])
                    act(sgt[:, 0:n], ps[pg][:, 0:n], AF.Sigmoid, r=[tps[pg]], w=[t_sgt])
                    tt("dve", oT[slot][:, j, t0:t0 + n], ps[pv][:, 0:n], sgt[:, 0:n], ALU.mult, r=[tps[pv], t_sgt],
                       w=[t_oT[slot][g]])
            tap("ocT", oT[slot], t_oT[slot])

        SLOT_OF = {0: 3, 1: 0, 2: 1, 3: 2}

        def merge_phase(l, last):
            P.barrier()
            al = mk_alloc(OT0)
            mT = al([8, NT], BF16)
            sig = al([512], F32)
            acc = al([512], F32)
            t_m = [T() for _ in range(5)]
            t_sig = T()
            t_acc = T()
            wsrc = dr["w_in"][l].rearrange("(k p) c -> p k c", p=128)
            wbsrc = dr["w_branch"][l].rearrange("n (j p) d -> p n j d", p=128)
            groups = [g for g in range(5) if not (last and g == 0)]
            bi = 0
            for d in range(8):
                gw = []
                for n in range(4):
                    c0 = 1952 + n * 1024 + d * 128
                    gw.append(load_ring(wsrc[:, :, c0:c0 + 128], "p (k c) -> p k c", c=128))
                wb, t_wb = load_ring(wbsrc[:, :, :, d * 128:(d + 1) * 128], "p (n j c) -> p n j c", j=2, c=128)
                for g in groups:
                    t0, n_ = GRP(g)
                    for n in range(4):
                        sl = SLOT_OF[n]
                        pa = bi % 2
                        pb = 2 + bi % 2
                        bi += 1
                        wv, t_wv = gw[n]
                        for k in range(8):
                            mm(ps[pa][:, 0:n_], wv[:, k, :], hT[:, k, t0:t0 + n_], start=(k == 0), stop=(k == 7),
                               r=[t_wv, t_h[g]], w=[tps[pa]])
                        for j in range(2):
                            mm(ps[pb][:, 0:n_], wb[:, n, j, :], oT[sl][:, j, t0:t0 + n_], start=(j == 0), stop=(j == 1),
                               r=[t_wb, t_oT[sl][g]], w=[tps[pb]])
                        act(sig[:, 0:n_], ps[pa][:, 0:n_], AF.Sigmoid, r=[tps[pa]], w=[t_sig])
                        if n == 0:
                            tt("dve", acc[:, 0:n_], ps[pb][:, 0:n_], sig[:, 0:n_], ALU.mult, r=[tps[pb], t_sig], w=[t_acc])
                        else:
                            tt("dve", ps[pb][:, 0:n_], ps[pb][:, 0:n_], sig[:, 0:n_], ALU.mult, r=[tps[pb], t_sig], w=[tps[pb]])
                            if n < 3:
                                tt("dve", acc[:, 0:n_], acc[:, 0:n_], ps[pb][:, 0:n_], ALU.add, r=[tps[pb], t_acc], w=[t_acc])
                            else:
                                tt("dve", mT[:, d, t0:t0 + n_], acc[:, 0:n_], ps[pb][:, 0:n_], ALU.add,
                                   r=[tps[pb], t_acc], w=[t_m[g]])
            tap("mT", mT, t_m)
            wosrc = dr["w_out"][l].rearrange("(k p) c -> p k c", p=128)
            for d in range(8):
                wv, t_wv = load_ring(wosrc[:, :, d * 128:(d + 1) * 128], "p (k c) -> p k c", c=128)
                for g in groups:
                    t0, n_ = GRP(g)
                    s_ = 1 if g == 0 else 0
                    pb = 4 + bi % 4
                    bi += 1
                    for k in range(8):
                        mm(ps[pb][:, 0:n_], wv[:, k, :], mT[:, k, t0:t0 + n_], start=(k == 0), stop=(k == 7),
                           r=[t_wv, t_m[g]], w=[tps[pb]])
                    stt(xT[:, d, t0:t0 + n_], ps[pb][:, 0:n_], mod[:, l, 16 + d, s_:s_ + 1], xT[:, d, t0:t0 + n_],
                        ALU.mult, ALU.add, r=[tps[pb], t_mod, t_x[d][g]], w=[t_x[d][g]])

        def ffn_phase(l, last):
            groups = [g for g in range(5) if not (last and g == 0)]
            norm_phase(l, 1, groups)
            P.barrier()
            al = mk_alloc(A.nbytes)
            aT = al([8, NT], BF16)
            rl = al([2, 512], F32)
            t_a = [[T() for _ in range(5)] for _ in range(8)]
            t_rl = [T(), T()]
            w1src = dr["ffn_w1"][l].rearrange("(k p) c -> p k c", p=128)
            w2src = dr["ffn_w2"][l].rearrange("(f p) c -> p f c", p=128)
            bi = 0
            for fb in range(4):
                for f in range(8):
                    F_ = fb * 8 + f
                    wv, t_wv = load_ring(w1src[:, :, F_ * 128:(F_ + 1) * 128], "p (k c) -> p k c", c=128)
                    for g in groups:
                        t0, n_ = GRP(g)
                        pb = bi % 4
                        rb = bi % 2
                        bi += 1
                        for k in range(8):
                            mm(ps[pb][:, 0:n_], wv[:, k, :], hT[:, k, t0:t0 + n_], start=(k == 0), stop=(k == 7),
                               r=[t_wv, t_h[g]], w=[tps[pb]])
                        act(rl[:, rb, 0:n_], ps[pb][:, 0:n_], AF.Relu, r=[tps[pb]], w=[t_rl[rb]])
                        tt("pool", aT[:, f, t0:t0 + n_], rl[:, rb, 0:n_], rl[:, rb, 0:n_], ALU.mult, r=[t_rl[rb]], w=[t_a[f][g]])
                for d in range(8):
                    wv, t_wv = load_ring(w2src[:, fb * 8:(fb + 1) * 8, d * 128:(d + 1) * 128], "p (f c) -> p f c", c=128)
                    for g in groups:
                        t0, n_ = GRP(g)
                        s_ = 1 if g == 0 else 0
                        pb = 4 + bi % 4
                        bi += 1
                        for f in range(8):
                            mm(ps[pb][:, 0:n_], wv[:, f, :], aT[:, f, t0:t0 + n_], start=(f == 0), stop=(f == 7),
                               r=[t_wv, t_a[f][g]], w=[tps[pb]])
                        stt(xT[:, d, t0:t0 + n_], ps[pb][:, 0:n_], mod[:, l, 40 + d, s_:s_ + 1], xT[:, d, t0:t0 + n_],
                            ALU.mult, ALU.add, r=[tps[pb], t_mod, t_x[d][g]], w=[t_x[d][g]])

        for l in range(DEPTH):
            last = (l == DEPTH - 1)
            norm_phase(l, 0, list(range(5)))
            if l == 0:
                tap("hT", hT, t_h)
            if stage <= 0.1:
                break
            if "mla" in mixers:
                mla_phase(l, last, 3)
            if "ret" in mixers:
                ret_phase(l, last, 2)
            if "s5" in mixers:
                s5_phase(l, last, 1)
            if "fnet" in mixers:
                fnet_phase(l, last, 0)
            if stage <= 1:
                break
            merge_phase(l, last)
            ffn_phase(l, last)
            if l == 0:
                tap("x1", xT, [t for k in range(8) for t in t_x[k]])
            if stage <= 2:
                break

        osrc = outT.rearrange("(k p) t -> p k t", p=128)
        for k in range(8):
            out_handles.append(dma(osrc[:, k, :], xT[:, k, LAT0:NT], r=t_x[k]))
        P.wait_all("sp", out_handles)
        P.emit()
    return nc


_CACHE = {}


def _specs_of(d):
    sp = {}
    for k, v in d.items():
        sp[k] = (v.shape, BF16 if v.dtype == NPBF else F32)
    return sp


def run(inputs, stage=99, taps=(), ncores=8, mixers=("mla", "s5", "ret", "fnet")):
    com = prep_common(inputs)
    cores = [prep_core(inputs, b) for b in range(ncores)]
    in_maps = [dict(com, **c) for c in cores]
    nc = build(_specs_of(in_maps[0]), stage=stage, taps=taps, mixers=mixers)
    res = run_bass_kernel_spmd(nc, in_maps, core_ids=list(range(ncores)))
    return res.results


def kernel(**inputs):
    inputs = {k: np.asarray(v) for k, v in inputs.items()}
    res = run(inputs)
    out = np.stack([r["outT"].T for r in res], axis=0)
    return np.ascontiguousarray(out.astype(np.float32))
```

```python
import contextlib
import math
import os
import numpy as np
import ml_dtypes
import concourse.bass as bass
import concourse.mybir as mybir
from concourse.bass_utils import run_bass_kernel_spmd

F32 = mybir.dt.float32
BF16 = mybir.dt.bfloat16
ALU = mybir.AluOpType
AF = mybir.ActivationFunctionType
AX = mybir.AxisListType
NPBF = ml_dtypes.bfloat16

ENGS = ("pe", "act", "dve", "pool", "sp")
NOSELF = tuple(os.environ.get("NOSELF", "pe").split(","))
RELAX = os.environ.get("RELAX", "1") == "1"
NDSLOT = 8

D = 1024
NT = 2304
NTT = 18
LAT0 = 256
EPS = 1e-6
DEPTH = 2


class T:
    __slots__ = ("name", "w", "rs", "excl", "tw")

    def __init__(self, name="", excl=False):
        self.name = name
        self.w = None
        self.tw = None
        self.rs = []
        self.excl = excl


class Prog:
    def __init__(self, nc, stack, same_sync=True):
        self.nc = nc
        self.same_sync = same_sync
        self.q = {e: [] for e in ENGS}
        self.cnt = {e: 0 for e in ENGS}
        self.sems = {}
        for e in ENGS:
            self.sems[("c", e)] = stack.enter_context(nc.semaphore("c_" + e))
        self.dq = ("sp", "pool", "act")
        self.dcnt = {}
        self.dn = {e: 0 for e in self.dq}
        for e in self.dq:
            for s in range(NDSLOT):
                self.sems[("d", e, s)] = stack.enter_context(nc.semaphore("d_%s%d" % (e, s)))
                self.dcnt[(e, s)] = 0
        self.known = {e: {} for e in ENGS}
        self.kstop = None
        self.kcount = 0
        import threading
        self._tls = threading.local()

    def _deps(self, eng, r, w):
        deps = {}

        def add(h):
            if h is None:
                return
            k, v = h
            if k == ("c", eng) and (eng in NOSELF or not self.same_sync):
                return
            if deps.get(k, 0) < v:
                deps[k] = v
        for t in r:
            add(t.w)
        for t in w:
            add(t.w)
            for h in t.rs:
                add(h)
        out = []
        kn = self.known[eng]
        for k, v in deps.items():
            if kn.get(k, 0) >= v:
                continue
            kn[k] = v
            out.append((k, v))
        return out

    def _mark(self, h, r, w):
        for t in w:
            t.tw = h
        for t in r:
            t.rs.append(h)
            if len(t.rs) > 64:
                best = {}
                for k, v in t.rs:
                    if best.get(k, 0) < v:
                        best[k] = v
                t.rs = list(best.items())
        for t in w:
            t.w = h
            t.rs = []

    def interleave(self, thunks):
        import threading
        n = len(thunks)
        if n == 1 or os.environ.get("NOIL") == "1":
            for t in thunks:
                t()
            return
        cv = threading.Condition()
        st = {"turn": 0, "alive": [True] * n, "err": None}

        def nxt(i):
            for d in range(1, n + 1):
                j = (i + d) % n
                if st["alive"][j]:
                    return j
            return None

        def yp(i):
            with cv:
                j = nxt(i)
                if j is None or j == i:
                    return
                st["turn"] = j
                cv.notify_all()
                cv.wait_for(lambda: st["turn"] == i)

        def runner(i):
            try:
                with cv:
                    cv.wait_for(lambda: st["turn"] == i)
                self._tls.yp = (lambda: yp(i))
                thunks[i]()
            except BaseException as e:
                st["err"] = e
            finally:
                self._tls.yp = None
                with cv:
                    st["alive"][i] = False
                    j = nxt(i)
                    st["turn"] = j if j is not None else -1
                    cv.notify_all()

        ths = [threading.Thread(target=runner, args=(i,)) for i in range(n)]
        for t in ths:
            t.start()
        for t in ths:
            t.join()
        if st["err"] is not None:
            raise st["err"]

    def _yield(self):
        yp = getattr(self._tls, "yp", None)
        if yp is not None:
            yp()

    def op(self, eng, fn, r=(), w=()):
        self._yield()
        if self.kstop is not None:
            self.kcount += 1
            if self.kcount > self.kstop:
                return None
        if RELAX:
            return self._op_relaxed(eng, fn, r, w)
        if eng != "pe":
            ex = [t for t in r if t.excl]
            if ex:
                r = [t for t in r if not t.excl]
                w = list(w) + ex
        waits = self._deps(eng, r, w)
        self.cnt[eng] += 1
        h = (("c", eng), self.cnt[eng])
        self.q[eng].append((fn, waits, (h[0], 1)))
        self._mark(h, r, w)
        return h

    def _op_relaxed(self, eng, fn, r, w):
        deps = {}
        me = ("c", eng)

        def add(h, same_ok):
            if h is None:
                return
            k, v = h
            if k == me and (eng in NOSELF or not same_ok):
                return
            if deps.get(k, 0) < v:
                deps[k] = v
        wset = set(id(t) for t in w)
        for t in r:
            if id(t) in wset:
                continue
            add(t.tw, True)
            if t.excl and eng != "pe":
                add(t.w, False)
        for t in w:
            add(t.tw, False)
            add(t.w, False)
            for h in t.rs:
                add(h, False)
        for t in r:
            if id(t) in wset:
                add(t.tw, True)
        waits = []
        kn = self.known[eng]
        for k, v in deps.items():
            if kn.get(k, 0) >= v:
                continue
            kn[k] = v
            waits.append((k, v))
        self.cnt[eng] += 1
        h = (me, self.cnt[eng])
        self.q[eng].append((fn, waits, (me, 1)))
        for t in r:
            if id(t) in wset:
                continue
            if t.excl and eng != "pe":
                t.w = h
                t.rs = []
            else:
                t.rs.append(h)
                if len(t.rs) > 64:
                    best = {}
                    for k, v in t.rs:
                        if best.get(k, 0) < v:
                            best[k] = v
                    t.rs = list(best.items())
        for t in w:
            t.w = h
            t.tw = h
            t.rs = []
        return h

    def dma(self, eng, fn, r=(), w=()):
        self._yield()
        s = self.dn[eng] % NDSLOT
        self.dn[eng] += 1
        waits = self._deps(eng, r, w)
        k = ("d", eng, s)
        prev = self.dcnt[(eng, s)]
        if prev > 0 and self.known[eng].get(k, 0) < prev:
            self.known[eng][k] = prev
            waits.append((k, prev))
        self.dcnt[(eng, s)] = prev + 16
        h = (k, prev + 16)
        self.q[eng].append((fn, waits, (k, 16)))
        self._mark(h, r, w)
        return h

    def barrier(self):
        hs = [(("c", e), self.cnt[e]) for e in ENGS if self.cnt[e] > 0]
        hs += [(("d", e, s), v) for (e, s), v in self.dcnt.items() if v > 0]
        for e in ENGS:
            self.wait_all(e, [h for h in hs if h[0] != ("c", e)])

    def wait_all(self, eng, hs):
        waits = []
        for k, v in hs:
            if self.known[eng].get(k, 0) < v:
                self.known[eng][k] = v
                waits.append((k, v))
        self.q[eng].append((None, waits, None))

    def emit(self):
        nc = self.nc
        sems = self.sems
        q = self.q

        def run(e, engobj):
            for fn, waits, inc in q[e]:
                for k, v in waits:
                    engobj.wait_ge(sems[k], v)
                if fn is None:
                    continue
                ins = fn(engobj)
                ins.then_inc(sems[inc[0]], inc[1])

        with nc.Block() as block:
            @block.tensor
            def _(eng):
                run("pe", eng)

            @block.scalar
            def _(eng):
                run("act", eng)

            @block.vector
            def _(eng):
                run("dve", eng)

            @block.gpsimd
            def _(eng):
                run("pool", eng)

            @block.sync
            def _(eng):
                run("sp", eng)


class Arena:
    def __init__(self, nc, stack, nbytes):
        self.t = stack.enter_context(nc.sbuf_tensor("arena", [128, nbytes // 4], F32))
        self.nbytes = nbytes
        self.off = 0

    def alloc(self, shape, dtype, at=None):
        esz = 2 if dtype == BF16 else 4
        n = int(np.prod(shape)) * esz
        n4 = (n + 3) // 4
        if at is None:
            at = self.off
            self.off += n4 * 4
        assert at % 4 == 0 and at + n4 * 4 <= self.nbytes, (at, n, self.nbytes)
        ap = self.t[:, at // 4: at // 4 + n4]
        if dtype != F32:
            ap = ap.bitcast(dtype)
        if len(shape) == 2:
            ap = ap.rearrange("p (a b) -> p a b", b=shape[1])
        elif len(shape) == 3:
            ap = ap.rearrange("p (a b c) -> p a b c", b=shape[1], c=shape[2])
        elif len(shape) == 4:
            ap = ap.rearrange("p (a b c d) -> p a b c d", b=shape[1], c=shape[2], d=shape[3])
        return ap


def _rope_tables():
    half = 8
    freqs = (10000.0 ** (-np.arange(half, dtype=np.float32) / half)).astype(np.float32)
    t = np.arange(2048)
    rows = (t // 64).astype(np.float32)
    cols = (t % 64).astype(np.float32)
    ang = np.concatenate([rows[:, None] * freqs[None], cols[:, None] * freqs[None]], axis=1)
    cos = np.ones((NT, 16), np.float32)
    sin = np.zeros((NT, 16), np.float32)
    cos[LAT0:] = np.cos(ang)
    sin[LAT0:] = np.sin(ang)
    cos = cos.reshape(NTT, 128, 16).transpose(1, 0, 2)
    sin = sin.reshape(NTT, 128, 16).transpose(1, 0, 2)
    return np.ascontiguousarray(cos), np.ascontiguousarray(sin)


def _fnet_consts():
    ci = np.arange(64)
    c64 = np.cos(2 * np.pi * np.outer(ci, ci) / 64.0)
    s64 = np.sin(2 * np.pi * np.outer(ci, ci) / 64.0)
    cs = np.zeros((2, 128, 512), np.float64)
    for j in range(2):
        for gl in range(2):
            g = 2 * j + gl
            cs[j, gl * 64:(gl + 1) * 64, g * 64:(g + 1) * 64] = c64
            cs[j, gl * 64:(gl + 1) * 64, 256 + g * 64:256 + (g + 1) * 64] = s64
    out = {"cs64": np.ascontiguousarray(cs.transpose(1, 0, 2)).astype(np.float32).astype(NPBF)}
    for L in (2048, 256):
        li = np.arange(L)
        m = np.outer(li, li) % L
        ang = 2 * np.pi * m / L
        sc = 1.0 / math.sqrt(L * 64.0)
        tab = np.stack([np.cos(ang) * sc, -np.sin(ang) * sc], axis=0)
        out["dft%d" % L] = tab.astype(np.float32).astype(NPBF)
    return out


def _s5_layouts(inp):
    f = np.float32
    out = {}
    m = np.arange(128)[:, None]
    t = np.arange(128)[None, :]
    tri = np.stack([(m <= t), (m >= t)], axis=1).astype(f)
    out["s5tri"] = tri.astype(NPBF)
    erow = np.stack([np.broadcast_to(t.astype(f), (128, 128)), np.broadcast_to(127.0 - t.astype(f), (128, 128))], axis=1)
    out["s5erow"] = np.ascontiguousarray(erow, f)
    p = np.arange(128, dtype=f)[:, None]
    out["s5ecol"] = np.ascontiguousarray(np.concatenate([p, 127.0 - p, -p, -(127.0 - p)], axis=1), f)
    out["s5d"] = np.ascontiguousarray(np.asarray(inp["s5_d"], f).reshape(DEPTH, 2, 128).transpose(0, 2, 1))
    re = np.asarray(inp["s5_lam_re"], f).reshape(DEPTH, 2, 1024)
    im = np.asarray(inp["s5_lam_im"], f).reshape(DEPTH, 2, 1024)
    ls = np.repeat(np.asarray(inp["s5_log_step"], f), 64, axis=-1)
    trip = np.stack([re, im, ls], axis=2)
    out["s5pp"] = np.ascontiguousarray(trip.reshape(DEPTH, 2, 3, 8, 128).transpose(0, 1, 4, 2, 3))
    out["s5row"] = np.ascontiguousarray(np.broadcast_to(trip[:, :, None], (DEPTH, 2, 128, 3, 1024)), f)
    bre = np.asarray(inp["s5_b_re"], f)
    bim = np.asarray(inp["s5_b_im"], f)
    sb = np.zeros((DEPTH, 2, 128, 2, 2, 512), f)
    for ri, bb in enumerate((bre, bim)):
        for kt in range(2):
            for gl in range(8):
                g = 8 * kt + gl
                sb[:, :, gl * 16:(gl + 1) * 16, kt, ri, gl * 64:(gl + 1) * 64] = bb[:, :, g].transpose(0, 1, 3, 2)
    out["s5b"] = sb
    cre = np.asarray(inp["s5_c_re"], f)
    cim = np.asarray(inp["s5_c_im"], f)
    sc = np.zeros((DEPTH, 2, 2, 8, 128, 128), f)
    for ri, cm in enumerate((cre, cim)):
        for st in range(8):
            for gl in range(2):
                g = 2 * st + gl
                col = (g % 8) * 16
                sc[:, :, ri, st, gl * 64:(gl + 1) * 64, col:col + 16] = cm[:, :, g].transpose(0, 1, 3, 2)
    out["s5c"] = sc
    out["s5_w_glu"] = np.ascontiguousarray(inp["s5_w_glu"], f)
    id2 = np.zeros((128, 16), f)
    for j_ in range(16):
        id2[j_, j_] = 1.0
        id2[32 + j_, j_] = 1.0
    out["s5id2"] = id2.astype(NPBF)
    return out


def _ret_consts():
    half = 32
    freqs = (10000.0 ** (-np.arange(half, dtype=np.float32) / half)).astype(np.float32)
    pos = np.arange(2048, dtype=np.float32)
    ang = pos[:, None] * freqs[None]
    cos = np.ones((NT, 32), np.float32)
    sin = np.zeros((NT, 32), np.float32)
    cos[LAT0:] = np.cos(ang)
    sin[LAT0:] = np.sin(ang)
    tm = lambda a: np.ascontiguousarray(a.reshape(NTT, 128, 32).transpose(1, 0, 2))
    out = {"rrc": tm(cos), "rrs": tm(sin)}
    k = np.arange(128, dtype=np.float32)[:, None]
    q = np.arange(128, dtype=np.float32)[None, :]
    retc = np.stack([np.broadcast_to(q + 1.0, (128, 128)), np.broadcast_to(128.0 - q, (128, 128)),
                     np.maximum(q - k, 0.0), np.maximum(k - q, 0.0),
                     (q >= k).astype(np.float32), (k >= q).astype(np.float32)], axis=1)
    out["retc"] = np.ascontiguousarray(retc, np.float32)
    out["retp"] = np.ascontiguousarray(np.concatenate([k, 127.0 - k], axis=1), np.float32)
    return out


def prep_common(inp):
    f = np.float32
    c = {}
    c["ada_w"] = np.ascontiguousarray(inp["ada_w"], f)
    c["ada_bT"] = np.ascontiguousarray(inp["ada_b"].reshape(DEPTH, 48, 128).transpose(0, 2, 1), f)
    nw = np.concatenate([inp["norm_mix_w"].reshape(DEPTH, 8, 128), inp["norm_ffn_w"].reshape(DEPTH, 8, 128)], axis=1)
    c["nw"] = np.ascontiguousarray(nw.transpose(0, 2, 1), f)
    c["w_in"] = np.ascontiguousarray(inp["w_in"], f)
    bc = lambda v: np.ascontiguousarray(np.broadcast_to(v[:, None, :], (DEPTH, 128, v.shape[-1])), f)
    c["kvw"] = bc(inp["mla_kv_norm"])
    c["qnw"] = bc(inp["mla_q_norm"])
    c["qkq"] = bc(np.tile(inp["mla_qk_norm_q"], (1, 4)))
    c["qkk"] = bc(np.tile(inp["mla_qk_norm_k"], (1, 4)))
    c["w_ukv"] = np.ascontiguousarray(inp["mla_w_ukv"], f)
    c["w_uq"] = np.ascontiguousarray(inp["mla_w_uq"], f)
    cos, sin = _rope_tables()
    c["ropec"] = cos
    c["ropes"] = sin
    c["w_branch"] = np.ascontiguousarray(inp["w_branch"], f)
    c["w_out"] = np.ascontiguousarray(inp["w_out"], f)
    c["ffn_w1"] = np.ascontiguousarray(inp["ffn_w1"], f)
    c["ffn_w2"] = np.ascontiguousarray(inp["ffn_w2"], f)
    c.update(_fnet_consts())
    c.update(_ret_consts())
    c.update(_s5_layouts(inp))
    lg = np.asarray(inp["ret_decay_logit"], f)
    c["rlog_row"] = np.ascontiguousarray(np.broadcast_to(lg.reshape(DEPTH, 1, 8), (DEPTH, 128, 8)), f)
    pp = np.zeros((DEPTH, 128, 2, 2), f)
    for pair in range(2):
        for d_ in range(2):
            pp[:, 0:64, pair, d_] = lg[:, d_, 2 * pair][:, None]
            pp[:, 64:128, pair, d_] = lg[:, d_, 2 * pair + 1][:, None]
    c["rlog_pp"] = pp
    c["gnw"] = bc(inp["ret_gn_w"])
    c["identb"] = np.eye(128, dtype=f).astype(NPBF)
    c["identf"] = np.eye(128, dtype=f)
    return c


def prep_core(inp, b):
    f = np.float32
    d = {}
    xt = np.concatenate([inp["ctx"][b], inp["x"][b]], axis=0).T
    d["xT"] = np.ascontiguousarray(xt, f)
    d["cT"] = np.ascontiguousarray(np.stack([inp["c"][b], inp["c_ctx"]], axis=1), f)
    return d


def build(specs, stage=99, taps=(), mixers=("mla", "s5", "ret", "fnet")):
    nc = bass.Bass("TRN2", target_bir_lowering=False)
    dr = {}
    for name, (shape, dt) in specs.items():
        dr[name] = nc.dram_tensor(name, list(shape), dt, kind="ExternalInput").ap()
    outT = nc.dram_tensor("outT", [D, 2048], F32, kind="ExternalOutput").ap()
    tapd = {}
    for name, shape, dt in taps:
        tapd[name] = nc.dram_tensor("tap_" + name, list(shape), dt, kind="ExternalOutput").ap()

    with contextlib.ExitStack() as st:
        P = Prog(nc, st)
        A = Arena(nc, st, 206 * 1024)
        ps = [st.enter_context(nc.psum_tensor("ps%d" % i, [128, 512], F32)) for i in range(8)]
        tps = [T("ps%d" % i, excl=True) for i in range(8)]
        out_handles = []

        def mm(out, lhsT, rhs, start=True, stop=True, r=(), w=()):
            return P.op("pe", lambda e: e.matmul(out, lhsT=lhsT, rhs=rhs, start=start, stop=stop,
                                                 skip_group_check=True), r, w)

        def tr(out, in_, ident, r=(), w=()):
            return P.op("pe", lambda e: e.transpose(out=out, in_=in_, identity=ident), r, w)

        def act(out, in_, func, r=(), w=(), scale=1.0, bias=0.0, accum=None):
            if accum is None:
                return P.op("act", lambda e: e.activation(out=out, in_=in_, func=func, scale=scale, bias=bias), r, w)
            return P.op("act", lambda e: e.activation(out=out, in_=in_, func=func, scale=scale, bias=bias,
                                                      accum_out=accum), r, w)

        def tt(eng, out, a, b, op, r=(), w=()):
            return P.op(eng, lambda e: e.tensor_tensor(out=out, in0=a, in1=b, op=op), r, w)

        def ts(eng, out, a, s1, s2, op0, op1=None, r=(), w=()):
            if op1 is None:
                return P.op(eng, lambda e: e.tensor_scalar(out=out, in0=a, scalar1=s1, scalar2=None, op0=op0), r, w)
            return P.op(eng, lambda e: e.tensor_scalar(out=out, in0=a, scalar1=s1, scalar2=s2, op0=op0, op1=op1), r, w)

        def stt(out, a, s, b, op0, op1, r=(), w=()):
            return P.op("dve", lambda e: e.scalar_tensor_tensor(out=out, in0=a, scalar=s, in1=b, op0=op0, op1=op1), r, w)

        def cp(eng, out, in_, r=(), w=()):
            if eng == "act":
                return P.op(eng, lambda e: e.activation(out=out, in_=in_, func=AF.Copy), r, w)
            return P.op(eng, lambda e: e.tensor_copy(out=out, in_=in_), r, w)

        def red(out, in_, r=(), w=()):
            return P.op("dve", lambda e: e.tensor_reduce(out=out, in_=in_, axis=AX.X, op=ALU.add), r, w)

        def rcp(out, in_, r=(), w=()):
            return P.op("dve", lambda e: e.reciprocal(out=out, in_=in_), r, w)

        def mset(eng, out, val, w=()):
            return P.op(eng, lambda e: e.memset(out, val), (), w)

        def dma(out, in_, r=(), w=(), q="sp"):
            return P.dma(q, lambda e: e.dma_start(out=out, in_=in_), r, w)

        def tap(name, src, r):
            if name in tapd:
                out_handles.append(dma(tapd[name], src, r=r))

        xT = A.alloc([8, NT], F32)
        hT = A.alloc([8, NT], BF16)
        t_x = [[T("x%d_%d" % (k, g)) for g in range(5)] for k in range(8)]
        t_h = [T("h%d" % g) for g in range(5)]
        identb = A.alloc([128], BF16)
        identf = A.alloc([128], F32)
        onesb = A.alloc([128], BF16)
        mod = A.alloc([DEPTH, 48, 2], F32)
        a1 = A.alloc([DEPTH, 16, 2], F32)
        nwt = A.alloc([DEPTH, 16], F32)
        scT = A.alloc([8, 2], F32)
        eps_t = A.alloc([1], F32)
        t_const = T("const")
        t_mod = T("mod")
        NSTG = 2
        NRING = 5
        stg = [A.alloc([1024], F32) for _ in range(NSTG)]
        t_stg = [T("stg%d" % i) for i in range(NSTG)]
        ring = [A.alloc([1024], BF16) for _ in range(NRING)]
        t_ring = [T("ring%d" % i) for i in range(NRING)]
        sidx = [0]
        ridx = [0]
        DYN0 = A.off
        OT0 = A.nbytes - 4 * 2 * NT * 2
        oT = [A.alloc([2, NT], BF16, at=OT0 + i * 2 * NT * 2) for i in range(4)]
        t_oT = [[T("o%d_%d" % (i, g)) for g in range(5)] for i in range(4)]

        def GRP(g):
            return (0, 256) if g == 0 else (LAT0 + 512 * (g - 1), 512)

        def grp_of_tile(tt_):
            return 0 if tt_ < 2 else 1 + (tt_ - 2) // 4

        def next_stg():
            s = sidx[0] % NSTG
            sidx[0] += 1
            return s

        def load_cast(dst, t_dst, src, shape_str=None, **kw):
            s = next_stg()
            n = int(np.prod(src.shape[1:]))
            assert n <= 1024, n
            sv = stg[s][:, 0:n]
            if shape_str is not None:
                sv = sv.rearrange(shape_str, **kw)
            dma(sv, src, w=[t_stg[s]])
            cp("pool", dst, sv, r=[t_stg[s]], w=[t_dst])

        def load_ring(src, shape_str=None, **kw):
            i = ridx[0] % NRING
            ridx[0] += 1
            n = int(np.prod(src.shape[1:]))
            dv = ring[i][:, 0:n]
            if shape_str is not None:
                dv = dv.rearrange(shape_str, **kw)
            load_cast(dv, t_ring[i], src, shape_str, **kw)
            return dv, t_ring[i]

        xsrc = dr["xT"].rearrange("(k p) t -> p k t", p=128)
        for k in range(8):
            dma(xT[:, k, :], xsrc[:, k, :], w=t_x[k])
        dma(identb, dr["identb"], w=[t_const])
        dma(identf, dr["identf"], w=[t_const])
        dma(scT, dr["cT"].rearrange("(k p) j -> p k j", p=128), w=[t_const])
        dma(nwt, dr["nw"].rearrange("l p k -> p l k"), w=[t_const])
        mset("dve", onesb, 1.0, w=[t_const])
        mset("dve", eps_t, EPS, w=[t_const])
        act(scT, scT, AF.Silu, r=[t_const], w=[t_const])

        for l in range(DEPTH):
            P.barrier()
            bT = A.alloc([48], F32, at=DYN0)
            modrow = A.alloc([6144], F32, at=DYN0 + 256)
            t_bT = T()
            t_mr = T()
            dma(bT, dr["ada_bT"][l], w=[t_bT])
            wsrc = dr["ada_w"][l].rearrange("(k p) c -> p k c", p=128)
            for nchunk in range(12):
                pb = nchunk % 2
                for j in range(4):
                    s = next_stg()
                    sv = stg[s][:, 0:1024].rearrange("p (k c) -> p k c", c=512)
                    dma(sv, wsrc[:, 2 * j:2 * j + 2, nchunk * 512:(nchunk + 1) * 512], w=[t_stg[s]])
                    for kk in range(2):
                        k = 2 * j + kk
                        mm(ps[pb][0:2, :], scT[:, k, :], sv[:, kk, :], start=(k == 0), stop=(k == 7),
                           r=[t_stg[s], t_const], w=[tps[pb]])
                cp("act", modrow[0:2, nchunk * 512:(nchunk + 1) * 512], ps[pb][0:2, :], r=[tps[pb]], w=[t_mr])
            psm = ps[2 + l][:, 0:96].rearrange("p (c s) -> p c s", s=2)
            for ct in range(48):
                tr(psm[:, ct, :], modrow[0:2, ct * 128:(ct + 1) * 128], identf[0:2, 0:2], r=[t_mr, t_const], w=[tps[2 + l]])
            tt("dve", mod[:, l, :, :], psm, bT[:, :].unsqueeze(2).to_broadcast([128, 48, 2]), ALU.add,
               r=[tps[2 + l], t_bT], w=[t_mod])
            for j, c0 in ((0, 8), (1, 32)):
                stt(a1[:, l, j * 8:(j + 1) * 8, :], mod[:, l, c0:c0 + 8, :], 1.0,
                    nwt[:, l, j * 8:(j + 1) * 8].unsqueeze(2).to_broadcast([128, 8, 2]),
                    ALU.add, ALU.mult, r=[t_mod, t_const], w=[t_mod])
        tap("mod", mod, [t_mod])

        def norm_phase(l, which, groups):
            sh0 = 0 if which == 0 else 24
            P.barrier()
            sq = A.alloc([2, 8, 512], BF16, at=DYN0)
            rstd = A.alloc([2, 512], F32, at=DYN0 + 2 * 8 * 512 * 2)
            tmp = A.alloc([2, 512], F32, at=DYN0 + 2 * 8 * 512 * 2 + 2 * 512 * 4)
            t_sq = [T(), T()]
            t_rs = [T(), T()]
            t_tmp = [T(), T()]
            for gi, g in enumerate(groups):
                t0, n = GRP(g)
                s = 1 if g == 0 else 0
                b = gi % 2
                pb = 2 + b
                for k in range(8):
                    act(sq[:, b, k, 0:n], xT[:, k, t0:t0 + n], AF.Square, r=[t_x[k][g]], w=[t_sq[b]])
                for k in range(8):
                    mm(ps[pb][:, 0:n], onesb, sq[:, b, k, 0:n], start=(k == 0), stop=(k == 7),
                       r=[t_sq[b], t_const], w=[tps[pb]])
                act(rstd[:, b, 0:n], ps[pb][:, 0:n], AF.Sqrt, r=[tps[pb], t_const], w=[t_rs[b]],
                    scale=1.0 / D, bias=eps_t[:, 0:1])
                rcp(rstd[:, b, 0:n], rstd[:, b, 0:n], r=[t_rs[b]], w=[t_rs[b]])
                for k in range(8):
                    tb = k % 2
                    tt("dve", tmp[:, tb, 0:n], xT[:, k, t0:t0 + n], rstd[:, b, 0:n], ALU.mult,
                       r=[t_x[k][g], t_rs[b]], w=[t_tmp[tb]])
                    act(hT[:, k, t0:t0 + n], tmp[:, tb, 0:n], AF.Identity, r=[t_tmp[tb], t_mod], w=[t_h[g]],
                        scale=a1[:, l, which * 8 + k, s:s + 1], bias=mod[:, l, sh0 + k, s:s + 1])

        def mk_alloc(limit):
            o = [DYN0]

            def al(shape, dt):
                ap = A.alloc(shape, dt, at=o[0])
                n = int(np.prod(shape)) * (2 if dt == BF16 else 4)
                o[0] += (n + 3) // 4 * 4
                assert o[0] <= limit, (o[0], limit)
                return ap
            al.o = o
            return al

        def mla_phase(l, last, slot):
            P.barrier()
            al = mk_alloc(OT0 + slot * 2 * NT * 2)
            Wm = al([8, 416], BF16)
            Wukv = al([512], BF16)
            Wuq = al([2, 384], BF16)
            kvw = al([128], F32)
            qnw = al([256], F32)
            qkq = al([4, 96], F32)
            qkk = al([4, 96], F32)
            rc = al([NTT, 2, 8], F32)
            rs_ = al([NTT, 2, 8], F32)
            KT = al([4, NT], BF16)
            QT = al([4, 512], BF16)
            V1 = al([NTT, 4, 65], BF16)
            PT = al([2, 512], BF16)
            kvn = al([2, 128], BF16)
            kvnT = al([2, 128], BF16)
            qn = al([2, 256], BF16)
            qnT = al([2, 2, 128], BF16)
            kf = al([2, 4, 96], F32)
            sqs2 = al([2, 4, 96], F32)
            kb = al([2, 4, 96], BF16)
            sm = al([2, 16], F32)
            rt2 = al([2, 4, 4, 2, 8], F32) if False else None
            rtA = al([4, 4, 2, 8], F32)
            rtB = al([4, 4, 2, 8], F32)
            oat = al([4, 256], BF16)
            rinv = al([4], F32)
            t_W = T("Wm")
            t_small = T("mlasmall")
            t_KT = [T() for _ in range(NTT)]
            t_QT = [T() for _ in range(4)]
            t_V1 = [T() for _ in range(NTT)]
            t_PT = [T(), T()]
            t_kvn = [T(), T()]
            t_kvnT = [T(), T()]
            t_qn = [T(), T()]
            t_qnT = [T(), T()]
            t_kf = [T(), T()]
            t_sqs2 = [T(), T()]
            t_kb = [T(), T()]
            t_sm = [T(), T()]
            t_rt2 = [T(), T()]
            t_oat = [T() for _ in range(4)]
            t_rinv = T()

            wsrc = dr["w_in"][l].rearrange("(k p) c -> p k c", p=128)
            for k in range(0, 8, 2):
                load_cast(Wm[:, k:k + 2, 0:160], t_W, wsrc[:, k:k + 2, 0:160], "p (k c) -> p k c", c=160)
                load_cast(Wm[:, k:k + 2, 160:416], t_W, wsrc[:, k:k + 2, 928:1184], "p (k c) -> p k c", c=256)
            load_cast(Wukv, t_W, dr["w_ukv"][l])
            load_cast(Wuq, t_W, dr["w_uq"][l].rearrange("(j p) c -> p j c", p=128), "p (j c) -> p j c", c=384)
            dma(kvw, dr["kvw"][l], w=[t_small])
            dma(qnw, dr["qnw"][l], w=[t_small])
            dma(qkq, dr["qkq"][l].rearrange("p (h d) -> p h d", d=96), w=[t_small])
            dma(qkk, dr["qkk"][l].rearrange("p (h d) -> p h d", d=96), w=[t_small])
            dma(rc, dr["ropec"].rearrange("p t (a b) -> p t a b", b=8), w=[t_small])
            dma(rs_, dr["ropes"].rearrange("p t (a b) -> p t a b", b=8), w=[t_small])
            mset("dve", V1[:, :, :, 64:65], 1.0, w=t_V1)
            if stage < 0.5:
                return

            def headnorm_rope(pb, wq, tt_, dst, t_dst, tbank):
                x = kf[:, pb]
                t_x_ = t_kf[pb]
                sqs = sqs2[:, pb]
                t_sqs = t_sqs2[pb]
                rt = rtA if pb == 0 else rtB
                t_rt = t_rt2[pb]
                tt("dve", sqs, x, x, ALU.mult, r=[t_x_], w=[t_sqs])
                st_ = sm[:, pb, 0:4]
                red(st_, sqs, r=[t_sqs], w=[t_sm[pb]])
                act(st_, st_, AF.Sqrt, r=[t_sm[pb], t_const], w=[t_sm[pb]], scale=1.0 / 96, bias=eps_t[:, 0:1])
                rcp(st_, st_, r=[t_sm[pb]], w=[t_sm[pb]])
                tt("dve", x, x, st_.unsqueeze(2).to_broadcast([128, 4, 96]), ALU.mult, r=[t_x_, t_sm[pb]], w=[t_x_])
                tt("dve", x, x, wq, ALU.mult, r=[t_x_, t_small], w=[t_x_])
                y = kb[:, pb]
                t_y = t_kb[pb]
                cp("dve", y[:, :, 0:64], x[:, :, 0:64], r=[t_x_], w=[t_y])
                xr = x[:, :, 64:96].rearrange("p h (a two b) -> p h a two b", two=2, b=8)
                yr = y[:, :, 64:96].rearrange("p h (a two b) -> p h a two b", two=2, b=8)
                cosb = rc[:, tt_].unsqueeze(1).to_broadcast([128, 4, 2, 8])
                sinb = rs_[:, tt_].unsqueeze(1).to_broadcast([128, 4, 2, 8])
                x1 = xr[:, :, :, 0, :]
                x2 = xr[:, :, :, 1, :]
                tt("dve", rt[:, 0], x1, cosb, ALU.mult, r=[t_x_, t_small], w=[t_rt])
                tt("dve", rt[:, 1], x2, sinb, ALU.mult, r=[t_x_, t_small], w=[t_rt])
                tt("dve", yr[:, :, :, 0, :], rt[:, 0], rt[:, 1], ALU.subtract, r=[t_rt], w=[t_y])
                tt("dve", rt[:, 2], x1, sinb, ALU.mult, r=[t_x_, t_small], w=[t_rt])
                tt("dve", rt[:, 3], x2, cosb, ALU.mult, r=[t_x_, t_small], w=[t_rt])
                tt("dve", yr[:, :, :, 1, :], rt[:, 2], rt[:, 3], ALU.add, r=[t_rt], w=[t_y])
                pst = ps[tbank][:, :].bitcast(BF16)
                for h in range(4):
                    tr(pst[0:96, h * 128:(h + 1) * 128], y[:, h, :], identb, r=[t_y, t_const], w=[tps[tbank]])
                cp("act", dst, pst[0:96, 0:512].rearrange("p (h t) -> p h t", t=128), r=[tps[tbank]], w=[t_dst])

            if os.environ.get("KSTOP"):
                P.kstop = int(os.environ["KSTOP"])
            def _kbody(tt_):
                g = grp_of_tile(tt_)
                pb = tt_ % 2
                t0 = tt_ * 128
                bz = pb
                for k in range(8):
                    mm(ps[bz][:, 0:160], hT[:, k, t0:t0 + 128], Wm[:, k, 0:160], start=(k == 0), stop=(k == 7),
                       r=[t_h[g], t_W], w=[tps[bz]])
                z = ps[bz]
                ssk = sm[:, pb, 8:9]
                act(kf[:, pb].rearrange("p h d -> p (h d)")[:, 0:128], z[:, 0:128], AF.Square,
                    r=[tps[bz]], w=[t_kf[pb], t_sm[pb]], accum=ssk)
                act(ssk, ssk, AF.Sqrt, r=[t_sm[pb], t_const], w=[t_sm[pb]], scale=1.0 / 128, bias=eps_t[:, 0:1])
                rcp(ssk, ssk, r=[t_sm[pb]], w=[t_sm[pb]])
                stt(kvn[:, pb], z[:, 0:128], ssk, kvw, ALU.mult, ALU.mult, r=[tps[bz], t_sm[pb], t_small], w=[t_kvn[pb]])
                pst = ps[2 + pb][:, :].bitcast(BF16)
                tr(pst[:, 0:128], kvn[:, pb], identb, r=[t_kvn[pb], t_const], w=[tps[2 + pb]])
                cp("act", kvnT[:, pb], pst[:, 0:128], r=[tps[2 + pb]], w=[t_kvnT[pb]])
                bkv = 4 + pb
                mm(ps[bkv][:, :], kvnT[:, pb], Wukv, r=[t_kvnT[pb], t_W], w=[tps[bkv]])
                kvv = ps[bkv][:, :].rearrange("p (h c) -> p h c", c=128)
                cp("act", V1[:, tt_, :, 0:64], kvv[:, :, 64:128], r=[tps[bkv]], w=[t_V1[tt_]])
                cp("dve", kf[:, pb, :, 0:64], kvv[:, :, 0:64], r=[tps[bkv]], w=[t_kf[pb]])
                cp("dve", kf[:, pb, :, 64:96], z[:, 128:160].unsqueeze(1).to_broadcast([128, 4, 32]),
                   r=[tps[bz]], w=[t_kf[pb]])
                if os.environ.get("NOHN") != "1":
                    headnorm_rope(pb, qkk, tt_, KT[0:96, :, t0:t0 + 128], t_KT[tt_], 2 + pb)
            for tt_ in range(0, NTT, 2):
                P.interleave([(lambda a=tt_: _kbody(a)), (lambda a=tt_ + 1: _kbody(a))])
            P.kstop = None
            tap("KT", KT[0:96], t_KT)
            tap("V1", V1, t_V1)
            if stage < 0.7:
                return

            scale = 96 ** -0.5
            qgroups = [(g, GRP(g)[0], GRP(g)[1], list(range(NTT))) for g in range(1, 5)]
            if not last:
                qgroups = [(0, 0, 256, [0, 1])] + qgroups
            it = 0
            for (g, q0, nq, ktiles) in qgroups:
                nqs = nq // 128

                def _qbody(qs, g=g, q0=q0):
                    tt_ = q0 // 128 + qs
                    pb = qs % 2
                    t0 = tt_ * 128
                    bz = pb
                    for k in range(8):
                        mm(ps[bz][:, 0:256], hT[:, k, t0:t0 + 128], Wm[:, k, 160:416], start=(k == 0), stop=(k == 7),
                           r=[t_h[g], t_W], w=[tps[bz]])
                    z = ps[bz]
                    ssq = sm[:, pb, 9:10]
                    act(qn[:, pb], z[:, 0:256], AF.Square, r=[tps[bz]], w=[t_qn[pb], t_sm[pb]], accum=ssq)
                    act(ssq, ssq, AF.Sqrt, r=[t_sm[pb], t_const], w=[t_sm[pb]], scale=1.0 / 256, bias=eps_t[:, 0:1])
                    rcp(ssq, ssq, r=[t_sm[pb]], w=[t_sm[pb]])
                    stt(qn[:, pb], z[:, 0:256], ssq, qnw, ALU.mult, ALU.mult, r=[tps[bz], t_sm[pb], t_small], w=[t_qn[pb]])
                    pst = ps[2 + pb][:, :].bitcast(BF16)
                    for j in range(2):
                        tr(pst[:, j * 128:(j + 1) * 128], qn[:, pb, j * 128:(j + 1) * 128], identb,
                           r=[t_qn[pb], t_const], w=[tps[2 + pb]])
                    cp("act", qnT[:, pb], pst[:, 0:256].rearrange("p (j t) -> p j t", t=128), r=[tps[2 + pb]], w=[t_qnT[pb]])
                    bq = 2 + pb
                    for j in range(2):
                        mm(ps[bq][:, 0:384], qnT[:, pb, j], Wuq[:, j, :], start=(j == 0), stop=(j == 1),
                           r=[t_qnT[pb], t_W], w=[tps[bq]])
                    cp("dve", kf[:, pb], ps[bq][:, 0:384].rearrange("p (h d) -> p h d", d=96), r=[tps[bq]], w=[t_kf[pb]])
                    headnorm_rope(pb, qkq, tt_, QT[0:96, :, qs * 128:(qs + 1) * 128], t_QT[qs], 2 + pb)
                for qs in range(0, nqs, 2):
                    P.interleave([(lambda a=qs: _qbody(a)), (lambda a=qs + 1: _qbody(a))])
                if g == 1:
                    tap("QT", QT[0:96], t_QT)
                if stage < 0.8:
                    continue
                for h in range(4):
                    for ki, kt in enumerate(ktiles):
                        sb = it % 2
                        it += 1
                        mm(ps[sb][:, 0:nq], KT[0:96, h, kt * 128:(kt + 1) * 128], QT[0:96, h, 0:nq],
                           r=[t_KT[kt]] + t_QT[0:nqs], w=[tps[sb]])
                        act(PT[:, sb, 0:nq], ps[sb][:, 0:nq], AF.Exp, r=[tps[sb]], w=[t_PT[sb]], scale=scale)
                        for qs in range(nqs):
                            mm(ps[4 + qs][:, 0:65], PT[:, sb, qs * 128:(qs + 1) * 128], V1[:, kt, h, :],
                               start=(ki == 0), stop=(ki == len(ktiles) - 1),
                               r=[t_PT[sb], t_V1[kt]], w=[tps[4 + qs]])
                    for qs in range(nqs):
                        rcp(rinv[:, qs:qs + 1], ps[4 + qs][:, 64:65], r=[tps[4 + qs]], w=[t_rinv])
                        ts("dve", oat[:, qs, h * 64:(h + 1) * 64], ps[4 + qs][:, 0:64], rinv[:, qs:qs + 1], None,
                           ALU.mult, r=[tps[4 + qs], t_rinv], w=[t_oat[qs]])
                for qs in range(nqs):
                    tb = 2 + qs % 2
                    pst = ps[tb][:, :].bitcast(BF16)
                    for j in range(2):
                        tr(pst[:, j * 128:(j + 1) * 128], oat[:, qs, j * 128:(j + 1) * 128], identb,
                           r=[t_oat[qs], t_const], w=[tps[tb]])
                    tq = q0 + qs * 128
                    cp("act", oT[slot][:, :, tq:tq + 128], pst[:, 0:256].rearrange("p (j t) -> p j t", t=128),
                       r=[tps[tb]], w=[t_oT[slot][g]])
            tap("oaT", oT[slot], t_oT[slot])

        def fnet_phase(l, last, slot):
            P.barrier()
            al = mk_alloc(OT0 + slot * 2 * NT * 2)
            UT = al([2, NT], BF16)
            AB = al([NTT, 512], BF16)
            CS = al([2, 512], BF16)
            tb = al([2, 2, 1024], BF16)
            c256 = al([2, 2, 256], BF16)
            t_UT = [T() for _ in range(5)]
            t_AB = [T() for _ in range(NTT)]
            t_CS = T()
            t_tb = [T(), T()]
            t_c256 = T()
            dma(CS, dr["cs64"], w=[t_CS])
            for lt in range(2):
                dma(c256[:, lt], dr["dft256"][:, lt * 128:(lt + 1) * 128, :].rearrange("c p n -> p c n"), w=[t_c256])
            wsrc = dr["w_in"][l].rearrange("(k p) c -> p k c", p=128)
            groups = [g for g in range(5) if not (last and g == 0)]
            bi = 0
            for j in range(2):
                wv, t_wv = load_ring(wsrc[:, :, 1184 + j * 128:1184 + (j + 1) * 128], "p (k c) -> p k c", c=128)
                for g in groups:
                    t0, n = GRP(g)
                    pb = bi % 4
                    bi += 1
                    for k in range(8):
                        mm(ps[pb][:, 0:n], wv[:, k, :], hT[:, k, t0:t0 + n], start=(k == 0), stop=(k == 7),
                           r=[t_wv, t_h[g]], w=[tps[pb]])
                    cp("act", UT[:, j, t0:t0 + n], ps[pb][:, 0:n], r=[tps[pb]], w=[t_UT[g]])
            for tt_ in (range(2, NTT) if last else range(NTT)):
                g = grp_of_tile(tt_)
                pb = 4 + tt_ % 4
                for j in range(2):
                    mm(ps[pb][:, :], UT[:, j, tt_ * 128:(tt_ + 1) * 128], CS[:, j, :], start=(j == 0), stop=(j == 1),
                       r=[t_UT[g], t_CS], w=[tps[pb]])
                cp("dve", AB[:, tt_, :], ps[pb][:, :], r=[tps[pb]], w=[t_AB[tt_]])
            if not last:
                for j in range(2):
                    pb = j
                    n_mm = 0
                    for lt in range(2):
                        for cs_ in range(2):
                            mm(ps[pb][:, 0:256], AB[:, lt, cs_ * 256 + j * 128:cs_ * 256 + (j + 1) * 128],
                               c256[:, lt, cs_, :], start=(n_mm == 0), stop=(n_mm == 3),
                               r=[t_AB[lt], t_c256], w=[tps[pb]])
                            n_mm += 1
                    cp("act", oT[slot][:, j, 0:256], ps[pb][:, 0:256], r=[tps[pb]], w=[t_oT[slot][0]])
            it = 0
            for half in range(2):
                banks = [4 * half + i for i in range(4)]
                for lt in range(16):
                    b_ = it % 2
                    it += 1
                    dma(tb[:, b_], dr["dft2048"][:, lt * 128:(lt + 1) * 128, half * 1024:(half + 1) * 1024]
                        .rearrange("c p n -> p c n"), w=[t_tb[b_]])
                    for j in range(2):
                        for cs_ in range(2):
                            for lg in range(2):
                                bk = banks[j * 2 + lg]
                                mm(ps[bk][:, :], AB[:, 2 + lt, cs_ * 256 + j * 128:cs_ * 256 + (j + 1) * 128],
                                   tb[:, b_, cs_, lg * 512:(lg + 1) * 512],
                                   start=(lt == 0 and cs_ == 0), stop=(lt == 15 and cs_ == 1),
                                   r=[t_AB[2 + lt], t_tb[b_]], w=[tps[bk]])
                for j in range(2):
                    for lg in range(2):
                        bk = banks[j * 2 + lg]
                        g = 1 + half * 2 + lg
                        t0 = LAT0 + half * 1024 + lg * 512
                        cp("act", oT[slot][:, j, t0:t0 + 512], ps[bk][:, :], r=[tps[bk]], w=[t_oT[slot][g]])
            tap("obT", oT[slot], t_oT[slot])

        def ret_phase(l, last, slot):
            P.barrier()
            al = mk_alloc(OT0 + slot * 2 * NT * 2)
            rrc = al([NTT, 32], F32)
            rrs = al([NTT, 32], F32)
            retc = al([6, 128], F32)
            retp = al([2], F32)
            lgrow = al([8], F32)
            lgpp = al([2, 2], F32)
            gnw = al([256], F32)
            Wr = al([8, 512], BF16)
            QZ = al([2, NT], BF16)
            KT_ = al([NT], BF16)
            Vb = al([NTT, 128], BF16)
            sg = al([NTT, 128], BF16)
            Sfp = al([NTT, 128], BF16)
            Sbn = Wr.rearrange("p k c -> p (k c)")[:, 0:NTT * 128].rearrange("p (i c) -> p i c", c=128)
            ub_off = al.o[0]
            Ub = al([NTT, 128], BF16)
            kdec = al([2, 128], F32)
            qdec = al([2, 128], F32)
            Mk = al([2, 128], F32)
            aab = al([2], F32)
            Sf = al([128], F32)
            Sb = al([128], F32)
            rt = al([2, 4, 2, 32], F32)
            Qb = al([2, 128], BF16)
            Kb = al([2, 128], BF16)
            Kd = al([2, 2, 128], BF16)
            Qd = A.alloc([2, 2, 2, 128], BF16, at=ub_off)
            attm = A.alloc([2, 2, 128], BF16, at=ub_off + 2048)
            xc2 = al([2, 2, 64], F32)
            xcq = xc2[:, 0].rearrange("p a d -> p (a d)")
            sq2 = al([2, 2, 64], F32)
            st2 = al([2, 2, 4], F32)
            od = A.alloc([2, 128], BF16, at=ub_off + 3072)
            t_c = T()
            t_lg = T()
            t_Wr = T()
            t_QK = [T() for _ in range(NTT)]
            t_Vb = [T() for _ in range(NTT)]
            t_sg = [T() for _ in range(NTT)]
            t_Sfp = [T() for _ in range(NTT)]
            t_Sbn = [T() for _ in range(NTT)]
            t_Ub = [T() for _ in range(NTT)]
            t_tab = T()
            t_S = T()
            t_rt = [T(), T()]
            t_Qb = [T(), T()]
            t_Kb = [T(), T()]
            t_Kd = [T(), T()]
            t_Qd = [T(), T()]
            t_attm = [T(), T()]
            t_gn2 = [T(), T()]
            t_od = [T(), T()]
            dma(rrc, dr["rrc"], w=[t_c])
            dma(rrs, dr["rrs"], w=[t_c])
            dma(retc, dr["retc"], w=[t_c])
            dma(retp, dr["retp"], w=[t_c])
            dma(lgrow, dr["rlog_row"][l], w=[t_lg])
            dma(lgpp, dr["rlog_pp"][l], w=[t_lg])
            dma(gnw, dr["gnw"][l], w=[t_c])
            for v in (lgrow, lgpp.rearrange("p a b -> p (a b)")):
                act(v, v, AF.Exp, r=[t_lg], w=[t_lg], scale=-1.0)
                act(v, v, AF.Ln, r=[t_lg], w=[t_lg], bias=1.0)
                ts("dve", v, v, -1.0, None, ALU.mult, r=[t_lg], w=[t_lg])
            wsrc = dr["w_in"][l].rearrange("(k p) c -> p k c", p=128)
            for pair in range(2):
                P.barrier()
                offs = [416 + pair * 128, 672 + pair * 128, 1440 + pair * 128, 1696 + pair * 128]
                for ci, c0 in enumerate(offs):
                    load_cast(Wr[:, :, ci * 128:(ci + 1) * 128], t_Wr, wsrc[:, :, c0:c0 + 128], "p (k c) -> p k c", c=128)
                for hh in range(2):
                    head = 2 * pair + hh
                    cs_ = slice(hh * 64, (hh + 1) * 64)
                    act(kdec[:, 0, cs_], lgrow[:, head:head + 1].to_broadcast([128, 64]), AF.Exp, r=[t_lg, t_c], w=[t_tab],
                        scale=retp[:, 1:2])
                    act(kdec[:, 1, cs_], lgrow[:, 4 + head:5 + head].to_broadcast([128, 64]), AF.Exp, r=[t_lg, t_c], w=[t_tab],
                        scale=retp[:, 0:1])
                    act(Mk[:, hh, :], retc[:, 2, :], AF.Exp, r=[t_lg, t_c], w=[t_tab], scale=lgrow[:, head:head + 1])
                    tt("dve", Mk[:, hh, :], Mk[:, hh, :], retc[:, 4, :], ALU.mult, r=[t_tab, t_c], w=[t_tab])
                    act(xcq, retc[:, 3, :], AF.Exp, r=[t_lg, t_c], w=[t_tab], scale=lgrow[:, 4 + head:5 + head])
                    tt("dve", xcq, xcq, retc[:, 5, :], ALU.mult, r=[t_tab, t_c], w=[t_tab])
                    stt(Mk[:, hh, :], Mk[:, hh, :], 1.0, xcq, ALU.mult, ALU.add, r=[t_tab], w=[t_tab])
                ts("dve", Mk, Mk, 0.125, None, ALU.mult, r=[t_tab], w=[t_tab])
                ts("dve", kdec, kdec, 0.125, None, ALU.mult, r=[t_tab], w=[t_tab])
                act(qdec[:, 0, :], retc[:, 0, :], AF.Exp, r=[t_lg, t_c], w=[t_tab], scale=lgpp[:, pair, 0:1])
                act(qdec[:, 1, :], retc[:, 1, :], AF.Exp, r=[t_lg, t_c], w=[t_tab], scale=lgpp[:, pair, 1:2])
                act(aab, lgpp[:, pair, :], AF.Exp, r=[t_lg], w=[t_tab], scale=128.0)
                mset("dve", Sf, 0.0, w=[t_S])
                mset("dve", Sb, 0.0, w=[t_S])
                mset("pool", QZ, 0.0, w=t_QK)

                def rope(src, cosT, sinT, dst, pb, t_dst, t_src):
                    x = src.rearrange("p (h two b) -> p h two b", two=2, b=32)
                    y = dst.rearrange("p (h two b) -> p h two b", two=2, b=32)
                    cb = cosT.unsqueeze(1).to_broadcast([128, 2, 32])
                    sb_ = sinT.unsqueeze(1).to_broadcast([128, 2, 32])
                    r_ = rt[:, pb]
                    tt("dve", r_[:, 0], x[:, :, 0, :], cb, ALU.mult, r=[t_src, t_c], w=[t_rt[pb]])
                    tt("dve", r_[:, 1], x[:, :, 1, :], sb_, ALU.mult, r=[t_src, t_c], w=[t_rt[pb]])
                    tt("dve", y[:, :, 0, :], r_[:, 0], r_[:, 1], ALU.subtract, r=[t_rt[pb]], w=[t_dst])
                    tt("dve", r_[:, 2], x[:, :, 0, :], sb_, ALU.mult, r=[t_src, t_c], w=[t_rt[pb]])
                    tt("dve", r_[:, 3], x[:, :, 1, :], cb, ALU.mult, r=[t_src, t_c], w=[t_rt[pb]])
                    tt("dve", y[:, :, 1, :], r_[:, 2], r_[:, 3], ALU.add, r=[t_rt[pb]], w=[t_dst])

                def _p1body(i):
                    g = grp_of_tile(i)
                    pb = i % 2
                    t0 = i * 128
                    bz = pb
                    for k in range(8):
                        mm(ps[bz][:, :], hT[:, k, t0:t0 + 128], Wr[:, k, :], start=(k == 0), stop=(k == 7),
                           r=[t_h[g], t_Wr], w=[tps[bz]])
                    z = ps[bz]
                    rope(z[:, 256:384], rrc[:, i, :], rrs[:, i, :], Qb[:, pb], pb, t_Qb[pb], tps[bz])
                    rope(z[:, 0:128], rrc[:, i, :], rrs[:, i, :], Kb[:, pb], pb, t_Kb[pb], tps[bz])
                    cp("act", Vb[:, i, :], z[:, 128:256], r=[tps[bz]], w=[t_Vb[i]])
                    act(sg[:, i, :], z[:, 384:512], AF.Silu, r=[tps[bz]], w=[t_sg[i]])
                    tb_ = 2 + pb
                    pst = ps[tb_][:, :].bitcast(BF16)
                    tr(pst[:, 0:128], Qb[:, pb], identb, r=[t_Qb[pb], t_const], w=[tps[tb_]])
                    tr(pst[:, 128:256], Kb[:, pb], identb, r=[t_Kb[pb], t_const], w=[tps[tb_]])
                    cp("act", QZ[0:64, 0, t0:t0 + 128], pst[0:64, 0:128], r=[tps[tb_]], w=[t_QK[i]])
                    cp("act", QZ[64:128, 1, t0:t0 + 128], pst[64:128, 0:128], r=[tps[tb_]], w=[t_QK[i]])
                    cp("act", KT_[:, t0:t0 + 128], pst[:, 128:256], r=[tps[tb_]], w=[t_QK[i]])
                    tt("pool", Kd[:, pb, 0], Kb[:, pb], kdec[:, 0], ALU.mult, r=[t_Kb[pb], t_tab], w=[t_Kd[pb]])
                    tt("pool", Kd[:, pb, 1], Kb[:, pb], kdec[:, 1], ALU.mult, r=[t_Kb[pb], t_tab], w=[t_Kd[pb]])
                    bu = 4 + pb
                    mm(ps[bu][:, 0:128], Kd[:, pb, 0], Vb[:, i, :], r=[t_Kd[pb], t_Vb[i]], w=[tps[bu]])
                    mm(ps[bu][:, 128:256], Kd[:, pb, 1], Vb[:, i, :], r=[t_Kd[pb], t_Vb[i]], w=[tps[bu]])

                for i0 in range(0, NTT, 2):
                    P.interleave([(lambda a=i0: _p1body(a)), (lambda a=i0 + 1: _p1body(a))])
                    for i in (i0, i0 + 1):
                        bu = 4 + i % 2
                        cp("dve", Sfp[:, i, :], Sf, r=[t_S], w=[t_Sfp[i]])
                        stt(Sf, Sf, aab[:, 0:1], ps[bu][:, 0:128], ALU.mult, ALU.add, r=[t_S, t_tab, tps[bu]], w=[t_S])
                        cp("act", Ub[:, i, :], ps[bu][:, 128:256], r=[tps[bu]], w=[t_Ub[i]])
                P.barrier()
                for i in [1, 0] + list(range(NTT - 1, 1, -1)):
                    cp("dve", Sbn[:, i, :], Sb, r=[t_S], w=[t_Sbn[i]])
                    stt(Sb, Sb, aab[:, 1:2], Ub[:, i, :], ALU.mult, ALU.add, r=[t_S, t_tab, t_Ub[i]], w=[t_S])
                P.barrier()
                def _p2body(i):
                    g = grp_of_tile(i)
                    pb = i % 2
                    t0 = i * 128
                    xc = xc2[:, pb]
                    sq = sq2[:, pb]
                    st_ = st2[:, pb]
                    t_gn = t_gn2[pb]
                    ba = pb
                    for hh in range(2):
                        mm(ps[ba][:, hh * 128:(hh + 1) * 128], KT_[:, t0:t0 + 128], QZ[:, hh, t0:t0 + 128],
                           r=[t_QK[i]], w=[tps[ba]])
                    tt("dve", attm[:, pb], ps[ba][:, 0:256].rearrange("p (a t) -> p a t", t=128), Mk, ALU.mult,
                       r=[tps[ba], t_tab], w=[t_attm[pb]])
                    for hh in range(2):
                        tt("pool", Qd[:, pb, hh, 0], QZ[:, hh, t0:t0 + 128], qdec[:, 0], ALU.mult, r=[t_QK[i], t_tab], w=[t_Qd[pb]])
                        tt("pool", Qd[:, pb, hh, 1], QZ[:, hh, t0:t0 + 128], qdec[:, 1], ALU.mult, r=[t_QK[i], t_tab], w=[t_Qd[pb]])
                    bo = 4 + pb
                    for hh in range(2):
                        rs_ = slice(hh * 64, (hh + 1) * 64)
                        mm(ps[bo][:, rs_], attm[:, pb, hh, :], Vb[:, i, rs_], start=True, stop=False,
                           r=[t_attm[pb], t_Vb[i]], w=[tps[bo]])
                        mm(ps[bo][:, rs_], Qd[:, pb, hh, 0, :], Sfp[:, i, rs_], start=False, stop=False,
                           r=[t_Qd[pb], t_Sfp[i]], w=[tps[bo]])
                        mm(ps[bo][:, rs_], Qd[:, pb, hh, 1, :], Sbn[:, i, rs_], start=False, stop=True,
                           r=[t_Qd[pb], t_Sbn[i]], w=[tps[bo]])
                    o = ps[bo][:, 0:128].rearrange("p (a d) -> p a d", d=64)
                    red(st_[:, 0, 0:2], o, r=[tps[bo]], w=[t_gn])
                    ts("dve", st_[:, 0, 0:2], st_[:, 0, 0:2], -1.0 / 64, None, ALU.mult, r=[t_gn], w=[t_gn])
                    tt("dve", xc, o, st_[:, 0, 0:2].unsqueeze(2).to_broadcast([128, 2, 64]), ALU.add, r=[tps[bo], t_gn], w=[t_gn])
                    tt("dve", sq, xc, xc, ALU.mult, r=[t_gn], w=[t_gn])
                    red(st_[:, 1, 0:2], sq, r=[t_gn], w=[t_gn])
                    act(st_[:, 1, 0:2], st_[:, 1, 0:2], AF.Sqrt, r=[t_gn, t_const], w=[t_gn], scale=1.0 / 64, bias=eps_t[:, 0:1])
                    rcp(st_[:, 1, 0:2], st_[:, 1, 0:2], r=[t_gn], w=[t_gn])
                    tt("dve", xc, xc, st_[:, 1, 0:2].unsqueeze(2).to_broadcast([128, 2, 64]), ALU.mult, r=[t_gn], w=[t_gn])
                    tt("dve", xc, xc, gnw[:, pair * 128:(pair + 1) * 128].rearrange("p (a d) -> p a d", d=64), ALU.mult,
                       r=[t_gn, t_c], w=[t_gn])
                    tt("dve", od[:, pb].rearrange("p (a d) -> p a d", d=64), xc,
                       sg[:, i, :].rearrange("p (a d) -> p a d", d=64), ALU.mult, r=[t_gn, t_sg[i]], w=[t_od[pb]])
                    tb_ = 2 + pb
                    pst = ps[tb_][:, :].bitcast(BF16)
                    tr(pst[:, 0:128], od[:, pb], identb, r=[t_od[pb], t_const], w=[tps[tb_]])
                    cp("act", oT[slot][:, pair, t0:t0 + 128], pst[:, 0:128], r=[tps[tb_]], w=[t_oT[slot][g]])

                for i0 in range(2 if last else 0, NTT, 2):
                    P.interleave([(lambda a=i0: _p2body(a)), (lambda a=i0 + 1: _p2body(a))])
            tap("odT", oT[slot], t_oT[slot])

        I32 = mybir.dt.int32
        TWO_PI = 2.0 * math.pi

        def s5_phase(l, last, slot):
            P.barrier()
            al = mk_alloc(OT0 + slot * 2 * NT * 2)
            uT = al([2, NT], BF16)
            yf = al([2, NT], BF16)
            E = al([2, 1024], BF16)
            Fm = al([8, 2, 128], BF16)
            Bb = al([2, 2, 512], BF16)
            Cc = al([2, 8, 128], BF16)
            Tri = al([2, 128], BF16)
            pp = al([3, 8], F32)
            sm = al([12, 8], F32)
            cst = al([4], F32)
            erow = al([2, 128], F32)
            ecol = al([4], F32)
            dvec = al([2], F32)
            xl = sm[:, 7:9, :].rearrange("p a b -> p (a b)").rearrange("p (s r) -> p s r", r=2)
            ccx = al([128], F32)
            cc = ccx[:, 0:16].rearrange("p (a b) -> p a b", b=2)
            id2 = al([16], BF16)
            woff = al.o[0]
            W = al([2, 1024], BF16)
            xx = al([2, 8, 128], BF16)
            tW = al([2, 256], F32)
            tq = al([2, 512], F32)
            ysc = A.alloc([4, 128], F32, at=al.o[0] - 2048)
            cT = al([128], BF16)
            t_u = [T() for _ in range(5)]
            t_yf = [T() for _ in range(NTT)]
            t_tab = T()
            t_pp = T()
            t_c = T()
            t_W = [T(), T()]
            t_xx = [T(), T()]
            t_tW = T()
            t_tq = T()
            t_xl = T()
            t_cc = T()
            t_ysc = t_tq
            t_cT = T()
            t_blk = T()

            dma(Tri, dr["s5tri"], w=[t_c])
            dma(id2, dr["s5id2"], w=[t_c])
            dma(erow, dr["s5erow"], w=[t_c])
            dma(ecol, dr["s5ecol"], w=[t_c])
            dma(dvec, dr["s5d"][l], w=[t_c])
            mset("dve", cst[:, 0:1], -math.pi, w=[t_c])

            wsrc = dr["w_in"][l].rearrange("(k p) c -> p k c", p=128)
            bi = 0
            for j in range(2):
                wv, t_wv = load_ring(wsrc[:, :, 160 + j * 128:160 + (j + 1) * 128], "p (k c) -> p k c", c=128)
                for g in range(5):
                    t0, n = GRP(g)
                    pb = bi % 4
                    bi += 1
                    for k in range(8):
                        mm(ps[pb][:, 0:n], wv[:, k, :], hT[:, k, t0:t0 + n], start=(k == 0), stop=(k == 7),
                           r=[t_wv, t_h[g]], w=[tps[pb]])
                    cp("act", uT[:, j, t0:t0 + n], ps[pb][:, 0:n], r=[tps[pb]], w=[t_u[g]])
            tap("uT", uT, t_u)

            def cplx_pow(out_re, out_im, phase, mag, n, conj, tmp):
                r_, n_i, f_, m_ = tmp
                ts("dve", r_, phase, 1.0 / TWO_PI, None, ALU.mult, r=[t_blk], w=[t_blk])
                cp("dve", n_i.bitcast(I32), r_, r=[t_blk], w=[t_blk])
                cp("dve", f_, n_i.bitcast(I32), r=[t_blk], w=[t_blk])
                tt("dve", f_, r_, f_, ALU.subtract, r=[t_blk], w=[t_blk])
                ts("dve", m_, f_, 0.0, None, ALU.is_lt, r=[t_blk], w=[t_blk])
                tt("dve", f_, f_, m_, ALU.add, r=[t_blk], w=[t_blk])
                act(r_, f_, AF.Sin, r=[t_blk, t_c], w=[t_blk], scale=TWO_PI, bias=cst[:, 0:1])
                ts("dve", f_, f_, 0.25, None, ALU.add, r=[t_blk], w=[t_blk])
                ts("dve", m_, f_, 1.0, None, ALU.is_ge, r=[t_blk], w=[t_blk])
                tt("dve", f_, f_, m_, ALU.subtract, r=[t_blk], w=[t_blk])
                act(m_, f_, AF.Sin, r=[t_blk, t_c], w=[t_blk], scale=TWO_PI, bias=cst[:, 0:1])
                stt(out_re, mag, -1.0, m_, ALU.mult, ALU.mult, r=[t_blk], w=[t_blk, t_tab])
                if conj:
                    tt("dve", out_im, mag, r_, ALU.mult, r=[t_blk], w=[t_blk, t_tab])
                else:
                    stt(out_im, mag, -1.0, r_, ALU.mult, ALU.mult, r=[t_blk], w=[t_blk, t_tab])

            glw = None
            for d_ in range(2):
                P.barrier()
                B_ = [A.alloc([256], F32, at=woff + i * 1024) for i in range(16)]
                dma(pp, dr["s5pp"][l, d_], w=[t_pp])
                act(pp[:, 2, :], pp[:, 2, :], AF.Exp, r=[t_pp], w=[t_pp])
                App = sm[:, 0, :]
                Bpp = sm[:, 1, :]
                tt("dve", App, pp[:, 0, :], pp[:, 2, :], ALU.mult, r=[t_pp], w=[t_blk])
                tt("dve", Bpp, pp[:, 1, :], pp[:, 2, :], ALU.mult, r=[t_pp], w=[t_blk])
                l1re = sm[:, 2, :]
                l1im = sm[:, 3, :]
                mg = sm[:, 4, :]
                act(mg, App, AF.Exp, r=[t_blk], w=[t_blk])
                tmp8 = [B_[0][:, 0:8], B_[0][:, 8:16], B_[0][:, 16:24], B_[0][:, 24:32]]
                cplx_pow(l1re, l1im, Bpp, mg, 8, False, tmp8)
                br = sm[:, 5, :]
                den = sm[:, 6, :]
                kre = sm[:, 7, :]
                kim = sm[:, 8, :]
                nkre = sm[:, 9, :]
                nkim = sm[:, 10, :]
                t8 = sm[:, 11, :]
                ts("dve", br, l1re, -1.0, None, ALU.add, r=[t_blk], w=[t_blk])
                tt("dve", den, pp[:, 0, :], pp[:, 0, :], ALU.mult, r=[t_pp], w=[t_blk])
                tt("dve", t8, pp[:, 1, :], pp[:, 1, :], ALU.mult, r=[t_pp], w=[t_blk])
                tt("dve", den, den, t8, ALU.add, r=[t_blk], w=[t_blk])
                rcp(den, den, r=[t_blk], w=[t_blk])
                tt("dve", kre, br, pp[:, 0, :], ALU.mult, r=[t_blk, t_pp], w=[t_blk])
                tt("dve", t8, l1im, pp[:, 1, :], ALU.mult, r=[t_blk, t_pp], w=[t_blk])
                tt("dve", kre, kre, t8, ALU.add, r=[t_blk], w=[t_blk])
                tt("dve", kre, kre, den, ALU.mult, r=[t_blk], w=[t_blk])
                tt("dve", kim, l1im, pp[:, 0, :], ALU.mult, r=[t_blk, t_pp], w=[t_blk])
                tt("dve", t8, br, pp[:, 1, :], ALU.mult, r=[t_blk, t_pp], w=[t_blk])
                tt("dve", kim, kim, t8, ALU.subtract, r=[t_blk], w=[t_blk])
                tt("dve", kim, kim, den, ALU.mult, r=[t_blk], w=[t_blk])
                ts("dve", nkre, kre, -1.0, None, ALU.mult, r=[t_blk], w=[t_blk])
                ts("dve", nkim, kim, -1.0, None, ALU.mult, r=[t_blk], w=[t_blk])
                Cre = A.alloc([8, 128], F32, at=woff + 1 * 1024)
                Cim = A.alloc([8, 128], F32, at=woff + 5 * 1024)
                for ri, Cdst in enumerate((Cre, Cim)):
                    dma(Cdst, dr["s5c"][l, d_, ri].rearrange("a s c -> s a c"), w=[t_blk])
                tC = B_[9][:, 0:128]
                for st in range(8):
                    ts("dve", tC, Cre[:, st, :], kre[:, st:st + 1], None, ALU.mult, r=[t_blk], w=[t_blk])
                    stt(Cc[:, 0, st, :], Cim[:, st, :], nkim[:, st:st + 1], tC, ALU.mult, ALU.add, r=[t_blk], w=[t_tab])
                    ts("dve", tC, Cre[:, st, :], nkim[:, st:st + 1], None, ALU.mult, r=[t_blk], w=[t_blk])
                    stt(Cc[:, 1, st, :], Cim[:, st, :], nkre[:, st:st + 1], tC, ALU.mult, ALU.add, r=[t_blk], w=[t_tab])
                for kt in range(2):
                    load_cast(Bb[:, kt], t_tab, dr["s5b"][l, d_, :, kt], "p (a c) -> p a c", c=512)
                er = erow[:, d_, :]
                for st in range(8):
                    ph = B_[9][:, 0:128]
                    mgb = B_[9][:, 128:256]
                    ts("dve", ph, er, Bpp[:, st:st + 1], None, ALU.mult, r=[t_c, t_blk], w=[t_blk])
                    act(mgb, er, AF.Exp, r=[t_c, t_blk], w=[t_blk], scale=App[:, st:st + 1])
                    tmpb = [B_[10][:, 0:128], B_[10][:, 128:256], B_[11][:, 0:128], B_[11][:, 128:256]]
                    cplx_pow(Fm[:, st, 0, :], Fm[:, st, 1, :], ph, mgb, 128, False, tmpb)
                row = A.alloc([3, 256], F32, at=woff + 1 * 1024)
                for cb in range(4):
                    dma(row, dr["s5row"][l, d_, :, :, cb * 256:(cb + 1) * 256], w=[t_blk])
                    act(row[:, 2, :], row[:, 2, :], AF.Exp, r=[t_blk], w=[t_blk])
                    Ab = B_[4]
                    Bk = B_[5]
                    tt("dve", Ab, row[:, 0, :], row[:, 2, :], ALU.mult, r=[t_blk], w=[t_blk])
                    tt("dve", Bk, row[:, 1, :], row[:, 2, :], ALU.mult, r=[t_blk], w=[t_blk])
                    ph = B_[6]
                    mgb = B_[7]
                    ts("dve", ph, Bk, ecol[:, d_:d_ + 1], None, ALU.mult, r=[t_blk, t_c], w=[t_blk])
                    act(mgb, Ab, AF.Exp, r=[t_blk, t_c], w=[t_blk], scale=ecol[:, 2 + d_:3 + d_])
                    tmpb = [B_[8], B_[9], B_[10], B_[11]]
                    cplx_pow(E[:, 0, cb * 256:(cb + 1) * 256], E[:, 1, cb * 256:(cb + 1) * 256], ph, mgb, 256, True, tmpb)
                if d_ == 0:
                    tap("s5E", E, [t_tab])
                    tap("s5F", Fm, [t_tab])
                    tap("s5C", Cc, [t_tab])
                P.barrier()
                order = list(range(NTT)) if d_ == 0 else [1, 0] + list(range(NTT - 1, 1, -1))
                lastcol = 127 if d_ == 0 else 0
                mset("dve", ccx, 0.0, w=[t_cc])
                for idx, i in enumerate(order):
                    g = grp_of_tile(i)
                    t0 = i * 128
                    pb = idx % 2
                    for nb in range(4):
                        kt = nb % 2
                        ri = nb // 2
                        mm(ps[nb][:, :], uT[:, kt, t0:t0 + 128], Bb[:, kt, ri, :], r=[t_u[g], t_tab], w=[tps[nb]])
                    for qb in range(4):
                        hb = qb // 2
                        sl = slice(qb * 256, (qb + 1) * 256)
                        pl = slice((qb % 2) * 256, (qb % 2 + 1) * 256)
                        tt("dve", tW[:, 0, :], ps[hb][:, pl], E[:, 0, sl], ALU.mult, r=[tps[hb], t_tab], w=[t_tW])
                        tt("dve", tW[:, 1, :], ps[2 + hb][:, pl], E[:, 1, sl], ALU.mult, r=[tps[2 + hb], t_tab], w=[t_tW])
                        tt("dve", W[:, 0, sl], tW[:, 0, :], tW[:, 1, :], ALU.subtract, r=[t_tW], w=[t_W[0]])
                        tt("dve", tW[:, 0, :], ps[hb][:, pl], E[:, 1, sl], ALU.mult, r=[tps[hb], t_tab], w=[t_tW])
                        tt("dve", tW[:, 1, :], ps[2 + hb][:, pl], E[:, 0, sl], ALU.mult, r=[tps[2 + hb], t_tab], w=[t_tW])
                        tt("dve", W[:, 1, sl], tW[:, 0, :], tW[:, 1, :], ALU.add, r=[t_tW], w=[t_W[1]])
                    tr(ps[2][:, 0:128], ccx, identf, r=[t_cc, t_const], w=[tps[2]])
                    cp("act", cT[:, :], ps[2][:, 0:128], r=[tps[2]], w=[t_cT])
                    tt("dve", cT[32:64, :], ps[2][32:64, 0:128], cT[32:64, :], ALU.subtract, r=[tps[2], t_cT], w=[t_cT])
                    for ri in range(2):
                        for st in range(8):
                            bk = 4 + ri * 2 + st // 4
                            mm(ps[bk][:, (st % 4) * 128:(st % 4 + 1) * 128], W[:, ri, st * 128:(st + 1) * 128], Tri[:, d_, :],
                               start=(st % 4 == 0), stop=False, r=[t_W[ri], t_c], w=[tps[bk]])
                    for ri in range(2):
                        for st in range(8):
                            bk = 4 + ri * 2 + st // 4
                            jj = st * 2 + ri
                            mm(ps[bk][:, (st % 4) * 128:(st % 4 + 1) * 128], cT[:, :],
                               id2[:, jj:jj + 1].to_broadcast([128, 128]),
                               start=False, stop=True, r=[t_cT, t_c], w=[tps[bk]])
                    for hf in range(2):
                        Sre = ps[4 + hf][:, :].rearrange("p (a t) -> p a t", t=128)
                        Sim = ps[6 + hf][:, :].rearrange("p (a t) -> p a t", t=128)
                        Fre = Fm[:, 4 * hf:4 * hf + 4, 0, :]
                        Fim = Fm[:, 4 * hf:4 * hf + 4, 1, :]
                        q0 = tq[:, 0, :].rearrange("p (a t) -> p a t", t=128)
                        q1 = tq[:, 1, :].rearrange("p (a t) -> p a t", t=128)
                        tt("dve", q0, Sre, Fre, ALU.mult, r=[tps[4 + hf], t_tab], w=[t_tq])
                        tt("dve", q1, Sim, Fim, ALU.mult, r=[tps[6 + hf], t_tab], w=[t_tq])
                        tt("dve", xl[:, 4 * hf:4 * hf + 4, 0], q0[:, :, lastcol], q1[:, :, lastcol], ALU.subtract, r=[t_tq], w=[t_xl])
                        tt("dve", xx[:, 0, 4 * hf:4 * hf + 4, :], q0, q1, ALU.subtract, r=[t_tq], w=[t_xx[0]])
                        tt("dve", q0, Sre, Fim, ALU.mult, r=[tps[4 + hf], t_tab], w=[t_tq])
                        tt("dve", q1, Sim, Fre, ALU.mult, r=[tps[6 + hf], t_tab], w=[t_tq])
                        tt("dve", xl[:, 4 * hf:4 * hf + 4, 1], q0[:, :, lastcol], q1[:, :, lastcol], ALU.add, r=[t_tq], w=[t_xl])
                        tt("dve", xx[:, 1, 4 * hf:4 * hf + 4, :], q0, q1, ALU.add, r=[t_tq], w=[t_xx[1]])
                    ta_ = sm[:, 5, :]
                    tb_ = sm[:, 6, :]
                    tt("dve", ta_, l1re, xl[:, :, 0], ALU.mult, r=[t_xl, t_blk], w=[t_blk])
                    tt("dve", tb_, l1im, xl[:, :, 1], ALU.mult, r=[t_xl, t_blk], w=[t_blk])
                    tt("dve", cc[:, :, 0], ta_, tb_, ALU.subtract, r=[t_blk], w=[t_cc])
                    tt("dve", ta_, l1re, xl[:, :, 1], ALU.mult, r=[t_xl, t_blk], w=[t_blk])
                    tt("dve", tb_, l1im, xl[:, :, 0], ALU.mult, r=[t_xl, t_blk], w=[t_blk])
                    tt("dve", cc[:, :, 1], ta_, tb_, ALU.add, r=[t_blk], w=[t_cc])
                    cp("dve", ccx[:, 32:48], ccx[:, 0:16], r=[t_cc], w=[t_cc])
                    if last and i < 2:
                        continue
                    for j in range(2):
                        n_mm = 0
                        for st in range(4 * j, 4 * j + 4):
                            for ri in range(2):
                                mm(ps[j][:, 0:128], Cc[:, ri, st, :], xx[:, ri, st, :], start=(n_mm == 0), stop=(n_mm == 7),
                                   r=[t_tab, t_xx[ri]], w=[tps[j]])
                                n_mm += 1
                        if d_ == 0:
                            cp("act", yf[:, j, t0:t0 + 128], ps[j][:, 0:128], r=[tps[j]], w=[t_yf[i]])
                        else:
                            y = ysc[:, 0, :]
                            tt("dve", y, ps[j][:, 0:128], yf[:, j, t0:t0 + 128], ALU.add, r=[tps[j], t_yf[i]], w=[t_ysc])
                            stt(y, uT[:, j, t0:t0 + 128], dvec[:, j:j + 1], y, ALU.mult, ALU.add, r=[t_u[g], t_c, t_ysc], w=[t_ysc])
                            tt("dve", ysc[:, 1, :], y, y, ALU.mult, r=[t_ysc], w=[t_ysc])
                            ts("dve", ysc[:, 1, :], ysc[:, 1, :], 0.044715, 1.0, ALU.mult, ALU.add, r=[t_ysc], w=[t_ysc])
                            tt("dve", ysc[:, 1, :], ysc[:, 1, :], y, ALU.mult, r=[t_ysc], w=[t_ysc])
                            act(ysc[:, 2, :], ysc[:, 1, :], AF.Sigmoid, r=[t_ysc], w=[t_ysc], scale=1.5957691216057308)
                            tt("dve", yf[:, j, t0:t0 + 128], y, ysc[:, 2, :], ALU.mult, r=[t_ysc], w=[t_yf[i]])
            tap("s5g", yf, t_yf)
            P.barrier()
            glw = A.alloc([2, 512], BF16, at=woff)
            t_glw = T()
            load_cast(glw[:, 0], t_glw, dr["s5_w_glu"][l][0:128, :])
            load_cast(glw[:, 1], t_glw, dr["s5_w_glu"][l][128:256, :])
            sgt = A.alloc([512], F32, at=woff + 2048)
            t_sgt = T()
            bi = 0
            for g in range(1 if last else 0, 5):
                t0, n = GRP(g)
                tiles = list(range(t0 // 128, (t0 + n) // 128))
                for j in range(2):
                    pv = bi % 2
                    pg = 2 + bi % 2
                    bi += 1
                    for kt in range(2):
                        mm(ps[pv][:, 0:n], glw[:, kt, j * 128:(j + 1) * 128], yf[:, kt, t0:t0 + n], start=(kt == 0), stop=(kt == 1),
                           r=[t_glw] + [t_yf[i] for i in tiles], w=[tps[pv]])
                    for kt in range(2):
                        mm(ps[pg][:, 0:n], glw[:, kt, 256 + j * 128:256 + (j + 1) * 128], yf[:, kt, t0:t0 + n],
                           start=(kt == 0), stop=(kt == 1), r=[t_glw] + [t_yf[i] for i in tiles], w=[tps[pg]])
                    act(sgt[:, 0:n], ps[pg][:, 0:n], AF.Sigmoid, r=[tps[pg]], w=[t_sgt])
                    tt("dve", oT[slot][:, j, t0:t0 + n], ps[pv][:, 0:n], sgt[:, 0:n], ALU.mult, r=[tps[pv], t_sgt],
                       w=[t_oT[slot][g]])
            tap("ocT", oT[slot], t_oT[slot])

        SLOT_OF = {0: 3, 1: 0, 2: 1, 3: 2}

        def merge_phase(l, last):
            P.barrier()
            al = mk_alloc(OT0)
            mT = al([8, NT], BF16)
            sig = al([512], F32)
            acc = al([512], F32)
            t_m = [T() for _ in range(5)]
            t_sig = T()
            t_acc = T()
            wsrc = dr["w_in"][l].rearrange("(k p) c -> p k c", p=128)
            wbsrc = dr["w_branch"][l].rearrange("n (j p) d -> p n j d", p=128)
            groups = [g for g in range(5) if not (last and g == 0)]
            bi = 0
            for d in range(8):
                gw = []
                for n in range(4):
                    c0 = 1952 + n * 1024 + d * 128
                    gw.append(load_ring(wsrc[:, :, c0:c0 + 128], "p (k c) -> p k c", c=128))
                wb, t_wb = load_ring(wbsrc[:, :, :, d * 128:(d + 1) * 128], "p (n j c) -> p n j c", j=2, c=128)
                for g in groups:
                    t0, n_ = GRP(g)
                    for n in range(4):
                        sl = SLOT_OF[n]
                        pa = bi % 2
                        pb = 2 + bi % 2
                        bi += 1
                        wv, t_wv = gw[n]
                        for k in range(8):
                            mm(ps[pa][:, 0:n_], wv[:, k, :], hT[:, k, t0:t0 + n_], start=(k == 0), stop=(k == 7),
                               r=[t_wv, t_h[g]], w=[tps[pa]])
                        for j in range(2):
                            mm(ps[pb][:, 0:n_], wb[:, n, j, :], oT[sl][:, j, t0:t0 + n_], start=(j == 0), stop=(j == 1),
                               r=[t_wb, t_oT[sl][g]], w=[tps[pb]])
                        act(sig[:, 0:n_], ps[pa][:, 0:n_], AF.Sigmoid, r=[tps[pa]], w=[t_sig])
                        if n == 0:
                            tt("dve", acc[:, 0:n_], ps[pb][:, 0:n_], sig[:, 0:n_], ALU.mult, r=[tps[pb], t_sig], w=[t_acc])
                        else:
                            tt("dve", ps[pb][:, 0:n_], ps[pb][:, 0:n_], sig[:, 0:n_], ALU.mult, r=[tps[pb], t_sig], w=[tps[pb]])
                            if n < 3:
                                tt("dve", acc[:, 0:n_], acc[:, 0:n_], ps[pb][:, 0:n_], ALU.add, r=[tps[pb], t_acc], w=[t_acc])
                            else:
                                tt("dve", mT[:, d, t0:t0 + n_], acc[:, 0:n_], ps[pb][:, 0:n_], ALU.add,
                                   r=[tps[pb], t_acc], w=[t_m[g]])
            tap("mT", mT, t_m)
            wosrc = dr["w_out"][l].rearrange("(k p) c -> p k c", p=128)
            for d in range(8):
                wv, t_wv = load_ring(wosrc[:, :, d * 128:(d + 1) * 128], "p (k c) -> p k c", c=128)
                for g in groups:
                    t0, n_ = GRP(g)
                    s_ = 1 if g == 0 else 0
                    pb = 4 + bi % 4
                    bi += 1
                    for k in range(8):
                        mm(ps[pb][:, 0:n_], wv[:, k, :], mT[:, k, t0:t0 + n_], start=(k == 0), stop=(k == 7),
                           r=[t_wv, t_m[g]], w=[tps[pb]])
                    stt(xT[:, d, t0:t0 + n_], ps[pb][:, 0:n_], mod[:, l, 16 + d, s_:s_ + 1], xT[:, d, t0:t0 + n_],
                        ALU.mult, ALU.add, r=[tps[pb], t_mod, t_x[d][g]], w=[t_x[d][g]])

        def ffn_phase(l, last):
            groups = [g for g in range(5) if not (last and g == 0)]
            norm_phase(l, 1, groups)
            P.barrier()
            al = mk_alloc(A.nbytes)
            aT = al([8, NT], BF16)
            rl = al([2, 512], F32)
            t_a = [[T() for _ in range(5)] for _ in range(8)]
            t_rl = [T(), T()]
            w1src = dr["ffn_w1"][l].rearrange("(k p) c -> p k c", p=128)
            w2src = dr["ffn_w2"][l].rearrange("(f p) c -> p f c", p=128)
            bi = 0
            for fb in range(4):
                for f in range(8):
                    F_ = fb * 8 + f
                    wv, t_wv = load_ring(w1src[:, :, F_ * 128:(F_ + 1) * 128], "p (k c) -> p k c", c=128)
                    for g in groups:
                        t0, n_ = GRP(g)
                        pb = bi % 4
                        rb = bi % 2
                        bi += 1
                        for k in range(8):
                            mm(ps[pb][:, 0:n_], wv[:, k, :], hT[:, k, t0:t0 + n_], start=(k == 0), stop=(k == 7),
                               r=[t_wv, t_h[g]], w=[tps[pb]])
                        act(rl[:, rb, 0:n_], ps[pb][:, 0:n_], AF.Relu, r=[tps[pb]], w=[t_rl[rb]])
                        tt("pool", aT[:, f, t0:t0 + n_], rl[:, rb, 0:n_], rl[:, rb, 0:n_], ALU.mult, r=[t_rl[rb]], w=[t_a[f][g]])
                for d in range(8):
                    wv, t_wv = load_ring(w2src[:, fb * 8:(fb + 1) * 8, d * 128:(d + 1) * 128], "p (f c) -> p f c", c=128)
                    for g in groups:
                        t0, n_ = GRP(g)
                        s_ = 1 if g == 0 else 0
                        pb = 4 + bi % 4
                        bi += 1
                        for f in range(8):
                            mm(ps[pb][:, 0:n_], wv[:, f, :], aT[:, f, t0:t0 + n_], start=(f == 0), stop=(f == 7),
                               r=[t_wv, t_a[f][g]], w=[tps[pb]])
                        stt(xT[:, d, t0:t0 + n_], ps[pb][:, 0:n_], mod[:, l, 40 + d, s_:s_ + 1], xT[:, d, t0:t0 + n_],
                            ALU.mult, ALU.add, r=[tps[pb], t_mod, t_x[d][g]], w=[t_x[d][g]])

        for l in range(DEPTH):
            last = (l == DEPTH - 1)
            norm_phase(l, 0, list(range(5)))
            if l == 0:
                tap("hT", hT, t_h)
            if stage <= 0.1:
                break
            if "mla" in mixers:
                mla_phase(l, last, 3)
            if "ret" in mixers:
                ret_phase(l, last, 2)
            if "s5" in mixers:
                s5_phase(l, last, 1)
            if "fnet" in mixers:
                fnet_phase(l, last, 0)
            if stage <= 1:
                break
            merge_phase(l, last)
            ffn_phase(l, last)
            if l == 0:
                tap("x1", xT, [t for k in range(8) for t in t_x[k]])
            if stage <= 2:
                break

        osrc = outT.rearrange("(k p) t -> p k t", p=128)
        for k in range(8):
            out_handles.append(dma(osrc[:, k, :], xT[:, k, LAT0:NT], r=t_x[k]))
        P.wait_all("sp", out_handles)
        P.emit()
    return nc


_CACHE = {}


def _specs_of(d):
    sp = {}
    for k, v in d.items():
        sp[k] = (v.shape, BF16 if v.dtype == NPBF else F32)
    return sp


def run(inputs, stage=99, taps=(), ncores=8, mixers=("mla", "s5", "ret", "fnet")):
    com = prep_common(inputs)
    cores = [prep_core(inputs, b) for b in range(ncores)]
    in_maps = [dict(com, **c) for c in cores]
    nc = build(_specs_of(in_maps[0]), stage=stage, taps=taps, mixers=mixers)
    res = run_bass_kernel_spmd(nc, in_maps, core_ids=list(range(ncores)))
    return res.results


def kernel(**inputs):
    inputs = {k: np.asarray(v) for k, v in inputs.items()}
    res = run(inputs)
    out = np.stack([r["outT"].T for r in res], axis=0)
    return np.ascontiguousarray(out.astype(np.float32))
```

```python
import contextlib
import math
import os
import numpy as np
import ml_dtypes
import concourse.bass as bass
import concourse.mybir as mybir
from concourse.bass_utils import run_bass_kernel_spmd

F32 = mybir.dt.float32
BF16 = mybir.dt.bfloat16
ALU = mybir.AluOpType
AF = mybir.ActivationFunctionType
AX = mybir.AxisListType
NPBF = ml_dtypes.bfloat16

ENGS = ("pe", "act", "dve", "pool", "sp")
NOSELF = tuple(os.environ.get("NOSELF", "pe").split(","))
RELAX = os.environ.get("RELAX", "1") == "1"
NDSLOT = 8

D = 1024
NT = 2304
NTT = 18
LAT0 = 256
EPS = 1e-6
DEPTH = 2


class T:
    __slots__ = ("name", "w", "rs", "excl", "tw")

    def __init__(self, name="", excl=False):
        self.name = name
        self.w = None
        self.tw = None
        self.rs = []
        self.excl = excl


class Prog:
    def __init__(self, nc, stack, same_sync=True):
        self.nc = nc
        self.same_sync = same_sync
        self.q = {e: [] for e in ENGS}
        self.cnt = {e: 0 for e in ENGS}
        self.sems = {}
        for e in ENGS:
            self.sems[("c", e)] = stack.enter_context(nc.semaphore("c_" + e))
        self.dq = ("sp", "pool", "act")
        self.dcnt = {}
        self.dn = {e: 0 for e in self.dq}
        for e in self.dq:
            for s in range(NDSLOT):
                self.sems[("d", e, s)] = stack.enter_context(nc.semaphore("d_%s%d" % (e, s)))
                self.dcnt[(e, s)] = 0
        self.known = {e: {} for e in ENGS}
        self.kstop = None
        self.kcount = 0
        import threading
        self._tls = threading.local()

    def _deps(self, eng, r, w):
        deps = {}

        def add(h):
            if h is None:
                return
            k, v = h
            if k == ("c", eng) and (eng in NOSELF or not self.same_sync):
                return
            if deps.get(k, 0) < v:
                deps[k] = v
        for t in r:
            add(t.w)
        for t in w:
            add(t.w)
            for h in t.rs:
                add(h)
        out = []
        kn = self.known[eng]
        for k, v in deps.items():
            if kn.get(k, 0) >= v:
                continue
            kn[k] = v
            out.append((k, v))
        return out

    def _mark(self, h, r, w):
        for t in w:
            t.tw = h
        for t in r:
            t.rs.append(h)
            if len(t.rs) > 64:
                best = {}
                for k, v in t.rs:
                    if best.get(k, 0) < v:
                        best[k] = v
                t.rs = list(best.items())
        for t in w:
            t.w = h
            t.rs = []

    def interleave(self, thunks):
        import threading
        n = len(thunks)
        if n == 1 or os.environ.get("NOIL") == "1":
            for t in thunks:
                t()
            return
        cv = threading.Condition()
        st = {"turn": 0, "alive": [True] * n, "err": None}

        def nxt(i):
            for d in range(1, n + 1):
                j = (i + d) % n
                if st["alive"][j]:
                    return j
            return None

        def yp(i):
            with cv:
                j = nxt(i)
                if j is None or j == i:
                    return
                st["turn"] = j
                cv.notify_all()
                cv.wait_for(lambda: st["turn"] == i)

        def runner(i):
            try:
                with cv:
                    cv.wait_for(lambda: st["turn"] == i)
                self._tls.yp = (lambda: yp(i))
                thunks[i]()
            except BaseException as e:
                st["err"] = e
            finally:
                self._tls.yp = None
                with cv:
                    st["alive"][i] = False
                    j = nxt(i)
                    st["turn"] = j if j is not None else -1
                    cv.notify_all()

        ths = [threading.Thread(target=runner, args=(i,)) for i in range(n)]
        for t in ths:
            t.start()
        for t in ths:
            t.join()
        if st["err"] is not None:
            raise st["err"]

    def _yield(self):
        yp = getattr(self._tls, "yp", None)
        if yp is not None:
            yp()

    def op(self, eng, fn, r=(), w=()):
        self._yield()
        if self.kstop is not None:
            self.kcount += 1
            if self.kcount > self.kstop:
                return None
        if RELAX:
            return self._op_relaxed(eng, fn, r, w)
        if eng != "pe":
            ex = [t for t in r if t.excl]
            if ex:
                r = [t for t in r if not t.excl]
                w = list(w) + ex
        waits = self._deps(eng, r, w)
        self.cnt[eng] += 1
        h = (("c", eng), self.cnt[eng])
        self.q[eng].append((fn, waits, (h[0], 1)))
        self._mark(h, r, w)
        return h

    def _op_relaxed(self, eng, fn, r, w):
        deps = {}
        me = ("c", eng)

        def add(h, same_ok):
            if h is None:
                return
            k, v = h
            if k == me and (eng in NOSELF or not same_ok):
                return
            if deps.get(k, 0) < v:
                deps[k] = v
        wset = set(id(t) for t in w)
        for t in r:
            if id(t) in wset:
                continue
            add(t.tw, True)
            if t.excl and eng != "pe":
                add(t.w, False)
        for t in w:
            add(t.tw, False)
            add(t.w, False)
            for h in t.rs:
                add(h, False)
        for t in r:
            if id(t) in wset:
                add(t.tw, True)
        waits = []
        kn = self.known[eng]
        for k, v in deps.items():
            if kn.get(k, 0) >= v:
                continue
            kn[k] = v
            waits.append((k, v))
        self.cnt[eng] += 1
        h = (me, self.cnt[eng])
        self.q[eng].append((fn, waits, (me, 1)))
        for t in r:
            if id(t) in wset:
                continue
            if t.excl and eng != "pe":
                t.w = h
                t.rs = []
            else:
                t.rs.append(h)
                if len(t.rs) > 64:
                    best = {}
                    for k, v in t.rs:
                        if best.get(k, 0) < v:
                            best[k] = v
                    t.rs = list(best.items())
        for t in w:
            t.w = h
            t.tw = h
            t.rs = []
        return h

    def dma(self, eng, fn, r=(), w=()):
        self._yield()
        s = self.dn[eng] % NDSLOT
        self.dn[eng] += 1
        waits = self._deps(eng, r, w)
        k = ("d", eng, s)
        prev = self.dcnt[(eng, s)]
        if prev > 0 and self.known[eng].get(k, 0) < prev:
            self.known[eng][k] = prev
            waits.append((k, prev))
        self.dcnt[(eng, s)] = prev + 16
        h = (k, prev + 16)
        self.q[eng].append((fn, waits, (k, 16)))
        self._mark(h, r, w)
        return h

    def barrier(self):
        hs = [(("c", e), self.cnt[e]) for e in ENGS if self.cnt[e] > 0]
        hs += [(("d", e, s), v) for (e, s), v in self.dcnt.items() if v > 0]
        for e in ENGS:
            self.wait_all(e, [h for h in hs if h[0] != ("c", e)])

    def wait_all(self, eng, hs):
        waits = []
        for k, v in hs:
            if self.known[eng].get(k, 0) < v:
                self.known[eng][k] = v
                waits.append((k, v))
        self.q[eng].append((None, waits, None))

    def emit(self):
        nc = self.nc
        sems = self.sems
        q = self.q

        def run(e, engobj):
            for fn, waits, inc in q[e]:
                for k, v in waits:
                    engobj.wait_ge(sems[k], v)
                if fn is None:
                    continue
                ins = fn(engobj)
                ins.then_inc(sems[inc[0]], inc[1])

        with nc.Block() as block:
            @block.tensor
            def _(eng):
                run("pe", eng)

            @block.scalar
            def _(eng):
                run("act", eng)

            @block.vector
            def _(eng):
                run("dve", eng)

            @block.gpsimd
            def _(eng):
                run("pool", eng)

            @block.sync
            def _(eng):
                run("sp", eng)


class Arena:
    def __init__(self, nc, stack, nbytes):
        self.t = stack.enter_context(nc.sbuf_tensor("arena", [128, nbytes // 4], F32))
        self.nbytes = nbytes
        self.off = 0

    def alloc(self, shape, dtype, at=None):
        esz = 2 if dtype == BF16 else 4
        n = int(np.prod(shape)) * esz
        n4 = (n + 3) // 4
        if at is None:
            at = self.off
            self.off += n4 * 4
        assert at % 4 == 0 and at + n4 * 4 <= self.nbytes, (at, n, self.nbytes)
        ap = self.t[:, at // 4: at // 4 + n4]
        if dtype != F32:
            ap = ap.bitcast(dtype)
        if len(shape) == 2:
            ap = ap.rearrange("p (a b) -> p a b", b=shape[1])
        elif len(shape) == 3:
            ap = ap.rearrange("p (a b c) -> p a b c", b=shape[1], c=shape[2])
        elif len(shape) == 4:
            ap = ap.rearrange("p (a b c d) -> p a b c d", b=shape[1], c=shape[2], d=shape[3])
        return ap


def _rope_tables():
    half = 8
    freqs = (10000.0 ** (-np.arange(half, dtype=np.float32) / half)).astype(np.float32)
    t = np.arange(2048)
    rows = (t // 64).astype(np.float32)
    cols = (t % 64).astype(np.float32)
    ang = np.concatenate([rows[:, None] * freqs[None], cols[:, None] * freqs[None]], axis=1)
    cos = np.ones((NT, 16), np.float32)
    sin = np.zeros((NT, 16), np.float32)
    cos[LAT0:] = np.cos(ang)
    sin[LAT0:] = np.sin(ang)
    cos = cos.reshape(NTT, 128, 16).transpose(1, 0, 2)
    sin = sin.reshape(NTT, 128, 16).transpose(1, 0, 2)
    return np.ascontiguousarray(cos), np.ascontiguousarray(sin)


def _fnet_consts():
    ci = np.arange(64)
    c64 = np.cos(2 * np.pi * np.outer(ci, ci) / 64.0)
    s64 = np.sin(2 * np.pi * np.outer(ci, ci) / 64.0)
    cs = np.zeros((2, 128, 512), np.float64)
    for j in range(2):
        for gl in range(2):
            g = 2 * j + gl
            cs[j, gl * 64:(gl + 1) * 64, g * 64:(g + 1) * 64] = c64
            cs[j, gl * 64:(gl + 1) * 64, 256 + g * 64:256 + (g + 1) * 64] = s64
    out = {"cs64": np.ascontiguousarray(cs.transpose(1, 0, 2)).astype(np.float32).astype(NPBF)}
    for L in (2048, 256):
        li = np.arange(L)
        m = np.outer(li, li) % L
        ang = 2 * np.pi * m / L
        sc = 1.0 / math.sqrt(L * 64.0)
        tab = np.stack([np.cos(ang) * sc, -np.sin(ang) * sc], axis=0)
        out["dft%d" % L] = tab.astype(np.float32).astype(NPBF)
    return out


def _s5_layouts(inp):
    f = np.float32
    out = {}
    m = np.arange(128)[:, None]
    t = np.arange(128)[None, :]
    tri = np.stack([(m <= t), (m >= t)], axis=1).astype(f)
    out["s5tri"] = tri.astype(NPBF)
    erow = np.stack([np.broadcast_to(t.astype(f), (128, 128)), np.broadcast_to(127.0 - t.astype(f), (128, 128))], axis=1)
    out["s5erow"] = np.ascontiguousarray(erow, f)
    p = np.arange(128, dtype=f)[:, None]
    out["s5ecol"] = np.ascontiguousarray(np.concatenate([p, 127.0 - p, -p, -(127.0 - p)], axis=1), f)
    out["s5d"] = np.ascontiguousarray(np.asarray(inp["s5_d"], f).reshape(DEPTH, 2, 128).transpose(0, 2, 1))
    re = np.asarray(inp["s5_lam_re"], f).reshape(DEPTH, 2, 1024)
    im = np.asarray(inp["s5_lam_im"], f).reshape(DEPTH, 2, 1024)
    ls = np.repeat(np.asarray(inp["s5_log_step"], f), 64, axis=-1)
    trip = np.stack([re, im, ls], axis=2)
    out["s5pp"] = np.ascontiguousarray(trip.reshape(DEPTH, 2, 3, 8, 128).transpose(0, 1, 4, 2, 3))
    out["s5row"] = np.ascontiguousarray(np.broadcast_to(trip[:, :, None], (DEPTH, 2, 128, 3, 1024)), f)
    bre = np.asarray(inp["s5_b_re"], f)
    bim = np.asarray(inp["s5_b_im"], f)
    sb = np.zeros((DEPTH, 2, 128, 2, 2, 512), f)
    for ri, bb in enumerate((bre, bim)):
        for kt in range(2):
            for gl in range(8):
                g = 8 * kt + gl
                sb[:, :, gl * 16:(gl + 1) * 16, kt, ri, gl * 64:(gl + 1) * 64] = bb[:, :, g].transpose(0, 1, 3, 2)
    out["s5b"] = sb
    cre = np.asarray(inp["s5_c_re"], f)
    cim = np.asarray(inp["s5_c_im"], f)
    sc = np.zeros((DEPTH, 2, 2, 8, 128, 128), f)
    for ri, cm in enumerate((cre, cim)):
        for st in range(8):
            for gl in range(2):
                g = 2 * st + gl
                col = (g % 8) * 16
                sc[:, :, ri, st, gl * 64:(gl + 1) * 64, col:col + 16] = cm[:, :, g].transpose(0, 1, 3, 2)
    out["s5c"] = sc
    out["s5_w_glu"] = np.ascontiguousarray(inp["s5_w_glu"], f)
    id2 = np.zeros((128, 16), f)
    for j_ in range(16):
        id2[j_, j_] = 1.0
        id2[32 + j_, j_] = 1.0
    out["s5id2"] = id2.astype(NPBF)
    return out


def _ret_consts():
    half = 32
    freqs = (10000.0 ** (-np.arange(half, dtype=np.float32) / half)).astype(np.float32)
    pos = np.arange(2048, dtype=np.float32)
    ang = pos[:, None] * freqs[None]
    cos = np.ones((NT, 32), np.float32)
    sin = np.zeros((NT, 32), np.float32)
    cos[LAT0:] = np.cos(ang)
    sin[LAT0:] = np.sin(ang)
    tm = lambda a: np.ascontiguousarray(a.reshape(NTT, 128, 32).transpose(1, 0, 2))
    out = {"rrc": tm(cos), "rrs": tm(sin)}
    k = np.arange(128, dtype=np.float32)[:, None]
    q = np.arange(128, dtype=np.float32)[None, :]
    retc = np.stack([np.broadcast_to(q + 1.0, (128, 128)), np.broadcast_to(128.0 - q, (128, 128)),
                     np.maximum(q - k, 0.0), np.maximum(k - q, 0.0),
                     (q >= k).astype(np.float32), (k >= q).astype(np.float32)], axis=1)
    out["retc"] = np.ascontiguousarray(retc, np.float32)
    out["retp"] = np.ascontiguousarray(np.concatenate([k, 127.0 - k], axis=1), np.float32)
    return out


def prep_common(inp):
    f = np.float32
    c = {}
    c["ada_w"] = np.ascontiguousarray(inp["ada_w"], f)
    c["ada_bT"] = np.ascontiguousarray(inp["ada_b"].reshape(DEPTH, 48, 128).transpose(0, 2, 1), f)
    nw = np.concatenate([inp["norm_mix_w"].reshape(DEPTH, 8, 128), inp["norm_ffn_w"].reshape(DEPTH, 8, 128)], axis=1)
    c["nw"] = np.ascontiguousarray(nw.transpose(0, 2, 1), f)
    c["w_in"] = np.ascontiguousarray(inp["w_in"], f)
    bc = lambda v: np.ascontiguousarray(np.broadcast_to(v[:, None, :], (DEPTH, 128, v.shape[-1])), f)
    c["kvw"] = bc(inp["mla_kv_norm"])
    c["qnw"] = bc(inp["mla_q_norm"])
    c["qkq"] = bc(np.tile(inp["mla_qk_norm_q"], (1, 4)))
    c["qkk"] = bc(np.tile(inp["mla_qk_norm_k"], (1, 4)))
    c["w_ukv"] = np.ascontiguousarray(inp["mla_w_ukv"], f)
    c["w_uq"] = np.ascontiguousarray(inp["mla_w_uq"], f)
    cos, sin = _rope_tables()
    c["ropec"] = cos
    c["ropes"] = sin
    c["w_branch"] = np.ascontiguousarray(inp["w_branch"], f)
    c["w_out"] = np.ascontiguousarray(inp["w_out"], f)
    c["ffn_w1"] = np.ascontiguousarray(inp["ffn_w1"], f)
    c["ffn_w2"] = np.ascontiguousarray(inp["ffn_w2"], f)
    c.update(_fnet_consts())
    c.update(_ret_consts())
    c.update(_s5_layouts(inp))
    lg = np.asarray(inp["ret_decay_logit"], f)
    c["rlog_row"] = np.ascontiguousarray(np.broadcast_to(lg.reshape(DEPTH, 1, 8), (DEPTH, 128, 8)), f)
    pp = np.zeros((DEPTH, 128, 2, 2), f)
    for pair in range(2):
        for d_ in range(2):
            pp[:, 0:64, pair, d_] = lg[:, d_, 2 * pair][:, None]
            pp[:, 64:128, pair, d_] = lg[:, d_, 2 * pair + 1][:, None]
    c["rlog_pp"] = pp
    c["gnw"] = bc(inp["ret_gn_w"])
    c["identb"] = np.eye(128, dtype=f).astype(NPBF)
    c["identf"] = np.eye(128, dtype=f)
    return c


def prep_core(inp, b):
    f = np.float32
    d = {}
    xt = np.concatenate([inp["ctx"][b], inp["x"][b]], axis=0).T
    d["xT"] = np.ascontiguousarray(xt, f)
    d["cT"] = np.ascontiguousarray(np.stack([inp["c"][b], inp["c_ctx"]], axis=1), f)
    return d


def build(specs, stage=99, taps=(), mixers=("mla", "s5", "ret", "fnet")):
    nc = bass.Bass("TRN2", target_bir_lowering=False)
    dr = {}
    for name, (shape, dt) in specs.items():
        dr[name] = nc.dram_tensor(name, list(shape), dt, kind="ExternalInput").ap()
    outT = nc.dram_tensor("outT", [D, 2048], F32, kind="ExternalOutput").ap()
    tapd = {}
    for name, shape, dt in taps:
        tapd[name] = nc.dram_tensor("tap_" + name, list(shape), dt, kind="ExternalOutput").ap()

    with contextlib.ExitStack() as st:
        P = Prog(nc, st)
        A = Arena(nc, st, 206 * 1024)
        ps = [st.enter_context(nc.psum_tensor("ps%d" % i, [128, 512], F32)) for i in range(8)]
        tps = [T("ps%d" % i, excl=True) for i in range(8)]
        out_handles = []

        def mm(out, lhsT, rhs, start=True, stop=True, r=(), w=()):
            return P.op("pe", lambda e: e.matmul(out, lhsT=lhsT, rhs=rhs, start=start, stop=stop,
                                                 skip_group_check=True), r, w)

        def tr(out, in_, ident, r=(), w=()):
            return P.op("pe", lambda e: e.transpose(out=out, in_=in_, identity=ident), r, w)

        def act(out, in_, func, r=(), w=(), scale=1.0, bias=0.0, accum=None):
            if accum is None:
                return P.op("act", lambda e: e.activation(out=out, in_=in_, func=func, scale=scale, bias=bias), r, w)
            return P.op("act", lambda e: e.activation(out=out, in_=in_, func=func, scale=scale, bias=bias,
                                                      accum_out=accum), r, w)

        def tt(eng, out, a, b, op, r=(), w=()):
            return P.op(eng, lambda e: e.tensor_tensor(out=out, in0=a, in1=b, op=op), r, w)

        def ts(eng, out, a, s1, s2, op0, op1=None, r=(), w=()):
            if op1 is None:
                return P.op(eng, lambda e: e.tensor_scalar(out=out, in0=a, scalar1=s1, scalar2=None, op0=op0), r, w)
            return P.op(eng, lambda e: e.tensor_scalar(out=out, in0=a, scalar1=s1, scalar2=s2, op0=op0, op1=op1), r, w)

        def stt(out, a, s, b, op0, op1, r=(), w=()):
            return P.op("dve", lambda e: e.scalar_tensor_tensor(out=out, in0=a, scalar=s, in1=b, op0=op0, op1=op1), r, w)

        def cp(eng, out, in_, r=(), w=()):
            if eng == "act":
                return P.op(eng, lambda e: e.activation(out=out, in_=in_, func=AF.Copy), r, w)
            return P.op(eng, lambda e: e.tensor_copy(out=out, in_=in_), r, w)

        def red(out, in_, r=(), w=()):
            return P.op("dve", lambda e: e.tensor_reduce(out=out, in_=in_, axis=AX.X, op=ALU.add), r, w)

        def rcp(out, in_, r=(), w=()):
            return P.op("dve", lambda e: e.reciprocal(out=out, in_=in_), r, w)

        def mset(eng, out, val, w=()):
            return P.op(eng, lambda e: e.memset(out, val), (), w)

        def dma(out, in_, r=(), w=(), q="sp"):
            return P.dma(q, lambda e: e.dma_start(out=out, in_=in_), r, w)

        def tap(name, src, r):
            if name in tapd:
                out_handles.append(dma(tapd[name], src, r=r))

        xT = A.alloc([8, NT], F32)
        hT = A.alloc([8, NT], BF16)
        t_x = [[T("x%d_%d" % (k, g)) for g in range(5)] for k in range(8)]
        t_h = [T("h%d" % g) for g in range(5)]
        identb = A.alloc([128], BF16)
        identf = A.alloc([128], F32)
        onesb = A.alloc([128], BF16)
        mod = A.alloc([DEPTH, 48, 2], F32)
        a1 = A.alloc([DEPTH, 16, 2], F32)
        nwt = A.alloc([DEPTH, 16], F32)
        scT = A.alloc([8, 2], F32)
        eps_t = A.alloc([1], F32)
        t_const = T("const")
        t_mod = T("mod")
        NSTG = 2
        NRING = 5
        stg = [A.alloc([1024], F32) for _ in range(NSTG)]
        t_stg = [T("stg%d" % i) for i in range(NSTG)]
        ring = [A.alloc([1024], BF16) for _ in range(NRING)]
        t_ring = [T("ring%d" % i) for i in range(NRING)]
        sidx = [0]
        ridx = [0]
        DYN0 = A.off
        OT0 = A.nbytes - 4 * 2 * NT * 2
        oT = [A.alloc([2, NT], BF16, at=OT0 + i * 2 * NT * 2) for i in range(4)]
        t_oT = [[T("o%d_%d" % (i, g)) for g in range(5)] for i in range(4)]

        def GRP(g):
            return (0, 256) if g == 0 else (LAT0 + 512 * (g - 1), 512)

        def grp_of_tile(tt_):
            return 0 if tt_ < 2 else 1 + (tt_ - 2) // 4

        def next_stg():
            s = sidx[0] % NSTG
            sidx[0] += 1
            return s

        def load_cast(dst, t_dst, src, shape_str=None, **kw):
            s = next_stg()
            n = int(np.prod(src.shape[1:]))
            assert n <= 1024, n
            sv = stg[s][:, 0:n]
            if shape_str is not None:
                sv = sv.rearrange(shape_str, **kw)
            dma(sv, src, w=[t_stg[s]])
            cp("pool", dst, sv, r=[t_stg[s]], w=[t_dst])

        def load_ring(src, shape_str=None, **kw):
            i = ridx[0] % NRING
            ridx[0] += 1
            n = int(np.prod(src.shape[1:]))
            dv = ring[i][:, 0:n]
            if shape_str is not None:
                dv = dv.rearrange(shape_str, **kw)
            load_cast(dv, t_ring[i], src, shape_str, **kw)
            return dv, t_ring[i]

        xsrc = dr["xT"].rearrange("(k p) t -> p k t", p=128)
        for k in range(8):
            dma(xT[:, k, :], xsrc[:, k, :], w=t_x[k])
        dma(identb, dr["identb"], w=[t_const])
        dma(identf, dr["identf"], w=[t_const])
        dma(scT, dr["cT"].rearrange("(k p) j -> p k j", p=128), w=[t_const])
        dma(nwt, dr["nw"].rearrange("l p k -> p l k"), w=[t_const])
        mset("dve", onesb, 1.0, w=[t_const])
        mset("dve", eps_t, EPS, w=[t_const])
        act(scT, scT, AF.Silu, r=[t_const], w=[t_const])

        for l in range(DEPTH):
            P.barrier()
            bT = A.alloc([48], F32, at=DYN0)
            modrow = A.alloc([6144], F32, at=DYN0 + 256)
            t_bT = T()
            t_mr = T()
            dma(bT, dr["ada_bT"][l], w=[t_bT])
            wsrc = dr["ada_w"][l].rearrange("(k p) c -> p k c", p=128)
            for nchunk in range(12):
                pb = nchunk % 2
                for j in range(4):
                    s = next_stg()
                    sv = stg[s][:, 0:1024].rearrange("p (k c) -> p k c", c=512)
                    dma(sv, wsrc[:, 2 * j:2 * j + 2, nchunk * 512:(nchunk + 1) * 512], w=[t_stg[s]])
                    for kk in range(2):
                        k = 2 * j + kk
                        mm(ps[pb][0:2, :], scT[:, k, :], sv[:, kk, :], start=(k == 0), stop=(k == 7),
                           r=[t_stg[s], t_const], w=[tps[pb]])
                cp("act", modrow[0:2, nchunk * 512:(nchunk + 1) * 512], ps[pb][0:2, :], r=[tps[pb]], w=[t_mr])
            psm = ps[2 + l][:, 0:96].rearrange("p (c s) -> p c s", s=2)
            for ct in range(48):
                tr(psm[:, ct, :], modrow[0:2, ct * 128:(ct + 1) * 128], identf[0:2, 0:2], r=[t_mr, t_const], w=[tps[2 + l]])
            tt("dve", mod[:, l, :, :], psm, bT[:, :].unsqueeze(2).to_broadcast([128, 48, 2]), ALU.add,
               r=[tps[2 + l], t_bT], w=[t_mod])
            for j, c0 in ((0, 8), (1, 32)):
                stt(a1[:, l, j * 8:(j + 1) * 8, :], mod[:, l, c0:c0 + 8, :], 1.0,
                    nwt[:, l, j * 8:(j + 1) * 8].unsqueeze(2).to_broadcast([128, 8, 2]),
                    ALU.add, ALU.mult, r=[t_mod, t_const], w=[t_mod])
        tap("mod", mod, [t_mod])

        def norm_phase(l, which, groups):
            sh0 = 0 if which == 0 else 24
            P.barrier()
            sq = A.alloc([2, 8, 512], BF16, at=DYN0)
            rstd = A.alloc([2, 512], F32, at=DYN0 + 2 * 8 * 512 * 2)
            tmp = A.alloc([2, 512], F32, at=DYN0 + 2 * 8 * 512 * 2 + 2 * 512 * 4)
            t_sq = [T(), T()]
            t_rs = [T(), T()]
            t_tmp = [T(), T()]
            for gi, g in enumerate(groups):
                t0, n = GRP(g)
                s = 1 if g == 0 else 0
                b = gi % 2
                pb = 2 + b
                for k in range(8):
                    act(sq[:, b, k, 0:n], xT[:, k, t0:t0 + n], AF.Square, r=[t_x[k][g]], w=[t_sq[b]])
                for k in range(8):
                    mm(ps[pb][:, 0:n], onesb, sq[:, b, k, 0:n], start=(k == 0), stop=(k == 7),
                       r=[t_sq[b], t_const], w=[tps[pb]])
                act(rstd[:, b, 0:n], ps[pb][:, 0:n], AF.Sqrt, r=[tps[pb], t_const], w=[t_rs[b]],
                    scale=1.0 / D, bias=eps_t[:, 0:1])
                rcp(rstd[:, b, 0:n], rstd[:, b, 0:n], r=[t_rs[b]], w=[t_rs[b]])
                for k in range(8):
                    tb = k % 2
                    tt("dve", tmp[:, tb, 0:n], xT[:, k, t0:t0 + n], rstd[:, b, 0:n], ALU.mult,
                       r=[t_x[k][g], t_rs[b]], w=[t_tmp[tb]])
                    act(hT[:, k, t0:t0 + n], tmp[:, tb, 0:n], AF.Identity, r=[t_tmp[tb], t_mod], w=[t_h[g]],
                        scale=a1[:, l, which * 8 + k, s:s + 1], bias=mod[:, l, sh0 + k, s:s + 1])

        def mk_alloc(limit):
            o = [DYN0]

            def al(shape, dt):
                ap = A.alloc(shape, dt, at=o[0])
                n = int(np.prod(shape)) * (2 if dt == BF16 else 4)
                o[0] += (n + 3) // 4 * 4
                assert o[0] <= limit, (o[0], limit)
                return ap
            al.o = o
            return al

        def mla_phase(l, last, slot):
            P.barrier()
            al = mk_alloc(OT0 + slot * 2 * NT * 2)
            Wm = al([8, 416], BF16)
            Wukv = al([512], BF16)
            Wuq = al([2, 384], BF16)
            kvw = al([128], F32)
            qnw = al([256], F32)
            qkq = al([4, 96], F32)
            qkk = al([4, 96], F32)
            rc = al([NTT, 2, 8], F32)
            rs_ = al([NTT, 2, 8], F32)
            KT = al([4, NT], BF16)
            QT = al([4, 512], BF16)
            V1 = al([NTT, 4, 65], BF16)
            PT = al([2, 512], BF16)
            kvn = al([2, 128], BF16)
            kvnT = al([2, 128], BF16)
            qn = al([2, 256], BF16)
            qnT = al([2, 2, 128], BF16)
            kf = al([2, 4, 96], F32)
            sqs2 = al([2, 4, 96], F32)
            kb = al([2, 4, 96], BF16)
            sm = al([2, 16], F32)
            rt2 = al([2, 4, 4, 2, 8], F32) if False else None
            rtA = al([4, 4, 2, 8], F32)
            rtB = al([4, 4, 2, 8], F32)
            oat = al([4, 256], BF16)
            rinv = al([4], F32)
            t_W = T("Wm")
            t_small = T("mlasmall")
            t_KT = [T() for _ in range(NTT)]
            t_QT = [T() for _ in range(4)]
            t_V1 = [T() for _ in range(NTT)]
            t_PT = [T(), T()]
            t_kvn = [T(), T()]
            t_kvnT = [T(), T()]
            t_qn = [T(), T()]
            t_qnT = [T(), T()]
            t_kf = [T(), T()]
            t_sqs2 = [T(), T()]
            t_kb = [T(), T()]
            t_sm = [T(), T()]
            t_rt2 = [T(), T()]
            t_oat = [T() for _ in range(4)]
            t_rinv = T()

            wsrc = dr["w_in"][l].rearrange("(k p) c -> p k c", p=128)
            for k in range(0, 8, 2):
                load_cast(Wm[:, k:k + 2, 0:160], t_W, wsrc[:, k:k + 2, 0:160], "p (k c) -> p k c", c=160)
                load_cast(Wm[:, k:k + 2, 160:416], t_W, wsrc[:, k:k + 2, 928:1184], "p (k c) -> p k c", c=256)
            load_cast(Wukv, t_W, dr["w_ukv"][l])
            load_cast(Wuq, t_W, dr["w_uq"][l].rearrange("(j p) c -> p j c", p=128), "p (j c) -> p j c", c=384)
            dma(kvw, dr["kvw"][l], w=[t_small])
            dma(qnw, dr["qnw"][l], w=[t_small])
            dma(qkq, dr["qkq"][l].rearrange("p (h d) -> p h d", d=96), w=[t_small])
            dma(qkk, dr["qkk"][l].rearrange("p (h d) -> p h d", d=96), w=[t_small])
            dma(rc, dr["ropec"].rearrange("p t (a b) -> p t a b", b=8), w=[t_small])
            dma(rs_, dr["ropes"].rearrange("p t (a b) -> p t a b", b=8), w=[t_small])
            mset("dve", V1[:, :, :, 64:65], 1.0, w=t_V1)
            if stage < 0.5:
                return

            def headnorm_rope(pb, wq, tt_, dst, t_dst, tbank):
                x = kf[:, pb]
                t_x_ = t_kf[pb]
                sqs = sqs2[:, pb]
                t_sqs = t_sqs2[pb]
                rt = rtA if pb == 0 else rtB
                t_rt = t_rt2[pb]
                tt("dve", sqs, x, x, ALU.mult, r=[t_x_], w=[t_sqs])
                st_ = sm[:, pb, 0:4]
                red(st_, sqs, r=[t_sqs], w=[t_sm[pb]])
                act(st_, st_, AF.Sqrt, r=[t_sm[pb], t_const], w=[t_sm[pb]], scale=1.0 / 96, bias=eps_t[:, 0:1])
                rcp(st_, st_, r=[t_sm[pb]], w=[t_sm[pb]])
                tt("dve", x, x, st_.unsqueeze(2).to_broadcast([128, 4, 96]), ALU.mult, r=[t_x_, t_sm[pb]], w=[t_x_])
                tt("dve", x, x, wq, ALU.mult, r=[t_x_, t_small], w=[t_x_])
                y = kb[:, pb]
                t_y = t_kb[pb]
                cp("dve", y[:, :, 0:64], x[:, :, 0:64], r=[t_x_], w=[t_y])
                xr = x[:, :, 64:96].rearrange("p h (a two b) -> p h a two b", two=2, b=8)
                yr = y[:, :, 64:96].rearrange("p h (a two b) -> p h a two b", two=2, b=8)
                cosb = rc[:, tt_].unsqueeze(1).to_broadcast([128, 4, 2, 8])
                sinb = rs_[:, tt_].unsqueeze(1).to_broadcast([128, 4, 2, 8])
                x1 = xr[:, :, :, 0, :]
                x2 = xr[:, :, :, 1, :]
                tt("dve", rt[:, 0], x1, cosb, ALU.mult, r=[t_x_, t_small], w=[t_rt])
                tt("dve", rt[:, 1], x2, sinb, ALU.mult, r=[t_x_, t_small], w=[t_rt])
                tt("dve", yr[:, :, :, 0, :], rt[:, 0], rt[:, 1], ALU.subtract, r=[t_rt], w=[t_y])
                tt("dve", rt[:, 2], x1, sinb, ALU.mult, r=[t_x_, t_small], w=[t_rt])
                tt("dve", rt[:, 3], x2, cosb, ALU.mult, r=[t_x_, t_small], w=[t_rt])
                tt("dve", yr[:, :, :, 1, :], rt[:, 2], rt[:, 3], ALU.add, r=[t_rt], w=[t_y])
                pst = ps[tbank][:, :].bitcast(BF16)
                for h in range(4):
                    tr(pst[0:96, h * 128:(h + 1) * 128], y[:, h, :], identb, r=[t_y, t_const], w=[tps[tbank]])
                cp("act", dst, pst[0:96, 0:512].rearrange("p (h t) -> p h t", t=128), r=[tps[tbank]], w=[t_dst])

            if os.environ.get("KSTOP"):
                P.kstop = int(os.environ["KSTOP"])
            def _kbody(tt_):
                g = grp_of_tile(tt_)
                pb = tt_ % 2
                t0 = tt_ * 128
                bz = pb
                for k in range(8):
                    mm(ps[bz][:, 0:160], hT[:, k, t0:t0 + 128], Wm[:, k, 0:160], start=(k == 0), stop=(k == 7),
                       r=[t_h[g], t_W], w=[tps[bz]])
                z = ps[bz]
                ssk = sm[:, pb, 8:9]
                act(kf[:, pb].rearrange("p h d -> p (h d)")[:, 0:128], z[:, 0:128], AF.Square,
                    r=[tps[bz]], w=[t_kf[pb], t_sm[pb]], accum=ssk)
                act(ssk, ssk, AF.Sqrt, r=[t_sm[pb], t_const], w=[t_sm[pb]], scale=1.0 / 128, bias=eps_t[:, 0:1])
                rcp(ssk, ssk, r=[t_sm[pb]], w=[t_sm[pb]])
                stt(kvn[:, pb], z[:, 0:128], ssk, kvw, ALU.mult, ALU.mult, r=[tps[bz], t_sm[pb], t_small], w=[t_kvn[pb]])
                pst = ps[2 + pb][:, :].bitcast(BF16)
                tr(pst[:, 0:128], kvn[:, pb], identb, r=[t_kvn[pb], t_const], w=[tps[2 + pb]])
                cp("act", kvnT[:, pb], pst[:, 0:128], r=[tps[2 + pb]], w=[t_kvnT[pb]])
                bkv = 4 + pb
                mm(ps[bkv][:, :], kvnT[:, pb], Wukv, r=[t_kvnT[pb], t_W], w=[tps[bkv]])
                kvv = ps[bkv][:, :].rearrange("p (h c) -> p h c", c=128)
                cp("act", V1[:, tt_, :, 0:64], kvv[:, :, 64:128], r=[tps[bkv]], w=[t_V1[tt_]])
                cp("dve", kf[:, pb, :, 0:64], kvv[:, :, 0:64], r=[tps[bkv]], w=[t_kf[pb]])
                cp("dve", kf[:, pb, :, 64:96], z[:, 128:160].unsqueeze(1).to_broadcast([128, 4, 32]),
                   r=[tps[bz]], w=[t_kf[pb]])
                if os.environ.get("NOHN") != "1":
                    headnorm_rope(pb, qkk, tt_, KT[0:96, :, t0:t0 + 128], t_KT[tt_], 2 + pb)
            for tt_ in range(0, NTT, 2):
                P.interleave([(lambda a=tt_: _kbody(a)), (lambda a=tt_ + 1: _kbody(a))])
            P.kstop = None
            tap("KT", KT[0:96], t_KT)
            tap("V1", V1, t_V1)
            if stage < 0.7:
                return

            scale = 96 ** -0.5
            qgroups = [(g, GRP(g)[0], GRP(g)[1], list(range(NTT))) for g in range(1, 5)]
            if not last:
                qgroups = [(0, 0, 256, [0, 1])] + qgroups
            it = 0
            for (g, q0, nq, ktiles) in qgroups:
                nqs = nq // 128

                def _qbody(qs, g=g, q0=q0):
                    tt_ = q0 // 128 + qs
                    pb = qs % 2
                    t0 = tt_ * 128
                    bz = pb
                    for k in range(8):
                        mm(ps[bz][:, 0:256], hT[:, k, t0:t0 + 128], Wm[:, k, 160:416], start=(k == 0), stop=(k == 7),
                           r=[t_h[g], t_W], w=[tps[bz]])
                    z = ps[bz]
                    ssq = sm[:, pb, 9:10]
                    act(qn[:, pb], z[:, 0:256], AF.Square, r=[tps[bz]], w=[t_qn[pb], t_sm[pb]], accum=ssq)
                    act(ssq, ssq, AF.Sqrt, r=[t_sm[pb], t_const], w=[t_sm[pb]], scale=1.0 / 256, bias=eps_t[:, 0:1])
                    rcp(ssq, ssq, r=[t_sm[pb]], w=[t_sm[pb]])
                    stt(qn[:, pb], z[:, 0:256], ssq, qnw, ALU.mult, ALU.mult, r=[tps[bz], t_sm[pb], t_small], w=[t_qn[pb]])
                    pst = ps[2 + pb][:, :].bitcast(BF16)
                    for j in range(2):
                        tr(pst[:, j * 128:(j + 1) * 128], qn[:, pb, j * 128:(j + 1) * 128], identb,
                           r=[t_qn[pb], t_const], w=[tps[2 + pb]])
                    cp("act", qnT[:, pb], pst[:, 0:256].rearrange("p (j t) -> p j t", t=128), r=[tps[2 + pb]], w=[t_qnT[pb]])
                    bq = 2 + pb
                    for j in range(2):
                        mm(ps[bq][:, 0:384], qnT[:, pb, j], Wuq[:, j, :], start=(j == 0), stop=(j == 1),
                           r=[t_qnT[pb], t_W], w=[tps[bq]])
                    cp("dve", kf[:, pb], ps[bq][:, 0:384].rearrange("p (h d) -> p h d", d=96), r=[tps[bq]], w=[t_kf[pb]])
                    headnorm_rope(pb, qkq, tt_, QT[0:96, :, qs * 128:(qs + 1) * 128], t_QT[qs], 2 + pb)
                for qs in range(0, nqs, 2):
                    P.interleave([(lambda a=qs: _qbody(a)), (lambda a=qs + 1: _qbody(a))])
                if g == 1:
                    tap("QT", QT[0:96], t_QT)
                if stage < 0.8:
                    continue
                for h in range(4):
                    for ki, kt in enumerate(ktiles):
                        sb = it % 2
                        it += 1
                        mm(ps[sb][:, 0:nq], KT[0:96, h, kt * 128:(kt + 1) * 128], QT[0:96, h, 0:nq],
                           r=[t_KT[kt]] + t_QT[0:nqs], w=[tps[sb]])
                        act(PT[:, sb, 0:nq], ps[sb][:, 0:nq], AF.Exp, r=[tps[sb]], w=[t_PT[sb]], scale=scale)
                        for qs in range(nqs):
                            mm(ps[4 + qs][:, 0:65], PT[:, sb, qs * 128:(qs + 1) * 128], V1[:, kt, h, :],
                               start=(ki == 0), stop=(ki == len(ktiles) - 1),
                               r=[t_PT[sb], t_V1[kt]], w=[tps[4 + qs]])
                    for qs in range(nqs):
                        rcp(rinv[:, qs:qs + 1], ps[4 + qs][:, 64:65], r=[tps[4 + qs]], w=[t_rinv])
                        ts("dve", oat[:, qs, h * 64:(h + 1) * 64], ps[4 + qs][:, 0:64], rinv[:, qs:qs + 1], None,
                           ALU.mult, r=[tps[4 + qs], t_rinv], w=[t_oat[qs]])
                for qs in range(nqs):
                    tb = 2 + qs % 2
                    pst = ps[tb][:, :].bitcast(BF16)
                    for j in range(2):
                        tr(pst[:, j * 128:(j + 1) * 128], oat[:, qs, j * 128:(j + 1) * 128], identb,
                           r=[t_oat[qs], t_const], w=[tps[tb]])
                    tq = q0 + qs * 128
                    cp("act", oT[slot][:, :, tq:tq + 128], pst[:, 0:256].rearrange("p (j t) -> p j t", t=128),
                       r=[tps[tb]], w=[t_oT[slot][g]])
            tap("oaT", oT[slot], t_oT[slot])

        def fnet_phase(l, last, slot):
            P.barrier()
            al = mk_alloc(OT0 + slot * 2 * NT * 2)
            UT = al([2, NT], BF16)
            AB = al([NTT, 512], BF16)
            CS = al([2, 512], BF16)
            tb = al([2, 2, 1024], BF16)
            c256 = al([2, 2, 256], BF16)
            t_UT = [T() for _ in range(5)]
            t_AB = [T() for _ in range(NTT)]
            t_CS = T()
            t_tb = [T(), T()]
            t_c256 = T()
            dma(CS, dr["cs64"], w=[t_CS])
            for lt in range(2):
                dma(c256[:, lt], dr["dft256"][:, lt * 128:(lt + 1) * 128, :].rearrange("c p n -> p c n"), w=[t_c256])
            wsrc = dr["w_in"][l].rearrange("(k p) c -> p k c", p=128)
            groups = [g for g in range(5) if not (last and g == 0)]
            bi = 0
            for j in range(2):
                wv, t_wv = load_ring(wsrc[:, :, 1184 + j * 128:1184 + (j + 1) * 128], "p (k c) -> p k c", c=128)
                for g in groups:
                    t0, n = GRP(g)
                    pb = bi % 4
                    bi += 1
                    for k in range(8):
                        mm(ps[pb][:, 0:n], wv[:, k, :], hT[:, k, t0:t0 + n], start=(k == 0), stop=(k == 7),
                           r=[t_wv, t_h[g]], w=[tps[pb]])
                    cp("act", UT[:, j, t0:t0 + n], ps[pb][:, 0:n], r=[tps[pb]], w=[t_UT[g]])
            for tt_ in (range(2, NTT) if last else range(NTT)):
                g = grp_of_tile(tt_)
                pb = 4 + tt_ % 4
                for j in range(2):
                    mm(ps[pb][:, :], UT[:, j, tt_ * 128:(tt_ + 1) * 128], CS[:, j, :], start=(j == 0), stop=(j == 1),
                       r=[t_UT[g], t_CS], w=[tps[pb]])
                cp("dve", AB[:, tt_, :], ps[pb][:, :], r=[tps[pb]], w=[t_AB[tt_]])
            if not last:
                for j in range(2):
                    pb = j
                    n_mm = 0
                    for lt in range(2):
                        for cs_ in range(2):
                            mm(ps[pb][:, 0:256], AB[:, lt, cs_ * 256 + j * 128:cs_ * 256 + (j + 1) * 128],
                               c256[:, lt, cs_, :], start=(n_mm == 0), stop=(n_mm == 3),
                               r=[t_AB[lt], t_c256], w=[tps[pb]])
                            n_mm += 1
                    cp("act", oT[slot][:, j, 0:256], ps[pb][:, 0:256], r=[tps[pb]], w=[t_oT[slot][0]])
            it = 0
            for half in range(2):
                banks = [4 * half + i for i in range(4)]
                for lt in range(16):
                    b_ = it % 2
                    it += 1
                    dma(tb[:, b_], dr["dft2048"][:, lt * 128:(lt + 1) * 128, half * 1024:(half + 1) * 1024]
                        .rearrange("c p n -> p c n"), w=[t_tb[b_]])
                    for j in range(2):
                        for cs_ in range(2):
                            for lg in range(2):
                                bk = banks[j * 2 + lg]
                                mm(ps[bk][:, :], AB[:, 2 + lt, cs_ * 256 + j * 128:cs_ * 256 + (j + 1) * 128],
                                   tb[:, b_, cs_, lg * 512:(lg + 1) * 512],
                                   start=(lt == 0 and cs_ == 0), stop=(lt == 15 and cs_ == 1),
                                   r=[t_AB[2 + lt], t_tb[b_]], w=[tps[bk]])
                for j in range(2):
                    for lg in range(2):
                        bk = banks[j * 2 + lg]
                        g = 1 + half * 2 + lg
                        t0 = LAT0 + half * 1024 + lg * 512
                        cp("act", oT[slot][:, j, t0:t0 + 512], ps[bk][:, :], r=[tps[bk]], w=[t_oT[slot][g]])
            tap("obT", oT[slot], t_oT[slot])

        def ret_phase(l, last, slot):
            P.barrier()
            al = mk_alloc(OT0 + slot * 2 * NT * 2)
            rrc = al([NTT, 32], F32)
            rrs = al([NTT, 32], F32)
            retc = al([6, 128], F32)
            retp = al([2], F32)
            lgrow = al([8], F32)
            lgpp = al([2, 2], F32)
            gnw = al([256], F32)
            Wr = al([8, 512], BF16)
            QZ = al([2, NT], BF16)
            KT_ = al([NT], BF16)
            Vb = al([NTT, 128], BF16)
            sg = al([NTT, 128], BF16)
            Sfp = al([NTT, 128], BF16)
            Sbn = Wr.rearrange("p k c -> p (k c)")[:, 0:NTT * 128].rearrange("p (i c) -> p i c", c=128)
            ub_off = al.o[0]
            Ub = al([NTT, 128], BF16)
            kdec = al([2, 128], F32)
            qdec = al([2, 128], F32)
            Mk = al([2, 128], F32)
            aab = al([2], F32)
            Sf = al([128], F32)
            Sb = al([128], F32)
            rt = al([2, 4, 2, 32], F32)
            Qb = al([2, 128], BF16)
            Kb = al([2, 128], BF16)
            Kd = al([2, 2, 128], BF16)
            Qd = A.alloc([2, 2, 2, 128], BF16, at=ub_off)
            attm = A.alloc([2, 2, 128], BF16, at=ub_off + 2048)
            xc2 = al([2, 2, 64], F32)
            xcq = xc2[:, 0].rearrange("p a d -> p (a d)")
            sq2 = al([2, 2, 64], F32)
            st2 = al([2, 2, 4], F32)
            od = A.alloc([2, 128], BF16, at=ub_off + 3072)
            t_c = T()
            t_lg = T()
            t_Wr = T()
            t_QK = [T() for _ in range(NTT)]
            t_Vb = [T() for _ in range(NTT)]
            t_sg = [T() for _ in range(NTT)]
            t_Sfp = [T() for _ in range(NTT)]
            t_Sbn = [T() for _ in range(NTT)]
            t_Ub = [T() for _ in range(NTT)]
            t_tab = T()
            t_S = T()
            t_rt = [T(), T()]
            t_Qb = [T(), T()]
            t_Kb = [T(), T()]
            t_Kd = [T(), T()]
            t_Qd = [T(), T()]
            t_attm = [T(), T()]
            t_gn2 = [T(), T()]
            t_od = [T(), T()]
            dma(rrc, dr["rrc"], w=[t_c])
            dma(rrs, dr["rrs"], w=[t_c])
            dma(retc, dr["retc"], w=[t_c])
            dma(retp, dr["retp"], w=[t_c])
            dma(lgrow, dr["rlog_row"][l], w=[t_lg])
            dma(lgpp, dr["rlog_pp"][l], w=[t_lg])
            dma(gnw, dr["gnw"][l], w=[t_c])
            for v in (lgrow, lgpp.rearrange("p a b -> p (a b)")):
                act(v, v, AF.Exp, r=[t_lg], w=[t_lg], scale=-1.0)
                act(v, v, AF.Ln, r=[t_lg], w=[t_lg], bias=1.0)
                ts("dve", v, v, -1.0, None, ALU.mult, r=[t_lg], w=[t_lg])
            wsrc = dr["w_in"][l].rearrange("(k p) c -> p k c", p=128)
            for pair in range(2):
                P.barrier()
                offs = [416 + pair * 128, 672 + pair * 128, 1440 + pair * 128, 1696 + pair * 128]
                for ci, c0 in enumerate(offs):
                    load_cast(Wr[:, :, ci * 128:(ci + 1) * 128], t_Wr, wsrc[:, :, c0:c0 + 128], "p (k c) -> p k c", c=128)
                for hh in range(2):
                    head = 2 * pair + hh
                    cs_ = slice(hh * 64, (hh + 1) * 64)
                    act(kdec[:, 0, cs_], lgrow[:, head:head + 1].to_broadcast([128, 64]), AF.Exp, r=[t_lg, t_c], w=[t_tab],
                        scale=retp[:, 1:2])
                    act(kdec[:, 1, cs_], lgrow[:, 4 + head:5 + head].to_broadcast([128, 64]), AF.Exp, r=[t_lg, t_c], w=[t_tab],
                        scale=retp[:, 0:1])
                    act(Mk[:, hh, :], retc[:, 2, :], AF.Exp, r=[t_lg, t_c], w=[t_tab], scale=lgrow[:, head:head + 1])
                    tt("dve", Mk[:, hh, :], Mk[:, hh, :], retc[:, 4, :], ALU.mult, r=[t_tab, t_c], w=[t_tab])
                    act(xcq, retc[:, 3, :], AF.Exp, r=[t_lg, t_c], w=[t_tab], scale=lgrow[:, 4 + head:5 + head])
                    tt("dve", xcq, xcq, retc[:, 5, :], ALU.mult, r=[t_tab, t_c], w=[t_tab])
                    stt(Mk[:, hh, :], Mk[:, hh, :], 1.0, xcq, ALU.mult, ALU.add, r=[t_tab], w=[t_tab])
                ts("dve", Mk, Mk, 0.125, None, ALU.mult, r=[t_tab], w=[t_tab])
                ts("dve", kdec, kdec, 0.125, None, ALU.mult, r=[t_tab], w=[t_tab])
                act(qdec[:, 0, :], retc[:, 0, :], AF.Exp, r=[t_lg, t_c], w=[t_tab], scale=lgpp[:, pair, 0:1])
                act(qdec[:, 1, :], retc[:, 1, :], AF.Exp, r=[t_lg, t_c], w=[t_tab], scale=lgpp[:, pair, 1:2])
                act(aab, lgpp[:, pair, :], AF.Exp, r=[t_lg], w=[t_tab], scale=128.0)
                mset("dve", Sf, 0.0, w=[t_S])
                mset("dve", Sb, 0.0, w=[t_S])
                mset("pool", QZ, 0.0, w=t_QK)

                def rope(src, cosT, sinT, dst, pb, t_dst, t_src):
                    x = src.rearrange("p (h two b) -> p h two b", two=2, b=32)
                    y = dst.rearrange("p (h two b) -> p h two b", two=2, b=32)
                    cb = cosT.unsqueeze(1).to_broadcast([128, 2, 32])
                    sb_ = sinT.unsqueeze(1).to_broadcast([128, 2, 32])
                    r_ = rt[:, pb]
                    tt("dve", r_[:, 0], x[:, :, 0, :], cb, ALU.mult, r=[t_src, t_c], w=[t_rt[pb]])
                    tt("dve", r_[:, 1], x[:, :, 1, :], sb_, ALU.mult, r=[t_src, t_c], w=[t_rt[pb]])
                    tt("dve", y[:, :, 0, :], r_[:, 0], r_[:, 1], ALU.subtract, r=[t_rt[pb]], w=[t_dst])
                    tt("dve", r_[:, 2], x[:, :, 0, :], sb_, ALU.mult, r=[t_src, t_c], w=[t_rt[pb]])
                    tt("dve", r_[:, 3], x[:, :, 1, :], cb, ALU.mult, r=[t_src, t_c], w=[t_rt[pb]])
                    tt("dve", y[:, :, 1, :], r_[:, 2], r_[:, 3], ALU.add, r=[t_rt[pb]], w=[t_dst])

                def _p1body(i):
                    g = grp_of_tile(i)
                    pb = i % 2
                    t0 = i * 128
                    bz = pb
                    for k in range(8):
                        mm(ps[bz][:, :], hT[:, k, t0:t0 + 128], Wr[:, k, :], start=(k == 0), stop=(k == 7),
                           r=[t_h[g], t_Wr], w=[tps[bz]])
                    z = ps[bz]
                    rope(z[:, 256:384], rrc[:, i, :], rrs[:, i, :], Qb[:, pb], pb, t_Qb[pb], tps[bz])
                    rope(z[:, 0:128], rrc[:, i, :], rrs[:, i, :], Kb[:, pb], pb, t_Kb[pb], tps[bz])
                    cp("act", Vb[:, i, :], z[:, 128:256], r=[tps[bz]], w=[t_Vb[i]])
                    act(sg[:, i, :], z[:, 384:512], AF.Silu, r=[tps[bz]], w=[t_sg[i]])
                    tb_ = 2 + pb
                    pst = ps[tb_][:, :].bitcast(BF16)
                    tr(pst[:, 0:128], Qb[:, pb], identb, r=[t_Qb[pb], t_const], w=[tps[tb_]])
                    tr(pst[:, 128:256], Kb[:, pb], identb, r=[t_Kb[pb], t_const], w=[tps[tb_]])
                    cp("act", QZ[0:64, 0, t0:t0 + 128], pst[0:64, 0:128], r=[tps[tb_]], w=[t_QK[i]])
                    cp("act", QZ[64:128, 1, t0:t0 + 128], pst[64:128, 0:128], r=[tps[tb_]], w=[t_QK[i]])
                    cp("act", KT_[:, t0:t0 + 128], pst[:, 128:256], r=[tps[tb_]], w=[t_QK[i]])
                    tt("pool", Kd[:, pb, 0], Kb[:, pb], kdec[:, 0], ALU.mult, r=[t_Kb[pb], t_tab], w=[t_Kd[pb]])
                    tt("pool", Kd[:, pb, 1], Kb[:, pb], kdec[:, 1], ALU.mult, r=[t_Kb[pb], t_tab], w=[t_Kd[pb]])
                    bu = 4 + pb
                    mm(ps[bu][:, 0:128], Kd[:, pb, 0], Vb[:, i, :], r=[t_Kd[pb], t_Vb[i]], w=[tps[bu]])
                    mm(ps[bu][:, 128:256], Kd[:, pb, 1], Vb[:, i, :], r=[t_Kd[pb], t_Vb[i]], w=[tps[bu]])

                for i0 in range(0, NTT, 2):
                    P.interleave([(lambda a=i0: _p1body(a)), (lambda a=i0 + 1: _p1body(a))])
                    for i in (i0, i0 + 1):
                        bu = 4 + i % 2
                        cp("dve", Sfp[:, i, :], Sf, r=[t_S], w=[t_Sfp[i]])
                        stt(Sf, Sf, aab[:, 0:1], ps[bu][:, 0:128], ALU.mult, ALU.add, r=[t_S, t_tab, tps[bu]], w=[t_S])
                        cp("act", Ub[:, i, :], ps[bu][:, 128:256], r=[tps[bu]], w=[t_Ub[i]])
                P.barrier()
                for i in [1, 0] + list(range(NTT - 1, 1, -1)):
                    cp("dve", Sbn[:, i, :], Sb, r=[t_S], w=[t_Sbn[i]])
                    stt(Sb, Sb, aab[:, 1:2], Ub[:, i, :], ALU.mult, ALU.add, r=[t_S, t_tab, t_Ub[i]], w=[t_S])
                P.barrier()
                def _p2body(i):
                    g = grp_of_tile(i)
                    pb = i % 2
                    t0 = i * 128
                    xc = xc2[:, pb]
                    sq = sq2[:, pb]
                    st_ = st2[:, pb]
                    t_gn = t_gn2[pb]
                    ba = pb
                    for hh in range(2):
                        mm(ps[ba][:, hh * 128:(hh + 1) * 128], KT_[:, t0:t0 + 128], QZ[:, hh, t0:t0 + 128],
                           r=[t_QK[i]], w=[tps[ba]])
                    tt("dve", attm[:, pb], ps[ba][:, 0:256].rearrange("p (a t) -> p a t", t=128), Mk, ALU.mult,
                       r=[tps[ba], t_tab], w=[t_attm[pb]])
                    for hh in range(2):
                        tt("pool", Qd[:, pb, hh, 0], QZ[:, hh, t0:t0 + 128], qdec[:, 0], ALU.mult, r=[t_QK[i], t_tab], w=[t_Qd[pb]])
                        tt("pool", Qd[:, pb, hh, 1], QZ[:, hh, t0:t0 + 128], qdec[:, 1], ALU.mult, r=[t_QK[i], t_tab], w=[t_Qd[pb]])
                    bo = 4 + pb
                    for hh in range(2):
                        rs_ = slice(hh * 64, (hh + 1) * 64)
                        mm(ps[bo][:, rs_], attm[:, pb, hh, :], Vb[:, i, rs_], start=True, stop=False,
                           r=[t_attm[pb], t_Vb[i]], w=[tps[bo]])
                        mm(ps[bo][:, rs_], Qd[:, pb, hh, 0, :], Sfp[:, i, rs_], start=False, stop=False,
                           r=[t_Qd[pb], t_Sfp[i]], w=[tps[bo]])
                        mm(ps[bo][:, rs_], Qd[:, pb, hh, 1, :], Sbn[:, i, rs_], start=False, stop=True,
                           r=[t_Qd[pb], t_Sbn[i]], w=[tps[bo]])
                    o = ps[bo][:, 0:128].rearrange("p (a d) -> p a d", d=64)
                    red(st_[:, 0, 0:2], o, r=[tps[bo]], w=[t_gn])
                    ts("dve", st_[:, 0, 0:2], st_[:, 0, 0:2], -1.0 / 64, None, ALU.mult, r=[t_gn], w=[t_gn])
                    tt("dve", xc, o, st_[:, 0, 0:2].unsqueeze(2).to_broadcast([128, 2, 64]), ALU.add, r=[tps[bo], t_gn], w=[t_gn])
                    tt("dve", sq, xc, xc, ALU.mult, r=[t_gn], w=[t_gn])
                    red(st_[:, 1, 0:2], sq, r=[t_gn], w=[t_gn])
                    act(st_[:, 1, 0:2], st_[:, 1, 0:2], AF.Sqrt, r=[t_gn, t_const], w=[t_gn], scale=1.0 / 64, bias=eps_t[:, 0:1])
                    rcp(st_[:, 1, 0:2], st_[:, 1, 0:2], r=[t_gn], w=[t_gn])
                    tt("dve", xc, xc, st_[:, 1, 0:2].unsqueeze(2).to_broadcast([128, 2, 64]), ALU.mult, r=[t_gn], w=[t_gn])
                    tt("dve", xc, xc, gnw[:, pair * 128:(pair + 1) * 128].rearrange("p (a d) -> p a d", d=64), ALU.mult,
                       r=[t_gn, t_c], w=[t_gn])
                    tt("dve", od[:, pb].rearrange("p (a d) -> p a d", d=64), xc,
                       sg[:, i, :].rearrange("p (a d) -> p a d", d=64), ALU.mult, r=[t_gn, t_sg[i]], w=[t_od[pb]])
                    tb_ = 2 + pb
                    pst = ps[tb_][:, :].bitcast(BF16)
                    tr(pst[:, 0:128], od[:, pb], identb, r=[t_od[pb], t_const], w=[tps[tb_]])
                    cp("act", oT[slot][:, pair, t0:t0 + 128], pst[:, 0:128], r=[tps[tb_]], w=[t_oT[slot][g]])

                for i0 in range(2 if last else 0, NTT, 2):
                    P.interleave([(lambda a=i0: _p2body(a)), (lambda a=i0 + 1: _p2body(a))])
            tap("odT", oT[slot], t_oT[slot])

        I32 = mybir.dt.int32
        TWO_PI = 2.0 * math.pi

        def s5_phase(l, last, slot):
            P.barrier()
            al = mk_alloc(OT0 + slot * 2 * NT * 2)
            uT = al([2, NT], BF16)
            yf = al([2, NT], BF16)
            E = al([2, 1024], BF16)
            Fm = al([8, 2, 128], BF16)
            Bb = al([2, 2, 512], BF16)
            Cc = al([2, 8, 128], BF16)
            Tri = al([2, 128], BF16)
            pp = al([3, 8], F32)
            sm = al([12, 8], F32)
            cst = al([4], F32)
            erow = al([2, 128], F32)
            ecol = al([4], F32)
            dvec = al([2], F32)
            xl = sm[:, 7:9, :].rearrange("p a b -> p (a b)").rearrange("p (s r) -> p s r", r=2)
            ccx = al([128], F32)
            cc = ccx[:, 0:16].rearrange("p (a b) -> p a b", b=2)
            id2 = al([16], BF16)
            woff = al.o[0]
            W = al([2, 1024], BF16)
            xx = al([2, 8, 128], BF16)
            tW = al([2, 2, 256], BF16)
            tq = al([2, 512], F32)
            ysc = A.alloc([4, 128], F32, at=al.o[0] - 2048)
            cT = al([128], BF16)
            t_u = [T() for _ in range(5)]
            t_yf = [T() for _ in range(NTT)]
            t_tab = T()
            t_pp = T()
            t_c = T()
            t_W = [T(), T()]
            t_xx = [T(), T()]
            t_tW2 = [T(), T()]
            t_tq = T()
            t_xl = T()
            t_cc = T()
            t_ysc = t_tq
            t_cT = T()
            t_blk = T()

            dma(Tri, dr["s5tri"], w=[t_c])
            dma(id2, dr["s5id2"], w=[t_c])
            dma(erow, dr["s5erow"], w=[t_c])
            dma(ecol, dr["s5ecol"], w=[t_c])
            dma(dvec, dr["s5d"][l], w=[t_c])
            mset("dve", cst[:, 0:1], -math.pi, w=[t_c])

            wsrc = dr["w_in"][l].rearrange("(k p) c -> p k c", p=128)
            bi = 0
            for j in range(2):
                wv, t_wv = load_ring(wsrc[:, :, 160 + j * 128:160 + (j + 1) * 128], "p (k c) -> p k c", c=128)
                for g in range(5):
                    t0, n = GRP(g)
                    pb = bi % 4
                    bi += 1
                    for k in range(8):
                        mm(ps[pb][:, 0:n], wv[:, k, :], hT[:, k, t0:t0 + n], start=(k == 0), stop=(k == 7),
                           r=[t_wv, t_h[g]], w=[tps[pb]])
                    cp("act", uT[:, j, t0:t0 + n], ps[pb][:, 0:n], r=[tps[pb]], w=[t_u[g]])
            tap("uT", uT, t_u)

            def cplx_pow(out_re, out_im, phase, mag, n, conj, tmp):
                r_, n_i, f_, m_ = tmp
                ts("dve", r_, phase, 1.0 / TWO_PI, None, ALU.mult, r=[t_blk], w=[t_blk])
                cp("dve", n_i.bitcast(I32), r_, r=[t_blk], w=[t_blk])
                cp("dve", f_, n_i.bitcast(I32), r=[t_blk], w=[t_blk])
                tt("dve", f_, r_, f_, ALU.subtract, r=[t_blk], w=[t_blk])
                ts("dve", m_, f_, 0.0, None, ALU.is_lt, r=[t_blk], w=[t_blk])
                tt("dve", f_, f_, m_, ALU.add, r=[t_blk], w=[t_blk])
                act(r_, f_, AF.Sin, r=[t_blk, t_c], w=[t_blk], scale=TWO_PI, bias=cst[:, 0:1])
                ts("dve", f_, f_, 0.25, None, ALU.add, r=[t_blk], w=[t_blk])
                ts("dve", m_, f_, 1.0, None, ALU.is_ge, r=[t_blk], w=[t_blk])
                tt("dve", f_, f_, m_, ALU.subtract, r=[t_blk], w=[t_blk])
                act(m_, f_, AF.Sin, r=[t_blk, t_c], w=[t_blk], scale=TWO_PI, bias=cst[:, 0:1])
                stt(out_re, mag, -1.0, m_, ALU.mult, ALU.mult, r=[t_blk], w=[t_blk, t_tab])
                if conj:
                    tt("dve", out_im, mag, r_, ALU.mult, r=[t_blk], w=[t_blk, t_tab])
                else:
                    stt(out_im, mag, -1.0, r_, ALU.mult, ALU.mult, r=[t_blk], w=[t_blk, t_tab])

            glw = None
            for d_ in range(2):
                P.barrier()
                B_ = [A.alloc([256], F32, at=woff + i * 1024) for i in range(16)]
                dma(pp, dr["s5pp"][l, d_], w=[t_pp])
                act(pp[:, 2, :], pp[:, 2, :], AF.Exp, r=[t_pp], w=[t_pp])
                App = sm[:, 0, :]
                Bpp = sm[:, 1, :]
                tt("dve", App, pp[:, 0, :], pp[:, 2, :], ALU.mult, r=[t_pp], w=[t_blk])
                tt("dve", Bpp, pp[:, 1, :], pp[:, 2, :], ALU.mult, r=[t_pp], w=[t_blk])
                l1re = sm[:, 2, :]
                l1im = sm[:, 3, :]
                mg = sm[:, 4, :]
                act(mg, App, AF.Exp, r=[t_blk], w=[t_blk])
                tmp8 = [B_[0][:, 0:8], B_[0][:, 8:16], B_[0][:, 16:24], B_[0][:, 24:32]]
                cplx_pow(l1re, l1im, Bpp, mg, 8, False, tmp8)
                br = sm[:, 5, :]
                den = sm[:, 6, :]
                kre = sm[:, 7, :]
                kim = sm[:, 8, :]
                nkre = sm[:, 9, :]
                nkim = sm[:, 10, :]
                t8 = sm[:, 11, :]
                ts("dve", br, l1re, -1.0, None, ALU.add, r=[t_blk], w=[t_blk])
                tt("dve", den, pp[:, 0, :], pp[:, 0, :], ALU.mult, r=[t_pp], w=[t_blk])
                tt("dve", t8, pp[:, 1, :], pp[:, 1, :], ALU.mult, r=[t_pp], w=[t_blk])
                tt("dve", den, den, t8, ALU.add, r=[t_blk], w=[t_blk])
                rcp(den, den, r=[t_blk], w=[t_blk])
                tt("dve", kre, br, pp[:, 0, :], ALU.mult, r=[t_blk, t_pp], w=[t_blk])
                tt("dve", t8, l1im, pp[:, 1, :], ALU.mult, r=[t_blk, t_pp], w=[t_blk])
                tt("dve", kre, kre, t8, ALU.add, r=[t_blk], w=[t_blk])
                tt("dve", kre, kre, den, ALU.mult, r=[t_blk], w=[t_blk])
                tt("dve", kim, l1im, pp[:, 0, :], ALU.mult, r=[t_blk, t_pp], w=[t_blk])
                tt("dve", t8, br, pp[:, 1, :], ALU.mult, r=[t_blk, t_pp], w=[t_blk])
                tt("dve", kim, kim, t8, ALU.subtract, r=[t_blk], w=[t_blk])
                tt("dve", kim, kim, den, ALU.mult, r=[t_blk], w=[t_blk])
                ts("dve", nkre, kre, -1.0, None, ALU.mult, r=[t_blk], w=[t_blk])
                ts("dve", nkim, kim, -1.0, None, ALU.mult, r=[t_blk], w=[t_blk])
                Cre = A.alloc([8, 128], F32, at=woff + 1 * 1024)
                Cim = A.alloc([8, 128], F32, at=woff + 5 * 1024)
                for ri, Cdst in enumerate((Cre, Cim)):
                    dma(Cdst, dr["s5c"][l, d_, ri].rearrange("a s c -> s a c"), w=[t_blk])
                tC = B_[9][:, 0:128]
                for st in range(8):
                    ts("dve", tC, Cre[:, st, :], kre[:, st:st + 1], None, ALU.mult, r=[t_blk], w=[t_blk])
                    stt(Cc[:, 0, st, :], Cim[:, st, :], nkim[:, st:st + 1], tC, ALU.mult, ALU.add, r=[t_blk], w=[t_tab])
                    ts("dve", tC, Cre[:, st, :], nkim[:, st:st + 1], None, ALU.mult, r=[t_blk], w=[t_blk])
                    stt(Cc[:, 1, st, :], Cim[:, st, :], nkre[:, st:st + 1], tC, ALU.mult, ALU.add, r=[t_blk], w=[t_tab])
                for kt in range(2):
                    load_cast(Bb[:, kt], t_tab, dr["s5b"][l, d_, :, kt], "p (a c) -> p a c", c=512)
                er = erow[:, d_, :]
                for st in range(8):
                    ph = B_[9][:, 0:128]
                    mgb = B_[9][:, 128:256]
                    ts("dve", ph, er, Bpp[:, st:st + 1], None, ALU.mult, r=[t_c, t_blk], w=[t_blk])
                    act(mgb, er, AF.Exp, r=[t_c, t_blk], w=[t_blk], scale=App[:, st:st + 1])
                    tmpb = [B_[10][:, 0:128], B_[10][:, 128:256], B_[11][:, 0:128], B_[11][:, 128:256]]
                    cplx_pow(Fm[:, st, 0, :], Fm[:, st, 1, :], ph, mgb, 128, False, tmpb)
                row = A.alloc([3, 256], F32, at=woff + 1 * 1024)
                for cb in range(4):
                    dma(row, dr["s5row"][l, d_, :, :, cb * 256:(cb + 1) * 256], w=[t_blk])
                    act(row[:, 2, :], row[:, 2, :], AF.Exp, r=[t_blk], w=[t_blk])
                    Ab = B_[4]
                    Bk = B_[5]
                    tt("dve", Ab, row[:, 0, :], row[:, 2, :], ALU.mult, r=[t_blk], w=[t_blk])
                    tt("dve", Bk, row[:, 1, :], row[:, 2, :], ALU.mult, r=[t_blk], w=[t_blk])
                    ph = B_[6]
                    mgb = B_[7]
                    ts("dve", ph, Bk, ecol[:, d_:d_ + 1], None, ALU.mult, r=[t_blk, t_c], w=[t_blk])
                    act(mgb, Ab, AF.Exp, r=[t_blk, t_c], w=[t_blk], scale=ecol[:, 2 + d_:3 + d_])
                    tmpb = [B_[8], B_[9], B_[10], B_[11]]
                    cplx_pow(E[:, 0, cb * 256:(cb + 1) * 256], E[:, 1, cb * 256:(cb + 1) * 256], ph, mgb, 256, True, tmpb)
                if d_ == 0:
                    tap("s5E", E, [t_tab])
                    tap("s5F", Fm, [t_tab])
                    tap("s5C", Cc, [t_tab])
                P.barrier()
                order = list(range(NTT)) if d_ == 0 else [1, 0] + list(range(NTT - 1, 1, -1))
                lastcol = 127 if d_ == 0 else 0
                mset("dve", ccx, 0.0, w=[t_cc])
                for idx, i in enumerate(order):
                    g = grp_of_tile(i)
                    t0 = i * 128
                    pb = idx % 2
                    for nb in range(4):
                        kt = nb % 2
                        ri = nb // 2
                        mm(ps[nb][:, :], uT[:, kt, t0:t0 + 128], Bb[:, kt, ri, :], r=[t_u[g], t_tab], w=[tps[nb]])
                    for qb in range(4):
                        hb = qb // 2
                        sl = slice(qb * 256, (qb + 1) * 256)
                        pl = slice((qb % 2) * 256, (qb % 2 + 1) * 256)
                        for ri_o in range(2):
                            wb_ = (qb * 2 + ri_o) % 2
                            tw_ = tW[:, wb_]
                            t_tW = t_tW2[wb_]
                            e0, e1 = (0, 1) if ri_o == 0 else (1, 0)
                            tt("dve", tw_[:, 0, :], ps[hb][:, pl], E[:, e0, sl], ALU.mult, r=[tps[hb], t_tab], w=[t_tW])
                            tt("dve", tw_[:, 1, :], ps[2 + hb][:, pl], E[:, e1, sl], ALU.mult, r=[tps[2 + hb], t_tab], w=[t_tW])
                            tt("pool", W[:, ri_o, sl], tw_[:, 0, :], tw_[:, 1, :], ALU.subtract if ri_o == 0 else ALU.add,
                               r=[t_tW], w=[t_W[ri_o]])
                    tr(ps[2][:, 0:128], ccx, identf, r=[t_cc, t_const], w=[tps[2]])
                    cp("act", cT[:, :], ps[2][:, 0:128], r=[tps[2]], w=[t_cT])
                    tt("dve", cT[32:64, :], ps[2][32:64, 0:128], cT[32:64, :], ALU.subtract, r=[tps[2], t_cT], w=[t_cT])
                    for ri in range(2):
                        for st in range(8):
                            bk = 4 + ri * 2 + st // 4
                            mm(ps[bk][:, (st % 4) * 128:(st % 4 + 1) * 128], W[:, ri, st * 128:(st + 1) * 128], Tri[:, d_, :],
                               start=(st % 4 == 0), stop=False, r=[t_W[ri], t_c], w=[tps[bk]])
                    for ri in range(2):
                        for st in range(8):
                            bk = 4 + ri * 2 + st // 4
                            jj = st * 2 + ri
                            mm(ps[bk][:, (st % 4) * 128:(st % 4 + 1) * 128], cT[:, :],
                               id2[:, jj:jj + 1].to_broadcast([128, 128]),
                               start=False, stop=True, r=[t_cT, t_c], w=[tps[bk]])
                    for hf in range(2):
                        Sre = ps[4 + hf][:, :].rearrange("p (a t) -> p a t", t=128)
                        Sim = ps[6 + hf][:, :].rearrange("p (a t) -> p a t", t=128)
                        Fre = Fm[:, 4 * hf:4 * hf + 4, 0, :]
                        Fim = Fm[:, 4 * hf:4 * hf + 4, 1, :]
                        q0 = tq[:, 0, :].rearrange("p (a t) -> p a t", t=128)
                        q1 = tq[:, 1, :].rearrange("p (a t) -> p a t", t=128)
                        tt("dve", q0, Sre, Fre, ALU.mult, r=[tps[4 + hf], t_tab], w=[t_tq])
                        tt("dve", q1, Sim, Fim, ALU.mult, r=[tps[6 + hf], t_tab], w=[t_tq])
                        tt("dve", xl[:, 4 * hf:4 * hf + 4, 0], q0[:, :, lastcol], q1[:, :, lastcol], ALU.subtract, r=[t_tq], w=[t_xl])
                        tt("dve", xx[:, 0, 4 * hf:4 * hf + 4, :], q0, q1, ALU.subtract, r=[t_tq], w=[t_xx[0]])
                        tt("dve", q0, Sre, Fim, ALU.mult, r=[tps[4 + hf], t_tab], w=[t_tq])
                        tt("dve", q1, Sim, Fre, ALU.mult, r=[tps[6 + hf], t_tab], w=[t_tq])
                        tt("dve", xl[:, 4 * hf:4 * hf + 4, 1], q0[:, :, lastcol], q1[:, :, lastcol], ALU.add, r=[t_tq], w=[t_xl])
                        tt("dve", xx[:, 1, 4 * hf:4 * hf + 4, :], q0, q1, ALU.add, r=[t_tq], w=[t_xx[1]])
                    ta_ = sm[:, 5, :]
                    tb_ = sm[:, 6, :]
                    tt("dve", ta_, l1re, xl[:, :, 0], ALU.mult, r=[t_xl, t_blk], w=[t_blk])
                    tt("dve", tb_, l1im, xl[:, :, 1], ALU.mult, r=[t_xl, t_blk], w=[t_blk])
                    tt("dve", cc[:, :, 0], ta_, tb_, ALU.subtract, r=[t_blk], w=[t_cc])
                    tt("dve", ta_, l1re, xl[:, :, 1], ALU.mult, r=[t_xl, t_blk], w=[t_blk])
                    tt("dve", tb_, l1im, xl[:, :, 0], ALU.mult, r=[t_xl, t_blk], w=[t_blk])
                    tt("dve", cc[:, :, 1], ta_, tb_, ALU.add, r=[t_blk], w=[t_cc])
                    cp("dve", ccx[:, 32:48], ccx[:, 0:16], r=[t_cc], w=[t_cc])
                    if last and i < 2:
                        continue
                    for j in range(2):
                        n_mm = 0
                        for st in range(4 * j, 4 * j + 4):
                            for ri in range(2):
                                mm(ps[j][:, 0:128], Cc[:, ri, st, :], xx[:, ri, st, :], start=(n_mm == 0), stop=(n_mm == 7),
                                   r=[t_tab, t_xx[ri]], w=[tps[j]])
                                n_mm += 1
                        if d_ == 0:
                            cp("act", yf[:, j, t0:t0 + 128], ps[j][:, 0:128], r=[tps[j]], w=[t_yf[i]])
                        else:
                            y = ysc[:, 0, :]
                            tt("dve", y, ps[j][:, 0:128], yf[:, j, t0:t0 + 128], ALU.add, r=[tps[j], t_yf[i]], w=[t_ysc])
                            stt(y, uT[:, j, t0:t0 + 128], dvec[:, j:j + 1], y, ALU.mult, ALU.add, r=[t_u[g], t_c, t_ysc], w=[t_ysc])
                            tt("dve", ysc[:, 1, :], y, y, ALU.mult, r=[t_ysc], w=[t_ysc])
                            ts("dve", ysc[:, 1, :], ysc[:, 1, :], 0.044715, 1.0, ALU.mult, ALU.add, r=[t_ysc], w=[t_ysc])
                            tt("dve", ysc[:, 1, :], ysc[:, 1, :], y, ALU.mult, r=[t_ysc], w=[t_ysc])
                            act(ysc[:, 2, :], ysc[:, 1, :], AF.Sigmoid, r=[t_ysc], w=[t_ysc], scale=1.5957691216057308)
                            tt("dve", yf[:, j, t0:t0 + 128], y, ysc[:, 2, :], ALU.mult, r=[t_ysc], w=[t_yf[i]])
            tap("s5g", yf, t_yf)
            P.barrier()
            glw = A.alloc([2, 512], BF16, at=woff)
            t_glw = T()
            load_cast(glw[:, 0], t_glw, dr["s5_w_glu"][l][0:128, :])
            load_cast(glw[:, 1], t_glw, dr["s5_w_glu"][l][128:256, :])
            sgt = A.alloc([512], F32, at=woff + 2048)
            t_sgt = T()
            bi = 0
            for g in range(1 if last else 0, 5):
                t0, n = GRP(g)
                tiles = list(range(t0 // 128, (t0 + n) // 128))
                for j in range(2):
                    pv = bi % 2
                    pg = 2 + bi % 2
                    bi += 1
                    for kt in range(2):
                        mm(ps[pv][:, 0:n], glw[:, kt, j * 128:(j + 1) * 128], yf[:, kt, t0:t0 + n], start=(kt == 0), stop=(kt == 1),
                           r=[t_glw] + [t_yf[i] for i in tiles], w=[tps[pv]])
                    for kt in range(2):
                        mm(ps[pg][:, 0:n], glw[:, kt, 256 + j * 128:256 + (j + 1) * 128], yf[:, kt, t0:t0 + n],
                           start=(kt == 0), stop=(kt == 1), r=[t_glw] + [t_yf[i] for i in tiles], w=[tps[pg]])
                    act(sgt[:, 0:n], ps[pg][:, 0:n], AF.Sigmoid, r=[tps[pg]], w=[t_sgt])
                    tt("dve", oT[slot][:, j, t0:t0 + n], ps[pv][:, 0:n], sgt[:, 0:n], ALU.mult, r=[tps[pv], t_sgt],
                       w=[t_oT[slot][g]])
            tap("ocT", oT[slot], t_oT[slot])

        SLOT_OF = {0: 3, 1: 0, 2: 1, 3: 2}

        def merge_phase(l, last):
            P.barrier()
            al = mk_alloc(OT0)
            mT = al([8, NT], BF16)
            sig = al([512], F32)
            acc = al([512], F32)
            t_m = [T() for _ in range(5)]
            t_sig = T()
            t_acc = T()
            wsrc = dr["w_in"][l].rearrange("(k p) c -> p k c", p=128)
            wbsrc = dr["w_branch"][l].rearrange("n (j p) d -> p n j d", p=128)
            groups = [g for g in range(5) if not (last and g == 0)]
            bi = 0
            for d in range(8):
                gw = []
                for n in range(4):
                    c0 = 1952 + n * 1024 + d * 128
                    gw.append(load_ring(wsrc[:, :, c0:c0 + 128], "p (k c) -> p k c", c=128))
                wb, t_wb = load_ring(wbsrc[:, :, :, d * 128:(d + 1) * 128], "p (n j c) -> p n j c", j=2, c=128)
                for g in groups:
                    t0, n_ = GRP(g)
                    for n in range(4):
                        sl = SLOT_OF[n]
                        pa = bi % 2
                        pb = 2 + bi % 2
                        bi += 1
                        wv, t_wv = gw[n]
                        for k in range(8):
                            mm(ps[pa][:, 0:n_], wv[:, k, :], hT[:, k, t0:t0 + n_], start=(k == 0), stop=(k == 7),
                               r=[t_wv, t_h[g]], w=[tps[pa]])
                        for j in range(2):
                            mm(ps[pb][:, 0:n_], wb[:, n, j, :], oT[sl][:, j, t0:t0 + n_], start=(j == 0), stop=(j == 1),
                               r=[t_wb, t_oT[sl][g]], w=[tps[pb]])
                        act(sig[:, 0:n_], ps[pa][:, 0:n_], AF.Sigmoid, r=[tps[pa]], w=[t_sig])
                        if n == 0:
                            tt("dve", acc[:, 0:n_], ps[pb][:, 0:n_], sig[:, 0:n_], ALU.mult, r=[tps[pb], t_sig], w=[t_acc])
                        else:
                            tt("dve", ps[pb][:, 0:n_], ps[pb][:, 0:n_], sig[:, 0:n_], ALU.mult, r=[tps[pb], t_sig], w=[tps[pb]])
                            if n < 3:
                                tt("dve", acc[:, 0:n_], acc[:, 0:n_], ps[pb][:, 0:n_], ALU.add, r=[tps[pb], t_acc], w=[t_acc])
                            else:
                                tt("dve", mT[:, d, t0:t0 + n_], acc[:, 0:n_], ps[pb][:, 0:n_], ALU.add,
                                   r=[tps[pb], t_acc], w=[t_m[g]])
            tap("mT", mT, t_m)
            wosrc = dr["w_out"][l].rearrange("(k p) c -> p k c", p=128)
            for d in range(8):
                wv, t_wv = load_ring(wosrc[:, :, d * 128:(d + 1) * 128], "p (k c) -> p k c", c=128)
                for g in groups:
                    t0, n_ = GRP(g)
                    s_ = 1 if g == 0 else 0
                    pb = 4 + bi % 4
                    bi += 1
                    for k in range(8):
                        mm(ps[pb][:, 0:n_], wv[:, k, :], mT[:, k, t0:t0 + n_], start=(k == 0), stop=(k == 7),
                           r=[t_wv, t_m[g]], w=[tps[pb]])
                    stt(xT[:, d, t0:t0 + n_], ps[pb][:, 0:n_], mod[:, l, 16 + d, s_:s_ + 1], xT[:, d, t0:t0 + n_],
                        ALU.mult, ALU.add, r=[tps[pb], t_mod, t_x[d][g]], w=[t_x[d][g]])

        def ffn_phase(l, last):
            groups = [g for g in range(5) if not (last and g == 0)]
            norm_phase(l, 1, groups)
            P.barrier()
            al = mk_alloc(A.nbytes)
            aT = al([8, NT], BF16)
            rl = al([2, 512], F32)
            t_a = [[T() for _ in range(5)] for _ in range(8)]
            t_rl = [T(), T()]
            w1src = dr["ffn_w1"][l].rearrange("(k p) c -> p k c", p=128)
            w2src = dr["ffn_w2"][l].rearrange("(f p) c -> p f c", p=128)
            bi = 0
            for fb in range(4):
                for f in range(8):
                    F_ = fb * 8 + f
                    wv, t_wv = load_ring(w1src[:, :, F_ * 128:(F_ + 1) * 128], "p (k c) -> p k c", c=128)
                    for g in groups:
                        t0, n_ = GRP(g)
                        pb = bi % 4
                        rb = bi % 2
                        bi += 1
                        for k in range(8):
                            mm(ps[pb][:, 0:n_], wv[:, k, :], hT[:, k, t0:t0 + n_], start=(k == 0), stop=(k == 7),
                               r=[t_wv, t_h[g]], w=[tps[pb]])
                        act(rl[:, rb, 0:n_], ps[pb][:, 0:n_], AF.Relu, r=[tps[pb]], w=[t_rl[rb]])
                        tt("pool", aT[:, f, t0:t0 + n_], rl[:, rb, 0:n_], rl[:, rb, 0:n_], ALU.mult, r=[t_rl[rb]], w=[t_a[f][g]])
                for d in range(8):
                    wv, t_wv = load_ring(w2src[:, fb * 8:(fb + 1) * 8, d * 128:(d + 1) * 128], "p (f c) -> p f c", c=128)
                    for g in groups:
                        t0, n_ = GRP(g)
                        s_ = 1 if g == 0 else 0
                        pb = 4 + bi % 4
                        bi += 1
                        for f in range(8):
                            mm(ps[pb][:, 0:n_], wv[:, f, :], aT[:, f, t0:t0 + n_], start=(f == 0), stop=(f == 7),
                               r=[t_wv, t_a[f][g]], w=[tps[pb]])
                        stt(xT[:, d, t0:t0 + n_], ps[pb][:, 0:n_], mod[:, l, 40 + d, s_:s_ + 1], xT[:, d, t0:t0 + n_],
                            ALU.mult, ALU.add, r=[tps[pb], t_mod, t_x[d][g]], w=[t_x[d][g]])

        for l in range(DEPTH):
            last = (l == DEPTH - 1)
            norm_phase(l, 0, list(range(5)))
            if l == 0:
                tap("hT", hT, t_h)
            if stage <= 0.1:
                break
            if "mla" in mixers:
                mla_phase(l, last, 3)
            if "ret" in mixers:
                ret_phase(l, last, 2)
            if "s5" in mixers:
                s5_phase(l, last, 1)
            if "fnet" in mixers:
                fnet_phase(l, last, 0)
            if stage <= 1:
                break
            merge_phase(l, last)
            ffn_phase(l, last)
            if l == 0:
                tap("x1", xT, [t for k in range(8) for t in t_x[k]])
            if stage <= 2:
                break

        osrc = outT.rearrange("(k p) t -> p k t", p=128)
        for k in range(8):
            out_handles.append(dma(osrc[:, k, :], xT[:, k, LAT0:NT], r=t_x[k]))
        P.wait_all("sp", out_handles)
        P.emit()
    return nc


_CACHE = {}


def _specs_of(d):
    sp = {}
    for k, v in d.items():
        sp[k] = (v.shape, BF16 if v.dtype == NPBF else F32)
    return sp


def run(inputs, stage=99, taps=(), ncores=8, mixers=("mla", "s5", "ret", "fnet")):
    com = prep_common(inputs)
    cores = [prep_core(inputs, b) for b in range(ncores)]
    in_maps = [dict(com, **c) for c in cores]
    nc = build(_specs_of(in_maps[0]), stage=stage, taps=taps, mixers=mixers)
    res = run_bass_kernel_spmd(nc, in_maps, core_ids=list(range(ncores)))
    return res.results


def kernel(**inputs):
    inputs = {k: np.asarray(v) for k, v in inputs.items()}
    res = run(inputs)
    out = np.stack([r["outT"].T for r in res], axis=0)
    return np.ascontiguousarray(out.astype(np.float32))
```

```python
import contextlib
import math
import os
import numpy as np
import ml_dtypes
import concourse.bass as bass
import concourse.mybir as mybir
from concourse.bass_utils import run_bass_kernel_spmd

F32 = mybir.dt.float32
BF16 = mybir.dt.bfloat16
ALU = mybir.AluOpType
AF = mybir.ActivationFunctionType
AX = mybir.AxisListType
NPBF = ml_dtypes.bfloat16

ENGS = ("pe", "act", "dve", "pool", "sp")
NOSELF = tuple(os.environ.get("NOSELF", "pe").split(","))
RELAX = os.environ.get("RELAX", "1") == "1"
NDSLOT = 8

D = 1024
NT = 2304
NTT = 18
LAT0 = 256
EPS = 1e-6
DEPTH = 2


class T:
    __slots__ = ("name", "w", "rs", "excl", "tw")

    def __init__(self, name="", excl=False):
        self.name = name
        self.w = None
        self.tw = None
        self.rs = []
        self.excl = excl


class Prog:
    def __init__(self, nc, stack, same_sync=True):
        self.nc = nc
        self.same_sync = same_sync
        self.q = {e: [] for e in ENGS}
        self.cnt = {e: 0 for e in ENGS}
        self.sems = {}
        for e in ENGS:
            self.sems[("c", e)] = stack.enter_context(nc.semaphore("c_" + e))
        self.dq = ("sp", "pool", "act")
        self.dcnt = {}
        self.dn = {e: 0 for e in self.dq}
        for e in self.dq:
            for s in range(NDSLOT):
                self.sems[("d", e, s)] = stack.enter_context(nc.semaphore("d_%s%d" % (e, s)))
                self.dcnt[(e, s)] = 0
        self.known = {e: {} for e in ENGS}
        self.kstop = None
        self.kcount = 0
        import threading
        self._tls = threading.local()

    def _deps(self, eng, r, w):
        deps = {}

        def add(h):
            if h is None:
                return
            k, v = h
            if k == ("c", eng) and (eng in NOSELF or not self.same_sync):
                return
            if deps.get(k, 0) < v:
                deps[k] = v
        for t in r:
            add(t.w)
        for t in w:
            add(t.w)
            for h in t.rs:
                add(h)
        out = []
        kn = self.known[eng]
        for k, v in deps.items():
            if kn.get(k, 0) >= v:
                continue
            kn[k] = v
            out.append((k, v))
        return out

    def _mark(self, h, r, w):
        for t in w:
            t.tw = h
        for t in r:
            t.rs.append(h)
            if len(t.rs) > 64:
                best = {}
                for k, v in t.rs:
                    if best.get(k, 0) < v:
                        best[k] = v
                t.rs = list(best.items())
        for t in w:
            t.w = h
            t.rs = []

    def interleave(self, thunks):
        import threading
        n = len(thunks)
        if n == 1 or os.environ.get("NOIL") == "1":
            for t in thunks:
                t()
            return
        cv = threading.Condition()
        st = {"turn": 0, "alive": [True] * n, "err": None}

        def nxt(i):
            for d in range(1, n + 1):
                j = (i + d) % n
                if st["alive"][j]:
                    return j
            return None

        def yp(i):
            with cv:
                j = nxt(i)
                if j is None or j == i:
                    return
                st["turn"] = j
                cv.notify_all()
                cv.wait_for(lambda: st["turn"] == i)

        def runner(i):
            try:
                with cv:
                    cv.wait_for(lambda: st["turn"] == i)
                self._tls.yp = (lambda: yp(i))
                thunks[i]()
            except BaseException as e:
                st["err"] = e
            finally:
                self._tls.yp = None
                with cv:
                    st["alive"][i] = False
                    j = nxt(i)
                    st["turn"] = j if j is not None else -1
                    cv.notify_all()

        ths = [threading.Thread(target=runner, args=(i,)) for i in range(n)]
        for t in ths:
            t.start()
        for t in ths:
            t.join()
        if st["err"] is not None:
            raise st["err"]

    def _yield(self):
        yp = getattr(self._tls, "yp", None)
        if yp is not None:
            yp()

    def op(self, eng, fn, r=(), w=()):
        self._yield()
        if self.kstop is not None:
            self.kcount += 1
            if self.kcount > self.kstop:
                return None
        if RELAX:
            return self._op_relaxed(eng, fn, r, w)
        if eng != "pe":
            ex = [t for t in r if t.excl]
            if ex:
                r = [t for t in r if not t.excl]
                w = list(w) + ex
        waits = self._deps(eng, r, w)
        self.cnt[eng] += 1
        h = (("c", eng), self.cnt[eng])
        self.q[eng].append((fn, waits, (h[0], 1)))
        self._mark(h, r, w)
        return h

    def _op_relaxed(self, eng, fn, r, w):
        deps = {}
        me = ("c", eng)

        def add(h, same_ok):
            if h is None:
                return
            k, v = h
            if k == me and (eng in NOSELF or not same_ok):
                return
            if deps.get(k, 0) < v:
                deps[k] = v
        wset = set(id(t) for t in w)
        for t in r:
            if id(t) in wset:
                continue
            add(t.tw, True)
            if t.excl and eng != "pe":
                add(t.w, False)
        for t in w:
            add(t.tw, False)
            add(t.w, False)
            for h in t.rs:
                add(h, False)
        for t in r:
            if id(t) in wset:
                add(t.tw, True)
        waits = []
        kn = self.known[eng]
        for k, v in deps.items():
            if kn.get(k, 0) >= v:
                continue
            kn[k] = v
            waits.append((k, v))
        self.cnt[eng] += 1
        h = (me, self.cnt[eng])
        self.q[eng].append((fn, waits, (me, 1)))
        for t in r:
            if id(t) in wset:
                continue
            if t.excl and eng != "pe":
                t.w = h
                t.rs = []
            else:
                t.rs.append(h)
                if len(t.rs) > 64:
                    best = {}
                    for k, v in t.rs:
                        if best.get(k, 0) < v:
                            best[k] = v
                    t.rs = list(best.items())
        for t in w:
            t.w = h
            t.tw = h
            t.rs = []
        return h

    def dma(self, eng, fn, r=(), w=()):
        self._yield()
        s = self.dn[eng] % NDSLOT
        self.dn[eng] += 1
        waits = self._deps(eng, r, w)
        k = ("d", eng, s)
        prev = self.dcnt[(eng, s)]
        if prev > 0 and self.known[eng].get(k, 0) < prev:
            self.known[eng][k] = prev
            waits.append((k, prev))
        self.dcnt[(eng, s)] = prev + 16
        h = (k, prev + 16)
        self.q[eng].append((fn, waits, (k, 16)))
        self._mark(h, r, w)
        return h

    def barrier(self):
        hs = [(("c", e), self.cnt[e]) for e in ENGS if self.cnt[e] > 0]
        hs += [(("d", e, s), v) for (e, s), v in self.dcnt.items() if v > 0]
        for e in ENGS:
            self.wait_all(e, [h for h in hs if h[0] != ("c", e)])

    def wait_all(self, eng, hs):
        waits = []
        for k, v in hs:
            if self.known[eng].get(k, 0) < v:
                self.known[eng][k] = v
                waits.append((k, v))
        self.q[eng].append((None, waits, None))

    def emit(self):
        nc = self.nc
        sems = self.sems
        q = self.q

        def run(e, engobj):
            for fn, waits, inc in q[e]:
                for k, v in waits:
                    engobj.wait_ge(sems[k], v)
                if fn is None:
                    continue
                ins = fn(engobj)
                ins.then_inc(sems[inc[0]], inc[1])

        with nc.Block() as block:
            @block.tensor
            def _(eng):
                run("pe", eng)

            @block.scalar
            def _(eng):
                run("act", eng)

            @block.vector
            def _(eng):
                run("dve", eng)

            @block.gpsimd
            def _(eng):
                run("pool", eng)

            @block.sync
            def _(eng):
                run("sp", eng)


class Arena:
    def __init__(self, nc, stack, nbytes):
        self.t = stack.enter_context(nc.sbuf_tensor("arena", [128, nbytes // 4], F32))
        self.nbytes = nbytes
        self.off = 0

    def alloc(self, shape, dtype, at=None):
        esz = 2 if dtype == BF16 else 4
        n = int(np.prod(shape)) * esz
        n4 = (n + 3) // 4
        if at is None:
            at = self.off
            self.off += n4 * 4
        assert at % 4 == 0 and at + n4 * 4 <= self.nbytes, (at, n, self.nbytes)
        ap = self.t[:, at // 4: at // 4 + n4]
        if dtype != F32:
            ap = ap.bitcast(dtype)
        if len(shape) == 2:
            ap = ap.rearrange("p (a b) -> p a b", b=shape[1])
        elif len(shape) == 3:
            ap = ap.rearrange("p (a b c) -> p a b c", b=shape[1], c=shape[2])
        elif len(shape) == 4:
            ap = ap.rearrange("p (a b c d) -> p a b c d", b=shape[1], c=shape[2], d=shape[3])
        return ap


def _rope_tables():
    half = 8
    freqs = (10000.0 ** (-np.arange(half, dtype=np.float32) / half)).astype(np.float32)
    t = np.arange(2048)
    rows = (t // 64).astype(np.float32)
    cols = (t % 64).astype(np.float32)
    ang = np.concatenate([rows[:, None] * freqs[None], cols[:, None] * freqs[None]], axis=1)
    cos = np.ones((NT, 16), np.float32)
    sin = np.zeros((NT, 16), np.float32)
    cos[LAT0:] = np.cos(ang)
    sin[LAT0:] = np.sin(ang)
    cos = cos.reshape(NTT, 128, 16).transpose(1, 0, 2)
    sin = sin.reshape(NTT, 128, 16).transpose(1, 0, 2)
    return np.ascontiguousarray(cos), np.ascontiguousarray(sin)


def _fnet_consts():
    ci = np.arange(64)
    c64 = np.cos(2 * np.pi * np.outer(ci, ci) / 64.0)
    s64 = np.sin(2 * np.pi * np.outer(ci, ci) / 64.0)
    cs = np.zeros((2, 128, 512), np.float64)
    for j in range(2):
        for gl in range(2):
            g = 2 * j + gl
            cs[j, gl * 64:(gl + 1) * 64, g * 64:(g + 1) * 64] = c64
            cs[j, gl * 64:(gl + 1) * 64, 256 + g * 64:256 + (g + 1) * 64] = s64
    out = {"cs64": np.ascontiguousarray(cs.transpose(1, 0, 2)).astype(np.float32).astype(NPBF)}
    for L in (2048, 256):
        li = np.arange(L)
        m = np.outer(li, li) % L
        ang = 2 * np.pi * m / L
        sc = 1.0 / math.sqrt(L * 64.0)
        tab = np.stack([np.cos(ang) * sc, -np.sin(ang) * sc], axis=0)
        out["dft%d" % L] = tab.astype(np.float32).astype(NPBF)
    return out


def _s5_layouts(inp):
    f = np.float32
    out = {}
    m = np.arange(128)[:, None]
    t = np.arange(128)[None, :]
    tri = np.stack([(m <= t), (m >= t)], axis=1).astype(f)
    out["s5tri"] = tri.astype(NPBF)
    erow = np.stack([np.broadcast_to(t.astype(f), (128, 128)), np.broadcast_to(127.0 - t.astype(f), (128, 128))], axis=1)
    out["s5erow"] = np.ascontiguousarray(erow, f)
    p = np.arange(128, dtype=f)[:, None]
    out["s5ecol"] = np.ascontiguousarray(np.concatenate([p, 127.0 - p, -p, -(127.0 - p)], axis=1), f)
    out["s5d"] = np.ascontiguousarray(np.asarray(inp["s5_d"], f).reshape(DEPTH, 2, 128).transpose(0, 2, 1))
    re = np.asarray(inp["s5_lam_re"], f).reshape(DEPTH, 2, 1024)
    im = np.asarray(inp["s5_lam_im"], f).reshape(DEPTH, 2, 1024)
    ls = np.repeat(np.asarray(inp["s5_log_step"], f), 64, axis=-1)
    trip = np.stack([re, im, ls], axis=2)
    out["s5pp"] = np.ascontiguousarray(trip.reshape(DEPTH, 2, 3, 8, 128).transpose(0, 1, 4, 2, 3))
    out["s5row"] = np.ascontiguousarray(np.broadcast_to(trip[:, :, None], (DEPTH, 2, 128, 3, 1024)), f)
    bre = np.asarray(inp["s5_b_re"], f)
    bim = np.asarray(inp["s5_b_im"], f)
    sb = np.zeros((DEPTH, 2, 128, 2, 2, 512), f)
    for ri, bb in enumerate((bre, bim)):
        for kt in range(2):
            for gl in range(8):
                g = 8 * kt + gl
                sb[:, :, gl * 16:(gl + 1) * 16, kt, ri, gl * 64:(gl + 1) * 64] = bb[:, :, g].transpose(0, 1, 3, 2)
    out["s5b"] = sb
    cre = np.asarray(inp["s5_c_re"], f)
    cim = np.asarray(inp["s5_c_im"], f)
    sc = np.zeros((DEPTH, 2, 2, 8, 128, 128), f)
    for ri, cm in enumerate((cre, cim)):
        for st in range(8):
            for gl in range(2):
                g = 2 * st + gl
                col = (g % 8) * 16
                sc[:, :, ri, st, gl * 64:(gl + 1) * 64, col:col + 16] = cm[:, :, g].transpose(0, 1, 3, 2)
    out["s5c"] = sc
    out["s5_w_glu"] = np.ascontiguousarray(inp["s5_w_glu"], f)
    id2 = np.zeros((128, 16), f)
    for j_ in range(16):
        id2[j_, j_] = 1.0
        id2[32 + j_, j_] = 1.0
    out["s5id2"] = id2.astype(NPBF)
    return out


def _ret_consts():
    half = 32
    freqs = (10000.0 ** (-np.arange(half, dtype=np.float32) / half)).astype(np.float32)
    pos = np.arange(2048, dtype=np.float32)
    ang = pos[:, None] * freqs[None]
    cos = np.ones((NT, 32), np.float32)
    sin = np.zeros((NT, 32), np.float32)
    cos[LAT0:] = np.cos(ang)
    sin[LAT0:] = np.sin(ang)
    tm = lambda a: np.ascontiguousarray(a.reshape(NTT, 128, 32).transpose(1, 0, 2))
    out = {"rrc": tm(cos), "rrs": tm(sin)}
    k = np.arange(128, dtype=np.float32)[:, None]
    q = np.arange(128, dtype=np.float32)[None, :]
    retc = np.stack([np.broadcast_to(q + 1.0, (128, 128)), np.broadcast_to(128.0 - q, (128, 128)),
                     np.maximum(q - k, 0.0), np.maximum(k - q, 0.0),
                     (q >= k).astype(np.float32), (k >= q).astype(np.float32)], axis=1)
    out["retc"] = np.ascontiguousarray(retc, np.float32)
    out["retp"] = np.ascontiguousarray(np.concatenate([k, 127.0 - k], axis=1), np.float32)
    return out


def prep_common(inp):
    f = np.float32
    c = {}
    c["ada_w"] = np.ascontiguousarray(inp["ada_w"], f)
    c["ada_bT"] = np.ascontiguousarray(inp["ada_b"].reshape(DEPTH, 48, 128).transpose(0, 2, 1), f)
    nw = np.concatenate([inp["norm_mix_w"].reshape(DEPTH, 8, 128), inp["norm_ffn_w"].reshape(DEPTH, 8, 128)], axis=1)
    c["nw"] = np.ascontiguousarray(nw.transpose(0, 2, 1), f)
    c["w_in"] = np.ascontiguousarray(inp["w_in"], f)
    bc = lambda v: np.ascontiguousarray(np.broadcast_to(v[:, None, :], (DEPTH, 128, v.shape[-1])), f)
    c["kvw"] = bc(inp["mla_kv_norm"])
    c["qnw"] = bc(inp["mla_q_norm"])
    c["qkq"] = bc(np.tile(inp["mla_qk_norm_q"], (1, 4)))
    c["qkk"] = bc(np.tile(inp["mla_qk_norm_k"], (1, 4)))
    c["w_ukv"] = np.ascontiguousarray(inp["mla_w_ukv"], f)
    c["w_uq"] = np.ascontiguousarray(inp["mla_w_uq"], f)
    cos, sin = _rope_tables()
    c["ropec"] = cos
    c["ropes"] = sin
    c["w_branch"] = np.ascontiguousarray(inp["w_branch"], f)
    c["w_out"] = np.ascontiguousarray(inp["w_out"], f)
    c["ffn_w1"] = np.ascontiguousarray(inp["ffn_w1"], f)
    c["ffn_w2"] = np.ascontiguousarray(inp["ffn_w2"], f)
    c.update(_fnet_consts())
    c.update(_ret_consts())
    c.update(_s5_layouts(inp))
    lg = np.asarray(inp["ret_decay_logit"], f)
    c["rlog_row"] = np.ascontiguousarray(np.broadcast_to(lg.reshape(DEPTH, 1, 8), (DEPTH, 128, 8)), f)
    pp = np.zeros((DEPTH, 128, 2, 2), f)
    for pair in range(2):
        for d_ in range(2):
            pp[:, 0:64, pair, d_] = lg[:, d_, 2 * pair][:, None]
            pp[:, 64:128, pair, d_] = lg[:, d_, 2 * pair + 1][:, None]
    c["rlog_pp"] = pp
    c["gnw"] = bc(inp["ret_gn_w"])
    c["identb"] = np.eye(128, dtype=f).astype(NPBF)
    c["identf"] = np.eye(128, dtype=f)
    return c


def prep_core(inp, b):
    f = np.float32
    d = {}
    xt = np.concatenate([inp["ctx"][b], inp["x"][b]], axis=0).T
    d["xT"] = np.ascontiguousarray(xt, f)
    d["cT"] = np.ascontiguousarray(np.stack([inp["c"][b], inp["c_ctx"]], axis=1), f)
    return d


def build(specs, stage=99, taps=(), mixers=("mla", "s5", "ret", "fnet")):
    nc = bass.Bass("TRN2", target_bir_lowering=False)
    dr = {}
    for name, (shape, dt) in specs.items():
        dr[name] = nc.dram_tensor(name, list(shape), dt, kind="ExternalInput").ap()
    outT = nc.dram_tensor("outT", [D, 2048], F32, kind="ExternalOutput").ap()
    tapd = {}
    for name, shape, dt in taps:
        tapd[name] = nc.dram_tensor("tap_" + name, list(shape), dt, kind="ExternalOutput").ap()

    with contextlib.ExitStack() as st:
        P = Prog(nc, st)
        A = Arena(nc, st, 206 * 1024)
        ps = [st.enter_context(nc.psum_tensor("ps%d" % i, [128, 512], F32)) for i in range(8)]
        tps = [T("ps%d" % i, excl=True) for i in range(8)]
        out_handles = []

        def mm(out, lhsT, rhs, start=True, stop=True, r=(), w=()):
            return P.op("pe", lambda e: e.matmul(out, lhsT=lhsT, rhs=rhs, start=start, stop=stop,
                                                 skip_group_check=True), r, w)

        def tr(out, in_, ident, r=(), w=()):
            return P.op("pe", lambda e: e.transpose(out=out, in_=in_, identity=ident), r, w)

        def act(out, in_, func, r=(), w=(), scale=1.0, bias=0.0, accum=None):
            if accum is None:
                return P.op("act", lambda e: e.activation(out=out, in_=in_, func=func, scale=scale, bias=bias), r, w)
            return P.op("act", lambda e: e.activation(out=out, in_=in_, func=func, scale=scale, bias=bias,
                                                      accum_out=accum), r, w)

        def tt(eng, out, a, b, op, r=(), w=()):
            return P.op(eng, lambda e: e.tensor_tensor(out=out, in0=a, in1=b, op=op), r, w)

        def ts(eng, out, a, s1, s2, op0, op1=None, r=(), w=()):
            if op1 is None:
                return P.op(eng, lambda e: e.tensor_scalar(out=out, in0=a, scalar1=s1, scalar2=None, op0=op0), r, w)
            return P.op(eng, lambda e: e.tensor_scalar(out=out, in0=a, scalar1=s1, scalar2=s2, op0=op0, op1=op1), r, w)

        def stt(out, a, s, b, op0, op1, r=(), w=()):
            return P.op("dve", lambda e: e.scalar_tensor_tensor(out=out, in0=a, scalar=s, in1=b, op0=op0, op1=op1), r, w)

        def cp(eng, out, in_, r=(), w=()):
            if eng == "act":
                return P.op(eng, lambda e: e.activation(out=out, in_=in_, func=AF.Copy), r, w)
            return P.op(eng, lambda e: e.tensor_copy(out=out, in_=in_), r, w)

        def red(out, in_, r=(), w=()):
            return P.op("dve", lambda e: e.tensor_reduce(out=out, in_=in_, axis=AX.X, op=ALU.add), r, w)

        def rcp(out, in_, r=(), w=()):
            return P.op("dve", lambda e: e.reciprocal(out=out, in_=in_), r, w)

        def mset(eng, out, val, w=()):
            return P.op(eng, lambda e: e.memset(out, val), (), w)

        def dma(out, in_, r=(), w=(), q="sp"):
            return P.dma(q, lambda e: e.dma_start(out=out, in_=in_), r, w)

        def tap(name, src, r):
            if name in tapd:
                out_handles.append(dma(tapd[name], src, r=r))

        xT = A.alloc([8, NT], F32)
        hT = A.alloc([8, NT], BF16)
        t_x = [[T("x%d_%d" % (k, g)) for g in range(5)] for k in range(8)]
        t_h = [T("h%d" % g) for g in range(5)]
        identb = A.alloc([128], BF16)
        identf = A.alloc([128], F32)
        onesb = A.alloc([128], BF16)
        mod = A.alloc([DEPTH, 48, 2], F32)
        a1 = A.alloc([DEPTH, 16, 2], F32)
        nwt = A.alloc([DEPTH, 16], F32)
        scT = A.alloc([8, 2], F32)
        eps_t = A.alloc([1], F32)
        t_const = T("const")
        t_mod = T("mod")
        NSTG = 2
        NRING = 5
        stg = [A.alloc([1024], F32) for _ in range(NSTG)]
        t_stg = [T("stg%d" % i) for i in range(NSTG)]
        ring = [A.alloc([1024], BF16) for _ in range(NRING)]
        t_ring = [T("ring%d" % i) for i in range(NRING)]
        sidx = [0]
        ridx = [0]
        DYN0 = A.off
        OT0 = A.nbytes - 4 * 2 * NT * 2
        oT = [A.alloc([2, NT], BF16, at=OT0 + i * 2 * NT * 2) for i in range(4)]
        t_oT = [[T("o%d_%d" % (i, g)) for g in range(5)] for i in range(4)]

        def GRP(g):
            return (0, 256) if g == 0 else (LAT0 + 512 * (g - 1), 512)

        def grp_of_tile(tt_):
            return 0 if tt_ < 2 else 1 + (tt_ - 2) // 4

        def next_stg():
            s = sidx[0] % NSTG
            sidx[0] += 1
            return s

        def load_cast(dst, t_dst, src, shape_str=None, **kw):
            s = next_stg()
            n = int(np.prod(src.shape[1:]))
            assert n <= 1024, n
            sv = stg[s][:, 0:n]
            if shape_str is not None:
                sv = sv.rearrange(shape_str, **kw)
            dma(sv, src, w=[t_stg[s]])
            cp("pool", dst, sv, r=[t_stg[s]], w=[t_dst])

        def load_ring(src, shape_str=None, **kw):
            i = ridx[0] % NRING
            ridx[0] += 1
            n = int(np.prod(src.shape[1:]))
            dv = ring[i][:, 0:n]
            if shape_str is not None:
                dv = dv.rearrange(shape_str, **kw)
            load_cast(dv, t_ring[i], src, shape_str, **kw)
            return dv, t_ring[i]

        xsrc = dr["xT"].rearrange("(k p) t -> p k t", p=128)
        for k in range(8):
            dma(xT[:, k, :], xsrc[:, k, :], w=t_x[k])
        dma(identb, dr["identb"], w=[t_const])
        dma(identf, dr["identf"], w=[t_const])
        dma(scT, dr["cT"].rearrange("(k p) j -> p k j", p=128), w=[t_const])
        dma(nwt, dr["nw"].rearrange("l p k -> p l k"), w=[t_const])
        mset("dve", onesb, 1.0, w=[t_const])
        mset("dve", eps_t, EPS, w=[t_const])
        act(scT, scT, AF.Silu, r=[t_const], w=[t_const])

        for l in range(DEPTH):
            P.barrier()
            bT = A.alloc([48], F32, at=DYN0)
            modrow = A.alloc([6144], F32, at=DYN0 + 256)
            t_bT = T()
            t_mr = T()
            dma(bT, dr["ada_bT"][l], w=[t_bT])
            wsrc = dr["ada_w"][l].rearrange("(k p) c -> p k c", p=128)
            for nchunk in range(12):
                pb = nchunk % 2
                for j in range(4):
                    s = next_stg()
                    sv = stg[s][:, 0:1024].rearrange("p (k c) -> p k c", c=512)
                    dma(sv, wsrc[:, 2 * j:2 * j + 2, nchunk * 512:(nchunk + 1) * 512], w=[t_stg[s]])
                    for kk in range(2):
                        k = 2 * j + kk
                        mm(ps[pb][0:2, :], scT[:, k, :], sv[:, kk, :], start=(k == 0), stop=(k == 7),
                           r=[t_stg[s], t_const], w=[tps[pb]])
                cp("act", modrow[0:2, nchunk * 512:(nchunk + 1) * 512], ps[pb][0:2, :], r=[tps[pb]], w=[t_mr])
            psm = ps[2 + l][:, 0:96].rearrange("p (c s) -> p c s", s=2)
            for ct in range(48):
                tr(psm[:, ct, :], modrow[0:2, ct * 128:(ct + 1) * 128], identf[0:2, 0:2], r=[t_mr, t_const], w=[tps[2 + l]])
            tt("dve", mod[:, l, :, :], psm, bT[:, :].unsqueeze(2).to_broadcast([128, 48, 2]), ALU.add,
               r=[tps[2 + l], t_bT], w=[t_mod])
            for j, c0 in ((0, 8), (1, 32)):
                stt(a1[:, l, j * 8:(j + 1) * 8, :], mod[:, l, c0:c0 + 8, :], 1.0,
                    nwt[:, l, j * 8:(j + 1) * 8].unsqueeze(2).to_broadcast([128, 8, 2]),
                    ALU.add, ALU.mult, r=[t_mod, t_const], w=[t_mod])
        tap("mod", mod, [t_mod])

        def norm_phase(l, which, groups):
            sh0 = 0 if which == 0 else 24
            P.barrier()
            sq = A.alloc([2, 8, 512], BF16, at=DYN0)
            rstd = A.alloc([2, 512], F32, at=DYN0 + 2 * 8 * 512 * 2)
            tmp = A.alloc([2, 512], F32, at=DYN0 + 2 * 8 * 512 * 2 + 2 * 512 * 4)
            t_sq = [T(), T()]
            t_rs = [T(), T()]
            t_tmp = [T(), T()]
            for gi, g in enumerate(groups):
                t0, n = GRP(g)
                s = 1 if g == 0 else 0
                b = gi % 2
                pb = 2 + b
                for k in range(8):
                    act(sq[:, b, k, 0:n], xT[:, k, t0:t0 + n], AF.Square, r=[t_x[k][g]], w=[t_sq[b]])
                for k in range(8):
                    mm(ps[pb][:, 0:n], onesb, sq[:, b, k, 0:n], start=(k == 0), stop=(k == 7),
                       r=[t_sq[b], t_const], w=[tps[pb]])
                act(rstd[:, b, 0:n], ps[pb][:, 0:n], AF.Sqrt, r=[tps[pb], t_const], w=[t_rs[b]],
                    scale=1.0 / D, bias=eps_t[:, 0:1])
                rcp(rstd[:, b, 0:n], rstd[:, b, 0:n], r=[t_rs[b]], w=[t_rs[b]])
                for k in range(8):
                    tb = k % 2
                    tt("dve", tmp[:, tb, 0:n], xT[:, k, t0:t0 + n], rstd[:, b, 0:n], ALU.mult,
                       r=[t_x[k][g], t_rs[b]], w=[t_tmp[tb]])
                    act(hT[:, k, t0:t0 + n], tmp[:, tb, 0:n], AF.Identity, r=[t_tmp[tb], t_mod], w=[t_h[g]],
                        scale=a1[:, l, which * 8 + k, s:s + 1], bias=mod[:, l, sh0 + k, s:s + 1])

        def mk_alloc(limit):
            o = [DYN0]

            def al(shape, dt):
                ap = A.alloc(shape, dt, at=o[0])
                n = int(np.prod(shape)) * (2 if dt == BF16 else 4)
                o[0] += (n + 3) // 4 * 4
                assert o[0] <= limit, (o[0], limit)
                return ap
            al.o = o
            return al

        def mla_phase(l, last, slot):
            P.barrier()
            al = mk_alloc(OT0 + slot * 2 * NT * 2)
            Wm = al([8, 416], BF16)
            Wukv = al([512], BF16)
            Wuq = al([2, 384], BF16)
            kvw = al([128], F32)
            qnw = al([256], F32)
            qkq = al([4, 96], F32)
            qkk = al([4, 96], F32)
            rc = al([NTT, 2, 8], F32)
            rs_ = al([NTT, 2, 8], F32)
            KT = al([4, NT], BF16)
            QT = al([4, 512], BF16)
            V1 = al([NTT, 4, 65], BF16)
            PT = al([2, 512], BF16)
            kvn = al([2, 128], BF16)
            kvnT = al([2, 128], BF16)
            qn = al([2, 256], BF16)
            qnT = al([2, 2, 128], BF16)
            kf = al([2, 4, 96], F32)
            sqs2 = al([2, 4, 96], F32)
            kb = al([2, 4, 96], BF16)
            sm = al([2, 16], F32)
            rt2 = al([2, 4, 4, 2, 8], F32) if False else None
            rtA = al([4, 4, 2, 8], F32)
            rtB = al([4, 4, 2, 8], F32)
            oat = al([4, 256], BF16)
            rinv = al([4], F32)
            t_W = T("Wm")
            t_small = T("mlasmall")
            t_KT = [T() for _ in range(NTT)]
            t_QT = [T() for _ in range(4)]
            t_V1 = [T() for _ in range(NTT)]
            t_PT = [T(), T()]
            t_kvn = [T(), T()]
            t_kvnT = [T(), T()]
            t_qn = [T(), T()]
            t_qnT = [T(), T()]
            t_kf = [T(), T()]
            t_sqs2 = [T(), T()]
            t_kb = [T(), T()]
            t_sm = [T(), T()]
            t_rt2 = [T(), T()]
            t_oat = [T() for _ in range(4)]
            t_rinv = T()

            wsrc = dr["w_in"][l].rearrange("(k p) c -> p k c", p=128)
            for k in range(0, 8, 2):
                load_cast(Wm[:, k:k + 2, 0:160], t_W, wsrc[:, k:k + 2, 0:160], "p (k c) -> p k c", c=160)
                load_cast(Wm[:, k:k + 2, 160:416], t_W, wsrc[:, k:k + 2, 928:1184], "p (k c) -> p k c", c=256)
            load_cast(Wukv, t_W, dr["w_ukv"][l])
            load_cast(Wuq, t_W, dr["w_uq"][l].rearrange("(j p) c -> p j c", p=128), "p (j c) -> p j c", c=384)
            dma(kvw, dr["kvw"][l], w=[t_small])
            dma(qnw, dr["qnw"][l], w=[t_small])
            dma(qkq, dr["qkq"][l].rearrange("p (h d) -> p h d", d=96), w=[t_small])
            dma(qkk, dr["qkk"][l].rearrange("p (h d) -> p h d", d=96), w=[t_small])
            dma(rc, dr["ropec"].rearrange("p t (a b) -> p t a b", b=8), w=[t_small])
            dma(rs_, dr["ropes"].rearrange("p t (a b) -> p t a b", b=8), w=[t_small])
            mset("dve", V1[:, :, :, 64:65], 1.0, w=t_V1)
            if stage < 0.5:
                return

            def headnorm_rope(pb, wq, tt_, dst, t_dst, tbank):
                x = kf[:, pb]
                t_x_ = t_kf[pb]
                sqs = sqs2[:, pb]
                t_sqs = t_sqs2[pb]
                rt = rtA if pb == 0 else rtB
                t_rt = t_rt2[pb]
                tt("dve", sqs, x, x, ALU.mult, r=[t_x_], w=[t_sqs])
                st_ = sm[:, pb, 0:4]
                red(st_, sqs, r=[t_sqs], w=[t_sm[pb]])
                act(st_, st_, AF.Sqrt, r=[t_sm[pb], t_const], w=[t_sm[pb]], scale=1.0 / 96, bias=eps_t[:, 0:1])
                rcp(st_, st_, r=[t_sm[pb]], w=[t_sm[pb]])
                tt("dve", x, x, st_.unsqueeze(2).to_broadcast([128, 4, 96]), ALU.mult, r=[t_x_, t_sm[pb]], w=[t_x_])
                tt("dve", x, x, wq, ALU.mult, r=[t_x_, t_small], w=[t_x_])
                y = kb[:, pb]
                t_y = t_kb[pb]
                cp("dve", y[:, :, 0:64], x[:, :, 0:64], r=[t_x_], w=[t_y])
                xr = x[:, :, 64:96].rearrange("p h (a two b) -> p h a two b", two=2, b=8)
                yr = y[:, :, 64:96].rearrange("p h (a two b) -> p h a two b", two=2, b=8)
                cosb = rc[:, tt_].unsqueeze(1).to_broadcast([128, 4, 2, 8])
                sinb = rs_[:, tt_].unsqueeze(1).to_broadcast([128, 4, 2, 8])
                x1 = xr[:, :, :, 0, :]
                x2 = xr[:, :, :, 1, :]
                tt("dve", rt[:, 0], x1, cosb, ALU.mult, r=[t_x_, t_small], w=[t_rt])
                tt("dve", rt[:, 1], x2, sinb, ALU.mult, r=[t_x_, t_small], w=[t_rt])
                tt("dve", yr[:, :, :, 0, :], rt[:, 0], rt[:, 1], ALU.subtract, r=[t_rt], w=[t_y])
                tt("dve", rt[:, 2], x1, sinb, ALU.mult, r=[t_x_, t_small], w=[t_rt])
                tt("dve", rt[:, 3], x2, cosb, ALU.mult, r=[t_x_, t_small], w=[t_rt])
                tt("dve", yr[:, :, :, 1, :], rt[:, 2], rt[:, 3], ALU.add, r=[t_rt], w=[t_y])
                pst = ps[tbank][:, :].bitcast(BF16)
                for h in range(4):
                    tr(pst[0:96, h * 128:(h + 1) * 128], y[:, h, :], identb, r=[t_y, t_const], w=[tps[tbank]])
                cp("act", dst, pst[0:96, 0:512].rearrange("p (h t) -> p h t", t=128), r=[tps[tbank]], w=[t_dst])

            if os.environ.get("KSTOP"):
                P.kstop = int(os.environ["KSTOP"])
            def _kbody(tt_):
                g = grp_of_tile(tt_)
                pb = tt_ % 2
                t0 = tt_ * 128
                bz = pb
                for k in range(8):
                    mm(ps[bz][:, 0:160], hT[:, k, t0:t0 + 128], Wm[:, k, 0:160], start=(k == 0), stop=(k == 7),
                       r=[t_h[g], t_W], w=[tps[bz]])
                z = ps[bz]
                ssk = sm[:, pb, 8:9]
                act(kf[:, pb].rearrange("p h d -> p (h d)")[:, 0:128], z[:, 0:128], AF.Square,
                    r=[tps[bz]], w=[t_kf[pb], t_sm[pb]], accum=ssk)
                act(ssk, ssk, AF.Sqrt, r=[t_sm[pb], t_const], w=[t_sm[pb]], scale=1.0 / 128, bias=eps_t[:, 0:1])
                rcp(ssk, ssk, r=[t_sm[pb]], w=[t_sm[pb]])
                stt(kvn[:, pb], z[:, 0:128], ssk, kvw, ALU.mult, ALU.mult, r=[tps[bz], t_sm[pb], t_small], w=[t_kvn[pb]])
                pst = ps[2 + pb][:, :].bitcast(BF16)
                tr(pst[:, 0:128], kvn[:, pb], identb, r=[t_kvn[pb], t_const], w=[tps[2 + pb]])
                cp("act", kvnT[:, pb], pst[:, 0:128], r=[tps[2 + pb]], w=[t_kvnT[pb]])
                bkv = 4 + pb
                mm(ps[bkv][:, :], kvnT[:, pb], Wukv, r=[t_kvnT[pb], t_W], w=[tps[bkv]])
                kvv = ps[bkv][:, :].rearrange("p (h c) -> p h c", c=128)
                cp("act", V1[:, tt_, :, 0:64], kvv[:, :, 64:128], r=[tps[bkv]], w=[t_V1[tt_]])
                cp("dve", kf[:, pb, :, 0:64], kvv[:, :, 0:64], r=[tps[bkv]], w=[t_kf[pb]])
                cp("dve", kf[:, pb, :, 64:96], z[:, 128:160].unsqueeze(1).to_broadcast([128, 4, 32]),
                   r=[tps[bz]], w=[t_kf[pb]])
                if os.environ.get("NOHN") != "1":
                    headnorm_rope(pb, qkk, tt_, KT[0:96, :, t0:t0 + 128], t_KT[tt_], 2 + pb)
            for tt_ in range(0, NTT, 2):
                P.interleave([(lambda a=tt_: _kbody(a)), (lambda a=tt_ + 1: _kbody(a))])
            P.kstop = None
            tap("KT", KT[0:96], t_KT)
            tap("V1", V1, t_V1)
            if stage < 0.7:
                return

            scale = 96 ** -0.5
            qgroups = [(g, GRP(g)[0], GRP(g)[1], list(range(NTT))) for g in range(1, 5)]
            if not last:
                qgroups = [(0, 0, 256, [0, 1])] + qgroups
            it = 0
            for (g, q0, nq, ktiles) in qgroups:
                nqs = nq // 128

                def _qbody(qs, g=g, q0=q0):
                    tt_ = q0 // 128 + qs
                    pb = qs % 2
                    t0 = tt_ * 128
                    bz = pb
                    for k in range(8):
                        mm(ps[bz][:, 0:256], hT[:, k, t0:t0 + 128], Wm[:, k, 160:416], start=(k == 0), stop=(k == 7),
                           r=[t_h[g], t_W], w=[tps[bz]])
                    z = ps[bz]
                    ssq = sm[:, pb, 9:10]
                    act(qn[:, pb], z[:, 0:256], AF.Square, r=[tps[bz]], w=[t_qn[pb], t_sm[pb]], accum=ssq)
                    act(ssq, ssq, AF.Sqrt, r=[t_sm[pb], t_const], w=[t_sm[pb]], scale=1.0 / 256, bias=eps_t[:, 0:1])
                    rcp(ssq, ssq, r=[t_sm[pb]], w=[t_sm[pb]])
                    stt(qn[:, pb], z[:, 0:256], ssq, qnw, ALU.mult, ALU.mult, r=[tps[bz], t_sm[pb], t_small], w=[t_qn[pb]])
                    pst = ps[2 + pb][:, :].bitcast(BF16)
                    for j in range(2):
                        tr(pst[:, j * 128:(j + 1) * 128], qn[:, pb, j * 128:(j + 1) * 128], identb,
                           r=[t_qn[pb], t_const], w=[tps[2 + pb]])
                    cp("act", qnT[:, pb], pst[:, 0:256].rearrange("p (j t) -> p j t", t=128), r=[tps[2 + pb]], w=[t_qnT[pb]])
                    bq = 2 + pb
                    for j in range(2):
                        mm(ps[bq][:, 0:384], qnT[:, pb, j], Wuq[:, j, :], start=(j == 0), stop=(j == 1),
                           r=[t_qnT[pb], t_W], w=[tps[bq]])
                    cp("dve", kf[:, pb], ps[bq][:, 0:384].rearrange("p (h d) -> p h d", d=96), r=[tps[bq]], w=[t_kf[pb]])
                    headnorm_rope(pb, qkq, tt_, QT[0:96, :, qs * 128:(qs + 1) * 128], t_QT[qs], 2 + pb)
                for qs in range(0, nqs, 2):
                    P.interleave([(lambda a=qs: _qbody(a)), (lambda a=qs + 1: _qbody(a))])
                if g == 1:
                    tap("QT", QT[0:96], t_QT)
                if stage < 0.8:
                    continue
                for h in range(4):
                    for ki, kt in enumerate(ktiles):
                        sb = it % 2
                        it += 1
                        mm(ps[sb][:, 0:nq], KT[0:96, h, kt * 128:(kt + 1) * 128], QT[0:96, h, 0:nq],
                           r=[t_KT[kt]] + t_QT[0:nqs], w=[tps[sb]])
                        act(PT[:, sb, 0:nq], ps[sb][:, 0:nq], AF.Exp, r=[tps[sb]], w=[t_PT[sb]], scale=scale)
                        for qs in range(nqs):
                            mm(ps[4 + qs][:, 0:65], PT[:, sb, qs * 128:(qs + 1) * 128], V1[:, kt, h, :],
                               start=(ki == 0), stop=(ki == len(ktiles) - 1),
                               r=[t_PT[sb], t_V1[kt]], w=[tps[4 + qs]])
                    for qs in range(nqs):
                        rcp(rinv[:, qs:qs + 1], ps[4 + qs][:, 64:65], r=[tps[4 + qs]], w=[t_rinv])
                        ts("dve", oat[:, qs, h * 64:(h + 1) * 64], ps[4 + qs][:, 0:64], rinv[:, qs:qs + 1], None,
                           ALU.mult, r=[tps[4 + qs], t_rinv], w=[t_oat[qs]])
                for qs in range(nqs):
                    tb = 2 + qs % 2
                    pst = ps[tb][:, :].bitcast(BF16)
                    for j in range(2):
                        tr(pst[:, j * 128:(j + 1) * 128], oat[:, qs, j * 128:(j + 1) * 128], identb,
                           r=[t_oat[qs], t_const], w=[tps[tb]])
                    tq = q0 + qs * 128
                    cp("act", oT[slot][:, :, tq:tq + 128], pst[:, 0:256].rearrange("p (j t) -> p j t", t=128),
                       r=[tps[tb]], w=[t_oT[slot][g]])
            tap("oaT", oT[slot], t_oT[slot])

        def fnet_phase(l, last, slot):
            P.barrier()
            al = mk_alloc(OT0 + slot * 2 * NT * 2)
            UT = al([2, NT], BF16)
            AB = al([NTT, 512], BF16)
            CS = al([2, 512], BF16)
            tb = al([2, 2, 1024], BF16)
            c256 = al([2, 2, 256], BF16)
            t_UT = [T() for _ in range(5)]
            t_AB = [T() for _ in range(NTT)]
            t_CS = T()
            t_tb = [T(), T()]
            t_c256 = T()
            dma(CS, dr["cs64"], w=[t_CS])
            for lt in range(2):
                dma(c256[:, lt], dr["dft256"][:, lt * 128:(lt + 1) * 128, :].rearrange("c p n -> p c n"), w=[t_c256])
            wsrc = dr["w_in"][l].rearrange("(k p) c -> p k c", p=128)
            groups = [g for g in range(5) if not (last and g == 0)]
            bi = 0
            for j in range(2):
                wv, t_wv = load_ring(wsrc[:, :, 1184 + j * 128:1184 + (j + 1) * 128], "p (k c) -> p k c", c=128)
                for g in groups:
                    t0, n = GRP(g)
                    pb = bi % 4
                    bi += 1
                    for k in range(8):
                        mm(ps[pb][:, 0:n], wv[:, k, :], hT[:, k, t0:t0 + n], start=(k == 0), stop=(k == 7),
                           r=[t_wv, t_h[g]], w=[tps[pb]])
                    cp("act", UT[:, j, t0:t0 + n], ps[pb][:, 0:n], r=[tps[pb]], w=[t_UT[g]])
            for tt_ in (range(2, NTT) if last else range(NTT)):
                g = grp_of_tile(tt_)
                pb = 4 + tt_ % 4
                for j in range(2):
                    mm(ps[pb][:, :], UT[:, j, tt_ * 128:(tt_ + 1) * 128], CS[:, j, :], start=(j == 0), stop=(j == 1),
                       r=[t_UT[g], t_CS], w=[tps[pb]])
                cp("dve", AB[:, tt_, :], ps[pb][:, :], r=[tps[pb]], w=[t_AB[tt_]])
            if not last:
                for j in range(2):
                    pb = j
                    n_mm = 0
                    for lt in range(2):
                        for cs_ in range(2):
                            mm(ps[pb][:, 0:256], AB[:, lt, cs_ * 256 + j * 128:cs_ * 256 + (j + 1) * 128],
                               c256[:, lt, cs_, :], start=(n_mm == 0), stop=(n_mm == 3),
                               r=[t_AB[lt], t_c256], w=[tps[pb]])
                            n_mm += 1
                    cp("act", oT[slot][:, j, 0:256], ps[pb][:, 0:256], r=[tps[pb]], w=[t_oT[slot][0]])
            it = 0
            for half in range(2):
                banks = [4 * half + i for i in range(4)]
                for lt in range(16):
                    b_ = it % 2
                    it += 1
                    dma(tb[:, b_], dr["dft2048"][:, lt * 128:(lt + 1) * 128, half * 1024:(half + 1) * 1024]
                        .rearrange("c p n -> p c n"), w=[t_tb[b_]])
                    for j in range(2):
                        for cs_ in range(2):
                            for lg in range(2):
                                bk = banks[j * 2 + lg]
                                mm(ps[bk][:, :], AB[:, 2 + lt, cs_ * 256 + j * 128:cs_ * 256 + (j + 1) * 128],
                                   tb[:, b_, cs_, lg * 512:(lg + 1) * 512],
                                   start=(lt == 0 and cs_ == 0), stop=(lt == 15 and cs_ == 1),
                                   r=[t_AB[2 + lt], t_tb[b_]], w=[tps[bk]])
                for j in range(2):
                    for lg in range(2):
                        bk = banks[j * 2 + lg]
                        g = 1 + half * 2 + lg
                        t0 = LAT0 + half * 1024 + lg * 512
                        cp("act", oT[slot][:, j, t0:t0 + 512], ps[bk][:, :], r=[tps[bk]], w=[t_oT[slot][g]])
            tap("obT", oT[slot], t_oT[slot])

        def ret_phase(l, last, slot):
            P.barrier()
            al = mk_alloc(OT0 + slot * 2 * NT * 2)
            rrc = al([NTT, 32], F32)
            rrs = al([NTT, 32], F32)
            retc = al([6, 128], F32)
            retp = al([2], F32)
            lgrow = al([8], F32)
            lgpp = al([2, 2], F32)
            gnw = al([256], F32)
            Wr = al([8, 512], BF16)
            QZ = al([2, NT], BF16)
            KT_ = al([NT], BF16)
            Vb = al([NTT, 128], BF16)
            sg = al([NTT, 128], BF16)
            Sfp = al([NTT, 128], BF16)
            Sbn = Wr.rearrange("p k c -> p (k c)")[:, 0:NTT * 128].rearrange("p (i c) -> p i c", c=128)
            ub_off = al.o[0]
            Ub = al([NTT, 128], BF16)
            kdec = al([2, 128], F32)
            qdec = al([2, 128], F32)
            Mk = al([2, 128], F32)
            aab = al([2], F32)
            Sf = al([128], F32)
            Sb = al([128], F32)
            rt = al([2, 4, 2, 32], F32)
            Qb = al([2, 128], BF16)
            Kb = al([2, 128], BF16)
            Kd = al([2, 2, 128], BF16)
            Qd = A.alloc([2, 2, 2, 128], BF16, at=ub_off)
            attm = A.alloc([2, 2, 128], BF16, at=ub_off + 2048)
            xc2 = al([2, 2, 64], F32)
            xcq = xc2[:, 0].rearrange("p a d -> p (a d)")
            sq2 = al([2, 2, 64], F32)
            st2 = al([2, 2, 4], F32)
            od = A.alloc([2, 128], BF16, at=ub_off + 3072)
            t_c = T()
            t_lg = T()
            t_Wr = T()
            t_QK = [T() for _ in range(NTT)]
            t_Vb = [T() for _ in range(NTT)]
            t_sg = [T() for _ in range(NTT)]
            t_Sfp = [T() for _ in range(NTT)]
            t_Sbn = [T() for _ in range(NTT)]
            t_Ub = [T() for _ in range(NTT)]
            t_tab = T()
            t_S = T()
            t_rt = [T(), T()]
            t_Qb = [T(), T()]
            t_Kb = [T(), T()]
            t_Kd = [T(), T()]
            t_Qd = [T(), T()]
            t_attm = [T(), T()]
            t_gn2 = [T(), T()]
            t_od = [T(), T()]
            dma(rrc, dr["rrc"], w=[t_c])
            dma(rrs, dr["rrs"], w=[t_c])
            dma(retc, dr["retc"], w=[t_c])
            dma(retp, dr["retp"], w=[t_c])
            dma(lgrow, dr["rlog_row"][l], w=[t_lg])
            dma(lgpp, dr["rlog_pp"][l], w=[t_lg])
            dma(gnw, dr["gnw"][l], w=[t_c])
            for v in (lgrow, lgpp.rearrange("p a b -> p (a b)")):
                act(v, v, AF.Exp, r=[t_lg], w=[t_lg], scale=-1.0)
                act(v, v, AF.Ln, r=[t_lg], w=[t_lg], bias=1.0)
                ts("dve", v, v, -1.0, None, ALU.mult, r=[t_lg], w=[t_lg])
            wsrc = dr["w_in"][l].rearrange("(k p) c -> p k c", p=128)
            for pair in range(2):
                P.barrier()
                offs = [416 + pair * 128, 672 + pair * 128, 1440 + pair * 128, 1696 + pair * 128]
                for ci, c0 in enumerate(offs):
                    load_cast(Wr[:, :, ci * 128:(ci + 1) * 128], t_Wr, wsrc[:, :, c0:c0 + 128], "p (k c) -> p k c", c=128)
                for hh in range(2):
                    head = 2 * pair + hh
                    cs_ = slice(hh * 64, (hh + 1) * 64)
                    act(kdec[:, 0, cs_], lgrow[:, head:head + 1].to_broadcast([128, 64]), AF.Exp, r=[t_lg, t_c], w=[t_tab],
                        scale=retp[:, 1:2])
                    act(kdec[:, 1, cs_], lgrow[:, 4 + head:5 + head].to_broadcast([128, 64]), AF.Exp, r=[t_lg, t_c], w=[t_tab],
                        scale=retp[:, 0:1])
                    act(Mk[:, hh, :], retc[:, 2, :], AF.Exp, r=[t_lg, t_c], w=[t_tab], scale=lgrow[:, head:head + 1])
                    tt("dve", Mk[:, hh, :], Mk[:, hh, :], retc[:, 4, :], ALU.mult, r=[t_tab, t_c], w=[t_tab])
                    act(xcq, retc[:, 3, :], AF.Exp, r=[t_lg, t_c], w=[t_tab], scale=lgrow[:, 4 + head:5 + head])
                    tt("dve", xcq, xcq, retc[:, 5, :], ALU.mult, r=[t_tab, t_c], w=[t_tab])
                    stt(Mk[:, hh, :], Mk[:, hh, :], 1.0, xcq, ALU.mult, ALU.add, r=[t_tab], w=[t_tab])
                ts("dve", Mk, Mk, 0.125, None, ALU.mult, r=[t_tab], w=[t_tab])
                ts("dve", kdec, kdec, 0.125, None, ALU.mult, r=[t_tab], w=[t_tab])
                act(qdec[:, 0, :], retc[:, 0, :], AF.Exp, r=[t_lg, t_c], w=[t_tab], scale=lgpp[:, pair, 0:1])
                act(qdec[:, 1, :], retc[:, 1, :], AF.Exp, r=[t_lg, t_c], w=[t_tab], scale=lgpp[:, pair, 1:2])
                act(aab, lgpp[:, pair, :], AF.Exp, r=[t_lg], w=[t_tab], scale=128.0)
                mset("dve", Sf, 0.0, w=[t_S])
                mset("dve", Sb, 0.0, w=[t_S])
                mset("pool", QZ, 0.0, w=t_QK)

                def rope(src, cosT, sinT, dst, pb, t_dst, t_src):
                    x = src.rearrange("p (h two b) -> p h two b", two=2, b=32)
                    y = dst.rearrange("p (h two b) -> p h two b", two=2, b=32)
                    cb = cosT.unsqueeze(1).to_broadcast([128, 2, 32])
                    sb_ = sinT.unsqueeze(1).to_broadcast([128, 2, 32])
                    r_ = rt[:, pb]
                    tt("dve", r_[:, 0], x[:, :, 0, :], cb, ALU.mult, r=[t_src, t_c], w=[t_rt[pb]])
                    tt("dve", r_[:, 1], x[:, :, 1, :], sb_, ALU.mult, r=[t_src, t_c], w=[t_rt[pb]])
                    tt("dve", y[:, :, 0, :], r_[:, 0], r_[:, 1], ALU.subtract, r=[t_rt[pb]], w=[t_dst])
                    tt("dve", r_[:, 2], x[:, :, 0, :], sb_, ALU.mult, r=[t_src, t_c], w=[t_rt[pb]])
                    tt("dve", r_[:, 3], x[:, :, 1, :], cb, ALU.mult, r=[t_src, t_c], w=[t_rt[pb]])
                    tt("dve", y[:, :, 1, :], r_[:, 2], r_[:, 3], ALU.add, r=[t_rt[pb]], w=[t_dst])

                def _p1body(i):
                    g = grp_of_tile(i)
                    pb = i % 2
                    t0 = i * 128
                    bz = pb
                    for k in range(8):
                        mm(ps[bz][:, :], hT[:, k, t0:t0 + 128], Wr[:, k, :], start=(k == 0), stop=(k == 7),
                           r=[t_h[g], t_Wr], w=[tps[bz]])
                    z = ps[bz]
                    rope(z[:, 256:384], rrc[:, i, :], rrs[:, i, :], Qb[:, pb], pb, t_Qb[pb], tps[bz])
                    rope(z[:, 0:128], rrc[:, i, :], rrs[:, i, :], Kb[:, pb], pb, t_Kb[pb], tps[bz])
                    cp("act", Vb[:, i, :], z[:, 128:256], r=[tps[bz]], w=[t_Vb[i]])
                    act(sg[:, i, :], z[:, 384:512], AF.Silu, r=[tps[bz]], w=[t_sg[i]])
                    tb_ = 2 + pb
                    pst = ps[tb_][:, :].bitcast(BF16)
                    tr(pst[:, 0:128], Qb[:, pb], identb, r=[t_Qb[pb], t_const], w=[tps[tb_]])
                    tr(pst[:, 128:256], Kb[:, pb], identb, r=[t_Kb[pb], t_const], w=[tps[tb_]])
                    cp("act", QZ[0:64, 0, t0:t0 + 128], pst[0:64, 0:128], r=[tps[tb_]], w=[t_QK[i]])
                    cp("act", QZ[64:128, 1, t0:t0 + 128], pst[64:128, 0:128], r=[tps[tb_]], w=[t_QK[i]])
                    cp("act", KT_[:, t0:t0 + 128], pst[:, 128:256], r=[tps[tb_]], w=[t_QK[i]])
                    tt("pool", Kd[:, pb, 0], Kb[:, pb], kdec[:, 0], ALU.mult, r=[t_Kb[pb], t_tab], w=[t_Kd[pb]])
                    tt("pool", Kd[:, pb, 1], Kb[:, pb], kdec[:, 1], ALU.mult, r=[t_Kb[pb], t_tab], w=[t_Kd[pb]])
                    bu = 4 + pb
                    mm(ps[bu][:, 0:128], Kd[:, pb, 0], Vb[:, i, :], r=[t_Kd[pb], t_Vb[i]], w=[tps[bu]])
                    mm(ps[bu][:, 128:256], Kd[:, pb, 1], Vb[:, i, :], r=[t_Kd[pb], t_Vb[i]], w=[tps[bu]])

                for i0 in range(0, NTT, 2):
                    P.interleave([(lambda a=i0: _p1body(a)), (lambda a=i0 + 1: _p1body(a))])
                    for i in (i0, i0 + 1):
                        bu = 4 + i % 2
                        cp("dve", Sfp[:, i, :], Sf, r=[t_S], w=[t_Sfp[i]])
                        stt(Sf, Sf, aab[:, 0:1], ps[bu][:, 0:128], ALU.mult, ALU.add, r=[t_S, t_tab, tps[bu]], w=[t_S])
                        cp("act", Ub[:, i, :], ps[bu][:, 128:256], r=[tps[bu]], w=[t_Ub[i]])
                P.barrier()
                for i in [1, 0] + list(range(NTT - 1, 1, -1)):
                    cp("dve", Sbn[:, i, :], Sb, r=[t_S], w=[t_Sbn[i]])
                    stt(Sb, Sb, aab[:, 1:2], Ub[:, i, :], ALU.mult, ALU.add, r=[t_S, t_tab, t_Ub[i]], w=[t_S])
                P.barrier()
                def _p2body(i):
                    g = grp_of_tile(i)
                    pb = i % 2
                    t0 = i * 128
                    xc = xc2[:, pb]
                    sq = sq2[:, pb]
                    st_ = st2[:, pb]
                    t_gn = t_gn2[pb]
                    ba = pb
                    for hh in range(2):
                        mm(ps[ba][:, hh * 128:(hh + 1) * 128], KT_[:, t0:t0 + 128], QZ[:, hh, t0:t0 + 128],
                           r=[t_QK[i]], w=[tps[ba]])
                    tt("dve", attm[:, pb], ps[ba][:, 0:256].rearrange("p (a t) -> p a t", t=128), Mk, ALU.mult,
                       r=[tps[ba], t_tab], w=[t_attm[pb]])
                    for hh in range(2):
                        tt("pool", Qd[:, pb, hh, 0], QZ[:, hh, t0:t0 + 128], qdec[:, 0], ALU.mult, r=[t_QK[i], t_tab], w=[t_Qd[pb]])
                        tt("pool", Qd[:, pb, hh, 1], QZ[:, hh, t0:t0 + 128], qdec[:, 1], ALU.mult, r=[t_QK[i], t_tab], w=[t_Qd[pb]])
                    bo = 4 + pb
                    for hh in range(2):
                        rs_ = slice(hh * 64, (hh + 1) * 64)
                        mm(ps[bo][:, rs_], attm[:, pb, hh, :], Vb[:, i, rs_], start=True, stop=False,
                           r=[t_attm[pb], t_Vb[i]], w=[tps[bo]])
                        mm(ps[bo][:, rs_], Qd[:, pb, hh, 0, :], Sfp[:, i, rs_], start=False, stop=False,
                           r=[t_Qd[pb], t_Sfp[i]], w=[tps[bo]])
                        mm(ps[bo][:, rs_], Qd[:, pb, hh, 1, :], Sbn[:, i, rs_], start=False, stop=True,
                           r=[t_Qd[pb], t_Sbn[i]], w=[tps[bo]])
                    o = ps[bo][:, 0:128].rearrange("p (a d) -> p a d", d=64)
                    red(st_[:, 0, 0:2], o, r=[tps[bo]], w=[t_gn])
                    ts("dve", st_[:, 0, 0:2], st_[:, 0, 0:2], -1.0 / 64, None, ALU.mult, r=[t_gn], w=[t_gn])
                    tt("dve", xc, o, st_[:, 0, 0:2].unsqueeze(2).to_broadcast([128, 2, 64]), ALU.add, r=[tps[bo], t_gn], w=[t_gn])
                    tt("dve", sq, xc, xc, ALU.mult, r=[t_gn], w=[t_gn])
                    red(st_[:, 1, 0:2], sq, r=[t_gn], w=[t_gn])
                    act(st_[:, 1, 0:2], st_[:, 1, 0:2], AF.Sqrt, r=[t_gn, t_const], w=[t_gn], scale=1.0 / 64, bias=eps_t[:, 0:1])
                    rcp(st_[:, 1, 0:2], st_[:, 1, 0:2], r=[t_gn], w=[t_gn])
                    tt("dve", xc, xc, st_[:, 1, 0:2].unsqueeze(2).to_broadcast([128, 2, 64]), ALU.mult, r=[t_gn], w=[t_gn])
                    tt("dve", xc, xc, gnw[:, pair * 128:(pair + 1) * 128].rearrange("p (a d) -> p a d", d=64), ALU.mult,
                       r=[t_gn, t_c], w=[t_gn])
                    tt("dve", od[:, pb].rearrange("p (a d) -> p a d", d=64), xc,
                       sg[:, i, :].rearrange("p (a d) -> p a d", d=64), ALU.mult, r=[t_gn, t_sg[i]], w=[t_od[pb]])
                    tb_ = 2 + pb
                    pst = ps[tb_][:, :].bitcast(BF16)
                    tr(pst[:, 0:128], od[:, pb], identb, r=[t_od[pb], t_const], w=[tps[tb_]])
                    cp("act", oT[slot][:, pair, t0:t0 + 128], pst[:, 0:128], r=[tps[tb_]], w=[t_oT[slot][g]])

                for i0 in range(2 if last else 0, NTT, 2):
                    P.interleave([(lambda a=i0: _p2body(a)), (lambda a=i0 + 1: _p2body(a))])
            tap("odT", oT[slot], t_oT[slot])

        I32 = mybir.dt.int32
        TWO_PI = 2.0 * math.pi

        def s5_phase(l, last, slot):
            P.barrier()
            al = mk_alloc(OT0 + slot * 2 * NT * 2)
            uT = al([2, NT], BF16)
            yf = al([2, NT], BF16)
            E = al([2, 1024], BF16)
            Fm = al([8, 2, 128], BF16)
            Bb = al([2, 2, 512], BF16)
            Cc = al([2, 8, 128], BF16)
            Tri = al([2, 128], BF16)
            pp = al([3, 8], F32)
            sm = al([12, 8], F32)
            cst = al([4], F32)
            erow = al([2, 128], F32)
            ecol = al([4], F32)
            dvec = al([2], F32)
            xl = sm[:, 7:9, :].rearrange("p a b -> p (a b)").rearrange("p (s r) -> p s r", r=2)
            ccx = al([128], F32)
            cc = ccx[:, 0:16].rearrange("p (a b) -> p a b", b=2)
            id2 = al([16], BF16)
            woff = al.o[0]
            W = al([2, 1024], BF16)
            xx = al([2, 8, 128], BF16)
            tW = al([2, 2, 256], BF16)
            tq = al([2, 2, 512], BF16)
            ysc = A.alloc([4, 128], F32, at=al.o[0] - 2048)
            cT = al([128], BF16)
            t_u = [T() for _ in range(5)]
            t_yf = [T() for _ in range(NTT)]
            t_tab = T()
            t_pp = T()
            t_c = T()
            t_W = [T(), T()]
            t_xx = [T(), T()]
            t_tW2 = [T(), T()]
            t_tq2 = [T(), T()]
            t_xl = T()
            t_cc = T()
            t_ysc = t_tq2[1]
            t_cT = T()
            t_blk = T()

            dma(Tri, dr["s5tri"], w=[t_c])
            dma(id2, dr["s5id2"], w=[t_c])
            dma(erow, dr["s5erow"], w=[t_c])
            dma(ecol, dr["s5ecol"], w=[t_c])
            dma(dvec, dr["s5d"][l], w=[t_c])
            mset("dve", cst[:, 0:1], -math.pi, w=[t_c])

            wsrc = dr["w_in"][l].rearrange("(k p) c -> p k c", p=128)
            bi = 0
            for j in range(2):
                wv, t_wv = load_ring(wsrc[:, :, 160 + j * 128:160 + (j + 1) * 128], "p (k c) -> p k c", c=128)
                for g in range(5):
                    t0, n = GRP(g)
                    pb = bi % 4
                    bi += 1
                    for k in range(8):
                        mm(ps[pb][:, 0:n], wv[:, k, :], hT[:, k, t0:t0 + n], start=(k == 0), stop=(k == 7),
                           r=[t_wv, t_h[g]], w=[tps[pb]])
                    cp("act", uT[:, j, t0:t0 + n], ps[pb][:, 0:n], r=[tps[pb]], w=[t_u[g]])
            tap("uT", uT, t_u)

            def cplx_pow(out_re, out_im, phase, mag, n, conj, tmp):
                r_, n_i, f_, m_ = tmp
                ts("dve", r_, phase, 1.0 / TWO_PI, None, ALU.mult, r=[t_blk], w=[t_blk])
                cp("dve", n_i.bitcast(I32), r_, r=[t_blk], w=[t_blk])
                cp("dve", f_, n_i.bitcast(I32), r=[t_blk], w=[t_blk])
                tt("dve", f_, r_, f_, ALU.subtract, r=[t_blk], w=[t_blk])
                ts("dve", m_, f_, 0.0, None, ALU.is_lt, r=[t_blk], w=[t_blk])
                tt("dve", f_, f_, m_, ALU.add, r=[t_blk], w=[t_blk])
                act(r_, f_, AF.Sin, r=[t_blk, t_c], w=[t_blk], scale=TWO_PI, bias=cst[:, 0:1])
                ts("dve", f_, f_, 0.25, None, ALU.add, r=[t_blk], w=[t_blk])
                ts("dve", m_, f_, 1.0, None, ALU.is_ge, r=[t_blk], w=[t_blk])
                tt("dve", f_, f_, m_, ALU.subtract, r=[t_blk], w=[t_blk])
                act(m_, f_, AF.Sin, r=[t_blk, t_c], w=[t_blk], scale=TWO_PI, bias=cst[:, 0:1])
                stt(out_re, mag, -1.0, m_, ALU.mult, ALU.mult, r=[t_blk], w=[t_blk, t_tab])
                if conj:
                    tt("dve", out_im, mag, r_, ALU.mult, r=[t_blk], w=[t_blk, t_tab])
                else:
                    stt(out_im, mag, -1.0, r_, ALU.mult, ALU.mult, r=[t_blk], w=[t_blk, t_tab])

            glw = None
            for d_ in range(2):
                P.barrier()
                B_ = [A.alloc([256], F32, at=woff + i * 1024) for i in range(16)]
                dma(pp, dr["s5pp"][l, d_], w=[t_pp])
                act(pp[:, 2, :], pp[:, 2, :], AF.Exp, r=[t_pp], w=[t_pp])
                App = sm[:, 0, :]
                Bpp = sm[:, 1, :]
                tt("dve", App, pp[:, 0, :], pp[:, 2, :], ALU.mult, r=[t_pp], w=[t_blk])
                tt("dve", Bpp, pp[:, 1, :], pp[:, 2, :], ALU.mult, r=[t_pp], w=[t_blk])
                l1re = sm[:, 2, :]
                l1im = sm[:, 3, :]
                mg = sm[:, 4, :]
                act(mg, App, AF.Exp, r=[t_blk], w=[t_blk])
                tmp8 = [B_[0][:, 0:8], B_[0][:, 8:16], B_[0][:, 16:24], B_[0][:, 24:32]]
                cplx_pow(l1re, l1im, Bpp, mg, 8, False, tmp8)
                br = sm[:, 5, :]
                den = sm[:, 6, :]
                kre = sm[:, 7, :]
                kim = sm[:, 8, :]
                nkre = sm[:, 9, :]
                nkim = sm[:, 10, :]
                t8 = sm[:, 11, :]
                ts("dve", br, l1re, -1.0, None, ALU.add, r=[t_blk], w=[t_blk])
                tt("dve", den, pp[:, 0, :], pp[:, 0, :], ALU.mult, r=[t_pp], w=[t_blk])
                tt("dve", t8, pp[:, 1, :], pp[:, 1, :], ALU.mult, r=[t_pp], w=[t_blk])
                tt("dve", den, den, t8, ALU.add, r=[t_blk], w=[t_blk])
                rcp(den, den, r=[t_blk], w=[t_blk])
                tt("dve", kre, br, pp[:, 0, :], ALU.mult, r=[t_blk, t_pp], w=[t_blk])
                tt("dve", t8, l1im, pp[:, 1, :], ALU.mult, r=[t_blk, t_pp], w=[t_blk])
                tt("dve", kre, kre, t8, ALU.add, r=[t_blk], w=[t_blk])
                tt("dve", kre, kre, den, ALU.mult, r=[t_blk], w=[t_blk])
                tt("dve", kim, l1im, pp[:, 0, :], ALU.mult, r=[t_blk, t_pp], w=[t_blk])
                tt("dve", t8, br, pp[:, 1, :], ALU.mult, r=[t_blk, t_pp], w=[t_blk])
                tt("dve", kim, kim, t8, ALU.subtract, r=[t_blk], w=[t_blk])
                tt("dve", kim, kim, den, ALU.mult, r=[t_blk], w=[t_blk])
                ts("dve", nkre, kre, -1.0, None, ALU.mult, r=[t_blk], w=[t_blk])
                ts("dve", nkim, kim, -1.0, None, ALU.mult, r=[t_blk], w=[t_blk])
                Cre = A.alloc([8, 128], F32, at=woff + 1 * 1024)
                Cim = A.alloc([8, 128], F32, at=woff + 5 * 1024)
                for ri, Cdst in enumerate((Cre, Cim)):
                    dma(Cdst, dr["s5c"][l, d_, ri].rearrange("a s c -> s a c"), w=[t_blk])
                tC = B_[9][:, 0:128]
                for st in range(8):
                    ts("dve", tC, Cre[:, st, :], kre[:, st:st + 1], None, ALU.mult, r=[t_blk], w=[t_blk])
                    stt(Cc[:, 0, st, :], Cim[:, st, :], nkim[:, st:st + 1], tC, ALU.mult, ALU.add, r=[t_blk], w=[t_tab])
                    ts("dve", tC, Cre[:, st, :], nkim[:, st:st + 1], None, ALU.mult, r=[t_blk], w=[t_blk])
                    stt(Cc[:, 1, st, :], Cim[:, st, :], nkre[:, st:st + 1], tC, ALU.mult, ALU.add, r=[t_blk], w=[t_tab])
                for kt in range(2):
                    load_cast(Bb[:, kt], t_tab, dr["s5b"][l, d_, :, kt], "p (a c) -> p a c", c=512)
                er = erow[:, d_, :]
                for st in range(8):
                    ph = B_[9][:, 0:128]
                    mgb = B_[9][:, 128:256]
                    ts("dve", ph, er, Bpp[:, st:st + 1], None, ALU.mult, r=[t_c, t_blk], w=[t_blk])
                    act(mgb, er, AF.Exp, r=[t_c, t_blk], w=[t_blk], scale=App[:, st:st + 1])
                    tmpb = [B_[10][:, 0:128], B_[10][:, 128:256], B_[11][:, 0:128], B_[11][:, 128:256]]
                    cplx_pow(Fm[:, st, 0, :], Fm[:, st, 1, :], ph, mgb, 128, False, tmpb)
                row = A.alloc([3, 256], F32, at=woff + 1 * 1024)
                for cb in range(4):
                    dma(row, dr["s5row"][l, d_, :, :, cb * 256:(cb + 1) * 256], w=[t_blk])
                    act(row[:, 2, :], row[:, 2, :], AF.Exp, r=[t_blk], w=[t_blk])
                    Ab = B_[4]
                    Bk = B_[5]
                    tt("dve", Ab, row[:, 0, :], row[:, 2, :], ALU.mult, r=[t_blk], w=[t_blk])
                    tt("dve", Bk, row[:, 1, :], row[:, 2, :], ALU.mult, r=[t_blk], w=[t_blk])
                    ph = B_[6]
                    mgb = B_[7]
                    ts("dve", ph, Bk, ecol[:, d_:d_ + 1], None, ALU.mult, r=[t_blk, t_c], w=[t_blk])
                    act(mgb, Ab, AF.Exp, r=[t_blk, t_c], w=[t_blk], scale=ecol[:, 2 + d_:3 + d_])
                    tmpb = [B_[8], B_[9], B_[10], B_[11]]
                    cplx_pow(E[:, 0, cb * 256:(cb + 1) * 256], E[:, 1, cb * 256:(cb + 1) * 256], ph, mgb, 256, True, tmpb)
                if d_ == 0:
                    tap("s5E", E, [t_tab])
                    tap("s5F", Fm, [t_tab])
                    tap("s5C", Cc, [t_tab])
                P.barrier()
                order = list(range(NTT)) if d_ == 0 else [1, 0] + list(range(NTT - 1, 1, -1))
                lastcol = 127 if d_ == 0 else 0
                mset("dve", ccx, 0.0, w=[t_cc])
                for idx, i in enumerate(order):
                    g = grp_of_tile(i)
                    t0 = i * 128
                    pb = idx % 2
                    for nb in range(4):
                        kt = nb % 2
                        ri = nb // 2
                        mm(ps[nb][:, :], uT[:, kt, t0:t0 + 128], Bb[:, kt, ri, :], r=[t_u[g], t_tab], w=[tps[nb]])
                    for qb in range(4):
                        hb = qb // 2
                        sl = slice(qb * 256, (qb + 1) * 256)
                        pl = slice((qb % 2) * 256, (qb % 2 + 1) * 256)
                        for ri_o in range(2):
                            wb_ = (qb * 2 + ri_o) % 2
                            tw_ = tW[:, wb_]
                            t_tW = t_tW2[wb_]
                            e0, e1 = (0, 1) if ri_o == 0 else (1, 0)
                            tt("dve", tw_[:, 0, :], ps[hb][:, pl], E[:, e0, sl], ALU.mult, r=[tps[hb], t_tab], w=[t_tW])
                            tt("dve", tw_[:, 1, :], ps[2 + hb][:, pl], E[:, e1, sl], ALU.mult, r=[tps[2 + hb], t_tab], w=[t_tW])
                            tt("pool", W[:, ri_o, sl], tw_[:, 0, :], tw_[:, 1, :], ALU.subtract if ri_o == 0 else ALU.add,
                               r=[t_tW], w=[t_W[ri_o]])
                    tr(ps[2][:, 0:128], ccx, identf, r=[t_cc, t_const], w=[tps[2]])
                    cp("act", cT[:, :], ps[2][:, 0:128], r=[tps[2]], w=[t_cT])
                    tt("dve", cT[32:64, :], ps[2][32:64, 0:128], cT[32:64, :], ALU.subtract, r=[tps[2], t_cT], w=[t_cT])
                    for ri in range(2):
                        for st in range(8):
                            bk = 4 + ri * 2 + st // 4
                            mm(ps[bk][:, (st % 4) * 128:(st % 4 + 1) * 128], W[:, ri, st * 128:(st + 1) * 128], Tri[:, d_, :],
                               start=(st % 4 == 0), stop=False, r=[t_W[ri], t_c], w=[tps[bk]])
                    for ri in range(2):
                        for st in range(8):
                            bk = 4 + ri * 2 + st // 4
                            jj = st * 2 + ri
                            mm(ps[bk][:, (st % 4) * 128:(st % 4 + 1) * 128], cT[:, :],
                               id2[:, jj:jj + 1].to_broadcast([128, 128]),
                               start=False, stop=True, r=[t_cT, t_c], w=[tps[bk]])
                    for hf in range(2):
                        Sre = ps[4 + hf][:, :].rearrange("p (a t) -> p a t", t=128)
                        Sim = ps[6 + hf][:, :].rearrange("p (a t) -> p a t", t=128)
                        Fre = Fm[:, 4 * hf:4 * hf + 4, 0, :]
                        Fim = Fm[:, 4 * hf:4 * hf + 4, 1, :]
                        for ri_o in range(2):
                            qb_ = (hf * 2 + ri_o) % 2
                            t_tq = t_tq2[qb_]
                            q0 = tq[:, qb_, 0, :].rearrange("p (a t) -> p a t", t=128)
                            q1 = tq[:, qb_, 1, :].rearrange("p (a t) -> p a t", t=128)
                            f0, f1 = (Fre, Fim) if ri_o == 0 else (Fim, Fre)
                            op_ = ALU.subtract if ri_o == 0 else ALU.add
                            tt("dve", q0, Sre, f0, ALU.mult, r=[tps[4 + hf], t_tab], w=[t_tq])
                            tt("dve", q1, Sim, f1, ALU.mult, r=[tps[6 + hf], t_tab], w=[t_tq])
                            tt("dve", xl[:, 4 * hf:4 * hf + 4, ri_o], q0[:, :, lastcol], q1[:, :, lastcol], op_, r=[t_tq], w=[t_xl])
                            tt("pool", xx[:, ri_o, 4 * hf:4 * hf + 4, :], q0, q1, op_, r=[t_tq], w=[t_xx[ri_o]])
                    ta_ = sm[:, 5, :]
                    tb_ = sm[:, 6, :]
                    tt("dve", ta_, l1re, xl[:, :, 0], ALU.mult, r=[t_xl, t_blk], w=[t_blk])
                    tt("dve", tb_, l1im, xl[:, :, 1], ALU.mult, r=[t_xl, t_blk], w=[t_blk])
                    tt("dve", cc[:, :, 0], ta_, tb_, ALU.subtract, r=[t_blk], w=[t_cc])
                    tt("dve", ta_, l1re, xl[:, :, 1], ALU.mult, r=[t_xl, t_blk], w=[t_blk])
                    tt("dve", tb_, l1im, xl[:, :, 0], ALU.mult, r=[t_xl, t_blk], w=[t_blk])
                    tt("dve", cc[:, :, 1], ta_, tb_, ALU.add, r=[t_blk], w=[t_cc])
                    cp("dve", ccx[:, 32:48], ccx[:, 0:16], r=[t_cc], w=[t_cc])
                    if last and i < 2:
                        continue
                    for j in range(2):
                        n_mm = 0
                        for st in range(4 * j, 4 * j + 4):
                            for ri in range(2):
                                mm(ps[j][:, 0:128], Cc[:, ri, st, :], xx[:, ri, st, :], start=(n_mm == 0), stop=(n_mm == 7),
                                   r=[t_tab, t_xx[ri]], w=[tps[j]])
                                n_mm += 1
                        if d_ == 0:
                            cp("act", yf[:, j, t0:t0 + 128], ps[j][:, 0:128], r=[tps[j]], w=[t_yf[i]])
                        else:
                            y = ysc[:, 0, :]
                            tt("dve", y, ps[j][:, 0:128], yf[:, j, t0:t0 + 128], ALU.add, r=[tps[j], t_yf[i]], w=[t_ysc])
                            stt(y, uT[:, j, t0:t0 + 128], dvec[:, j:j + 1], y, ALU.mult, ALU.add, r=[t_u[g], t_c, t_ysc], w=[t_ysc])
                            tt("dve", ysc[:, 1, :], y, y, ALU.mult, r=[t_ysc], w=[t_ysc])
                            ts("dve", ysc[:, 1, :], ysc[:, 1, :], 0.044715, 1.0, ALU.mult, ALU.add, r=[t_ysc], w=[t_ysc])
                            tt("dve", ysc[:, 1, :], ysc[:, 1, :], y, ALU.mult, r=[t_ysc], w=[t_ysc])
                            act(ysc[:, 2, :], ysc[:, 1, :], AF.Sigmoid, r=[t_ysc], w=[t_ysc], scale=1.5957691216057308)
                            tt("dve", yf[:, j, t0:t0 + 128], y, ysc[:, 2, :], ALU.mult, r=[t_ysc], w=[t_yf[i]])
            tap("s5g", yf, t_yf)
            P.barrier()
            glw = A.alloc([2, 512], BF16, at=woff)
            t_glw = T()
            load_cast(glw[:, 0], t_glw, dr["s5_w_glu"][l][0:128, :])
            load_cast(glw[:, 1], t_glw, dr["s5_w_glu"][l][128:256, :])
            sgt = A.alloc([512], F32, at=woff + 2048)
            t_sgt = T()
            bi = 0
            for g in range(1 if last else 0, 5):
                t0, n = GRP(g)
                tiles = list(range(t0 // 128, (t0 + n) // 128))
                for j in range(2):
                    pv = bi % 2
                    pg = 2 + bi % 2
                    bi += 1
                    for kt in range(2):
                        mm(ps[pv][:, 0:n], glw[:, kt, j * 128:(j + 1) * 128], yf[:, kt, t0:t0 + n], start=(kt == 0), stop=(kt == 1),
                           r=[t_glw] + [t_yf[i] for i in tiles], w=[tps[pv]])
                    for kt in range(2):
                        mm(ps[pg][:, 0:n], glw[:, kt, 256 + j * 128:256 + (j + 1) * 128], yf[:, kt, t0:t0 + n],
                           start=(kt == 0), stop=(kt == 1), r=[t_glw] + [t_yf[i] for i in tiles], w=[tps[pg]])
                    act(sgt[:, 0:n], ps[pg][:, 0:n], AF.Sigmoid, r=[tps[pg]], w=[t_sgt])
                    tt("dve", oT[slot][:, j, t0:t0 + n], ps[pv][:, 0:n], sgt[:, 0:n], ALU.mult, r=[tps[pv], t_sgt],
                       w=[t_oT[slot][g]])
            tap("ocT", oT[slot], t_oT[slot])

        SLOT_OF = {0: 3, 1: 0, 2: 1, 3: 2}

        def merge_phase(l, last):
            P.barrier()
            al = mk_alloc(OT0)
            mT = al([8, NT], BF16)
            sig = al([512], F32)
            acc = al([512], F32)
            t_m = [T() for _ in range(5)]
            t_sig = T()
            t_acc = T()
            wsrc = dr["w_in"][l].rearrange("(k p) c -> p k c", p=128)
            wbsrc = dr["w_branch"][l].rearrange("n (j p) d -> p n j d", p=128)
            groups = [g for g in range(5) if not (last and g == 0)]
            bi = 0
            for d in range(8):
                gw = []
                for n in range(4):
                    c0 = 1952 + n * 1024 + d * 128
                    gw.append(load_ring(wsrc[:, :, c0:c0 + 128], "p (k c) -> p k c", c=128))
                wb, t_wb = load_ring(wbsrc[:, :, :, d * 128:(d + 1) * 128], "p (n j c) -> p n j c", j=2, c=128)
                for g in groups:
                    t0, n_ = GRP(g)
                    for n in range(4):
                        sl = SLOT_OF[n]
                        pa = bi % 2
                        pb = 2 + bi % 2
                        bi += 1
                        wv, t_wv = gw[n]
                        for k in range(8):
                            mm(ps[pa][:, 0:n_], wv[:, k, :], hT[:, k, t0:t0 + n_], start=(k == 0), stop=(k == 7),
                               r=[t_wv, t_h[g]], w=[tps[pa]])
                        for j in range(2):
                            mm(ps[pb][:, 0:n_], wb[:, n, j, :], oT[sl][:, j, t0:t0 + n_], start=(j == 0), stop=(j == 1),
                               r=[t_wb, t_oT[sl][g]], w=[tps[pb]])
                        act(sig[:, 0:n_], ps[pa][:, 0:n_], AF.Sigmoid, r=[tps[pa]], w=[t_sig])
                        if n == 0:
                            tt("dve", acc[:, 0:n_], ps[pb][:, 0:n_], sig[:, 0:n_], ALU.mult, r=[tps[pb], t_sig], w=[t_acc])
                        else:
                            tt("dve", ps[pb][:, 0:n_], ps[pb][:, 0:n_], sig[:, 0:n_], ALU.mult, r=[tps[pb], t_sig], w=[tps[pb]])
                            if n < 3:
                                tt("dve", acc[:, 0:n_], acc[:, 0:n_], ps[pb][:, 0:n_], ALU.add, r=[tps[pb], t_acc], w=[t_acc])
                            else:
                                tt("dve", mT[:, d, t0:t0 + n_], acc[:, 0:n_], ps[pb][:, 0:n_], ALU.add,
                                   r=[tps[pb], t_acc], w=[t_m[g]])
            tap("mT", mT, t_m)
            wosrc = dr["w_out"][l].rearrange("(k p) c -> p k c", p=128)
            for d in range(8):
                wv, t_wv = load_ring(wosrc[:, :, d * 128:(d + 1) * 128], "p (k c) -> p k c", c=128)
                for g in groups:
                    t0, n_ = GRP(g)
                    s_ = 1 if g == 0 else 0
                    pb = 4 + bi % 4
                    bi += 1
                    for k in range(8):
                        mm(ps[pb][:, 0:n_], wv[:, k, :], mT[:, k, t0:t0 + n_], start=(k == 0), stop=(k == 7),
                           r=[t_wv, t_m[g]], w=[tps[pb]])
                    stt(xT[:, d, t0:t0 + n_], ps[pb][:, 0:n_], mod[:, l, 16 + d, s_:s_ + 1], xT[:, d, t0:t0 + n_],
                        ALU.mult, ALU.add, r=[tps[pb], t_mod, t_x[d][g]], w=[t_x[d][g]])

        def ffn_phase(l, last):
            groups = [g for g in range(5) if not (last and g == 0)]
            norm_phase(l, 1, groups)
            P.barrier()
            al = mk_alloc(A.nbytes)
            aT = al([8, NT], BF16)
            rl = al([2, 512], F32)
            t_a = [[T() for _ in range(5)] for _ in range(8)]
            t_rl = [T(), T()]
            w1src = dr["ffn_w1"][l].rearrange("(k p) c -> p k c", p=128)
            w2src = dr["ffn_w2"][l].rearrange("(f p) c -> p f c", p=128)
            bi = 0
            for fb in range(4):
                for f in range(8):
                    F_ = fb * 8 + f
                    wv, t_wv = load_ring(w1src[:, :, F_ * 128:(F_ + 1) * 128], "p (k c) -> p k c", c=128)
                    for g in groups:
                        t0, n_ = GRP(g)
                        pb = bi % 4
                        rb = bi % 2
                        bi += 1
                        for k in range(8):
                            mm(ps[pb][:, 0:n_], wv[:, k, :], hT[:, k, t0:t0 + n_], start=(k == 0), stop=(k == 7),
                               r=[t_wv, t_h[g]], w=[tps[pb]])
                        act(rl[:, rb, 0:n_], ps[pb][:, 0:n_], AF.Relu, r=[tps[pb]], w=[t_rl[rb]])
                        tt("pool", aT[:, f, t0:t0 + n_], rl[:, rb, 0:n_], rl[:, rb, 0:n_], ALU.mult, r=[t_rl[rb]], w=[t_a[f][g]])
                for d in range(8):
                    wv, t_wv = load_ring(w2src[:, fb * 8:(fb + 1) * 8, d * 128:(d + 1) * 128], "p (f c) -> p f c", c=128)
                    for g in groups:
                        t0, n_ = GRP(g)
                        s_ = 1 if g == 0 else 0
                        pb = 4 + bi % 4
                        bi += 1
                        for f in range(8):
                            mm(ps[pb][:, 0:n_], wv[:, f, :], aT[:, f, t0:t0 + n_], start=(f == 0), stop=(f == 7),
                               r=[t_wv, t_a[f][g]], w=[tps[pb]])
                        stt(xT[:, d, t0:t0 + n_], ps[pb][:, 0:n_], mod[:, l, 40 + d, s_:s_ + 1], xT[:, d, t0:t0 + n_],
                            ALU.mult, ALU.add, r=[tps[pb], t_mod, t_x[d][g]], w=[t_x[d][g]])

        for l in range(DEPTH):
            last = (l == DEPTH - 1)
            norm_phase(l, 0, list(range(5)))
            if l == 0:
                tap("hT", hT, t_h)
            if stage <= 0.1:
                break
            if "mla" in mixers:
                mla_phase(l, last, 3)
            if "ret" in mixers:
                ret_phase(l, last, 2)
            if "s5" in mixers:
                s5_phase(l, last, 1)
            if "fnet" in mixers:
                fnet_phase(l, last, 0)
            if stage <= 1:
                break
            merge_phase(l, last)
            ffn_phase(l, last)
            if l == 0:
                tap("x1", xT, [t for k in range(8) for t in t_x[k]])
            if stage <= 2:
                break

        osrc = outT.rearrange("(k p) t -> p k t", p=128)
        for k in range(8):
            out_handles.append(dma(osrc[:, k, :], xT[:, k, LAT0:NT], r=t_x[k]))
        P.wait_all("sp", out_handles)
        P.emit()
    return nc


_CACHE = {}


def _specs_of(d):
    sp = {}
    for k, v in d.items():
        sp[k] = (v.shape, BF16 if v.dtype == NPBF else F32)
    return sp


def run(inputs, stage=99, taps=(), ncores=8, mixers=("mla", "s5", "ret", "fnet")):
    com = prep_common(inputs)
    cores = [prep_core(inputs, b) for b in range(ncores)]
    in_maps = [dict(com, **c) for c in cores]
    nc = build(_specs_of(in_maps[0]), stage=stage, taps=taps, mixers=mixers)
    res = run_bass_kernel_spmd(nc, in_maps, core_ids=list(range(ncores)))
    return res.results


def kernel(**inputs):
    inputs = {k: np.asarray(v) for k, v in inputs.items()}
    res = run(inputs)
    out = np.stack([r["outT"].T for r in res], axis=0)
    return np.ascontiguousarray(out.astype(np.float32))
```

```python
import contextlib
import math
import os
import numpy as np
import ml_dtypes
import concourse.bass as bass
import concourse.mybir as mybir
from concourse.bass_utils import run_bass_kernel_spmd

F32 = mybir.dt.float32
BF16 = mybir.dt.bfloat16
ALU = mybir.AluOpType
AF = mybir.ActivationFunctionType
AX = mybir.AxisListType
NPBF = ml_dtypes.bfloat16

ENGS = ("pe", "act", "dve", "pool", "sp")
NOSELF = tuple(os.environ.get("NOSELF", "pe").split(","))
RELAX = os.environ.get("RELAX", "1") == "1"
NDSLOT = 8

D = 1024
NT = 2304
NTT = 18
LAT0 = 256
EPS = 1e-6
DEPTH = 2


class T:
    __slots__ = ("name", "w", "rs", "excl", "tw")

    def __init__(self, name="", excl=False):
        self.name = name
        self.w = None
        self.tw = None
        self.rs = []
        self.excl = excl


class Prog:
    def __init__(self, nc, stack, same_sync=True):
        self.nc = nc
        self.same_sync = same_sync
        self.q = {e: [] for e in ENGS}
        self.cnt = {e: 0 for e in ENGS}
        self.sems = {}
        for e in ENGS:
            self.sems[("c", e)] = stack.enter_context(nc.semaphore("c_" + e))
        self.dq = ("sp", "pool", "act")
        self.dcnt = {}
        self.dn = {e: 0 for e in self.dq}
        for e in self.dq:
            for s in range(NDSLOT):
                self.sems[("d", e, s)] = stack.enter_context(nc.semaphore("d_%s%d" % (e, s)))
                self.dcnt[(e, s)] = 0
        self.known = {e: {} for e in ENGS}
        self.kstop = None
        self.kcount = 0
        import threading
        self._tls = threading.local()

    def _deps(self, eng, r, w):
        deps = {}

        def add(h):
            if h is None:
                return
            k, v = h
            if k == ("c", eng) and (eng in NOSELF or not self.same_sync):
                return
            if deps.get(k, 0) < v:
                deps[k] = v
        for t in r:
            add(t.w)
        for t in w:
            add(t.w)
            for h in t.rs:
                add(h)
        out = []
        kn = self.known[eng]
        for k, v in deps.items():
            if kn.get(k, 0) >= v:
                continue
            kn[k] = v
            out.append((k, v))
        return out

    def _mark(self, h, r, w):
        for t in w:
            t.tw = h
        for t in r:
            t.rs.append(h)
            if len(t.rs) > 64:
                best = {}
                for k, v in t.rs:
                    if best.get(k, 0) < v:
                        best[k] = v
                t.rs = list(best.items())
        for t in w:
            t.w = h
            t.rs = []

    def interleave(self, thunks):
        import threading
        n = len(thunks)
        if n == 1 or os.environ.get("NOIL") == "1":
            for t in thunks:
                t()
            return
        cv = threading.Condition()
        st = {"turn": 0, "alive": [True] * n, "err": None}

        def nxt(i):
            for d in range(1, n + 1):
                j = (i + d) % n
                if st["alive"][j]:
                    return j
            return None

        def yp(i):
            with cv:
                j = nxt(i)
                if j is None or j == i:
                    return
                st["turn"] = j
                cv.notify_all()
                cv.wait_for(lambda: st["turn"] == i)

        def runner(i):
            try:
                with cv:
                    cv.wait_for(lambda: st["turn"] == i)
                self._tls.yp = (lambda: yp(i))
                thunks[i]()
            except BaseException as e:
                st["err"] = e
            finally:
                self._tls.yp = None
                with cv:
                    st["alive"][i] = False
                    j = nxt(i)
                    st["turn"] = j if j is not None else -1
                    cv.notify_all()

        ths = [threading.Thread(target=runner, args=(i,)) for i in range(n)]
        for t in ths:
            t.start()
        for t in ths:
            t.join()
        if st["err"] is not None:
            raise st["err"]

    def _yield(self):
        yp = getattr(self._tls, "yp", None)
        if yp is not None:
            yp()

    def op(self, eng, fn, r=(), w=()):
        self._yield()
        if self.kstop is not None:
            self.kcount += 1
            if self.kcount > self.kstop:
                return None
        if RELAX:
            return self._op_relaxed(eng, fn, r, w)
        if eng != "pe":
            ex = [t for t in r if t.excl]
            if ex:
                r = [t for t in r if not t.excl]
                w = list(w) + ex
        waits = self._deps(eng, r, w)
        self.cnt[eng] += 1
        h = (("c", eng), self.cnt[eng])
        self.q[eng].append((fn, waits, (h[0], 1)))
        self._mark(h, r, w)
        return h

    def _op_relaxed(self, eng, fn, r, w):
        deps = {}
        me = ("c", eng)

        def add(h, same_ok):
            if h is None:
                return
            k, v = h
            if k == me and (eng in NOSELF or not same_ok):
                return
            if deps.get(k, 0) < v:
                deps[k] = v
        wset = set(id(t) for t in w)
        for t in r:
            if id(t) in wset:
                continue
            add(t.tw, True)
            if t.excl and eng != "pe":
                add(t.w, False)
        for t in w:
            add(t.tw, False)
            add(t.w, False)
            for h in t.rs:
                add(h, False)
        for t in r:
            if id(t) in wset:
                add(t.tw, True)
        waits = []
        kn = self.known[eng]
        for k, v in deps.items():
            if kn.get(k, 0) >= v:
                continue
            kn[k] = v
            waits.append((k, v))
        self.cnt[eng] += 1
        h = (me, self.cnt[eng])
        self.q[eng].append((fn, waits, (me, 1)))
        for t in r:
            if id(t) in wset:
                continue
            if t.excl and eng != "pe":
                t.w = h
                t.rs = []
            else:
                t.rs.append(h)
                if len(t.rs) > 64:
                    best = {}
                    for k, v in t.rs:
                        if best.get(k, 0) < v:
                            best[k] = v
                    t.rs = list(best.items())
        for t in w:
            t.w = h
            t.tw = h
            t.rs = []
        return h

    def dma(self, eng, fn, r=(), w=()):
        self._yield()
        s = self.dn[eng] % NDSLOT
        self.dn[eng] += 1
        waits = self._deps(eng, r, w)
        k = ("d", eng, s)
        prev = self.dcnt[(eng, s)]
        if prev > 0 and self.known[eng].get(k, 0) < prev:
            self.known[eng][k] = prev
            waits.append((k, prev))
        self.dcnt[(eng, s)] = prev + 16
        h = (k, prev + 16)
        self.q[eng].append((fn, waits, (k, 16)))
        self._mark(h, r, w)
        return h

    def barrier(self):
        hs = [(("c", e), self.cnt[e]) for e in ENGS if self.cnt[e] > 0]
        hs += [(("d", e, s), v) for (e, s), v in self.dcnt.items() if v > 0]
        for e in ENGS:
            self.wait_all(e, [h for h in hs if h[0] != ("c", e)])

    def wait_all(self, eng, hs):
        waits = []
        for k, v in hs:
            if self.known[eng].get(k, 0) < v:
                self.known[eng][k] = v
                waits.append((k, v))
        self.q[eng].append((None, waits, None))

    def emit(self):
        nc = self.nc
        sems = self.sems
        q = self.q

        def run(e, engobj):
            for fn, waits, inc in q[e]:
                for k, v in waits:
                    engobj.wait_ge(sems[k], v)
                if fn is None:
                    continue
                ins = fn(engobj)
                ins.then_inc(sems[inc[0]], inc[1])

        with nc.Block() as block:
            @block.tensor
            def _(eng):
                run("pe", eng)

            @block.scalar
            def _(eng):
                run("act", eng)

            @block.vector
            def _(eng):
                run("dve", eng)

            @block.gpsimd
            def _(eng):
                run("pool", eng)

            @block.sync
            def _(eng):
                run("sp", eng)


class Arena:
    def __init__(self, nc, stack, nbytes):
        self.t = stack.enter_context(nc.sbuf_tensor("arena", [128, nbytes // 4], F32))
        self.nbytes = nbytes
        self.off = 0

    def alloc(self, shape, dtype, at=None):
        esz = 2 if dtype == BF16 else 4
        n = int(np.prod(shape)) * esz
        n4 = (n + 3) // 4
        if at is None:
            at = self.off
            self.off += n4 * 4
        assert at % 4 == 0 and at + n4 * 4 <= self.nbytes, (at, n, self.nbytes)
        ap = self.t[:, at // 4: at // 4 + n4]
        if dtype != F32:
            ap = ap.bitcast(dtype)
        if len(shape) == 2:
            ap = ap.rearrange("p (a b) -> p a b", b=shape[1])
        elif len(shape) == 3:
            ap = ap.rearrange("p (a b c) -> p a b c", b=shape[1], c=shape[2])
        elif len(shape) == 4:
            ap = ap.rearrange("p (a b c d) -> p a b c d", b=shape[1], c=shape[2], d=shape[3])
        return ap


def _rope_tables():
    half = 8
    freqs = (10000.0 ** (-np.arange(half, dtype=np.float32) / half)).astype(np.float32)
    t = np.arange(2048)
    rows = (t // 64).astype(np.float32)
    cols = (t % 64).astype(np.float32)
    ang = np.concatenate([rows[:, None] * freqs[None], cols[:, None] * freqs[None]], axis=1)
    cos = np.ones((NT, 16), np.float32)
    sin = np.zeros((NT, 16), np.float32)
    cos[LAT0:] = np.cos(ang)
    sin[LAT0:] = np.sin(ang)
    cos = cos.reshape(NTT, 128, 16).transpose(1, 0, 2)
    sin = sin.reshape(NTT, 128, 16).transpose(1, 0, 2)
    return np.ascontiguousarray(cos), np.ascontiguousarray(sin)


def _fnet_consts():
    ci = np.arange(64)
    c64 = np.cos(2 * np.pi * np.outer(ci, ci) / 64.0)
    s64 = np.sin(2 * np.pi * np.outer(ci, ci) / 64.0)
    cs = np.zeros((2, 128, 512), np.float64)
    for j in range(2):
        for gl in range(2):
            g = 2 * j + gl
            cs[j, gl * 64:(gl + 1) * 64, g * 64:(g + 1) * 64] = c64
            cs[j, gl * 64:(gl + 1) * 64, 256 + g * 64:256 + (g + 1) * 64] = s64
    out = {"cs64": np.ascontiguousarray(cs.transpose(1, 0, 2)).astype(np.float32).astype(NPBF)}
    for L in (2048, 256):
        li = np.arange(L)
        m = np.outer(li, li) % L
        ang = 2 * np.pi * m / L
        sc = 1.0 / math.sqrt(L * 64.0)
        tab = np.stack([np.cos(ang) * sc, -np.sin(ang) * sc], axis=0)
        out["dft%d" % L] = tab.astype(np.float32).astype(NPBF)
    return out


def _s5_layouts(inp):
    f = np.float32
    out = {}
    m = np.arange(128)[:, None]
    t = np.arange(128)[None, :]
    tri = np.stack([(m <= t), (m >= t)], axis=1).astype(f)
    out["s5tri"] = tri.astype(NPBF)
    erow = np.stack([np.broadcast_to(t.astype(f), (128, 128)), np.broadcast_to(127.0 - t.astype(f), (128, 128))], axis=1)
    out["s5erow"] = np.ascontiguousarray(erow, f)
    p = np.arange(128, dtype=f)[:, None]
    out["s5ecol"] = np.ascontiguousarray(np.concatenate([p, 127.0 - p, -p, -(127.0 - p)], axis=1), f)
    out["s5d"] = np.ascontiguousarray(np.asarray(inp["s5_d"], f).reshape(DEPTH, 2, 128).transpose(0, 2, 1))
    re = np.asarray(inp["s5_lam_re"], f).reshape(DEPTH, 2, 1024)
    im = np.asarray(inp["s5_lam_im"], f).reshape(DEPTH, 2, 1024)
    ls = np.repeat(np.asarray(inp["s5_log_step"], f), 64, axis=-1)
    trip = np.stack([re, im, ls], axis=2)
    out["s5pp"] = np.ascontiguousarray(trip.reshape(DEPTH, 2, 3, 8, 128).transpose(0, 1, 4, 2, 3))
    out["s5row"] = np.ascontiguousarray(np.broadcast_to(trip[:, :, None], (DEPTH, 2, 128, 3, 1024)), f)
    bre = np.asarray(inp["s5_b_re"], f)
    bim = np.asarray(inp["s5_b_im"], f)
    sb = np.zeros((DEPTH, 2, 128, 2, 2, 512), f)
    for ri, bb in enumerate((bre, bim)):
        for kt in range(2):
            for gl in range(8):
                g = 8 * kt + gl
                sb[:, :, gl * 16:(gl + 1) * 16, kt, ri, gl * 64:(gl + 1) * 64] = bb[:, :, g].transpose(0, 1, 3, 2)
    out["s5b"] = sb
    cre = np.asarray(inp["s5_c_re"], f)
    cim = np.asarray(inp["s5_c_im"], f)
    sc = np.zeros((DEPTH, 2, 2, 8, 128, 128), f)
    for ri, cm in enumerate((cre, cim)):
        for st in range(8):
            for gl in range(2):
                g = 2 * st + gl
                col = (g % 8) * 16
                sc[:, :, ri, st, gl * 64:(gl + 1) * 64, col:col + 16] = cm[:, :, g].transpose(0, 1, 3, 2)
    out["s5c"] = sc
    out["s5_w_glu"] = np.ascontiguousarray(inp["s5_w_glu"], f)
    id2 = np.zeros((128, 16), f)
    for j_ in range(16):
        id2[j_, j_] = 1.0
        id2[32 + j_, j_] = 1.0
    out["s5id2"] = id2.astype(NPBF)
    return out


def _ret_consts():
    half = 32
    freqs = (10000.0 ** (-np.arange(half, dtype=np.float32) / half)).astype(np.float32)
    pos = np.arange(2048, dtype=np.float32)
    ang = pos[:, None] * freqs[None]
    cos = np.ones((NT, 32), np.float32)
    sin = np.zeros((NT, 32), np.float32)
    cos[LAT0:] = np.cos(ang)
    sin[LAT0:] = np.sin(ang)
    tm = lambda a: np.ascontiguousarray(a.reshape(NTT, 128, 32).transpose(1, 0, 2))
    out = {"rrc": tm(cos), "rrs": tm(sin)}
    k = np.arange(128, dtype=np.float32)[:, None]
    q = np.arange(128, dtype=np.float32)[None, :]
    retc = np.stack([np.broadcast_to(q + 1.0, (128, 128)), np.broadcast_to(128.0 - q, (128, 128)),
                     np.maximum(q - k, 0.0), np.maximum(k - q, 0.0),
                     (q >= k).astype(np.float32), (k >= q).astype(np.float32)], axis=1)
    out["retc"] = np.ascontiguousarray(retc, np.float32)
    out["retp"] = np.ascontiguousarray(np.concatenate([k, 127.0 - k], axis=1), np.float32)
    return out


def prep_common(inp):
    f = np.float32
    c = {}
    c["ada_w"] = np.ascontiguousarray(inp["ada_w"], f)
    c["ada_bT"] = np.ascontiguousarray(inp["ada_b"].reshape(DEPTH, 48, 128).transpose(0, 2, 1), f)
    nw = np.concatenate([inp["norm_mix_w"].reshape(DEPTH, 8, 128), inp["norm_ffn_w"].reshape(DEPTH, 8, 128)], axis=1)
    c["nw"] = np.ascontiguousarray(nw.transpose(0, 2, 1), f)
    c["w_in"] = np.ascontiguousarray(inp["w_in"], f)
    bc = lambda v: np.ascontiguousarray(np.broadcast_to(v[:, None, :], (DEPTH, 128, v.shape[-1])), f)
    c["kvw"] = bc(inp["mla_kv_norm"])
    c["qnw"] = bc(inp["mla_q_norm"])
    c["qkq"] = bc(np.tile(inp["mla_qk_norm_q"], (1, 4)))
    c["qkk"] = bc(np.tile(inp["mla_qk_norm_k"], (1, 4)))
    c["w_ukv"] = np.ascontiguousarray(inp["mla_w_ukv"], f)
    c["w_uq"] = np.ascontiguousarray(inp["mla_w_uq"], f)
    cos, sin = _rope_tables()
    c["ropec"] = cos
    c["ropes"] = sin
    c["w_branch"] = np.ascontiguousarray(inp["w_branch"], f)
    c["w_out"] = np.ascontiguousarray(inp["w_out"], f)
    c["ffn_w1"] = np.ascontiguousarray(inp["ffn_w1"], f)
    c["ffn_w2"] = np.ascontiguousarray(inp["ffn_w2"], f)
    c.update(_fnet_consts())
    c.update(_ret_consts())
    c.update(_s5_layouts(inp))
    lg = np.asarray(inp["ret_decay_logit"], f)
    c["rlog_row"] = np.ascontiguousarray(np.broadcast_to(lg.reshape(DEPTH, 1, 8), (DEPTH, 128, 8)), f)
    pp = np.zeros((DEPTH, 128, 2, 2), f)
    for pair in range(2):
        for d_ in range(2):
            pp[:, 0:64, pair, d_] = lg[:, d_, 2 * pair][:, None]
            pp[:, 64:128, pair, d_] = lg[:, d_, 2 * pair + 1][:, None]
    c["rlog_pp"] = pp
    c["gnw"] = bc(inp["ret_gn_w"])
    c["identb"] = np.eye(128, dtype=f).astype(NPBF)
    c["identf"] = np.eye(128, dtype=f)
    return c


def prep_core(inp, b):
    f = np.float32
    d = {}
    xt = np.concatenate([inp["ctx"][b], inp["x"][b]], axis=0).T
    d["xT"] = np.ascontiguousarray(xt, f)
    d["cT"] = np.ascontiguousarray(np.stack([inp["c"][b], inp["c_ctx"]], axis=1), f)
    return d


def build(specs, stage=99, taps=(), mixers=("mla", "s5", "ret", "fnet")):
    nc = bass.Bass("TRN2", target_bir_lowering=False)
    dr = {}
    for name, (shape, dt) in specs.items():
        dr[name] = nc.dram_tensor(name, list(shape), dt, kind="ExternalInput").ap()
    outT = nc.dram_tensor("outT", [D, 2048], F32, kind="ExternalOutput").ap()
    tapd = {}
    for name, shape, dt in taps:
        tapd[name] = nc.dram_tensor("tap_" + name, list(shape), dt, kind="ExternalOutput").ap()

    with contextlib.ExitStack() as st:
        P = Prog(nc, st)
        A = Arena(nc, st, 206 * 1024)
        ps = [st.enter_context(nc.psum_tensor("ps%d" % i, [128, 512], F32)) for i in range(8)]
        tps = [T("ps%d" % i, excl=True) for i in range(8)]
        out_handles = []

        def mm(out, lhsT, rhs, start=True, stop=True, r=(), w=()):
            return P.op("pe", lambda e: e.matmul(out, lhsT=lhsT, rhs=rhs, start=start, stop=stop,
                                                 skip_group_check=True), r, w)

        def tr(out, in_, ident, r=(), w=()):
            return P.op("pe", lambda e: e.transpose(out=out, in_=in_, identity=ident), r, w)

        def act(out, in_, func, r=(), w=(), scale=1.0, bias=0.0, accum=None):
            if accum is None:
                return P.op("act", lambda e: e.activation(out=out, in_=in_, func=func, scale=scale, bias=bias), r, w)
            return P.op("act", lambda e: e.activation(out=out, in_=in_, func=func, scale=scale, bias=bias,
                                                      accum_out=accum), r, w)

        def tt(eng, out, a, b, op, r=(), w=()):
            return P.op(eng, lambda e: e.tensor_tensor(out=out, in0=a, in1=b, op=op), r, w)

        def ts(eng, out, a, s1, s2, op0, op1=None, r=(), w=()):
            if op1 is None:
                return P.op(eng, lambda e: e.tensor_scalar(out=out, in0=a, scalar1=s1, scalar2=None, op0=op0), r, w)
            return P.op(eng, lambda e: e.tensor_scalar(out=out, in0=a, scalar1=s1, scalar2=s2, op0=op0, op1=op1), r, w)

        def stt(out, a, s, b, op0, op1, r=(), w=()):
            return P.op("dve", lambda e: e.scalar_tensor_tensor(out=out, in0=a, scalar=s, in1=b, op0=op0, op1=op1), r, w)

        def cp(eng, out, in_, r=(), w=()):
            if eng == "act":
                return P.op(eng, lambda e: e.activation(out=out, in_=in_, func=AF.Copy), r, w)
            return P.op(eng, lambda e: e.tensor_copy(out=out, in_=in_), r, w)

        def red(out, in_, r=(), w=()):
            return P.op("dve", lambda e: e.tensor_reduce(out=out, in_=in_, axis=AX.X, op=ALU.add), r, w)

        def rcp(out, in_, r=(), w=()):
            return P.op("dve", lambda e: e.reciprocal(out=out, in_=in_), r, w)

        def mset(eng, out, val, w=()):
            return P.op(eng, lambda e: e.memset(out, val), (), w)

        def dma(out, in_, r=(), w=(), q="sp"):
            return P.dma(q, lambda e: e.dma_start(out=out, in_=in_), r, w)

        def tap(name, src, r):
            if name in tapd:
                out_handles.append(dma(tapd[name], src, r=r))

        xT = A.alloc([8, NT], F32)
        hT = A.alloc([8, NT], BF16)
        t_x = [[T("x%d_%d" % (k, g)) for g in range(5)] for k in range(8)]
        t_h = [T("h%d" % g) for g in range(5)]
        identb = A.alloc([128], BF16)
        identf = A.alloc([128], F32)
        onesb = A.alloc([128], BF16)
        mod = A.alloc([DEPTH, 48, 2], F32)
        a1 = A.alloc([DEPTH, 16, 2], F32)
        nwt = A.alloc([DEPTH, 16], F32)
        scT = A.alloc([8, 2], F32)
        eps_t = A.alloc([1], F32)
        t_const = T("const")
        t_mod = T("mod")
        NSTG = 2
        NRING = 5
        stg = [A.alloc([1024], F32) for _ in range(NSTG)]
        t_stg = [T("stg%d" % i) for i in range(NSTG)]
        ring = [A.alloc([1024], BF16) for _ in range(NRING)]
        t_ring = [T("ring%d" % i) for i in range(NRING)]
        sidx = [0]
        ridx = [0]
        DYN0 = A.off
        OT0 = A.nbytes - 4 * 2 * NT * 2
        oT = [A.alloc([2, NT], BF16, at=OT0 + i * 2 * NT * 2) for i in range(4)]
        t_oT = [[T("o%d_%d" % (i, g)) for g in range(5)] for i in range(4)]

        def GRP(g):
            return (0, 256) if g == 0 else (LAT0 + 512 * (g - 1), 512)

        def grp_of_tile(tt_):
            return 0 if tt_ < 2 else 1 + (tt_ - 2) // 4

        def next_stg():
            s = sidx[0] % NSTG
            sidx[0] += 1
            return s

        def load_cast(dst, t_dst, src, shape_str=None, **kw):
            s = next_stg()
            n = int(np.prod(src.shape[1:]))
            assert n <= 1024, n
            sv = stg[s][:, 0:n]
            if shape_str is not None:
                sv = sv.rearrange(shape_str, **kw)
            dma(sv, src, w=[t_stg[s]])
            cp("pool", dst, sv, r=[t_stg[s]], w=[t_dst])

        def load_ring(src, shape_str=None, **kw):
            i = ridx[0] % NRING
            ridx[0] += 1
            n = int(np.prod(src.shape[1:]))
            dv = ring[i][:, 0:n]
            if shape_str is not None:
                dv = dv.rearrange(shape_str, **kw)
            load_cast(dv, t_ring[i], src, shape_str, **kw)
            return dv, t_ring[i]

        xsrc = dr["xT"].rearrange("(k p) t -> p k t", p=128)
        for k in range(8):
            dma(xT[:, k, :], xsrc[:, k, :], w=t_x[k])
        dma(identb, dr["identb"], w=[t_const])
        dma(identf, dr["identf"], w=[t_const])
        dma(scT, dr["cT"].rearrange("(k p) j -> p k j", p=128), w=[t_const])
        dma(nwt, dr["nw"].rearrange("l p k -> p l k"), w=[t_const])
        mset("dve", onesb, 1.0, w=[t_const])
        mset("dve", eps_t, EPS, w=[t_const])
        act(scT, scT, AF.Silu, r=[t_const], w=[t_const])

        for l in range(DEPTH):
            P.barrier()
            bT = A.alloc([48], F32, at=DYN0)
            modrow = A.alloc([6144], F32, at=DYN0 + 256)
            t_bT = T()
            t_mr = T()
            dma(bT, dr["ada_bT"][l], w=[t_bT])
            wsrc = dr["ada_w"][l].rearrange("(k p) c -> p k c", p=128)
            for nchunk in range(12):
                pb = nchunk % 2
                for j in range(4):
                    s = next_stg()
                    sv = stg[s][:, 0:1024].rearrange("p (k c) -> p k c", c=512)
                    dma(sv, wsrc[:, 2 * j:2 * j + 2, nchunk * 512:(nchunk + 1) * 512], w=[t_stg[s]])
                    for kk in range(2):
                        k = 2 * j + kk
                        mm(ps[pb][0:2, :], scT[:, k, :], sv[:, kk, :], start=(k == 0), stop=(k == 7),
                           r=[t_stg[s], t_const], w=[tps[pb]])
                cp("act", modrow[0:2, nchunk * 512:(nchunk + 1) * 512], ps[pb][0:2, :], r=[tps[pb]], w=[t_mr])
            psm = ps[2 + l][:, 0:96].rearrange("p (c s) -> p c s", s=2)
            for ct in range(48):
                tr(psm[:, ct, :], modrow[0:2, ct * 128:(ct + 1) * 128], identf[0:2, 0:2], r=[t_mr, t_const], w=[tps[2 + l]])
            tt("dve", mod[:, l, :, :], psm, bT[:, :].unsqueeze(2).to_broadcast([128, 48, 2]), ALU.add,
               r=[tps[2 + l], t_bT], w=[t_mod])
            for j, c0 in ((0, 8), (1, 32)):
                stt(a1[:, l, j * 8:(j + 1) * 8, :], mod[:, l, c0:c0 + 8, :], 1.0,
                    nwt[:, l, j * 8:(j + 1) * 8].unsqueeze(2).to_broadcast([128, 8, 2]),
                    ALU.add, ALU.mult, r=[t_mod, t_const], w=[t_mod])
        tap("mod", mod, [t_mod])

        def norm_phase(l, which, groups):
            sh0 = 0 if which == 0 else 24
            P.barrier()
            sq = A.alloc([2, 8, 512], BF16, at=DYN0)
            rstd = A.alloc([2, 512], F32, at=DYN0 + 2 * 8 * 512 * 2)
            tmp = A.alloc([2, 2, 512], F32, at=DYN0 + 2 * 8 * 512 * 2 + 2 * 512 * 4)
            t_sq = [T(), T()]
            t_rs = [T(), T()]
            t_tmp = [[T(), T()], [T(), T()]]

            def _nbody(gi, g):
                t0, n = GRP(g)
                s = 1 if g == 0 else 0
                b = gi % 2
                pb = 2 + b
                for k in range(8):
                    act(sq[:, b, k, 0:n], xT[:, k, t0:t0 + n], AF.Square, r=[t_x[k][g]], w=[t_sq[b]])
                for k in range(8):
                    mm(ps[pb][:, 0:n], onesb, sq[:, b, k, 0:n], start=(k == 0), stop=(k == 7),
                       r=[t_sq[b], t_const], w=[tps[pb]])
                act(rstd[:, b, 0:n], ps[pb][:, 0:n], AF.Sqrt, r=[tps[pb], t_const], w=[t_rs[b]],
                    scale=1.0 / D, bias=eps_t[:, 0:1])
                rcp(rstd[:, b, 0:n], rstd[:, b, 0:n], r=[t_rs[b]], w=[t_rs[b]])
                for k in range(8):
                    tb = k % 2
                    tt("dve", tmp[:, b, tb, 0:n], xT[:, k, t0:t0 + n], rstd[:, b, 0:n], ALU.mult,
                       r=[t_x[k][g], t_rs[b]], w=[t_tmp[b][tb]])
                    act(hT[:, k, t0:t0 + n], tmp[:, b, tb, 0:n], AF.Identity, r=[t_tmp[b][tb], t_mod], w=[t_h[g]],
                        scale=a1[:, l, which * 8 + k, s:s + 1], bias=mod[:, l, sh0 + k, s:s + 1])

            gl = list(enumerate(groups))
            for i0 in range(0, len(gl), 2):
                P.interleave([(lambda a=a_: _nbody(a[0], a[1])) for a_ in gl[i0:i0 + 2]])

        def mk_alloc(limit):
            o = [DYN0]

            def al(shape, dt):
                ap = A.alloc(shape, dt, at=o[0])
                n = int(np.prod(shape)) * (2 if dt == BF16 else 4)
                o[0] += (n + 3) // 4 * 4
                assert o[0] <= limit, (o[0], limit)
                return ap
            al.o = o
            return al

        def mla_phase(l, last, slot):
            P.barrier()
            al = mk_alloc(OT0 + slot * 2 * NT * 2)
            Wm = al([8, 416], BF16)
            Wukv = al([512], BF16)
            Wuq = al([2, 384], BF16)
            kvw = al([128], F32)
            qnw = al([256], F32)
            qkq = al([4, 96], F32)
            qkk = al([4, 96], F32)
            rc = al([NTT, 2, 8], F32)
            rs_ = al([NTT, 2, 8], F32)
            KT = al([4, NT], BF16)
            QT = al([4, 512], BF16)
            V1 = al([NTT, 4, 65], BF16)
            PT = al([2, 512], BF16)
            kvn = al([2, 128], BF16)
            kvnT = al([2, 128], BF16)
            qn = al([2, 256], BF16)
            qnT = al([2, 2, 128], BF16)
            kf = al([2, 4, 96], F32)
            sqs2 = al([2, 4, 96], F32)
            kb = al([2, 4, 96], BF16)
            sm = al([2, 16], F32)
            rt2 = al([2, 4, 4, 2, 8], F32) if False else None
            rtA = al([4, 4, 2, 8], F32)
            rtB = al([4, 4, 2, 8], F32)
            oat = al([4, 256], BF16)
            rinv = al([4], F32)
            t_W = T("Wm")
            t_small = T("mlasmall")
            t_KT = [T() for _ in range(NTT)]
            t_QT = [T() for _ in range(4)]
            t_V1 = [T() for _ in range(NTT)]
            t_PT = [T(), T()]
            t_kvn = [T(), T()]
            t_kvnT = [T(), T()]
            t_qn = [T(), T()]
            t_qnT = [T(), T()]
            t_kf = [T(), T()]
            t_sqs2 = [T(), T()]
            t_kb = [T(), T()]
            t_sm = [T(), T()]
            t_rt2 = [T(), T()]
            t_oat = [T() for _ in range(4)]
            t_rinv = T()

            wsrc = dr["w_in"][l].rearrange("(k p) c -> p k c", p=128)
            for k in range(0, 8, 2):
                load_cast(Wm[:, k:k + 2, 0:160], t_W, wsrc[:, k:k + 2, 0:160], "p (k c) -> p k c", c=160)
                load_cast(Wm[:, k:k + 2, 160:416], t_W, wsrc[:, k:k + 2, 928:1184], "p (k c) -> p k c", c=256)
            load_cast(Wukv, t_W, dr["w_ukv"][l])
            load_cast(Wuq, t_W, dr["w_uq"][l].rearrange("(j p) c -> p j c", p=128), "p (j c) -> p j c", c=384)
            dma(kvw, dr["kvw"][l], w=[t_small])
            dma(qnw, dr["qnw"][l], w=[t_small])
            dma(qkq, dr["qkq"][l].rearrange("p (h d) -> p h d", d=96), w=[t_small])
            dma(qkk, dr["qkk"][l].rearrange("p (h d) -> p h d", d=96), w=[t_small])
            dma(rc, dr["ropec"].rearrange("p t (a b) -> p t a b", b=8), w=[t_small])
            dma(rs_, dr["ropes"].rearrange("p t (a b) -> p t a b", b=8), w=[t_small])
            mset("dve", V1[:, :, :, 64:65], 1.0, w=t_V1)
            if stage < 0.5:
                return

            def headnorm_rope(pb, wq, tt_, dst, t_dst, tbank):
                x = kf[:, pb]
                t_x_ = t_kf[pb]
                sqs = sqs2[:, pb]
                t_sqs = t_sqs2[pb]
                rt = rtA if pb == 0 else rtB
                t_rt = t_rt2[pb]
                tt("dve", sqs, x, x, ALU.mult, r=[t_x_], w=[t_sqs])
                st_ = sm[:, pb, 0:4]
                red(st_, sqs, r=[t_sqs], w=[t_sm[pb]])
                act(st_, st_, AF.Sqrt, r=[t_sm[pb], t_const], w=[t_sm[pb]], scale=1.0 / 96, bias=eps_t[:, 0:1])
                rcp(st_, st_, r=[t_sm[pb]], w=[t_sm[pb]])
                tt("dve", x, x, st_.unsqueeze(2).to_broadcast([128, 4, 96]), ALU.mult, r=[t_x_, t_sm[pb]], w=[t_x_])
                tt("dve", x, x, wq, ALU.mult, r=[t_x_, t_small], w=[t_x_])
                y = kb[:, pb]
                t_y = t_kb[pb]
                cp("dve", y[:, :, 0:64], x[:, :, 0:64], r=[t_x_], w=[t_y])
                xr = x[:, :, 64:96].rearrange("p h (a two b) -> p h a two b", two=2, b=8)
                yr = y[:, :, 64:96].rearrange("p h (a two b) -> p h a two b", two=2, b=8)
                cosb = rc[:, tt_].unsqueeze(1).to_broadcast([128, 4, 2, 8])
                sinb = rs_[:, tt_].unsqueeze(1).to_broadcast([128, 4, 2, 8])
                x1 = xr[:, :, :, 0, :]
                x2 = xr[:, :, :, 1, :]
                tt("dve", rt[:, 0], x1, cosb, ALU.mult, r=[t_x_, t_small], w=[t_rt])
                tt("dve", rt[:, 1], x2, sinb, ALU.mult, r=[t_x_, t_small], w=[t_rt])
                tt("dve", yr[:, :, :, 0, :], rt[:, 0], rt[:, 1], ALU.subtract, r=[t_rt], w=[t_y])
                tt("dve", rt[:, 2], x1, sinb, ALU.mult, r=[t_x_, t_small], w=[t_rt])
                tt("dve", rt[:, 3], x2, cosb, ALU.mult, r=[t_x_, t_small], w=[t_rt])
                tt("dve", yr[:, :, :, 1, :], rt[:, 2], rt[:, 3], ALU.add, r=[t_rt], w=[t_y])
                pst = ps[tbank][:, :].bitcast(BF16)
                for h in range(4):
                    tr(pst[0:96, h * 128:(h + 1) * 128], y[:, h, :], identb, r=[t_y, t_const], w=[tps[tbank]])
                cp("act", dst, pst[0:96, 0:512].rearrange("p (h t) -> p h t", t=128), r=[tps[tbank]], w=[t_dst])

            if os.environ.get("KSTOP"):
                P.kstop = int(os.environ["KSTOP"])
            def _kbody(tt_):
                g = grp_of_tile(tt_)
                pb = tt_ % 2
                t0 = tt_ * 128
                bz = pb
                for k in range(8):
                    mm(ps[bz][:, 0:160], hT[:, k, t0:t0 + 128], Wm[:, k, 0:160], start=(k == 0), stop=(k == 7),
                       r=[t_h[g], t_W], w=[tps[bz]])
                z = ps[bz]
                ssk = sm[:, pb, 8:9]
                act(kf[:, pb].rearrange("p h d -> p (h d)")[:, 0:128], z[:, 0:128], AF.Square,
                    r=[tps[bz]], w=[t_kf[pb], t_sm[pb]], accum=ssk)
                act(ssk, ssk, AF.Sqrt, r=[t_sm[pb], t_const], w=[t_sm[pb]], scale=1.0 / 128, bias=eps_t[:, 0:1])
                rcp(ssk, ssk, r=[t_sm[pb]], w=[t_sm[pb]])
                stt(kvn[:, pb], z[:, 0:128], ssk, kvw, ALU.mult, ALU.mult, r=[tps[bz], t_sm[pb], t_small], w=[t_kvn[pb]])
                pst = ps[2 + pb][:, :].bitcast(BF16)
                tr(pst[:, 0:128], kvn[:, pb], identb, r=[t_kvn[pb], t_const], w=[tps[2 + pb]])
                cp("act", kvnT[:, pb], pst[:, 0:128], r=[tps[2 + pb]], w=[t_kvnT[pb]])
                bkv = 4 + pb
                mm(ps[bkv][:, :], kvnT[:, pb], Wukv, r=[t_kvnT[pb], t_W], w=[tps[bkv]])
                kvv = ps[bkv][:, :].rearrange("p (h c) -> p h c", c=128)
                cp("act", V1[:, tt_, :, 0:64], kvv[:, :, 64:128], r=[tps[bkv]], w=[t_V1[tt_]])
                cp("dve", kf[:, pb, :, 0:64], kvv[:, :, 0:64], r=[tps[bkv]], w=[t_kf[pb]])
                cp("dve", kf[:, pb, :, 64:96], z[:, 128:160].unsqueeze(1).to_broadcast([128, 4, 32]),
                   r=[tps[bz]], w=[t_kf[pb]])
                if os.environ.get("NOHN") != "1":
                    headnorm_rope(pb, qkk, tt_, KT[0:96, :, t0:t0 + 128], t_KT[tt_], 2 + pb)
            for tt_ in range(0, NTT, 2):
                P.interleave([(lambda a=tt_: _kbody(a)), (lambda a=tt_ + 1: _kbody(a))])
            P.kstop = None
            tap("KT", KT[0:96], t_KT)
            tap("V1", V1, t_V1)
            if stage < 0.7:
                return

            scale = 96 ** -0.5
            qgroups = [(g, GRP(g)[0], GRP(g)[1], list(range(NTT))) for g in range(1, 5)]
            if not last:
                qgroups = [(0, 0, 256, [0, 1])] + qgroups
            it = 0
            for (g, q0, nq, ktiles) in qgroups:
                nqs = nq // 128

                def _qbody(qs, g=g, q0=q0):
                    tt_ = q0 // 128 + qs
                    pb = qs % 2
                    t0 = tt_ * 128
                    bz = pb
                    for k in range(8):
                        mm(ps[bz][:, 0:256], hT[:, k, t0:t0 + 128], Wm[:, k, 160:416], start=(k == 0), stop=(k == 7),
                           r=[t_h[g], t_W], w=[tps[bz]])
                    z = ps[bz]
                    ssq = sm[:, pb, 9:10]
                    act(qn[:, pb], z[:, 0:256], AF.Square, r=[tps[bz]], w=[t_qn[pb], t_sm[pb]], accum=ssq)
                    act(ssq, ssq, AF.Sqrt, r=[t_sm[pb], t_const], w=[t_sm[pb]], scale=1.0 / 256, bias=eps_t[:, 0:1])
                    rcp(ssq, ssq, r=[t_sm[pb]], w=[t_sm[pb]])
                    stt(qn[:, pb], z[:, 0:256], ssq, qnw, ALU.mult, ALU.mult, r=[tps[bz], t_sm[pb], t_small], w=[t_qn[pb]])
                    pst = ps[2 + pb][:, :].bitcast(BF16)
                    for j in range(2):
                        tr(pst[:, j * 128:(j + 1) * 128], qn[:, pb, j * 128:(j + 1) * 128], identb,
                           r=[t_qn[pb], t_const], w=[tps[2 + pb]])
                    cp("act", qnT[:, pb], pst[:, 0:256].rearrange("p (j t) -> p j t", t=128), r=[tps[2 + pb]], w=[t_qnT[pb]])
                    bq = 2 + pb
                    for j in range(2):
                        mm(ps[bq][:, 0:384], qnT[:, pb, j], Wuq[:, j, :], start=(j == 0), stop=(j == 1),
                           r=[t_qnT[pb], t_W], w=[tps[bq]])
                    cp("dve", kf[:, pb], ps[bq][:, 0:384].rearrange("p (h d) -> p h d", d=96), r=[tps[bq]], w=[t_kf[pb]])
                    headnorm_rope(pb, qkq, tt_, QT[0:96, :, qs * 128:(qs + 1) * 128], t_QT[qs], 2 + pb)
                for qs in range(0, nqs, 2):
                    P.interleave([(lambda a=qs: _qbody(a)), (lambda a=qs + 1: _qbody(a))])
                if g == 1:
                    tap("QT", QT[0:96], t_QT)
                if stage < 0.8:
                    continue
                for h in range(4):
                    for ki, kt in enumerate(ktiles):
                        sb = it % 2
                        it += 1
                        mm(ps[sb][:, 0:nq], KT[0:96, h, kt * 128:(kt + 1) * 128], QT[0:96, h, 0:nq],
                           r=[t_KT[kt]] + t_QT[0:nqs], w=[tps[sb]])
                        act(PT[:, sb, 0:nq], ps[sb][:, 0:nq], AF.Exp, r=[tps[sb]], w=[t_PT[sb]], scale=scale)
                        for qs in range(nqs):
                            mm(ps[4 + qs][:, 0:65], PT[:, sb, qs * 128:(qs + 1) * 128], V1[:, kt, h, :],
                               start=(ki == 0), stop=(ki == len(ktiles) - 1),
                               r=[t_PT[sb], t_V1[kt]], w=[tps[4 + qs]])
                    for qs in range(nqs):
                        rcp(rinv[:, qs:qs + 1], ps[4 + qs][:, 64:65], r=[tps[4 + qs]], w=[t_rinv])
                        ts("dve", oat[:, qs, h * 64:(h + 1) * 64], ps[4 + qs][:, 0:64], rinv[:, qs:qs + 1], None,
                           ALU.mult, r=[tps[4 + qs], t_rinv], w=[t_oat[qs]])
                for qs in range(nqs):
                    tb = 2 + qs % 2
                    pst = ps[tb][:, :].bitcast(BF16)
                    for j in range(2):
                        tr(pst[:, j * 128:(j + 1) * 128], oat[:, qs, j * 128:(j + 1) * 128], identb,
                           r=[t_oat[qs], t_const], w=[tps[tb]])
                    tq = q0 + qs * 128
                    cp("act", oT[slot][:, :, tq:tq + 128], pst[:, 0:256].rearrange("p (j t) -> p j t", t=128),
                       r=[tps[tb]], w=[t_oT[slot][g]])
            tap("oaT", oT[slot], t_oT[slot])

        def fnet_phase(l, last, slot):
            P.barrier()
            al = mk_alloc(OT0 + slot * 2 * NT * 2)
            UT = al([2, NT], BF16)
            AB = al([NTT, 512], BF16)
            CS = al([2, 512], BF16)
            tb = al([2, 2, 1024], BF16)
            c256 = al([2, 2, 256], BF16)
            t_UT = [T() for _ in range(5)]
            t_AB = [T() for _ in range(NTT)]
            t_CS = T()
            t_tb = [T(), T()]
            t_c256 = T()
            dma(CS, dr["cs64"], w=[t_CS])
            for lt in range(2):
                dma(c256[:, lt], dr["dft256"][:, lt * 128:(lt + 1) * 128, :].rearrange("c p n -> p c n"), w=[t_c256])
            wsrc = dr["w_in"][l].rearrange("(k p) c -> p k c", p=128)
            groups = [g for g in range(5) if not (last and g == 0)]
            bi = 0
            for j in range(2):
                wv, t_wv = load_ring(wsrc[:, :, 1184 + j * 128:1184 + (j + 1) * 128], "p (k c) -> p k c", c=128)
                for g in groups:
                    t0, n = GRP(g)
                    pb = bi % 4
                    bi += 1
                    for k in range(8):
                        mm(ps[pb][:, 0:n], wv[:, k, :], hT[:, k, t0:t0 + n], start=(k == 0), stop=(k == 7),
                           r=[t_wv, t_h[g]], w=[tps[pb]])
                    cp("act", UT[:, j, t0:t0 + n], ps[pb][:, 0:n], r=[tps[pb]], w=[t_UT[g]])
            for tt_ in (range(2, NTT) if last else range(NTT)):
                g = grp_of_tile(tt_)
                pb = 4 + tt_ % 4
                for j in range(2):
                    mm(ps[pb][:, :], UT[:, j, tt_ * 128:(tt_ + 1) * 128], CS[:, j, :], start=(j == 0), stop=(j == 1),
                       r=[t_UT[g], t_CS], w=[tps[pb]])
                cp("dve", AB[:, tt_, :], ps[pb][:, :], r=[tps[pb]], w=[t_AB[tt_]])
            if not last:
                for j in range(2):
                    pb = j
                    n_mm = 0
                    for lt in range(2):
                        for cs_ in range(2):
                            mm(ps[pb][:, 0:256], AB[:, lt, cs_ * 256 + j * 128:cs_ * 256 + (j + 1) * 128],
                               c256[:, lt, cs_, :], start=(n_mm == 0), stop=(n_mm == 3),
                               r=[t_AB[lt], t_c256], w=[tps[pb]])
                            n_mm += 1
                    cp("act", oT[slot][:, j, 0:256], ps[pb][:, 0:256], r=[tps[pb]], w=[t_oT[slot][0]])
            it = 0
            for half in range(2):
                banks = [4 * half + i for i in range(4)]
                for lt in range(16):
                    b_ = it % 2
                    it += 1
                    dma(tb[:, b_], dr["dft2048"][:, lt * 128:(lt + 1) * 128, half * 1024:(half + 1) * 1024]
                        .rearrange("c p n -> p c n"), w=[t_tb[b_]])
                    for j in range(2):
                        for cs_ in range(2):
                            for lg in range(2):
                                bk = banks[j * 2 + lg]
                                mm(ps[bk][:, :], AB[:, 2 + lt, cs_ * 256 + j * 128:cs_ * 256 + (j + 1) * 128],
                                   tb[:, b_, cs_, lg * 512:(lg + 1) * 512],
                                   start=(lt == 0 and cs_ == 0), stop=(lt == 15 and cs_ == 1),
                                   r=[t_AB[2 + lt], t_tb[b_]], w=[tps[bk]])
                for j in range(2):
                    for lg in range(2):
                        bk = banks[j * 2 + lg]
                        g = 1 + half * 2 + lg
                        t0 = LAT0 + half * 1024 + lg * 512
                        cp("act", oT[slot][:, j, t0:t0 + 512], ps[bk][:, :], r=[tps[bk]], w=[t_oT[slot][g]])
            tap("obT", oT[slot], t_oT[slot])

        def ret_phase(l, last, slot):
            P.barrier()
            al = mk_alloc(OT0 + slot * 2 * NT * 2)
            rrc = al([NTT, 32], F32)
            rrs = al([NTT, 32], F32)
            retc = al([6, 128], F32)
            retp = al([2], F32)
            lgrow = al([8], F32)
            lgpp = al([2, 2], F32)
            gnw = al([256], F32)
            Wr = al([8, 512], BF16)
            QZ = al([2, NT], BF16)
            KT_ = al([NT], BF16)
            Vb = al([NTT, 128], BF16)
            sg = al([NTT, 128], BF16)
            Sfp = al([NTT, 128], BF16)
            Sbn = Wr.rearrange("p k c -> p (k c)")[:, 0:NTT * 128].rearrange("p (i c) -> p i c", c=128)
            ub_off = al.o[0]
            Ub = al([NTT, 128], BF16)
            kdec = al([2, 128], F32)
            qdec = al([2, 128], F32)
            Mk = al([2, 128], F32)
            aab = al([2], F32)
            Sf = al([128], F32)
            Sb = al([128], F32)
            rt = al([2, 4, 2, 32], F32)
            Qb = al([2, 128], BF16)
            Kb = al([2, 128], BF16)
            Kd = al([2, 2, 128], BF16)
            Qd = A.alloc([2, 2, 2, 128], BF16, at=ub_off)
            attm = A.alloc([2, 2, 128], BF16, at=ub_off + 2048)
            xc2 = al([2, 2, 64], F32)
            xcq = xc2[:, 0].rearrange("p a d -> p (a d)")
            sq2 = al([2, 2, 64], F32)
            st2 = al([2, 2, 4], F32)
            od = A.alloc([2, 128], BF16, at=ub_off + 3072)
            t_c = T()
            t_lg = T()
            t_Wr = T()
            t_QK = [T() for _ in range(NTT)]
            t_Vb = [T() for _ in range(NTT)]
            t_sg = [T() for _ in range(NTT)]
            t_Sfp = [T() for _ in range(NTT)]
            t_Sbn = [T() for _ in range(NTT)]
            t_Ub = [T() for _ in range(NTT)]
            t_tab = T()
            t_S = T()
            t_rt = [T(), T()]
            t_Qb = [T(), T()]
            t_Kb = [T(), T()]
            t_Kd = [T(), T()]
            t_Qd = [T(), T()]
            t_attm = [T(), T()]
            t_gn2 = [T(), T()]
            t_od = [T(), T()]
            dma(rrc, dr["rrc"], w=[t_c])
            dma(rrs, dr["rrs"], w=[t_c])
            dma(retc, dr["retc"], w=[t_c])
            dma(retp, dr["retp"], w=[t_c])
            dma(lgrow, dr["rlog_row"][l], w=[t_lg])
            dma(lgpp, dr["rlog_pp"][l], w=[t_lg])
            dma(gnw, dr["gnw"][l], w=[t_c])
            for v in (lgrow, lgpp.rearrange("p a b -> p (a b)")):
                act(v, v, AF.Exp, r=[t_lg], w=[t_lg], scale=-1.0)
                act(v, v, AF.Ln, r=[t_lg], w=[t_lg], bias=1.0)
                ts("dve", v, v, -1.0, None, ALU.mult, r=[t_lg], w=[t_lg])
            wsrc = dr["w_in"][l].rearrange("(k p) c -> p k c", p=128)
            for pair in range(2):
                P.barrier()
                offs = [416 + pair * 128, 672 + pair * 128, 1440 + pair * 128, 1696 + pair * 128]
                for ci, c0 in enumerate(offs):
                    load_cast(Wr[:, :, ci * 128:(ci + 1) * 128], t_Wr, wsrc[:, :, c0:c0 + 128], "p (k c) -> p k c", c=128)
                for hh in range(2):
                    head = 2 * pair + hh
                    cs_ = slice(hh * 64, (hh + 1) * 64)
                    act(kdec[:, 0, cs_], lgrow[:, head:head + 1].to_broadcast([128, 64]), AF.Exp, r=[t_lg, t_c], w=[t_tab],
                        scale=retp[:, 1:2])
                    act(kdec[:, 1, cs_], lgrow[:, 4 + head:5 + head].to_broadcast([128, 64]), AF.Exp, r=[t_lg, t_c], w=[t_tab],
                        scale=retp[:, 0:1])
                    act(Mk[:, hh, :], retc[:, 2, :], AF.Exp, r=[t_lg, t_c], w=[t_tab], scale=lgrow[:, head:head + 1])
                    tt("dve", Mk[:, hh, :], Mk[:, hh, :], retc[:, 4, :], ALU.mult, r=[t_tab, t_c], w=[t_tab])
                    act(xcq, retc[:, 3, :], AF.Exp, r=[t_lg, t_c], w=[t_tab], scale=lgrow[:, 4 + head:5 + head])
                    tt("dve", xcq, xcq, retc[:, 5, :], ALU.mult, r=[t_tab, t_c], w=[t_tab])
                    stt(Mk[:, hh, :], Mk[:, hh, :], 1.0, xcq, ALU.mult, ALU.add, r=[t_tab], w=[t_tab])
                ts("dve", Mk, Mk, 0.125, None, ALU.mult, r=[t_tab], w=[t_tab])
                ts("dve", kdec, kdec, 0.125, None, ALU.mult, r=[t_tab], w=[t_tab])
                act(qdec[:, 0, :], retc[:, 0, :], AF.Exp, r=[t_lg, t_c], w=[t_tab], scale=lgpp[:, pair, 0:1])
                act(qdec[:, 1, :], retc[:, 1, :], AF.Exp, r=[t_lg, t_c], w=[t_tab], scale=lgpp[:, pair, 1:2])
                act(aab, lgpp[:, pair, :], AF.Exp, r=[t_lg], w=[t_tab], scale=128.0)
                mset("dve", Sf, 0.0, w=[t_S])
                mset("dve", Sb, 0.0, w=[t_S])
                mset("pool", QZ, 0.0, w=t_QK)

                def rope(src, cosT, sinT, dst, pb, t_dst, t_src):
                    x = src.rearrange("p (h two b) -> p h two b", two=2, b=32)
                    y = dst.rearrange("p (h two b) -> p h two b", two=2, b=32)
                    cb = cosT.unsqueeze(1).to_broadcast([128, 2, 32])
                    sb_ = sinT.unsqueeze(1).to_broadcast([128, 2, 32])
                    r_ = rt[:, pb]
                    tt("dve", r_[:, 0], x[:, :, 0, :], cb, ALU.mult, r=[t_src, t_c], w=[t_rt[pb]])
                    tt("dve", r_[:, 1], x[:, :, 1, :], sb_, ALU.mult, r=[t_src, t_c], w=[t_rt[pb]])
                    tt("dve", y[:, :, 0, :], r_[:, 0], r_[:, 1], ALU.subtract, r=[t_rt[pb]], w=[t_dst])
                    tt("dve", r_[:, 2], x[:, :, 0, :], sb_, ALU.mult, r=[t_src, t_c], w=[t_rt[pb]])
                    tt("dve", r_[:, 3], x[:, :, 1, :], cb, ALU.mult, r=[t_src, t_c], w=[t_rt[pb]])
                    tt("dve", y[:, :, 1, :], r_[:, 2], r_[:, 3], ALU.add, r=[t_rt[pb]], w=[t_dst])

                def _p1body(i):
                    g = grp_of_tile(i)
                    pb = i % 2
                    t0 = i * 128
                    bz = pb
                    for k in range(8):
                        mm(ps[bz][:, :], hT[:, k, t0:t0 + 128], Wr[:, k, :], start=(k == 0), stop=(k == 7),
                           r=[t_h[g], t_Wr], w=[tps[bz]])
                    z = ps[bz]
                    rope(z[:, 256:384], rrc[:, i, :], rrs[:, i, :], Qb[:, pb], pb, t_Qb[pb], tps[bz])
                    rope(z[:, 0:128], rrc[:, i, :], rrs[:, i, :], Kb[:, pb], pb, t_Kb[pb], tps[bz])
                    cp("act", Vb[:, i, :], z[:, 128:256], r=[tps[bz]], w=[t_Vb[i]])
                    act(sg[:, i, :], z[:, 384:512], AF.Silu, r=[tps[bz]], w=[t_sg[i]])
                    tb_ = 2 + pb
                    pst = ps[tb_][:, :].bitcast(BF16)
                    tr(pst[:, 0:128], Qb[:, pb], identb, r=[t_Qb[pb], t_const], w=[tps[tb_]])
                    tr(pst[:, 128:256], Kb[:, pb], identb, r=[t_Kb[pb], t_const], w=[tps[tb_]])
                    cp("act", QZ[0:64, 0, t0:t0 + 128], pst[0:64, 0:128], r=[tps[tb_]], w=[t_QK[i]])
                    cp("act", QZ[64:128, 1, t0:t0 + 128], pst[64:128, 0:128], r=[tps[tb_]], w=[t_QK[i]])
                    cp("act", KT_[:, t0:t0 + 128], pst[:, 128:256], r=[tps[tb_]], w=[t_QK[i]])
                    tt("pool", Kd[:, pb, 0], Kb[:, pb], kdec[:, 0], ALU.mult, r=[t_Kb[pb], t_tab], w=[t_Kd[pb]])
                    tt("pool", Kd[:, pb, 1], Kb[:, pb], kdec[:, 1], ALU.mult, r=[t_Kb[pb], t_tab], w=[t_Kd[pb]])
                    bu = 4 + pb
                    mm(ps[bu][:, 0:128], Kd[:, pb, 0], Vb[:, i, :], r=[t_Kd[pb], t_Vb[i]], w=[tps[bu]])
                    mm(ps[bu][:, 128:256], Kd[:, pb, 1], Vb[:, i, :], r=[t_Kd[pb], t_Vb[i]], w=[tps[bu]])

                for i0 in range(0, NTT, 2):
                    P.interleave([(lambda a=i0: _p1body(a)), (lambda a=i0 + 1: _p1body(a))])
                    for i in (i0, i0 + 1):
                        bu = 4 + i % 2
                        cp("dve", Sfp[:, i, :], Sf, r=[t_S], w=[t_Sfp[i]])
                        stt(Sf, Sf, aab[:, 0:1], ps[bu][:, 0:128], ALU.mult, ALU.add, r=[t_S, t_tab, tps[bu]], w=[t_S])
                        cp("act", Ub[:, i, :], ps[bu][:, 128:256], r=[tps[bu]], w=[t_Ub[i]])
                P.barrier()
                for i in [1, 0] + list(range(NTT - 1, 1, -1)):
                    cp("dve", Sbn[:, i, :], Sb, r=[t_S], w=[t_Sbn[i]])
                    stt(Sb, Sb, aab[:, 1:2], Ub[:, i, :], ALU.mult, ALU.add, r=[t_S, t_tab, t_Ub[i]], w=[t_S])
                P.barrier()
                def _p2body(i):
                    g = grp_of_tile(i)
                    pb = i % 2
                    t0 = i * 128
                    xc = xc2[:, pb]
                    sq = sq2[:, pb]
                    st_ = st2[:, pb]
                    t_gn = t_gn2[pb]
                    ba = pb
                    for hh in range(2):
                        mm(ps[ba][:, hh * 128:(hh + 1) * 128], KT_[:, t0:t0 + 128], QZ[:, hh, t0:t0 + 128],
                           r=[t_QK[i]], w=[tps[ba]])
                    tt("dve", attm[:, pb], ps[ba][:, 0:256].rearrange("p (a t) -> p a t", t=128), Mk, ALU.mult,
                       r=[tps[ba], t_tab], w=[t_attm[pb]])
                    for hh in range(2):
                        tt("pool", Qd[:, pb, hh, 0], QZ[:, hh, t0:t0 + 128], qdec[:, 0], ALU.mult, r=[t_QK[i], t_tab], w=[t_Qd[pb]])
                        tt("pool", Qd[:, pb, hh, 1], QZ[:, hh, t0:t0 + 128], qdec[:, 1], ALU.mult, r=[t_QK[i], t_tab], w=[t_Qd[pb]])
                    bo = 4 + pb
                    for hh in range(2):
                        rs_ = slice(hh * 64, (hh + 1) * 64)
                        mm(ps[bo][:, rs_], attm[:, pb, hh, :], Vb[:, i, rs_], start=True, stop=False,
                           r=[t_attm[pb], t_Vb[i]], w=[tps[bo]])
                        mm(ps[bo][:, rs_], Qd[:, pb, hh, 0, :], Sfp[:, i, rs_], start=False, stop=False,
                           r=[t_Qd[pb], t_Sfp[i]], w=[tps[bo]])
                        mm(ps[bo][:, rs_], Qd[:, pb, hh, 1, :], Sbn[:, i, rs_], start=False, stop=True,
                           r=[t_Qd[pb], t_Sbn[i]], w=[tps[bo]])
                    o = ps[bo][:, 0:128].rearrange("p (a d) -> p a d", d=64)
                    red(st_[:, 0, 0:2], o, r=[tps[bo]], w=[t_gn])
                    ts("dve", st_[:, 0, 0:2], st_[:, 0, 0:2], -1.0 / 64, None, ALU.mult, r=[t_gn], w=[t_gn])
                    tt("dve", xc, o, st_[:, 0, 0:2].unsqueeze(2).to_broadcast([128, 2, 64]), ALU.add, r=[tps[bo], t_gn], w=[t_gn])
                    tt("dve", sq, xc, xc, ALU.mult, r=[t_gn], w=[t_gn])
                    red(st_[:, 1, 0:2], sq, r=[t_gn], w=[t_gn])
                    act(st_[:, 1, 0:2], st_[:, 1, 0:2], AF.Sqrt, r=[t_gn, t_const], w=[t_gn], scale=1.0 / 64, bias=eps_t[:, 0:1])
                    rcp(st_[:, 1, 0:2], st_[:, 1, 0:2], r=[t_gn], w=[t_gn])
                    tt("dve", xc, xc, st_[:, 1, 0:2].unsqueeze(2).to_broadcast([128, 2, 64]), ALU.mult, r=[t_gn], w=[t_gn])
                    tt("dve", xc, xc, gnw[:, pair * 128:(pair + 1) * 128].rearrange("p (a d) -> p a d", d=64), ALU.mult,
                       r=[t_gn, t_c], w=[t_gn])
                    tt("dve", od[:, pb].rearrange("p (a d) -> p a d", d=64), xc,
                       sg[:, i, :].rearrange("p (a d) -> p a d", d=64), ALU.mult, r=[t_gn, t_sg[i]], w=[t_od[pb]])
                    tb_ = 2 + pb
                    pst = ps[tb_][:, :].bitcast(BF16)
                    tr(pst[:, 0:128], od[:, pb], identb, r=[t_od[pb], t_const], w=[tps[tb_]])
                    cp("act", oT[slot][:, pair, t0:t0 + 128], pst[:, 0:128], r=[tps[tb_]], w=[t_oT[slot][g]])

                for i0 in range(2 if last else 0, NTT, 2):
                    P.interleave([(lambda a=i0: _p2body(a)), (lambda a=i0 + 1: _p2body(a))])
            tap("odT", oT[slot], t_oT[slot])

        I32 = mybir.dt.int32
        TWO_PI = 2.0 * math.pi

        def s5_phase(l, last, slot):
            P.barrier()
            al = mk_alloc(OT0 + slot * 2 * NT * 2)
            uT = al([2, NT], BF16)
            yf = al([2, NT], BF16)
            E = al([2, 1024], BF16)
            Fm = al([8, 2, 128], BF16)
            Bb = al([2, 2, 512], BF16)
            Cc = al([2, 8, 128], BF16)
            Tri = al([2, 128], BF16)
            pp = al([3, 8], F32)
            sm = al([12, 8], F32)
            cst = al([4], F32)
            erow = al([2, 128], F32)
            ecol = al([4], F32)
            dvec = al([2], F32)
            xl = sm[:, 7:9, :].rearrange("p a b -> p (a b)").rearrange("p (s r) -> p s r", r=2)
            ccx = al([128], F32)
            cc = ccx[:, 0:16].rearrange("p (a b) -> p a b", b=2)
            id2 = al([16], BF16)
            woff = al.o[0]
            W = al([2, 1024], BF16)
            xx = al([2, 8, 128], BF16)
            tW = al([2, 2, 256], BF16)
            tq = al([2, 2, 512], BF16)
            ysc = A.alloc([4, 128], F32, at=al.o[0] - 2048)
            cT = al([128], BF16)
            t_u = [T() for _ in range(5)]
            t_yf = [T() for _ in range(NTT)]
            t_tab = T()
            t_pp = T()
            t_c = T()
            t_W = [T(), T()]
            t_xx = [T(), T()]
            t_tW2 = [T(), T()]
            t_tq2 = [T(), T()]
            t_xl = T()
            t_cc = T()
            t_ysc = t_tq2[1]
            t_cT = T()
            t_blk = T()

            dma(Tri, dr["s5tri"], w=[t_c])
            dma(id2, dr["s5id2"], w=[t_c])
            dma(erow, dr["s5erow"], w=[t_c])
            dma(ecol, dr["s5ecol"], w=[t_c])
            dma(dvec, dr["s5d"][l], w=[t_c])
            mset("dve", cst[:, 0:1], -math.pi, w=[t_c])

            wsrc = dr["w_in"][l].rearrange("(k p) c -> p k c", p=128)
            bi = 0
            for j in range(2):
                wv, t_wv = load_ring(wsrc[:, :, 160 + j * 128:160 + (j + 1) * 128], "p (k c) -> p k c", c=128)
                for g in range(5):
                    t0, n = GRP(g)
                    pb = bi % 4
                    bi += 1
                    for k in range(8):
                        mm(ps[pb][:, 0:n], wv[:, k, :], hT[:, k, t0:t0 + n], start=(k == 0), stop=(k == 7),
                           r=[t_wv, t_h[g]], w=[tps[pb]])
                    cp("act", uT[:, j, t0:t0 + n], ps[pb][:, 0:n], r=[tps[pb]], w=[t_u[g]])
            tap("uT", uT, t_u)

            def cplx_pow(out_re, out_im, phase, mag, n, conj, tmp):
                r_, n_i, f_, m_ = tmp
                ts("dve", r_, phase, 1.0 / TWO_PI, None, ALU.mult, r=[t_blk], w=[t_blk])
                cp("dve", n_i.bitcast(I32), r_, r=[t_blk], w=[t_blk])
                cp("dve", f_, n_i.bitcast(I32), r=[t_blk], w=[t_blk])
                tt("dve", f_, r_, f_, ALU.subtract, r=[t_blk], w=[t_blk])
                ts("dve", m_, f_, 0.0, None, ALU.is_lt, r=[t_blk], w=[t_blk])
                tt("dve", f_, f_, m_, ALU.add, r=[t_blk], w=[t_blk])
                act(r_, f_, AF.Sin, r=[t_blk, t_c], w=[t_blk], scale=TWO_PI, bias=cst[:, 0:1])
                ts("dve", f_, f_, 0.25, None, ALU.add, r=[t_blk], w=[t_blk])
                ts("dve", m_, f_, 1.0, None, ALU.is_ge, r=[t_blk], w=[t_blk])
                tt("dve", f_, f_, m_, ALU.subtract, r=[t_blk], w=[t_blk])
                act(m_, f_, AF.Sin, r=[t_blk, t_c], w=[t_blk], scale=TWO_PI, bias=cst[:, 0:1])
                stt(out_re, mag, -1.0, m_, ALU.mult, ALU.mult, r=[t_blk], w=[t_blk, t_tab])
                if conj:
                    tt("dve", out_im, mag, r_, ALU.mult, r=[t_blk], w=[t_blk, t_tab])
                else:
                    stt(out_im, mag, -1.0, r_, ALU.mult, ALU.mult, r=[t_blk], w=[t_blk, t_tab])

            glw = None
            for d_ in range(2):
                P.barrier()
                B_ = [A.alloc([256], F32, at=woff + i * 1024) for i in range(16)]
                dma(pp, dr["s5pp"][l, d_], w=[t_pp])
                act(pp[:, 2, :], pp[:, 2, :], AF.Exp, r=[t_pp], w=[t_pp])
                App = sm[:, 0, :]
                Bpp = sm[:, 1, :]
                tt("dve", App, pp[:, 0, :], pp[:, 2, :], ALU.mult, r=[t_pp], w=[t_blk])
                tt("dve", Bpp, pp[:, 1, :], pp[:, 2, :], ALU.mult, r=[t_pp], w=[t_blk])
                l1re = sm[:, 2, :]
                l1im = sm[:, 3, :]
                mg = sm[:, 4, :]
                act(mg, App, AF.Exp, r=[t_blk], w=[t_blk])
                tmp8 = [B_[0][:, 0:8], B_[0][:, 8:16], B_[0][:, 16:24], B_[0][:, 24:32]]
                cplx_pow(l1re, l1im, Bpp, mg, 8, False, tmp8)
                br = sm[:, 5, :]
                den = sm[:, 6, :]
                kre = sm[:, 7, :]
                kim = sm[:, 8, :]
                nkre = sm[:, 9, :]
                nkim = sm[:, 10, :]
                t8 = sm[:, 11, :]
                ts("dve", br, l1re, -1.0, None, ALU.add, r=[t_blk], w=[t_blk])
                tt("dve", den, pp[:, 0, :], pp[:, 0, :], ALU.mult, r=[t_pp], w=[t_blk])
                tt("dve", t8, pp[:, 1, :], pp[:, 1, :], ALU.mult, r=[t_pp], w=[t_blk])
                tt("dve", den, den, t8, ALU.add, r=[t_blk], w=[t_blk])
                rcp(den, den, r=[t_blk], w=[t_blk])
                tt("dve", kre, br, pp[:, 0, :], ALU.mult, r=[t_blk, t_pp], w=[t_blk])
                tt("dve", t8, l1im, pp[:, 1, :], ALU.mult, r=[t_blk, t_pp], w=[t_blk])
                tt("dve", kre, kre, t8, ALU.add, r=[t_blk], w=[t_blk])
                tt("dve", kre, kre, den, ALU.mult, r=[t_blk], w=[t_blk])
                tt("dve", kim, l1im, pp[:, 0, :], ALU.mult, r=[t_blk, t_pp], w=[t_blk])
                tt("dve", t8, br, pp[:, 1, :], ALU.mult, r=[t_blk, t_pp], w=[t_blk])
                tt("dve", kim, kim, t8, ALU.subtract, r=[t_blk], w=[t_blk])
                tt("dve", kim, kim, den, ALU.mult, r=[t_blk], w=[t_blk])
                ts("dve", nkre, kre, -1.0, None, ALU.mult, r=[t_blk], w=[t_blk])
                ts("dve", nkim, kim, -1.0, None, ALU.mult, r=[t_blk], w=[t_blk])
                Cre = A.alloc([8, 128], F32, at=woff + 1 * 1024)
                Cim = A.alloc([8, 128], F32, at=woff + 5 * 1024)
                for ri, Cdst in enumerate((Cre, Cim)):
                    dma(Cdst, dr["s5c"][l, d_, ri].rearrange("a s c -> s a c"), w=[t_blk])
                tC = B_[9][:, 0:128]
                for st in range(8):
                    ts("dve", tC, Cre[:, st, :], kre[:, st:st + 1], None, ALU.mult, r=[t_blk], w=[t_blk])
                    stt(Cc[:, 0, st, :], Cim[:, st, :], nkim[:, st:st + 1], tC, ALU.mult, ALU.add, r=[t_blk], w=[t_tab])
                    ts("dve", tC, Cre[:, st, :], nkim[:, st:st + 1], None, ALU.mult, r=[t_blk], w=[t_blk])
                    stt(Cc[:, 1, st, :], Cim[:, st, :], nkre[:, st:st + 1], tC, ALU.mult, ALU.add, r=[t_blk], w=[t_tab])
                for kt in range(2):
                    load_cast(Bb[:, kt], t_tab, dr["s5b"][l, d_, :, kt], "p (a c) -> p a c", c=512)
                er = erow[:, d_, :]
                for st in range(8):
                    ph = B_[9][:, 0:128]
                    mgb = B_[9][:, 128:256]
                    ts("dve", ph, er, Bpp[:, st:st + 1], None, ALU.mult, r=[t_c, t_blk], w=[t_blk])
                    act(mgb, er, AF.Exp, r=[t_c, t_blk], w=[t_blk], scale=App[:, st:st + 1])
                    tmpb = [B_[10][:, 0:128], B_[10][:, 128:256], B_[11][:, 0:128], B_[11][:, 128:256]]
                    cplx_pow(Fm[:, st, 0, :], Fm[:, st, 1, :], ph, mgb, 128, False, tmpb)
                row = A.alloc([3, 256], F32, at=woff + 1 * 1024)
                for cb in range(4):
                    dma(row, dr["s5row"][l, d_, :, :, cb * 256:(cb + 1) * 256], w=[t_blk])
                    act(row[:, 2, :], row[:, 2, :], AF.Exp, r=[t_blk], w=[t_blk])
                    Ab = B_[4]
                    Bk = B_[5]
                    tt("dve", Ab, row[:, 0, :], row[:, 2, :], ALU.mult, r=[t_blk], w=[t_blk])
                    tt("dve", Bk, row[:, 1, :], row[:, 2, :], ALU.mult, r=[t_blk], w=[t_blk])
                    ph = B_[6]
                    mgb = B_[7]
                    ts("dve", ph, Bk, ecol[:, d_:d_ + 1], None, ALU.mult, r=[t_blk, t_c], w=[t_blk])
                    act(mgb, Ab, AF.Exp, r=[t_blk, t_c], w=[t_blk], scale=ecol[:, 2 + d_:3 + d_])
                    tmpb = [B_[8], B_[9], B_[10], B_[11]]
                    cplx_pow(E[:, 0, cb * 256:(cb + 1) * 256], E[:, 1, cb * 256:(cb + 1) * 256], ph, mgb, 256, True, tmpb)
                if d_ == 0:
                    tap("s5E", E, [t_tab])
                    tap("s5F", Fm, [t_tab])
                    tap("s5C", Cc, [t_tab])
                P.barrier()
                order = list(range(NTT)) if d_ == 0 else [1, 0] + list(range(NTT - 1, 1, -1))
                lastcol = 127 if d_ == 0 else 0
                mset("dve", ccx, 0.0, w=[t_cc])
                for idx, i in enumerate(order):
                    g = grp_of_tile(i)
                    t0 = i * 128
                    pb = idx % 2
                    for nb in range(4):
                        kt = nb % 2
                        ri = nb // 2
                        mm(ps[nb][:, :], uT[:, kt, t0:t0 + 128], Bb[:, kt, ri, :], r=[t_u[g], t_tab], w=[tps[nb]])
                    for qb in range(4):
                        hb = qb // 2
                        sl = slice(qb * 256, (qb + 1) * 256)
                        pl = slice((qb % 2) * 256, (qb % 2 + 1) * 256)
                        for ri_o in range(2):
                            wb_ = (qb * 2 + ri_o) % 2
                            tw_ = tW[:, wb_]
                            t_tW = t_tW2[wb_]
                            e0, e1 = (0, 1) if ri_o == 0 else (1, 0)
                            tt("dve", tw_[:, 0, :], ps[hb][:, pl], E[:, e0, sl], ALU.mult, r=[tps[hb], t_tab], w=[t_tW])
                            tt("dve", tw_[:, 1, :], ps[2 + hb][:, pl], E[:, e1, sl], ALU.mult, r=[tps[2 + hb], t_tab], w=[t_tW])
                            tt("pool", W[:, ri_o, sl], tw_[:, 0, :], tw_[:, 1, :], ALU.subtract if ri_o == 0 else ALU.add,
                               r=[t_tW], w=[t_W[ri_o]])
                    tr(ps[2][:, 0:128], ccx, identf, r=[t_cc, t_const], w=[tps[2]])
                    cp("act", cT[:, :], ps[2][:, 0:128], r=[tps[2]], w=[t_cT])
                    tt("dve", cT[32:64, :], ps[2][32:64, 0:128], cT[32:64, :], ALU.subtract, r=[tps[2], t_cT], w=[t_cT])
                    for ri in range(2):
                        for st in range(8):
                            bk = 4 + ri * 2 + st // 4
                            mm(ps[bk][:, (st % 4) * 128:(st % 4 + 1) * 128], W[:, ri, st * 128:(st + 1) * 128], Tri[:, d_, :],
                               start=(st % 4 == 0), stop=False, r=[t_W[ri], t_c], w=[tps[bk]])
                    for ri in range(2):
                        for st in range(8):
                            bk = 4 + ri * 2 + st // 4
                            jj = st * 2 + ri
                            mm(ps[bk][:, (st % 4) * 128:(st % 4 + 1) * 128], cT[:, :],
                               id2[:, jj:jj + 1].to_broadcast([128, 128]),
                               start=False, stop=True, r=[t_cT, t_c], w=[tps[bk]])
                    for hf in range(2):
                        Sre = ps[4 + hf][:, :].rearrange("p (a t) -> p a t", t=128)
                        Sim = ps[6 + hf][:, :].rearrange("p (a t) -> p a t", t=128)
                        Fre = Fm[:, 4 * hf:4 * hf + 4, 0, :]
                        Fim = Fm[:, 4 * hf:4 * hf + 4, 1, :]
                        for ri_o in range(2):
                            qb_ = (hf * 2 + ri_o) % 2
                            t_tq = t_tq2[qb_]
                            q0 = tq[:, qb_, 0, :].rearrange("p (a t) -> p a t", t=128)
                            q1 = tq[:, qb_, 1, :].rearrange("p (a t) -> p a t", t=128)
                            f0, f1 = (Fre, Fim) if ri_o == 0 else (Fim, Fre)
                            op_ = ALU.subtract if ri_o == 0 else ALU.add
                            tt("dve", q0, Sre, f0, ALU.mult, r=[tps[4 + hf], t_tab], w=[t_tq])
                            tt("dve", q1, Sim, f1, ALU.mult, r=[tps[6 + hf], t_tab], w=[t_tq])
                            tt("dve", xl[:, 4 * hf:4 * hf + 4, ri_o], q0[:, :, lastcol], q1[:, :, lastcol], op_, r=[t_tq], w=[t_xl])
                            tt("pool", xx[:, ri_o, 4 * hf:4 * hf + 4, :], q0, q1, op_, r=[t_tq], w=[t_xx[ri_o]])
                    ta_ = sm[:, 5, :]
                    tb_ = sm[:, 6, :]
                    tt("dve", ta_, l1re, xl[:, :, 0], ALU.mult, r=[t_xl, t_blk], w=[t_blk])
                    tt("dve", tb_, l1im, xl[:, :, 1], ALU.mult, r=[t_xl, t_blk], w=[t_blk])
                    tt("dve", cc[:, :, 0], ta_, tb_, ALU.subtract, r=[t_blk], w=[t_cc])
                    tt("dve", ta_, l1re, xl[:, :, 1], ALU.mult, r=[t_xl, t_blk], w=[t_blk])
                    tt("dve", tb_, l1im, xl[:, :, 0], ALU.mult, r=[t_xl, t_blk], w=[t_blk])
                    tt("dve", cc[:, :, 1], ta_, tb_, ALU.add, r=[t_blk], w=[t_cc])
                    cp("dve", ccx[:, 32:48], ccx[:, 0:16], r=[t_cc], w=[t_cc])
                    if last and i < 2:
                        continue
                    for j in range(2):
                        n_mm = 0
                        for st in range(4 * j, 4 * j + 4):
                            for ri in range(2):
                                mm(ps[j][:, 0:128], Cc[:, ri, st, :], xx[:, ri, st, :], start=(n_mm == 0), stop=(n_mm == 7),
                                   r=[t_tab, t_xx[ri]], w=[tps[j]])
                                n_mm += 1
                        if d_ == 0:
                            cp("act", yf[:, j, t0:t0 + 128], ps[j][:, 0:128], r=[tps[j]], w=[t_yf[i]])
                        else:
                            y = ysc[:, 0, :]
                            tt("dve", y, ps[j][:, 0:128], yf[:, j, t0:t0 + 128], ALU.add, r=[tps[j], t_yf[i]], w=[t_ysc])
                            stt(y, uT[:, j, t0:t0 + 128], dvec[:, j:j + 1], y, ALU.mult, ALU.add, r=[t_u[g], t_c, t_ysc], w=[t_ysc])
                            tt("dve", ysc[:, 1, :], y, y, ALU.mult, r=[t_ysc], w=[t_ysc])
                            ts("dve", ysc[:, 1, :], ysc[:, 1, :], 0.044715, 1.0, ALU.mult, ALU.add, r=[t_ysc], w=[t_ysc])
                            tt("dve", ysc[:, 1, :], ysc[:, 1, :], y, ALU.mult, r=[t_ysc], w=[t_ysc])
                            act(ysc[:, 2, :], ysc[:, 1, :], AF.Sigmoid, r=[t_ysc], w=[t_ysc], scale=1.5957691216057308)
                            tt("dve", yf[:, j, t0:t0 + 128], y, ysc[:, 2, :], ALU.mult, r=[t_ysc], w=[t_yf[i]])
            tap("s5g", yf, t_yf)
            P.barrier()
            glw = A.alloc([2, 512], BF16, at=woff)
            t_glw = T()
            load_cast(glw[:, 0], t_glw, dr["s5_w_glu"][l][0:128, :])
            load_cast(glw[:, 1], t_glw, dr["s5_w_glu"][l][128:256, :])
            sgt = A.alloc([512], F32, at=woff + 2048)
            t_sgt = T()
            bi = 0
            for g in range(1 if last else 0, 5):
                t0, n = GRP(g)
                tiles = list(range(t0 // 128, (t0 + n) // 128))
                for j in range(2):
                    pv = bi % 2
                    pg = 2 + bi % 2
                    bi += 1
                    for kt in range(2):
                        mm(ps[pv][:, 0:n], glw[:, kt, j * 128:(j + 1) * 128], yf[:, kt, t0:t0 + n], start=(kt == 0), stop=(kt == 1),
                           r=[t_glw] + [t_yf[i] for i in tiles], w=[tps[pv]])
                    for kt in range(2):
                        mm(ps[pg][:, 0:n], glw[:, kt, 256 + j * 128:256 + (j + 1) * 128], yf[:, kt, t0:t0 + n],
                           start=(kt == 0), stop=(kt == 1), r=[t_glw] + [t_yf[i] for i in tiles], w=[tps[pg]])
                    act(sgt[:, 0:n], ps[pg][:, 0:n], AF.Sigmoid, r=[tps[pg]], w=[t_sgt])
                    tt("dve", oT[slot][:, j, t0:t0 + n], ps[pv][:, 0:n], sgt[:, 0:n], ALU.mult, r=[tps[pv], t_sgt],
                       w=[t_oT[slot][g]])
            tap("ocT", oT[slot], t_oT[slot])

        SLOT_OF = {0: 3, 1: 0, 2: 1, 3: 2}

        def merge_phase(l, last):
            P.barrier()
            al = mk_alloc(OT0)
            mT = al([8, NT], BF16)
            sig = al([512], F32)
            acc = al([512], F32)
            t_m = [T() for _ in range(5)]
            t_sig = T()
            t_acc = T()
            wsrc = dr["w_in"][l].rearrange("(k p) c -> p k c", p=128)
            wbsrc = dr["w_branch"][l].rearrange("n (j p) d -> p n j d", p=128)
            groups = [g for g in range(5) if not (last and g == 0)]
            bi = 0
            for d in range(8):
                gw = []
                for n in range(4):
                    c0 = 1952 + n * 1024 + d * 128
                    gw.append(load_ring(wsrc[:, :, c0:c0 + 128], "p (k c) -> p k c", c=128))
                wb, t_wb = load_ring(wbsrc[:, :, :, d * 128:(d + 1) * 128], "p (n j c) -> p n j c", j=2, c=128)
                for g in groups:
                    t0, n_ = GRP(g)
                    for n in range(4):
                        sl = SLOT_OF[n]
                        pa = bi % 2
                        pb = 2 + bi % 2
                        bi += 1
                        wv, t_wv = gw[n]
                        for k in range(8):
                            mm(ps[pa][:, 0:n_], wv[:, k, :], hT[:, k, t0:t0 + n_], start=(k == 0), stop=(k == 7),
                               r=[t_wv, t_h[g]], w=[tps[pa]])
                        for j in range(2):
                            mm(ps[pb][:, 0:n_], wb[:, n, j, :], oT[sl][:, j, t0:t0 + n_], start=(j == 0), stop=(j == 1),
                               r=[t_wb, t_oT[sl][g]], w=[tps[pb]])
                        act(sig[:, 0:n_], ps[pa][:, 0:n_], AF.Sigmoid, r=[tps[pa]], w=[t_sig])
                        if n == 0:
                            tt("dve", acc[:, 0:n_], ps[pb][:, 0:n_], sig[:, 0:n_], ALU.mult, r=[tps[pb], t_sig], w=[t_acc])
                        else:
                            tt("dve", ps[pb][:, 0:n_], ps[pb][:, 0:n_], sig[:, 0:n_], ALU.mult, r=[tps[pb], t_sig], w=[tps[pb]])
                            if n < 3:
                                tt("dve", acc[:, 0:n_], acc[:, 0:n_], ps[pb][:, 0:n_], ALU.add, r=[tps[pb], t_acc], w=[t_acc])
                            else:
                                tt("dve", mT[:, d, t0:t0 + n_], acc[:, 0:n_], ps[pb][:, 0:n_], ALU.add,
                                   r=[tps[pb], t_acc], w=[t_m[g]])
            tap("mT", mT, t_m)
            wosrc = dr["w_out"][l].rearrange("(k p) c -> p k c", p=128)
            for d in range(8):
                wv, t_wv = load_ring(wosrc[:, :, d * 128:(d + 1) * 128], "p (k c) -> p k c", c=128)
                for g in groups:
                    t0, n_ = GRP(g)
                    s_ = 1 if g == 0 else 0
                    pb = 4 + bi % 4
                    bi += 1
                    for k in range(8):
                        mm(ps[pb][:, 0:n_], wv[:, k, :], mT[:, k, t0:t0 + n_], start=(k == 0), stop=(k == 7),
                           r=[t_wv, t_m[g]], w=[tps[pb]])
                    stt(xT[:, d, t0:t0 + n_], ps[pb][:, 0:n_], mod[:, l, 16 + d, s_:s_ + 1], xT[:, d, t0:t0 + n_],
                        ALU.mult, ALU.add, r=[tps[pb], t_mod, t_x[d][g]], w=[t_x[d][g]])

        def ffn_phase(l, last):
            groups = [g for g in range(5) if not (last and g == 0)]
            norm_phase(l, 1, groups)
            P.barrier()
            al = mk_alloc(A.nbytes)
            aT = al([8, NT], BF16)
            rl = al([2, 512], F32)
            t_a = [[T() for _ in range(5)] for _ in range(8)]
            t_rl = [T(), T()]
            w1src = dr["ffn_w1"][l].rearrange("(k p) c -> p k c", p=128)
            w2src = dr["ffn_w2"][l].rearrange("(f p) c -> p f c", p=128)
            bi = 0
            for fb in range(4):
                for f in range(8):
                    F_ = fb * 8 + f
                    wv, t_wv = load_ring(w1src[:, :, F_ * 128:(F_ + 1) * 128], "p (k c) -> p k c", c=128)
                    for g in groups:
                        t0, n_ = GRP(g)
                        pb = bi % 4
                        rb = bi % 2
                        bi += 1
                        for k in range(8):
                            mm(ps[pb][:, 0:n_], wv[:, k, :], hT[:, k, t0:t0 + n_], start=(k == 0), stop=(k == 7),
                               r=[t_wv, t_h[g]], w=[tps[pb]])
                        act(rl[:, rb, 0:n_], ps[pb][:, 0:n_], AF.Relu, r=[tps[pb]], w=[t_rl[rb]])
                        tt("pool", aT[:, f, t0:t0 + n_], rl[:, rb, 0:n_], rl[:, rb, 0:n_], ALU.mult, r=[t_rl[rb]], w=[t_a[f][g]])
                for d in range(8):
                    wv, t_wv = load_ring(w2src[:, fb * 8:(fb + 1) * 8, d * 128:(d + 1) * 128], "p (f c) -> p f c", c=128)
                    for g in groups:
                        t0, n_ = GRP(g)
                        s_ = 1 if g == 0 else 0
                        pb = 4 + bi % 4
                        bi += 1
                        for f in range(8):
                            mm(ps[pb][:, 0:n_], wv[:, f, :], aT[:, f, t0:t0 + n_], start=(f == 0), stop=(f == 7),
                               r=[t_wv, t_a[f][g]], w=[tps[pb]])
                        stt(xT[:, d, t0:t0 + n_], ps[pb][:, 0:n_], mod[:, l, 40 + d, s_:s_ + 1], xT[:, d, t0:t0 + n_],
                            ALU.mult, ALU.add, r=[tps[pb], t_mod, t_x[d][g]], w=[t_x[d][g]])

        for l in range(DEPTH):
            last = (l == DEPTH - 1)
            norm_phase(l, 0, list(range(5)))
            if l == 0:
                tap("hT", hT, t_h)
            if stage <= 0.1:
                break
            if "mla" in mixers:
                mla_phase(l, last, 3)
            if "ret" in mixers:
                ret_phase(l, last, 2)
            if "s5" in mixers:
                s5_phase(l, last, 1)
            if "fnet" in mixers:
                fnet_phase(l, last, 0)
            if stage <= 1:
                break
            merge_phase(l, last)
            ffn_phase(l, last)
            if l == 0:
                tap("x1", xT, [t for k in range(8) for t in t_x[k]])
            if stage <= 2:
                break

        osrc = outT.rearrange("(k p) t -> p k t", p=128)
        for k in range(8):
            out_handles.append(dma(osrc[:, k, :], xT[:, k, LAT0:NT], r=t_x[k]))
        P.wait_all("sp", out_handles)
        P.emit()
    return nc


_CACHE = {}


def _specs_of(d):
    sp = {}
    for k, v in d.items():
        sp[k] = (v.shape, BF16 if v.dtype == NPBF else F32)
    return sp


def run(inputs, stage=99, taps=(), ncores=8, mixers=("mla", "s5", "ret", "fnet")):
    com = prep_common(inputs)
    cores = [prep_core(inputs, b) for b in range(ncores)]
    in_maps = [dict(com, **c) for c in cores]
    nc = build(_specs_of(in_maps[0]), stage=stage, taps=taps, mixers=mixers)
    res = run_bass_kernel_spmd(nc, in_maps, core_ids=list(range(ncores)))
    return res.results


def kernel(**inputs):
    inputs = {k: np.asarray(v) for k, v in inputs.items()}
    res = run(inputs)
    out = np.stack([r["outT"].T for r in res], axis=0)
    return np.ascontiguousarray(out.astype(np.float32))
```
